# Optimizing a Trainium2 kernel written in Bass

```python
import math
import jax, jax.numpy as jnp
from jax import lax
import numpy as np

D_MODEL = 1024
BATCH = 8
SEQ = 4096
DEPTH = 1

D_MIX = 2 * D_MODEL
D_ATT = D_MIX // 2
ATT_HEAD_DIM = 64
ATT_HEADS = D_ATT // ATT_HEAD_DIM
ATT_STEPS = 128
DILATIONS = (1, 4, 16)
D_SSM = D_MIX - D_ATT
SSM_HEAD_DIM = 64
SSM_HEADS = D_SSM // SSM_HEAD_DIM
SSM_GROUPS = 2
SSM_STATE = 128
CONV_WIDTH = 4
CHUNK = 128
D_CONV = D_SSM + 2 * SSM_GROUPS * SSM_STATE
D_IN_PROJ = 4 * D_ATT + D_SSM + D_CONV + SSM_HEADS
SPLITS = (D_ATT, 2 * D_ATT, 3 * D_ATT, 4 * D_ATT, 4 * D_ATT + D_SSM, 4 * D_ATT + D_SSM + D_CONV)
NORM_EPS = 1e-5

kernel_name = 'hybrid_dilated_attn_ssd_block'


def rms_norm(u, g):
    uf = u.astype(jnp.float32)
    uf = uf * lax.rsqrt(jnp.mean(uf * uf, axis=-1, keepdims=True) + NORM_EPS)
    return (uf * g.astype(jnp.float32)).astype(u.dtype)


def layer_norm(u, g, b):
    uf = u.astype(jnp.float32)
    mu = jnp.mean(uf, axis=-1, keepdims=True)
    var = jnp.mean(jnp.square(uf - mu), axis=-1, keepdims=True)
    out = (uf - mu) * lax.rsqrt(var + NORM_EPS) * g.astype(jnp.float32) + b.astype(jnp.float32)
    return out.astype(u.dtype)


def alibi_slopes(n_heads):
    return jnp.asarray(2.0 ** (-8.0 * np.arange(1, n_heads + 1) / n_heads), dtype=jnp.float32)


def dilated_window_attention(q, k, v, slopes, dilation):
    B, S, H, E = q.shape
    unit = dilation * ATT_STEPS
    s_pad = -(-S // unit) * unit
    nb = s_pad // unit
    pad = ((0, 0), (0, s_pad - S), (0, 0), (0, 0))

    def to_blocks(t):
        return jnp.pad(t, pad).reshape(B, nb, ATT_STEPS, dilation, H, E)

    def with_prev(t):
        prev = jnp.pad(t, ((0, 0), (1, 0), (0, 0), (0, 0), (0, 0), (0, 0)))[:, :-1]
        return jnp.concatenate([prev, t], axis=2)

    qb = to_blocks(q)
    kk = with_prev(to_blocks(k))
    vv = with_prev(to_blocks(v))
    scores = jnp.einsum('bnqrhe,bnkrhe->bnrhqk', qb, kk).astype(jnp.float32)

    n_idx = jnp.arange(nb)[:, None, None]
    jq = n_idx * ATT_STEPS + jnp.arange(ATT_STEPS)[None, :, None]
    jk = (n_idx - 1) * ATT_STEPS + jnp.arange(2 * ATT_STEPS)[None, None, :]
    steps = jq - jk
    valid = (jk >= 0) & (steps >= 0) & (steps <= ATT_STEPS)
    dist = (steps * dilation).astype(jnp.float32)
    bias = -slopes[None, None, :, None, None] * dist[:, None, None, :, :]
    scores = jnp.where(valid[:, None, None], scores + bias, -jnp.inf)

    lse = jax.nn.logsumexp(scores, axis=-1)
    p = jnp.exp(scores - lse[..., None]).astype(vv.dtype)
    o = jnp.einsum('bnrhqk,bnkrhe->bnqrhe', p, vv)
    o = o.reshape(B, s_pad, H, E)[:, :S]
    lse = lse.transpose(0, 1, 4, 2, 3).reshape(B, s_pad, H)[:, :S]
    return o, lse


def causal_depthwise_conv(u, w, bias):
    out = lax.conv_general_dilated(
        u, w[:, None, :].astype(u.dtype), window_strides=(1,), padding=((CONV_WIDTH - 1, 0),),
        dimension_numbers=('NWC', 'WIO', 'NWC'), feature_group_count=u.shape[-1])
    return out + bias.astype(u.dtype)


def segsum(a):
    cs = jnp.cumsum(a, axis=-1)
    l = a.shape[-1]
    diff = cs[..., :, None] - cs[..., None, :]
    mask = jnp.tril(jnp.ones((l, l), dtype=bool))
    return jnp.where(mask, diff, -jnp.inf)


def ssd_chunked(x, dt, a, b, c):
    Bsz, S, H, P = x.shape
    G, N = b.shape[2], b.shape[3]
    J = H // G
    nc = S // CHUNK
    xg = (x * dt[..., None]).reshape(Bsz, nc, CHUNK, G, J, P)
    adt = (dt * a).reshape(Bsz, nc, CHUNK, G, J).transpose(0, 1, 3, 4, 2)
    bc = b.reshape(Bsz, nc, CHUNK, G, N)
    cc = c.reshape(Bsz, nc, CHUNK, G, N)
    a_cs = jnp.cumsum(adt, axis=-1)

    decay = jnp.exp(segsum(adt))
    cb = jnp.einsum('bclgn,bcsgn->bcgls', cc, bc)
    y_diag = jnp.einsum('bcgjls,bcsgjp->bclgjp', cb[:, :, :, None] * decay, xg)

    decay_to_end = jnp.exp(a_cs[..., -1:] - a_cs).transpose(0, 1, 4, 2, 3)[..., None]
    states = jnp.einsum('bcsgn,bcsgjp->bcgjpn', bc, xg * decay_to_end)

    chunk_decay = jnp.exp(a_cs[..., -1])

    def step(h, inp):
        st, dec = inp
        return h * dec[..., None, None] + st, h

    h0 = jnp.zeros((Bsz, G, J, P, N), dtype=jnp.float32)
    _, prev = lax.scan(step, h0, (jnp.moveaxis(states, 1, 0), jnp.moveaxis(chunk_decay, 1, 0)))
    prev = jnp.moveaxis(prev, 0, 1)

    state_decay = jnp.exp(a_cs).transpose(0, 1, 4, 2, 3)[..., None]
    y_off = jnp.einsum('bclgn,bcgjpn->bclgjp', cc, prev) * state_decay
    return (y_diag + y_off).reshape(Bsz, S, H, P)


def setup_inputs(seed: int = 0) -> dict:
    key = jax.random.key(seed)
    ks = jax.random.split(key, 14)
    beta = (8.0 * DEPTH) ** -0.25
    f32 = jnp.float32
    x = jax.random.normal(ks[0], (BATCH, SEQ, D_MODEL), f32)
    col_scale = np.ones((D_IN_PROJ,), np.float32)
    col_scale[2 * D_ATT:3 * D_ATT] = beta
    col_scale[4 * D_ATT + D_SSM:4 * D_ATT + 2 * D_SSM] = beta
    w_in = jax.random.normal(ks[1], (DEPTH, D_MODEL, D_IN_PROJ), f32) * (D_MODEL ** -0.5) * jnp.asarray(col_scale)
    conv_w = jax.random.normal(ks[2], (DEPTH, CONV_WIDTH, D_CONV), f32) * (CONV_WIDTH ** -0.5)
    conv_b = 0.02 * jax.random.normal(ks[3], (DEPTH, D_CONV), f32)
    dt0 = jnp.exp(jax.random.uniform(ks[4], (DEPTH, SSM_HEADS), f32, math.log(1e-3), math.log(1e-1)))
    dt_bias = dt0 + jnp.log(-jnp.expm1(-dt0))
    a_log = jnp.log(jax.random.uniform(ks[5], (DEPTH, SSM_HEADS), f32, 1.0, 16.0))
    d_skip = 1.0 + 0.1 * jax.random.normal(ks[6], (DEPTH, SSM_HEADS), f32)
    att_norm_g = 1.0 + 0.1 * jax.random.normal(ks[7], (DEPTH, D_ATT), f32)
    ssm_norm_g = 1.0 + 0.1 * jax.random.normal(ks[8], (DEPTH, D_SSM), f32)
    w_out = jax.random.normal(ks[9], (DEPTH, D_MIX, D_MODEL), f32) * (D_MIX ** -0.5) * beta
    ln_g = 1.0 + 0.1 * jax.random.normal(ks[10], (DEPTH, D_MODEL), f32)
    ln_b = 0.02 * jax.random.normal(ks[11], (DEPTH, D_MODEL), f32)
    return {'x': x, 'w_in': w_in, 'conv_w': conv_w, 'conv_b': conv_b, 'dt_bias': dt_bias,
            'a_log': a_log, 'd_skip': d_skip, 'att_norm_g': att_norm_g, 'ssm_norm_g': ssm_norm_g,
            'w_out': w_out, 'ln_g': ln_g, 'ln_b': ln_b}


def reference(x, w_in, conv_w, conv_b, dt_bias, a_log, d_skip, att_norm_g, ssm_norm_g, w_out, ln_g, ln_b):
    alpha = (2.0 * DEPTH) ** 0.25
    slopes = alibi_slopes(ATT_HEADS)
    Bsz, S, _ = x.shape
    for layer in range(DEPTH):
        proj = jnp.einsum('bsd,de->bse', x, w_in[layer])
        q, k, v, z_att, z_ssm, xbc, dt_raw = jnp.split(proj, list(SPLITS), axis=-1)

        q = q.reshape(Bsz, S, ATT_HEADS, ATT_HEAD_DIM) * (ATT_HEAD_DIM ** -0.5)
        k = k.reshape(Bsz, S, ATT_HEADS, ATT_HEAD_DIM)
        v = v.reshape(Bsz, S, ATT_HEADS, ATT_HEAD_DIM)
        o1, l1 = dilated_window_attention(q, k, v, slopes, DILATIONS[0])
        o2, l2 = dilated_window_attention(q, k, v, slopes, DILATIONS[1])
        o3, l3 = dilated_window_attention(q, k, v, slopes, DILATIONS[2])
        wts = jax.nn.softmax(jnp.stack([l1, l2, l3], axis=0), axis=0)
        y_att = (wts[0][..., None] * o1 + wts[1][..., None] * o2 + wts[2][..., None] * o3).astype(x.dtype)
        y_att = rms_norm(y_att.reshape(Bsz, S, D_ATT) * jax.nn.silu(z_att), att_norm_g[layer])

        xbc = jax.nn.silu(causal_depthwise_conv(xbc, conv_w[layer], conv_b[layer]))
        xs, bm, cm = jnp.split(xbc, [D_SSM, D_SSM + SSM_GROUPS * SSM_STATE], axis=-1)
        dt = jax.nn.softplus(dt_raw.astype(jnp.float32) + dt_bias[layer].astype(jnp.float32))
        a = -jnp.exp(a_log[layer].astype(jnp.float32))
        xs_h = xs.reshape(Bsz, S, SSM_HEADS, SSM_HEAD_DIM).astype(jnp.float32)
        y = ssd_chunked(xs_h, dt, a,
                        bm.reshape(Bsz, S, SSM_GROUPS, SSM_STATE).astype(jnp.float32),
                        cm.reshape(Bsz, S, SSM_GROUPS, SSM_STATE).astype(jnp.float32))
        y = y + d_skip[layer].astype(jnp.float32)[:, None] * xs_h
        y_ssm = rms_norm(y.reshape(Bsz, S, D_SSM).astype(x.dtype) * jax.nn.silu(z_ssm), ssm_norm_g[layer])

        mix = jnp.concatenate([y_att, y_ssm], axis=-1)
        out = jnp.einsum('bse,ed->bsd', mix, w_out[layer])
        x = layer_norm(alpha * x + out, ln_g[layer], ln_b[layer])
    return x
```

```python
import contextlib
import os
_SKIP = set(os.environ.get('KSKIP', '').split(','))
import numpy as np
import ml_dtypes
import concourse.bass as bass
import concourse.mybir as mybir
from concourse.bass_utils import run_bass_kernel_spmd

F32 = mybir.dt.float32
BF16 = mybir.dt.bfloat16
AF = mybir.ActivationFunctionType
ALU = mybir.AluOpType
AX = mybir.AxisListType

S = 4096
D = 1024
NH = 16
HD = 64
DIL = (1, 4, 16)
D_IN = 6672
OFF_Q, OFF_K, OFF_V, OFF_ZA, OFF_ZS, OFF_XBC, OFF_DT = 0, 1024, 2048, 3072, 4096, 5120, 6656
EPS = 1e-5


class Ev:
    __slots__ = ("eng", "idx", "sem", "val")

    def __init__(self, eng, idx, sem=None, val=None):
        self.eng, self.idx, self.sem, self.val = eng, idx, sem, val


class Prog:
    ENGS = ("sync", "scalar", "vector", "gpsimd", "tensor")

    def __init__(self, nc):
        self.nc = nc
        self.q = {e: [] for e in self.ENGS}
        self.dma_cnt = {}

    def op(self, eng, fn, deps=()):
        lst = self.q[eng]
        ev = Ev(eng, len(lst))
        lst.append([fn, [d for d in deps if d is not None], ev, False])
        return ev

    def dma(self, eng, out, in_, slot, deps=()):
        self.dma_cnt[slot] = self.dma_cnt.get(slot, 0) + 16
        ev = Ev(eng, len(self.q[eng]), sem=slot, val=self.dma_cnt[slot])
        self.q[eng].append([lambda e, o=out, i=in_: e.dma_start(out=o, in_=i),
                            [d for d in deps if d is not None], ev, True])
        return ev

    def emit(self, final_waits):
        nc = self.nc
        ref = {e: set() for e in self.ENGS}
        for e in self.ENGS:
            for fn, deps, ev, is_dma in self.q[e]:
                for d in deps:
                    if d.sem is None or d.sem.startswith("e_"):
                        ref[d.eng].add(d.idx)
        for d in final_waits:
            if d.sem is None:
                ref[d.eng].add(d.idx)
        for e in self.ENGS:
            c = 0
            for i, item in enumerate(self.q[e]):
                if item[3]:
                    continue
                if i in ref[e]:
                    c += 1
                    item[2].sem = "e_" + e
                    item[2].val = c
            assert c < 60000, (e, c)
        for s, v in self.dma_cnt.items():
            assert v < 60000, (s, v)
        names = ["e_" + e for e in self.ENGS] + sorted(self.dma_cnt)
        with contextlib.ExitStack() as st:
            sems = {n: st.enter_context(nc.semaphore(n)) for n in names}
            block = st.enter_context(nc.Block())
            for e in self.ENGS:
                items = self.q[e]
                fw = final_waits if e == "sync" else ()

                def body(eng, items=items, fw=fw):
                    seen = {}
                    for fn, deps, ev, is_dma in items:
                        need = {}
                        for d in deps:
                            assert d.sem is not None
                            if need.get(d.sem, 0) < d.val:
                                need[d.sem] = d.val
                        for sname, v in need.items():
                            if seen.get(sname, 0) < v:
                                eng.wait_ge(sems[sname], v)
                                seen[sname] = v
                        ins = fn(eng)
                        if is_dma:
                            ins.then_inc(sems[ev.sem], 16)
                        elif ev.sem is not None:
                            ins.then_inc(sems[ev.sem], 1)
                    for d in fw:
                        if seen.get(d.sem, 0) < d.val:
                            eng.wait_ge(sems[d.sem], d.val)
                            seen[d.sem] = d.val

                getattr(block, e)(body)


def latest(*evs):
    return [e for e in evs if e is not None]


def _bf16(a):
    return np.asarray(a, np.float32).astype(ml_dtypes.bfloat16)


def make_constants():
    c = {}
    slopes = 2.0 ** (-8.0 * np.arange(1, NH + 1) / NH)
    t = np.arange(S)
    hi_pos = (t >> 7).astype(np.float32)
    lo_pos = (t & 127).astype(np.float32)
    aug = np.zeros((NH, 2, 12, S), np.float32)
    for h in range(NH):
        cc = np.float64(slopes[h])
        c1 = np.float64(_bf16(cc).astype(np.float64))
        c2 = np.float64(_bf16(cc - c1).astype(np.float64))
        c3 = np.float64(_bf16(cc - c1 - c2).astype(np.float64))
        for j, cj in enumerate((c1, c2, c3)):
            aug[h, 0, j] = 128.0 * cj
            aug[h, 0, 3 + j] = cj
            aug[h, 0, 6 + j] = hi_pos
            aug[h, 0, 9 + j] = lo_pos
            aug[h, 1, j] = hi_pos
            aug[h, 1, 3 + j] = lo_pos
            aug[h, 1, 6 + j] = -128.0 * cj
            aug[h, 1, 9 + j] = -cj
    c["aug"] = _bf16(aug)
    ki = np.arange(128)[:, None]
    qi = np.arange(128)[None, :]
    mprev = np.where(ki >= qi, 0.0, -30000.0).astype(np.float32)
    mcur = np.where(ki <= qi, 0.0, -30000.0).astype(np.float32)
    c["mask4"] = np.concatenate([mprev, mcur, mprev, mcur], axis=1).astype(np.float32)
    c["tri"] = np.triu(np.ones((128, 128), np.float32))
    c["ident"] = np.eye(128, dtype=np.float32)
    c["identb"] = _bf16(np.eye(128, dtype=np.float32))
    si = np.arange(128)[:, None]
    li = np.arange(128)[None, :]
    c["negm4"] = _bf16(np.tile(np.where(si > li, -30000.0, 0.0), (1, 4)))
    return c


def I_act(P, out, in_, func, deps=(), bias=None, scale=None, accum_out=None, eng="scalar"):
    kw = {}
    if bias is not None:
        kw["bias"] = bias
    if scale is not None:
        kw["scale"] = scale
    if accum_out is not None:
        kw["accum_out"] = accum_out
    return P.op(eng, lambda e: e.activation(out=out, in_=in_, func=func, **kw), deps)


def I_copy(P, eng, out, in_, deps=()):
    if eng == "scalar":
        return P.op(eng, lambda e: e.activation(out=out, in_=in_, func=AF.Copy), deps)
    return P.op(eng, lambda e: e.tensor_copy(out=out, in_=in_), deps)


def I_tt(P, eng, out, in0, in1, op, deps=()):
    return P.op(eng, lambda e: e.tensor_tensor(out=out, in0=in0, in1=in1, op=op), deps)


def I_ts(P, eng, out, in0, s1, s2, op0, op1=None, deps=(), accum_out=None):
    kw = {}
    if op1 is not None:
        kw["op1"] = op1
    if accum_out is not None:
        kw["accum_out"] = accum_out
    return P.op(eng, lambda e: e.tensor_scalar(out=out, in0=in0, scalar1=s1, scalar2=s2, op0=op0, **kw), deps)


def I_stt(P, eng, out, in0, scalar, in1, op0, op1, deps=()):
    return P.op(eng, lambda e: e.scalar_tensor_tensor(out=out, in0=in0, scalar=scalar, in1=in1, op0=op0, op1=op1), deps)


def I_mm(P, out, lhsT, rhs, start, stop, deps=(), skip=True):
    return P.op("tensor", lambda e: e.matmul(out, lhsT=lhsT, rhs=rhs, start=start, stop=stop,
                                             skip_group_check=skip), deps)


def I_memset(P, eng, ap, val, deps=()):
    return P.op(eng, lambda e: e.memset(ap, val), deps)


def barrier(P):
    evs = []
    for e in P.ENGS:
        for item in reversed(P.q[e]):
            if not item[3]:
                evs.append(item[2])
                break
    last = {}
    for e in P.ENGS:
        for item in P.q[e]:
            if item[3]:
                last[item[2].sem] = item[2]
    evs += list(last.values())
    out = []
    for e in P.ENGS:
        out.append(P.op(e, lambda eng: eng.nop(), evs))
    return out


def tok_ap(t, d, r, n, cnt=128, lo=0, hi=None):
    start = d * (128 * n + lo) + r
    stop = start + d * (cnt - 1) + 1
    return slice(start, stop, d)


def build_program(debug_u=False, pairs=tuple(range(8)), phases=("A", "B", "C"), dbg=3, n_sc=8):
    nc = bass.Bass("TRN2", target_bir_lowering=False)
    xT = nc.dram_tensor("xT", [D, S], F32, kind="ExternalInput").ap()
    w_in = nc.dram_tensor("w_in", [D, D_IN], F32, kind="ExternalInput").ap()
    aug = nc.dram_tensor("aug", [NH, 2, 12, S], BF16, kind="ExternalInput").ap()
    mask4_d = nc.dram_tensor("mask4", [128, 512], F32, kind="ExternalInput").ap()
    dr = {}
    for name, shape, dt in (("tri", [128, 128], F32), ("ident", [128, 128], F32), ("identb", [128, 128], BF16), ("negm4", [128, 512], BF16),
                            ("conv_wT", [128, 12, 4], F32), ("conv_b2", [128, 12], F32), ("dt_bias", [16], F32),
                            ("a_log", [16], F32), ("d_skip", [16], F32), ("ssm_norm_g", [1024], F32),
                            ("att_norm_g2", [128, 8], F32), ("ssm_norm_g2", [128, 8], F32), ("ln_g", [1024], F32), ("ln_b", [1024], F32),
                            ("w_out", [2048, D], F32), ("x", [S, D], F32)):
        dr[name] = nc.dram_tensor(name, shape, dt, kind="ExternalInput").ap()
    U = nc.dram_tensor("U", [2048, S], BF16, kind="ExternalOutput" if debug_u else "Internal").ap()
    out_d = nc.dram_tensor("out", [S, D], F32, kind="ExternalOutput").ap()

    P = Prog(nc)
    final_waits = []
    with contextlib.ExitStack() as st:
        def sb(name, shape, dt, stack=st):
            return stack.enter_context(nc.sbuf_tensor(name, shape, dt))

        banks = [st.enter_context(nc.psum_tensor(f"bank{i}", [128, 512], F32)) for i in range(8)]
        w_in_v = w_in.rearrange("(kc p) c -> p kc c", p=128)
        with contextlib.ExitStack() as stx:
            XT = sb("XT", [128, 8, S], BF16, stx)
            with contextlib.ExitStack() as st0:
                xstage = [sb(f"xstage{i}", [128, S], F32, st0) for i in range(2)]
                free = [[], []]
                for kc in range(8):
                    b = kc % 2
                    ld = P.dma("sync", xstage[b][:], xT[kc * 128:(kc + 1) * 128, :], f"ld_x{b}", deps=free[b])
                    e1 = I_copy(P, "vector", XT[:, kc, 0:1536], xstage[b][:, 0:1536], [ld])
                    e2 = I_copy(P, "scalar", XT[:, kc, 1536:3072], xstage[b][:, 1536:3072], [ld])
                    e3 = I_copy(P, "gpsimd", XT[:, kc, 3072:4096], xstage[b][:, 3072:4096], [ld])
                    free[b] = [e1, e2, e3]
                barrier(P)

            if "A" in phases:
                phase_A(nc, P, st, banks, XT, w_in_v, aug, mask4_d, U, pairs, dbg, dr)
                barrier(P)
            if "B" in phases:
                (phase_B if 'oldB' in _SKIP else phase_B2)(nc, P, banks, XT, w_in_v, dr, U, n_sc)
                barrier(P)
        if "C" in phases:
            phase_C(nc, P, banks, dr, U, out_d)
            barrier(P)
        last = {}
        for e in P.ENGS:
            for item in P.q[e]:
                if item[3]:
                    last[item[2].sem] = item[2]
        final_waits = list(last.values())
        P.emit(final_waits)
    return nc


def phase_C(nc, P, banks, dr, U, out_d, n_tg=8):
    T = Trk(P)
    ALPHA = 2.0 ** 0.25
    with contextlib.ExitStack() as st:
        def sb(name, shape, dt, stack=None):
            return (stack or st).enter_context(nc.sbuf_tensor("C_" + name, shape, dt))

        Wo = sb("Wo", [128, 16, D], BF16)
        gA = sb("gA", [128, 16], F32)
        lg_b = sb("lg_b", [128, D], F32)
        lb_b = sb("lb_b", [128, D], F32)
        onesB = sb("onesB", [128, 1], BF16)
        rC = Res(); rWo = Res()
        T.dma("sync", gA[:, 0:8], dr["att_norm_g2"][:, :], "ld_d1", writes=[rC])
        T.dma("sync", gA[:, 8:16], dr["ssm_norm_g2"][:, :], "ld_d1b", writes=[rC])
        T.dma("sync", lg_b[:], dr["ln_g"].partition_broadcast(128), "ld_d2", writes=[rC])
        T.dma("sync", lb_b[:], dr["ln_b"].partition_broadcast(128), "ld_d3", writes=[rC])
        I_memset(P, "vector", onesB[:], 1.0)
        wo_v = dr["w_out"].rearrange("(f p) d -> p f d", p=128)
        with contextlib.ExitStack() as stw:
            wst = [sb(f"wstC{i}", [128, 2, D], F32, stw) for i in range(2)]
            rws = [Res(), Res()]
            for i in range(8):
                b = i % 2
                T.dma("sync", wst[b][:], wo_v[:, 2 * i:2 * i + 2, :], f"ld_wo{b}", writes=[rws[b]])
                for j in range(2):
                    f = 2 * i + j
                    if j == 0:
                        T.op("vector", lambda e, o=Wo[:, f, :], i_=wst[b][:, j, :], s_=gA[:, f:f + 1]:
                             e.tensor_scalar(out=o, in0=i_, scalar1=s_, scalar2=None, op0=ALU.mult),
                             reads=[rws[b], rC], writes=[])
                    else:
                        T.op("scalar", lambda e, o=Wo[:, f, :], i_=wst[b][:, j, :], s_=gA[:, f:f + 1]:
                             e.activation(out=o, in_=i_, func=AF.Copy, scale=s_),
                             reads=[rws[b], rC], writes=[])
            barrier(P)

        Ub = [sb(f"Ub{i}", [128, 16, 512], BF16) for i in range(2)]
        xt = [sb(f"xt{i}", [128, 4, D], F32) for i in range(2)]
        sq = [sb(f"sq{i}", [128, 8, 128], BF16) for i in range(2)]
        rr = [sb(f"rr{i}", [128, D], F32) for i in range(2)]
        ot = [sb(f"ot{i}", [128, D], F32) for i in range(2)]
        st6 = [sb(f"st6{i}", [128, 2, 6], F32) for i in range(2)]
        mv = [sb(f"mv{i}", [128, 2], F32) for i in range(2)]
        ra = [sb(f"ra{i}", [128, 1], F32) for i in range(2)]
        rl = [sb(f"rl{i}", [128, 1], F32) for i in range(2)]
        rUa = [Res(), Res()]; rUs = [Res(), Res()]; rxt = [[Res() for _ in range(4)] for _ in range(2)]; rot = [Res(), Res()]
        r = [{k: Res() for k in ("sq", "rr", "st6", "mv", "ra", "rl")} for _ in range(2)]
        slots = [(banks[0], banks[1]), (banks[2], banks[3]), (banks[4], banks[5])]
        rslot = [[Res(), Res()] for _ in range(3)]
        ssb = [banks[6], banks[7]]; rss = [Res(), Res()]
        U_v = U.rearrange("(f p) t -> p f t", p=128)
        x_v = dr["x"].rearrange("(c p) d -> p c d", p=128)
        slot_i = 0
        cnt = 0
        for tg in range(n_tg):
            b = tg % 2
            T0 = 512 * tg
            T.dma("sync", Ub[b][:, 0:8, :], U_v[:, 0:8, T0:T0 + 512], f"ld_ua{b}", writes=[rUa[b]])
            T.dma("sync", Ub[b][:, 8:16, :], U_v[:, 8:16, T0:T0 + 512], f"ld_us{b}", writes=[rUs[b]])
            T.dma("sync", xt[b][:], x_v[:, 4 * tg:4 * tg + 4, :], f"ld_xt{b}", writes=rxt[b])
            for ci in range(4):
                cc = slice(128 * ci, 128 * ci + 128)
                k = cnt % 2
                rk = r[k]
                T.op("scalar", lambda e, o=xt[b][:, ci, :]: e.activation(out=o, in_=o, func=AF.Copy, scale=ALPHA),
                     reads=[], writes=[rxt[b][ci]])
                T.op("scalar", lambda e, o=sq[k][:], i_=Ub[b][:, 0:8, cc]: e.activation(out=o, in_=i_, func=AF.Square),
                     reads=[rUa[b]], writes=[rk["sq"]])
                halves = []
                for half in range(2):
                    cols = slice(512 * half, 512 * (half + 1))
                    sl = slot_i; slot_i = (slot_i + 1) % 3
                    bkA, bkB = slots[sl]
                    T.mm_group([(bkA[:, :], Ub[b][:, f, cc], Wo[:, f, cols], f == 0, f == 7) for f in range(8)],
                               reads=[rUa[b]], writes=[rslot[sl][0]])
                    T.mm_group([(bkB[:, :], Ub[b][:, 8 + f, cc], Wo[:, 8 + f, cols], f == 0, f == 7) for f in range(8)],
                               reads=[rUs[b]], writes=[rslot[sl][1]])
                    halves.append((sl, cols))
                T.mm_group([(ssb[k][:, 0:1], sq[k][:, f, :], onesB[:], f == 0, f == 7) for f in range(8)],
                           reads=[rk["sq"]], writes=[rss[k]])
                T.op("scalar", lambda e, o=ra[k][:], i_=ssb[k][:, 0:1]: e.activation(out=o, in_=i_, func=AF.Ln, bias=EPS, scale=1.0 / 1024),
                     reads=[rss[k]], writes=[rk["ra"]])
                T.op("scalar", lambda e, o=ra[k][:]: e.activation(out=o, in_=o, func=AF.Exp, scale=-0.5), reads=[], writes=[rk["ra"]])
                for sl, cols in halves:
                    bkA, bkB = slots[sl]
                    T.op("vector", lambda e, o=rr[k][:, cols], i_=bkA[:, :], x_=xt[b][:, ci, cols], s_=ra[k][:, 0:1]:
                         e.scalar_tensor_tensor(out=o, in0=i_, scalar=s_, in1=x_, op0=ALU.mult, op1=ALU.add),
                         reads=[rslot[sl][0], rk["ra"], rxt[b][ci]], writes=[rk["rr"]])
                    T.op("vector", lambda e, o=rr[k][:, cols], i_=bkB[:, :]: e.tensor_tensor(out=o, in0=i_, in1=o, op=ALU.add),
                         reads=[rslot[sl][1]], writes=[rk["rr"]])
                for half in range(2):
                    T.op("vector", lambda e, o=st6[k][:, half, :], i_=rr[k][:, 512 * half:512 * (half + 1)]: e.bn_stats(out=o, in_=i_),
                         reads=[rk["rr"]], writes=[rk["st6"]])
                T.op("vector", lambda e, o=mv[k][:], i_=st6[k][:]: e.bn_aggr(out=o, in_=i_), reads=[rk["st6"]], writes=[rk["mv"]])
                T.op("scalar", lambda e, o=rl[k][:], i_=mv[k][:, 1:2]: e.activation(out=o, in_=i_, func=AF.Ln, bias=EPS),
                     reads=[rk["mv"]], writes=[rk["rl"]])
                T.op("scalar", lambda e, o=rl[k][:]: e.activation(out=o, in_=o, func=AF.Exp, scale=-0.5), reads=[], writes=[rk["rl"]])
                T.op("vector", lambda e, o=ot[k][:], i_=rr[k][:], m_=mv[k][:, 0:1], s_=rl[k][:, 0:1]:
                     e.tensor_scalar(out=o, in0=i_, scalar1=m_, scalar2=s_, op0=ALU.subtract, op1=ALU.mult),
                     reads=[rk["rr"], rk["mv"], rk["rl"]], writes=[rot[k]])
                T.op("gpsimd", lambda e, o=ot[k][:]: e.tensor_tensor(out=o, in0=o, in1=lg_b[:], op=ALU.mult),
                     reads=[rC], writes=[rot[k]])
                T.op("gpsimd", lambda e, o=ot[k][:]: e.tensor_tensor(out=o, in0=o, in1=lb_b[:], op=ALU.add),
                     reads=[rC], writes=[rot[k]])
                T.dma("gpsimd", out_d[T0 + 128 * ci:T0 + 128 * ci + 128, :], ot[k][:], f"st_o{k}", reads=[rot[k]])
                cnt += 1


def phase_A(nc, P, st_outer, banks, XT, w_in_v, aug, mask4_d, U, pairs, dbg=3, dr=None):
    with contextlib.ExitStack() as st:
        def sb(name, shape, dt):
            return st.enter_context(nc.sbuf_tensor(name, shape, dt))

        mask4 = sb("mask4s", [128, 512], F32)
        stmp = [sb(f"stmp{i}", [128, 512], F32) for i in range(4)]
        wstage = sb("wstage", [128, 8, 512], F32)
        wbf = sb("wbf", [128, 8, 512], BF16)
        qk = [[sb(f"qk{w}{h}", [128, S], BF16) for h in range(2)] for w in range(2)]
        sz = sb("sz", [128, S], BF16)
        V = sb("V", [128, 3, 32, 192], BF16)
        NPT = 6
        PT = [sb(f"PT{i}", [128, 512], BF16) for i in range(NPT)]
        rc = [sb(f"rc{i}", [128, 512], F32) for i in range(2)]
        t1 = [sb(f"t1{i}", [128, 512], F32) for i in range(2)]
        uT = sb("uT", [128, S], BF16)
        vT = sb("vT", [128, S], BF16)
        identBa = sb("identBa", [128, 128], BF16)

        acc = banks[0:4]
        sbank = banks[4:6]
        pbank = banks[6:8]

        ev_mask = P.dma("sync", mask4[:], mask4_d[:, :], "ld_c")
        ev_id = P.dma("sync", identBa[:], dr["identb"][:, :], "ld_cid")
        vT_free = []
        ev_ones = I_memset(P, "gpsimd", V[:, :, :, 64:128], 1.0) if "ones" not in _SKIP else None

        pb_free = [[], []]
        pb_i = 0
        sb_free = [None, None]
        pt_free = [None] * 6
        st_free = [None] * 4
        acc_free = [None] * 4
        w_free = []
        qk_free = [[[], []], [[], []]]
        sz_free = []
        V_free = []
        uT_free = None
        g_ctr = 0

        def load_w(hp_, deps_):
            out_ = []
            for wi, off in enumerate((OFF_Q, OFF_K, OFF_V, OFF_ZA)):
                out_.append(P.dma("sync", wstage[:, :, wi * 128:(wi + 1) * 128],
                                  w_in_v[:, :, off + hp_ * 128: off + (hp_ + 1) * 128], "ld_w", deps=deps_))
            return out_

        lds = load_w(pairs[0], [])
        w_casts = None
        rc_free = [None, None]
        ev_ctr = 0
        pending_evac = []
        for hp in pairs:
            hA, hB = 2 * hp, 2 * hp + 1
            if w_casts is None:
                c1 = I_copy(P, "vector", wbf[:, 0:4, :], wstage[:, 0:4, :], lds)
                c2 = I_copy(P, "scalar", wbf[:, 4:8, :], wstage[:, 4:8, :], lds)
                w_casts = [c1, c2]
            w_ready = w_casts
            nxt = pairs.index(hp) + 1
            if nxt < len(pairs):
                lds = load_w(pairs[nxt], w_casts)
            aug_ev = [[None, None], [None, None]]
            for w in range(2):
                for hh, h in enumerate((hA, hB)):
                    if "aug" in _SKIP:
                        continue
                    aug_ev[w][hh] = P.dma("sync", qk[w][hh][64:76, :], aug[h, w, :, :], f"ld_aug{w}{hh}",
                                          deps=qk_free[w][hh])
            w_last = []
            qk_ready = [[[], []], [[], []]]
            sz_ready = []
            vT_ready = []
            for wi in (0, 1, 2, 3):
                for tt in range(8):
                    pbk = pbank[pb_i]
                    deps = list(w_ready) + list(pb_free[pb_i])
                    for kc in range(8):
                        mm = I_mm(P, pbk[:, :], wbf[:, kc, wi * 128:(wi + 1) * 128],
                                  XT[:, kc, tt * 512:(tt + 1) * 512], kc == 0, kc == 7,
                                  deps if kc == 0 else ())
                    cols = slice(tt * 512, (tt + 1) * 512)
                    if wi == 3:
                        e = I_act(P, sz[:, cols], pbk[:, :], AF.Silu, [mm] + sz_free)
                        sz_ready.append(e)
                        pb_free[pb_i] = [e]
                    elif wi == 2:
                        e = I_copy(P, "vector" if tt % 2 == 0 else "scalar", vT[:, cols], pbk[:, :], [mm] + vT_free)
                        vT_ready.append(e)
                        pb_free[pb_i] = [e]
                    else:
                        w = wi
                        sc = 0.125 if w == 0 else 1.0
                        eA = I_act(P, qk[w][0][0:64, cols], pbk[0:64, :], AF.Copy, [mm] + qk_free[w][0], scale=sc)
                        if "dveB" in _SKIP:
                            eB = I_act(P, qk[w][1][0:64, cols], pbk[64:128, :], AF.Copy, [mm] + qk_free[w][1], scale=sc)
                        else:
                            eB = I_ts(P, "vector", qk[w][1][0:64, cols], pbk[64:128, :], sc, None, ALU.mult,
                                      deps=[mm] + qk_free[w][1])
                        qk_ready[w][0].append(eA)
                        qk_ready[w][1].append(eB)
                        pb_free[pb_i] = [eA, eB]
                    pb_i ^= 1
                    w_last = [mm]
            V_ready = {}
            tr_last = None
            for di, d in enumerate(DIL if dbg >= 2 else ()):
                nb = 32 // d
                for c0 in range(0, 32, 8):
                    pbk = pbank[pb_i]
                    pbk_b = pbk[:].bitcast(BF16)
                    for j in range(8):
                        r_, m_ = divmod(c0 + j, nb)
                        deps = ()
                        if j == 0:
                            deps = vT_ready + [ev_id] + list(pb_free[pb_i])
                        tr_last = P.op("tensor", lambda e, o=pbk_b[:, 128 * j:128 * (j + 1)], i_=vT[:, tok_ap(None, d, r_, m_)]:
                                       e.transpose(o, i_, identBa[:]), deps)
                    src = pbk_b[:, :].rearrange("p (c f) -> p c f", f=128)
                    eA = I_copy(P, "vector", V[:, di, c0:c0 + 8, 0:64], src[:, :, 0:64], [tr_last] + V_free)
                    eB = I_copy(P, "scalar", V[:, di, c0:c0 + 8, 128:192], src[:, :, 64:128], [tr_last, eA] + V_free)
                    for j in range(8):
                        V_ready[(di, c0 + j)] = [eA, eB]
                    pb_free[pb_i] = [eA, eB]
                    pb_i ^= 1
            vT_free = [tr_last] if tr_last is not None else []
            w_free = w_last
            if nxt < len(pairs):
                c1 = I_copy(P, "vector", wbf[:, 0:4, :], wstage[:, 0:4, :], lds + w_last)
                c2 = I_copy(P, "scalar", wbf[:, 4:8, :], wstage[:, 4:8, :], lds + w_last)
                w_casts = [c1, c2]
            V_free = []
            qk_free = [[[], []], [[], []]]
            sz_free = []

            u_written = []
            for hh, h in enumerate((hA, hB) if dbg >= 3 else ()):
                qT, kT = qk[0][hh], qk[1][hh]
                q_dep = qk_ready[0][hh] + [aug_ev[0][hh]]
                k_dep = qk_ready[1][hh] + [aug_ev[1][hh]]
                vcol = slice(0, 128) if hh == 0 else slice(64, 192)
                for sbk in range(2):
                    qbs = []
                    for di, d in enumerate(DIL):
                        nb = 32 // d
                        per_sb = nb // 2
                        for r in range(d):
                            for n in range(sbk * per_sb, (sbk + 1) * per_sb):
                                qbs.append((di, d, r, n))
                    groups = [qbs[i:i + 2] for i in range(0, len(qbs), 2)]
                    acc_started = [False] * 4
                    acc_last_mm = [None] * 4
                    pend = []

                    def do_pv(item):
                        ptb, grp, ev_mask_mul = item
                        last_mm = None
                        for j, (di, d, r, n) in enumerate(grp):
                            nb = 32 // d
                            for role in range(2):
                                m = n - 1 + role
                                if m < 0:
                                    continue
                                tile_cols = (2 * j + role) * 128
                                cid = r * nb + m
                                lhsT = V[:, di, cid, vcol]
                                if d == 16:
                                    pieces = [(pc, 32) for pc in range(4)]
                                else:
                                    pieces = [(0, 128)]
                                for pc, cnt in pieces:
                                    i0 = pc * 32 if d == 16 else 0
                                    t0 = d * (128 * n + i0) + r
                                    col0 = t0 - 2048 * sbk
                                    bk = col0 // 512
                                    c0 = col0 % 512
                                    out = acc[bk][:, c0: c0 + d * (cnt - 1) + 1: d]
                                    rhs = PT[ptb][:, tile_cols + i0: tile_cols + i0 + cnt]
                                    deps = [ev_mask_mul] + V_ready[(di, cid)] + latest(ev_ones)
                                    if not acc_started[bk]:
                                        deps = deps + latest(acc_free[bk])
                                    last_mm = I_mm(P, out, lhsT, rhs, not acc_started[bk], False, deps)
                                    acc_started[bk] = True
                                    acc_last_mm[bk] = last_mm
                        pt_free[ptb] = last_mm

                    for gi, grp in enumerate(groups):
                        if gi % 2 == 0 and pending_evac:
                            pending_evac.pop(0)()
                        sbi = g_ctr % 2
                        pti = g_ctr % 6
                        sti = g_ctr % 4
                        g_ctr += 1
                        sbb = sbank[sbi]
                        first = True
                        mm = None
                        for j, (di, d, r, n) in enumerate(grp):
                            for role in range(2):
                                m = n - 1 + role
                                nq = n
                                if m < 0:
                                    m, nq = 0, 1
                                tile_cols = (2 * j + role) * 128
                                deps = ()
                                if first:
                                    deps = q_dep + k_dep + latest(sb_free[sbi])
                                    first = False
                                mm = I_mm(P, sbb[:, tile_cols:tile_cols + 128],
                                          kT[0:76, tok_ap(None, d, r, m)], qT[0:76, tok_ap(None, d, r, nq)],
                                          True, True, deps)
                        mk0 = I_tt(P, "vector", stmp[sti][:, :], sbb[:, :], mask4[:, :], ALU.add,
                                   [mm, ev_mask] + latest(st_free[sti]))
                        sb_free[sbi] = mk0
                        mk = I_act(P, PT[pti][:, :], stmp[sti][:, :], AF.Exp, [mk0] + latest(pt_free[pti]))
                        st_free[sti] = mk
                        pend.append((pti, grp, mk))
                        if len(pend) > 4:
                            do_pv(pend.pop(0))
                    while pend:
                        do_pv(pend.pop(0))
                    def make_evac(bk, hh=hh, sbk=sbk, alm=acc_last_mm, szr=sz_ready):
                        def evac():
                            nonlocal ev_ctr, sz_free
                            cols = slice(2048 * sbk + 512 * bk, 2048 * sbk + 512 * (bk + 1))
                            if hh == 0:
                                o_rows, s_rows = slice(0, 64), slice(64, 128)
                            else:
                                o_rows, s_rows = slice(64, 128), slice(0, 64)
                            ei = ev_ctr % 2
                            ev_ctr += 1
                            a1 = I_act(P, rc[ei][o_rows, :], acc[bk][s_rows, :], AF.Ln, [alm[bk]] + latest(rc_free[ei]))
                            a2 = I_copy(P, "scalar", t1[ei][o_rows, :], acc[bk][o_rows, :], [alm[bk]] + latest(rc_free[ei]))
                            acc_free[bk] = a2
                            e1 = I_act(P, rc[ei][o_rows, :], rc[ei][o_rows, :], AF.Exp, [a1], scale=-1.0)
                            e2 = I_tt(P, "gpsimd", t1[ei][o_rows, :], t1[ei][o_rows, :], rc[ei][o_rows, :], ALU.mult, [e1, a2])
                            e3 = I_tt(P, "gpsimd", uT[o_rows, cols], t1[ei][o_rows, :], sz[o_rows, cols], ALU.mult,
                                      [e2] + szr + latest(uT_free))
                            rc_free[ei] = e3
                            u_written.append(e3)
                            sz_free = [e3]
                        return evac
                    pending_evac.extend(make_evac(bk) for bk in range(4))
                qk_free[0][hh] = [mm]
                qk_free[1][hh] = [mm]
            if dbg < 3:
                continue
            while pending_evac:
                pending_evac.pop(0)()
            V_free = latest(*[acc_last_mm[b] for b in range(4)])
            uT_free = P.dma("gpsimd", U[hp * 128:(hp + 1) * 128, :], uT[:, :], "st_u", deps=u_written)
            sz_free = u_written[-1:]


class Res:
    __slots__ = ("w", "r")

    def __init__(self):
        self.w = None
        self.r = {}


class Trk:
    def __init__(self, P):
        self.P = P

    def deps(self, reads, writes):
        d = []
        for t in reads:
            if t.w is not None:
                d.append(t.w)
        for t in writes:
            if t.w is not None:
                d.append(t.w)
            d.extend(t.r.values())
        return d

    def mark(self, ev, reads, writes):
        for t in reads:
            t.r[ev.sem if ev.sem is not None else ev.eng] = ev
        for t in writes:
            t.w = ev
            t.r = {}

    def op(self, eng, fn, reads=(), writes=(), extra=()):
        ev = self.P.op(eng, fn, self.deps(reads, writes) + list(extra))
        self.mark(ev, reads, writes)
        return ev

    def dma(self, eng, out, in_, slot, reads=(), writes=(), extra=()):
        ev = self.P.dma(eng, out, in_, slot, self.deps(reads, writes) + list(extra))
        self.mark(ev, reads, writes)
        return ev

    def mm_group(self, mms, reads, writes, extra=()):
        d = self.deps(reads, writes) + list(extra)
        ev = None
        for i, (out, lhsT, rhs, start, stop) in enumerate(mms):
            ev = I_mm(self.P, out, lhsT, rhs, start, stop, d if i == 0 else ())
        self.mark(ev, reads, writes)
        return ev


def bc64(ap16, h0, nh):
    return ap16[:, h0:h0 + nh].unsqueeze(2).broadcast_to([128, nh, 64])


def phase_B(nc, P, banks, XT, w_in_v, dr, U, n_sc=8, dbg_out=None):
    T = Trk(P)
    with contextlib.ExitStack() as st:
        def sb(name, shape, dt, stack=None):
            return (stack or st).enter_context(nc.sbuf_tensor("B_" + name, shape, dt))

        rXT = Res()
        Wz = sb("Wz", [128, 8, 1024], BF16)
        Wx = sb("Wx", [128, 8, 1536], BF16)
        Wdt = sb("Wdt", [128, 8, 16], BF16)
        tri = sb("tri", [128, 128], F32)
        onesF = sb("onesF", [128, 128], F32)
        identF = sb("identF", [128, 128], F32)
        identB = sb("identB", [128, 128], BF16)
        negm4 = sb("negm4", [128, 512], BF16)
        cw = sb("cw", [128, 12, 4], F32)
        cb = sb("cb", [128, 12], F32)
        dtb_b = sb("dtb_b", [128, 16], F32)
        a_b = sb("a_b", [128, 16], F32)
        dsk_b = sb("dsk_b", [128, 16], F32)
        gs_b = sb("gs_b", [128, 1024], F32)
        hal = sb("hal", [128, 12, 3], F32)
        rW = Res(); rC = Res(); rHal = Res()

        with contextlib.ExitStack() as stw:
            wst = sb("wstB", [128, 8, 512], F32, stw)
            rwst = Res()
            pieces = [(OFF_ZS, 0, 512, Wz), (OFF_ZS + 512, 512, 512, Wz),
                      (OFF_XBC, 0, 512, Wx), (OFF_XBC + 512, 512, 512, Wx), (OFF_XBC + 1024, 1024, 512, Wx),
                      (OFF_DT, 0, 16, Wdt)]
            for i, (off, dst0, n, Wt) in enumerate(pieces):
                T.dma("sync", wst[:, :, 0:n], w_in_v[:, :, off:off + n], "ld_wB", writes=[rwst])
                T.op("vector", lambda e, o=Wt[:, 0:4, dst0:dst0 + n], i_=wst[:, 0:4, 0:n]: e.tensor_copy(out=o, in_=i_),
                     reads=[rwst], writes=[rW])
                T.op("gpsimd", lambda e, o=Wt[:, 4:8, dst0:dst0 + n], i_=wst[:, 4:8, 0:n]: e.tensor_copy(out=o, in_=i_),
                     reads=[rwst], writes=[])
            T.dma("sync", tri[:], dr["tri"][:, :], "ld_c1", writes=[rC])
            T.dma("sync", identF[:], dr["ident"][:, :], "ld_c2", writes=[rC])
            T.dma("sync", negm4[:], dr["negm4"][:, :], "ld_c3", writes=[rC])
            T.dma("sync", cw[:], dr["conv_wT"][:, :, :], "ld_c4", writes=[rC])
            T.dma("sync", cb[:], dr["conv_b2"][:, :], "ld_c5", writes=[rC])
            T.dma("sync", dtb_b[:], dr["dt_bias"].partition_broadcast(128), "ld_c6", writes=[rC])
            T.dma("sync", a_b[:], dr["a_log"].partition_broadcast(128), "ld_c7", writes=[rC])
            T.dma("sync", dsk_b[:], dr["d_skip"].partition_broadcast(128), "ld_c8", writes=[rC])
            T.dma("sync", gs_b[:], dr["ssm_norm_g"].partition_broadcast(128), "ld_c9", writes=[rC])
            barrier(P)
            I_memset(P, "vector", onesF[:], 1.0)
            I_memset(P, "vector", hal[:], 0.0)
            I_copy(P, "vector", identB[:], identF[:])
            I_act(P, a_b[:], a_b[:], AF.Exp)
            barrier(P)
            I_ts(P, "vector", a_b[:], a_b[:], -1.0, None, ALU.mult)
            barrier(P)

        xb = [sb(f"xb{i}", [128, 515], F32) for i in range(2)]
        cvb = [sb(f"cvb{i}", [128, 512], F32) for i in range(2)]
        xsT = sb("xsT", [128, 8, 512], F32)
        BT = sb("BT", [128, 2, 512], BF16)
        CT = sb("CT", [128, 2, 512], BF16)
        dtx = sb("dtx", [128, 16], F32)
        dt_t = sb("dt_t", [128, 16], F32)
        adt = sb("adt", [128, 16], F32)
        nacs = sb("nacs", [128, 16], F32)
        D0 = sb("D0", [128, 16], F32)
        dte = sb("dte", [128, 16], F32)
        ddt = sb("ddt", [128, 16], F32)
        cdb = sb("cdb", [128, 16], F32)
        Btok = sb("Btok", [128, 2, 128], BF16)
        xs_tok = sb("xs_tok", [128, 1024], F32)
        xg = sb("xg", [128, 1024], BF16)
        xgd = sb("xgd", [128, 1024], BF16)
        Gm = sb("Gm", [128, 2, 128], BF16)
        Dm = [sb(f"Dm{i}", [128, 512], BF16) for i in range(2)]
        MT = [sb(f"MT{i}", [128, 512], BF16) for i in range(2)]
        H = sb("H", [128, 2, 512], F32)
        Hbf = sb("Hbf", [128, 2, 512], BF16)
        yo = sb("yo", [128, 512], F32)
        tsk = sb("tsk", [128, 512], F32)
        y = sb("y", [128, 1024], F32)
        zs = sb("zs", [128, 1024], F32)
        junk = sb("junkB", [128, 1024], F32)
        ssq = sb("ssq", [128, 1], F32)
        rstd = sb("rstd", [128, 1], F32)
        mixn = sb("mixn", [128, 1024], BF16)
        mixT = sb("mixT", [128, 8, 512], BF16)

        r = {k: Res() for k in ("xsT", "BT", "CT", "dtx", "dt", "adt", "nacs", "D0", "dte", "ddt", "cdb", "Btok",
                                "xs_tok", "xg", "xgd", "Gm", "H", "Hbf", "yo", "tsk", "y", "zs", "junk", "ssq", "rstd",
                                "mixn", "mixT")}
        rxb = [Res(), Res()]; rxbh = [Res(), Res()]; rcv = [Res(), Res()]; rmixT = [Res(), Res()]; rDm = [Res(), Res()]; rMT = [Res(), Res()]
        rhal = [Res() for _ in range(12)]
        pb = [banks[0], banks[1]]; rpb = [Res(), Res()]
        smb = banks[2]; rsmb = Res()
        trb = banks[3]; rtr = Res()
        Rb = [banks[4], banks[5]]; rRb = [Res(), Res()]
        Yb = banks[6]; rY = Res()
        osb = banks[7]; ros = Res()

        I_memset(P, "vector", H[:], 0.0)
        I_memset(P, "vector", Hbf[:], 0.0)
        barrier(P)

        pbi = 0
        quad_ctr = 0
        for sc in range(n_sc):
            T0 = 512 * sc
            for ft in range(12):
                b = pbi; pbi ^= 1
                xbi = ft % 2
                T.mm_group([(pb[b][:, :], Wx[:, kc, ft * 128:(ft + 1) * 128], XT[:, kc, T0:T0 + 512], kc == 0, kc == 7)
                            for kc in range(8)], reads=[rW, rXT], writes=[rpb[b]])
                T.op("scalar", lambda e, o=xb[xbi][:, 3:515], i_=pb[b][:, :]: e.activation(out=o, in_=i_, func=AF.Copy),
                     reads=[rpb[b]], writes=[rxb[xbi]])
                T.op("gpsimd", lambda e, o=xb[xbi][:, 0:3], i_=hal[:, ft, :]: e.tensor_copy(out=o, in_=i_),
                     reads=[rhal[ft]], writes=[rxbh[xbi]])
                T.op("gpsimd", lambda e, o=hal[:, ft, :], i_=xb[xbi][:, 512:515]: e.tensor_copy(out=o, in_=i_),
                     reads=[rxb[xbi]], writes=[rhal[ft]])
                ceng = "vector"
                cv = cvb[xbi]
                T.op(ceng, lambda e, o=cv[:, :], i_=xb[xbi][:, 0:512], s_=cw[:, ft, 0:1]:
                     e.tensor_scalar(out=o, in0=i_, scalar1=s_, scalar2=None, op0=ALU.mult),
                     reads=[rxb[xbi], rxbh[xbi], rC], writes=[rcv[xbi]])
                for j in range(1, 4):
                    T.op(ceng, lambda e, o=cv[:, :], i_=xb[xbi][:, j:j + 512], s_=cw[:, ft, j:j + 1]:
                         e.scalar_tensor_tensor(out=o, in0=i_, scalar=s_, in1=o, op0=ALU.mult, op1=ALU.add),
                         reads=[rxb[xbi], rxbh[xbi], rcv[xbi]], writes=[rcv[xbi]])
                if ft < 8:
                    dst, rd = xsT[:, ft, :], r["xsT"]
                elif ft < 10:
                    dst, rd = BT[:, ft - 8, :], r["BT"]
                else:
                    dst, rd = CT[:, ft - 10, :], r["CT"]
                T.op("scalar", lambda e, o=dst, i_=cv[:, :], b_=cb[:, ft:ft + 1]:
                     e.activation(out=o, in_=i_, func=AF.Silu, bias=b_), reads=[rcv[xbi], rC], writes=[rd])

            for ci in range(4):
                t0 = T0 + 128 * ci
                cc = slice(128 * ci, 128 * ci + 128)
                T.mm_group([(smb[:, 0:16], XT[:, kc, t0:t0 + 128], Wdt[:, kc, :], kc == 0, kc == 7) for kc in range(8)],
                           reads=[rW, rXT], writes=[rsmb])
                for half in range(2):
                    b = pbi; pbi ^= 1
                    T.mm_group([(pb[b][:, :], XT[:, kc, t0:t0 + 128], Wz[:, kc, 512 * half:512 * (half + 1)], kc == 0, kc == 7)
                                for kc in range(8)], reads=[rW, rXT], writes=[rpb[b]])
                    T.op("scalar", lambda e, o=zs[:, 512 * half:512 * (half + 1)], i_=pb[b][:, :]:
                         e.activation(out=o, in_=i_, func=AF.Silu), reads=[rpb[b]], writes=[r["zs"]])
                T.op("vector", lambda e: e.tensor_tensor(out=dtx[:], in0=smb[:, 0:16], in1=dtb_b[:], op=ALU.add),
                     reads=[rsmb, rC], writes=[r["dtx"]])
                T.op("scalar", lambda e: e.activation(out=dtx[:], in_=dtx[:], func=AF.Exp), reads=[], writes=[r["dtx"]])
                T.op("scalar", lambda e: e.activation(out=dt_t[:], in_=dtx[:], func=AF.Ln, bias=1.0),
                     reads=[r["dtx"]], writes=[r["dt"]])
                T.op("vector", lambda e: e.tensor_tensor(out=adt[:], in0=dt_t[:], in1=a_b[:], op=ALU.mult),
                     reads=[r["dt"], rC], writes=[r["adt"]])
                T.mm_group([(smb[:, 16:32], tri[:], adt[:], True, True)], reads=[r["adt"], rC], writes=[rsmb])
                T.mm_group([(smb[:, 32:48], onesF[:], adt[:], True, True)], reads=[r["adt"], rC], writes=[rsmb])
                T.op("vector", lambda e: e.tensor_scalar(out=nacs[:], in0=smb[:, 16:32], scalar1=-1.0, scalar2=None, op0=ALU.mult),
                     reads=[rsmb], writes=[r["nacs"]])
                T.op("vector", lambda e: e.tensor_copy(out=cdb[:], in_=smb[:, 32:48]), reads=[rsmb], writes=[r["cdb"]])
                T.op("vector", lambda e: e.tensor_tensor(out=dte[:], in0=cdb[:], in1=nacs[:], op=ALU.add),
                     reads=[r["cdb"], r["nacs"]], writes=[r["dte"]])
                T.op("scalar", lambda e: e.activation(out=D0[:], in_=nacs[:], func=AF.Exp, scale=-1.0),
                     reads=[r["nacs"]], writes=[r["D0"]])
                T.op("scalar", lambda e: e.activation(out=dte[:], in_=dte[:], func=AF.Exp), reads=[], writes=[r["dte"]])
                T.op("scalar", lambda e: e.activation(out=cdb[:], in_=cdb[:], func=AF.Exp), reads=[r["dte"]], writes=[r["cdb"]])
                T.op("vector", lambda e: e.tensor_tensor(out=ddt[:], in0=dt_t[:], in1=dte[:], op=ALU.mult),
                     reads=[r["dt"], r["dte"]], writes=[r["ddt"]])
                for g in range(2):
                    ev = None
                    for j in range(4):
                        ev = P.op("tensor", lambda e, o=trb[:, 128 * j:128 * (j + 1)], i_=xsT[:, 4 * g + j, cc]:
                                  e.transpose(o, i_, identF[:]), T.deps([r["xsT"], rC], [rtr]) if j == 0 else ())
                    T.mark(ev, [r["xsT"], rC], [rtr])
                    T.op("scalar", lambda e, o=xs_tok[:, 512 * g:512 * (g + 1)], i_=trb[:, :]: e.activation(out=o, in_=i_, func=AF.Copy),
                         reads=[rtr], writes=[r["xs_tok"]])
                for g in range(2):
                    v3 = lambda ap: ap[:, 512 * g:512 * (g + 1)].rearrange("p (h e) -> p h e", e=64)
                    T.op("vector", lambda e, o=v3(xg), i_=v3(xs_tok), b_=bc64(dt_t, 8 * g, 8):
                         e.tensor_tensor(out=o, in0=i_, in1=b_, op=ALU.mult), reads=[r["xs_tok"], r["dt"]], writes=[r["xg"]])
                    T.op("gpsimd", lambda e, o=v3(xgd), i_=v3(xs_tok), b_=bc64(ddt, 8 * g, 8):
                         e.tensor_tensor(out=o, in0=i_, in1=b_, op=ALU.mult), reads=[r["xs_tok"], r["ddt"]], writes=[r["xgd"]])
                trb_b = trb[:].bitcast(BF16)
                ev = None
                for g in range(2):
                    ev = P.op("tensor", lambda e, o=trb_b[:, 128 * g:128 * (g + 1)], i_=BT[:, g, cc]:
                              e.transpose(o, i_, identB[:]), T.deps([r["BT"], rC], [rtr]) if g == 0 else ())
                T.mark(ev, [r["BT"], rC], [rtr])
                T.op("vector", lambda e: e.tensor_copy(out=Btok[:].rearrange("p g n -> p (g n)"), in_=trb_b[:, 0:256]),
                     reads=[rtr], writes=[r["Btok"]])
                for g in range(2):
                    T.mm_group([(smb[:, 128 * (g + 1):128 * (g + 2)], BT[:, g, cc], CT[:, g, cc], True, True)],
                               reads=[r["BT"], r["CT"]], writes=[rsmb])
                    T.op("vector", lambda e, o=Gm[:, g, :], i_=smb[:, 128 * (g + 1):128 * (g + 2)]: e.tensor_copy(out=o, in_=i_),
                         reads=[rsmb], writes=[r["Gm"]])
                for g in range(2):
                    for qd in range(2):
                        qi = quad_ctr % 2; quad_ctr += 1
                        h0 = 8 * g + 4 * qd
                        mms = [(Rb[qi][:, :], identB[:], negm4[:], True, False)]
                        for j in range(4):
                            mms.append((Rb[qi][:, 128 * j:128 * (j + 1)], adt[:, h0 + j:h0 + j + 1].broadcast_to([128, 128]),
                                        tri[:], False, j == 3))
                        T.mm_group(mms, reads=[r["adt"], rC], writes=[rRb[qi]])
                        for j in range(4):
                            T.op("scalar", lambda e, o=Dm[qi][:, 128 * j:128 * (j + 1)], i_=Rb[qi][:, 128 * j:128 * (j + 1)],
                                 b_=nacs[:, h0 + j:h0 + j + 1]: e.activation(out=o, in_=i_, func=AF.Exp, bias=b_),
                                 reads=[rRb[qi], r["nacs"]], writes=[rDm[qi]])
                        T.op("vector", lambda e, o=MT[qi][:].rearrange("p (j l) -> p j l", l=128),
                             i_=Dm[qi][:].rearrange("p (j l) -> p j l", l=128),
                             b_=Gm[:, g, :].unsqueeze(1).broadcast_to([128, 4, 128]):
                             e.tensor_tensor(out=o, in0=i_, in1=b_, op=ALU.mult),
                             reads=[rDm[qi], r["Gm"]], writes=[rMT[qi]])
                        mms = []
                        for j in range(4):
                            hh = 4 * qd + j
                            mms.append((Yb[:, 64 * hh:64 * (hh + 1)], MT[qi][:, 128 * j:128 * (j + 1)],
                                        xg[:, 64 * (h0 + j):64 * (h0 + j + 1)], (qd == 0 and j == 0), False))
                        T.mm_group(mms, reads=[rMT[qi], r["xg"]], writes=[rY])
                    T.mm_group([(osb[:, :], CT[:, g, cc], Hbf[:, g, :], True, True)], reads=[r["CT"], r["Hbf"]], writes=[ros])
                    v3 = lambda ap: ap.rearrange("p (h e) -> p h e", e=64)
                    T.op("vector", lambda e, o=v3(yo[:, :]), i_=v3(osb[:, :]), b_=bc64(D0, 8 * g, 8):
                         e.tensor_tensor(out=o, in0=i_, in1=b_, op=ALU.mult), reads=[ros, r["D0"]], writes=[r["yo"]])
                    T.op("vector", lambda e, o=y[:, 512 * g:512 * (g + 1)], i_=Yb[:, :]:
                         e.tensor_tensor(out=o, in0=i_, in1=yo[:, :], op=ALU.add), reads=[rY, r["yo"]], writes=[r["y"]])
                    T.op("gpsimd", lambda e, o=v3(tsk[:, :]), i_=v3(xs_tok[:, 512 * g:512 * (g + 1)]), b_=bc64(dsk_b, 8 * g, 8):
                         e.tensor_tensor(out=o, in0=i_, in1=b_, op=ALU.mult), reads=[r["xs_tok"], rC], writes=[r["tsk"]])
                    T.op("gpsimd", lambda e, o=y[:, 512 * g:512 * (g + 1)]: e.tensor_tensor(out=o, in0=o, in1=tsk[:, :], op=ALU.add),
                         reads=[r["tsk"]], writes=[r["y"]])
                    T.mm_group([(osb[:, :], Btok[:, g, :], xgd[:, 512 * g:512 * (g + 1)], True, True)],
                               reads=[r["Btok"], r["xgd"]], writes=[ros])
                    T.op("vector", lambda e, o=v3(H[:, g, :]), b_=bc64(cdb, 8 * g, 8):
                         e.tensor_tensor(out=o, in0=o, in1=b_, op=ALU.mult), reads=[r["cdb"]], writes=[r["H"]])
                    T.op("vector", lambda e, o=H[:, g, :]: e.tensor_tensor(out=o, in0=osb[:, :], in1=o, op=ALU.add),
                         reads=[ros], writes=[r["H"]])
                    T.op("gpsimd", lambda e, o=Hbf[:, g, :], i_=H[:, g, :]: e.tensor_copy(out=o, in_=i_),
                         reads=[r["H"]], writes=[r["Hbf"]])
                T.op("vector", lambda e: e.tensor_tensor(out=y[:], in0=y[:], in1=zs[:], op=ALU.mult),
                     reads=[r["zs"]], writes=[r["y"]])
                T.op("vector", lambda e: e.memset(ssq[:], 0.0), reads=[], writes=[r["ssq"]])
                T.op("scalar", lambda e: e.activation(out=junk[:], in_=y[:], func=AF.Square, accum_out=ssq[:]),
                     reads=[r["y"]], writes=[r["junk"], r["ssq"]])
                T.op("scalar", lambda e: e.activation(out=rstd[:], in_=ssq[:], func=AF.Ln, bias=EPS, scale=1.0 / 1024),
                     reads=[r["ssq"]], writes=[r["rstd"]])
                T.op("scalar", lambda e: e.activation(out=rstd[:], in_=rstd[:], func=AF.Exp, scale=-0.5),
                     reads=[], writes=[r["rstd"]])
                T.op("vector", lambda e: e.scalar_tensor_tensor(out=mixn[:], in0=y[:], scalar=rstd[:, 0:1], in1=gs_b[:],
                                                                op0=ALU.mult, op1=ALU.mult),
                     reads=[r["y"], r["rstd"], rC], writes=[r["mixn"]])
                for half in range(2):
                    ev = None
                    for j in range(4):
                        ft = 4 * half + j
                        ev = P.op("tensor", lambda e, o=trb_b[:, 128 * j:128 * (j + 1)], i_=mixn[:, 128 * ft:128 * (ft + 1)]:
                                  e.transpose(o, i_, identB[:]), T.deps([r["mixn"], rC], [rtr]) if j == 0 else ())
                    T.mark(ev, [r["mixn"], rC], [rtr])
                    T.op("vector" if half == 0 else "scalar",
                         (lambda e, o=mixT[:, 4 * half:4 * half + 4, cc], i_=trb_b[:, 0:512].rearrange("p (j t) -> p j t", t=128):
                          e.tensor_copy(out=o, in_=i_)) if half == 0 else
                         (lambda e, o=mixT[:, 4 * half:4 * half + 4, cc], i_=trb_b[:, 0:512].rearrange("p (j t) -> p j t", t=128):
                          e.activation(out=o, in_=i_, func=AF.Copy)),
                         reads=[rtr], writes=[rmixT[half]])
            T.dma("gpsimd", U[1024:2048, T0:T0 + 512].rearrange("(f p) t -> p f t", p=128), mixT[:, :, :], "st_uB",
                  reads=[rmixT[0], rmixT[1]])
        barrier(P)


_NC_CACHE = {}


def _host_inputs(xb, p, c):
    return {
        "xT": np.ascontiguousarray(xb.T), "x": np.ascontiguousarray(xb),
        "w_in": p["w_in"], "w_out": p["w_out"],
        "aug": c["aug"], "mask4": c["mask4"], "tri": c["tri"], "ident": c["ident"], "identb": c["identb"], "negm4": c["negm4"],
        "conv_wT": np.ascontiguousarray(p["conv_w"].T.reshape(12, 128, 4).transpose(1, 0, 2)),
        "conv_b2": np.ascontiguousarray(p["conv_b"].reshape(12, 128).T),
        "dt_bias": p["dt_bias"], "a_log": p["a_log"], "d_skip": p["d_skip"], "ssm_norm_g": p["ssm_norm_g"],
        "att_norm_g2": np.ascontiguousarray(p["att_norm_g"].reshape(8, 128).T),
        "ssm_norm_g2": np.ascontiguousarray(p["ssm_norm_g"].reshape(8, 128).T),
        "ln_g": p["ln_g"], "ln_b": p["ln_b"],
    }


def kernel(x, w_in, conv_w, conv_b, dt_bias, a_log, d_skip, att_norm_g, ssm_norm_g, w_out, ln_g, ln_b):
    x = np.asarray(x, np.float32)
    p = {"w_in": w_in, "conv_w": conv_w, "conv_b": conv_b, "dt_bias": dt_bias, "a_log": a_log, "d_skip": d_skip,
         "att_norm_g": att_norm_g, "ssm_norm_g": ssm_norm_g, "w_out": w_out, "ln_g": ln_g, "ln_b": ln_b}
    p = {k: np.ascontiguousarray(np.asarray(v, np.float32)[0]) for k, v in p.items()}
    c = make_constants()
    n = x.shape[0]
    if "nc" not in _NC_CACHE:
        _NC_CACHE["nc"] = build_program()
    nc = _NC_CACHE["nc"]
    in_maps = [_host_inputs(x[b], p, c) for b in range(n)]
    res = run_bass_kernel_spmd(nc, in_maps, core_ids=list(range(n)))
    return np.stack([np.asarray(r["out"], np.float32) for r in res.results], axis=0)


def phase_B2(nc, P, banks, XT, w_in_v, dr, U, n_sc=8):
    T = Trk(P)
    with contextlib.ExitStack() as st:
        def sb(name, shape, dt, stack=None):
            return (stack or st).enter_context(nc.sbuf_tensor("B_" + name, shape, dt))

        rXT = Res()
        Wz = sb("Wz", [128, 8, 1024], BF16)
        Wx = sb("Wx", [128, 8, 1536], BF16)
        Wdt = sb("Wdt", [128, 8, 16], BF16)
        tri = sb("tri", [128, 128], F32)
        onesF = sb("onesF", [128, 128], F32)
        identF = sb("identF", [128, 128], F32)
        identB = sb("identB", [128, 128], BF16)
        negm4 = sb("negm4", [128, 512], BF16)
        cw = sb("cw", [128, 12, 4], F32)
        cb = sb("cb", [128, 12], F32)
        dtb_b = sb("dtb_b", [128, 16], F32)
        a_b = sb("a_b", [128, 16], F32)
        dsk_b = sb("dsk_b", [128, 16], F32)
        hal = sb("hal", [128, 12, 3], F32)
        rW = Res(); rC = Res()

        with contextlib.ExitStack() as stw:
            wst = sb("wstB", [128, 8, 512], F32, stw)
            rwst = Res()
            pieces = [(OFF_ZS, 0, 512, Wz), (OFF_ZS + 512, 512, 512, Wz),
                      (OFF_XBC, 0, 512, Wx), (OFF_XBC + 512, 512, 512, Wx), (OFF_XBC + 1024, 1024, 512, Wx),
                      (OFF_DT, 0, 16, Wdt)]
            for i, (off, dst0, n, Wt) in enumerate(pieces):
                T.dma("sync", wst[:, :, 0:n], w_in_v[:, :, off:off + n], "ld_wB", writes=[rwst])
                T.op("vector", lambda e, o=Wt[:, 0:4, dst0:dst0 + n], i_=wst[:, 0:4, 0:n]: e.tensor_copy(out=o, in_=i_),
                     reads=[rwst], writes=[rW])
                T.op("scalar", lambda e, o=Wt[:, 4:8, dst0:dst0 + n], i_=wst[:, 4:8, 0:n]: e.activation(out=o, in_=i_, func=AF.Copy),
                     reads=[rwst], writes=[])
            T.dma("sync", tri[:], dr["tri"][:, :], "ld_c1", writes=[rC])
            T.dma("sync", identF[:], dr["ident"][:, :], "ld_c2", writes=[rC])
            T.dma("sync", negm4[:], dr["negm4"][:, :], "ld_c3", writes=[rC])
            T.dma("sync", cw[:], dr["conv_wT"][:, :, :], "ld_c4", writes=[rC])
            T.dma("sync", cb[:], dr["conv_b2"][:, :], "ld_c5", writes=[rC])
            T.dma("sync", dtb_b[:], dr["dt_bias"].partition_broadcast(128), "ld_c6", writes=[rC])
            T.dma("sync", a_b[:], dr["a_log"].partition_broadcast(128), "ld_c7", writes=[rC])
            T.dma("sync", dsk_b[:], dr["d_skip"].partition_broadcast(128), "ld_c8", writes=[rC])
            barrier(P)
            I_memset(P, "vector", onesF[:], 1.0)
            I_memset(P, "vector", hal[:], 0.0)
            I_copy(P, "vector", identB[:], identF[:])
            I_act(P, a_b[:], a_b[:], AF.Exp)
            barrier(P)
            I_ts(P, "vector", a_b[:], a_b[:], -1.0, None, ALU.mult)
            barrier(P)

        xb = [sb(f"xb{i}", [128, 515], F32) for i in range(2)]
        cvb = [sb(f"cvb{i}", [128, 512], F32) for i in range(2)]
        xtmp = [sb(f"xtmp{i}", [128, 512], F32) for i in range(2)]
        BT = sb("BT", [128, 2, 512], BF16)
        CT = sb("CT", [128, 2, 512], BF16)
        xs_tok = sb("xs_tok", [128, 4, 1024], F32)
        zs = sb("zs", [128, 4, 1024], F32)
        Btok = sb("Btok", [128, 4, 2, 128], BF16)
        Gm = sb("Gm", [128, 2, 4, 128], BF16)
        sm = {k: sb(k, [128, 64], F32) for k in ("dtx", "dt4", "adt4", "nacs4", "D04", "dte4", "ddt4", "cdb4")}
        xg = [sb(f"xg{i}", [128, 1024], BF16) for i in range(2)]
        xgd = [sb(f"xgd{i}", [128, 1024], BF16) for i in range(2)]
        Dm = [sb(f"Dm{i}", [128, 512], BF16) for i in range(2)]
        MT = [sb(f"MT{i}", [128, 512], BF16) for i in range(2)]
        H = sb("H", [128, 2, 512], F32)
        Hbf = sb("Hbf", [128, 2, 512], BF16)
        yo = [sb(f"yo{i}", [128, 512], F32) for i in range(2)]
        tsk = [sb(f"tsk{i}", [128, 512], F32) for i in range(4)]
        y = [sb(f"y{i}", [128, 1024], F32) for i in range(1)]
        ssq = [sb(f"ssq{i}", [128, 1], F32) for i in range(2)]
        rstd = [sb(f"rstd{i}", [128, 1], F32) for i in range(2)]
        mixn = [sb(f"mixn{i}", [128, 1024], BF16) for i in range(2)]
        mixT = [sb(f"mixT{i}", [128, 8, 512], BF16) for i in range(1)]

        R_ = lambda n: [Res() for _ in range(n)]
        rxb, rxbh, rcv, rxtmp = R_(2), R_(2), R_(2), R_(2)
        rhal = R_(12)
        rBT, rCT, rxs, rzs, rBtok, rGm = Res(), Res(), Res(), Res(), Res(), Res()
        rsm = {k: Res() for k in sm}
        rxg, rxgd, rDm, rMT, ryo, rtsk, ry, rssq, rrstd, rmixn = R_(2), R_(2), R_(2), R_(2), R_(2), R_(4), R_(1), R_(2), R_(2), R_(2)
        rmixT = [R_(2)]
        rH, rHbf = R_(2), R_(2)
        rbank = R_(8)
        pb0, pb1, smb, trb, Rb0, Rb1, Yb, osb = banks
        B_PB0, B_PB1, B_SM, B_TR, B_R0, B_R1, B_Y, B_OS = range(8)
        trb_b = trb[:].bitcast(BF16)

        I_memset(P, "vector", H[:], 0.0)
        I_memset(P, "vector", Hbf[:], 0.0)
        barrier(P)

        def v3(ap):
            return ap.rearrange("p (h e) -> p h e", e=64)

        pbi = 0
        for sc in range(n_sc):
            T0 = 512 * sc
            def stage_A(ft):
                nonlocal pbi
                b = pbi; pbi ^= 1
                pbk = banks[b]
                xbi = ft % 2
                T.mm_group([(pbk[:, :], Wx[:, kc, ft * 128:(ft + 1) * 128], XT[:, kc, T0:T0 + 512], kc == 0, kc == 7)
                            for kc in range(8)], reads=[rW, rXT], writes=[rbank[b]])
                T.op("scalar", lambda e, o=xb[xbi][:, 3:515], i_=pbk[:, :]: e.activation(out=o, in_=i_, func=AF.Copy),
                     reads=[rbank[b]], writes=[rxb[xbi]])
                cv = cvb[xbi]
                T.op("scalar", lambda e, o=cv[:, :], i_=pbk[:, :], s_=cw[:, ft, 3:4]: e.activation(out=o, in_=i_, func=AF.Copy, scale=s_),
                     reads=[rbank[b], rC], writes=[rcv[xbi]])
                T.op("gpsimd", lambda e, o=xb[xbi][:, 0:3], i_=hal[:, ft, :]: e.tensor_copy(out=o, in_=i_),
                     reads=[rhal[ft]], writes=[rxbh[xbi]])
                T.op("gpsimd", lambda e, o=hal[:, ft, :], i_=xb[xbi][:, 512:515]: e.tensor_copy(out=o, in_=i_),
                     reads=[rxb[xbi]], writes=[rhal[ft]])

            def stage_B(ft):
                xbi = ft % 2
                cv = cvb[xbi]
                for j in range(3):
                    T.op("vector", lambda e, o=cv[:, :], i_=xb[xbi][:, j:j + 512], s_=cw[:, ft, j:j + 1]:
                         e.scalar_tensor_tensor(out=o, in0=i_, scalar=s_, in1=o, op0=ALU.mult, op1=ALU.add),
                         reads=[rxb[xbi], rxbh[xbi]], writes=[rcv[xbi]])
                if ft < 8:
                    xi = ft % 2
                    T.op("scalar", lambda e, o=xtmp[xi][:, :], i_=cv[:, :], b_=cb[:, ft:ft + 1]:
                         e.activation(out=o, in_=i_, func=AF.Silu, bias=b_), reads=[rcv[xbi], rC], writes=[rxtmp[xi]])
                elif ft < 10:
                    T.op("scalar", lambda e, o=BT[:, ft - 8, :], i_=cv[:, :], b_=cb[:, ft:ft + 1]:
                         e.activation(out=o, in_=i_, func=AF.Silu, bias=b_), reads=[rcv[xbi], rC], writes=[rBT])
                else:
                    T.op("scalar", lambda e, o=CT[:, ft - 10, :], i_=cv[:, :], b_=cb[:, ft:ft + 1]:
                         e.activation(out=o, in_=i_, func=AF.Silu, bias=b_), reads=[rcv[xbi], rC], writes=[rCT])
            def stage_T(ft):
                if ft < 0 or ft >= 8:
                    return
                xi = ft % 2
                tbk = B_TR if ft % 2 == 0 else B_Y
                ev = None
                for ci in range(4):
                    ev = P.op("tensor", lambda e, o=banks[tbk][:, 128 * ci:128 * (ci + 1)], i_=xtmp[xi][:, 128 * ci:128 * (ci + 1)]:
                              e.transpose(o, i_, identF[:]), T.deps([rxtmp[xi], rC], [rbank[tbk]]) if ci == 0 else ())
                T.mark(ev, [rxtmp[xi], rC], [rbank[tbk]])

            def stage_C(ft):
                if ft < 0 or ft >= 8:
                    return
                tbk = B_TR if ft % 2 == 0 else B_Y
                T.op("scalar", lambda e, o=xs_tok[:, :, ft * 128:(ft + 1) * 128], i_=banks[tbk][:, :].rearrange("p (c f) -> p c f", f=128):
                     e.activation(out=o, in_=i_, func=AF.Copy), reads=[rbank[tbk]], writes=[rxs])

            def z_group(zi):
                nonlocal pbi
                ci, half = divmod(zi, 2)
                t0 = T0 + 128 * ci
                b = pbi; pbi ^= 1
                T.mm_group([(banks[b][:, :], XT[:, kc, t0:t0 + 128], Wz[:, kc, 512 * half:512 * (half + 1)], kc == 0, kc == 7)
                            for kc in range(8)], reads=[rW, rXT], writes=[rbank[b]])
                T.op("scalar", lambda e, o=zs[:, ci, 512 * half:512 * (half + 1)], i_=banks[b][:, :]:
                     e.activation(out=o, in_=i_, func=AF.Silu), reads=[rbank[b]], writes=[rzs])

            stage_A(0)
            for ft in range(13):
                if ft + 1 < 12:
                    stage_A(ft + 1)
                stage_T(ft - 1)
                if ft < 12:
                    stage_B(ft)
                stage_C(ft - 1)
                if 2 <= ft < 10:
                    z_group(ft - 2)
            ev = None
            for ci in range(4):
                for g in range(2):
                    j = 2 * ci + g
                    ev = P.op("tensor", lambda e, o=trb_b[:, 128 * j:128 * (j + 1)], i_=BT[:, g, 128 * ci:128 * (ci + 1)]:
                              e.transpose(o, i_, identB[:]), T.deps([rBT, rC], [rbank[B_TR]]) if j == 0 else ())
            T.mark(ev, [rBT, rC], [rbank[B_TR]])
            T.op("vector", lambda e: e.tensor_copy(out=Btok[:].rearrange("p c g n -> p (c g n)"), in_=trb_b[:, 0:1024]),
                 reads=[rbank[B_TR]], writes=[rBtok])
            for g in range(2):
                bk = B_R0 + g
                T.mm_group([(banks[bk][:, 128 * ci:128 * (ci + 1)], BT[:, g, 128 * ci:128 * (ci + 1)], CT[:, g, 128 * ci:128 * (ci + 1)],
                             True, True) for ci in range(4)], reads=[rBT, rCT], writes=[rbank[bk]])
                T.op("vector", lambda e, o=Gm[:, g, :, :].rearrange("p c l -> p (c l)"), i_=banks[bk][:, :]: e.tensor_copy(out=o, in_=i_),
                     reads=[rbank[bk]], writes=[rGm])
            mms = []
            for ci in range(4):
                t0 = T0 + 128 * ci
                for kc in range(8):
                    mms.append((smb[:, 16 * ci:16 * (ci + 1)], XT[:, kc, t0:t0 + 128], Wdt[:, kc, :], kc == 0 and ci == 0, kc == 7))
            T.mm_group(mms, reads=[rW, rXT], writes=[rbank[B_SM]])
            b16 = lambda ap: ap[:, :].unsqueeze(1).broadcast_to([128, 4, 16])
            c4 = lambda ap: ap[:, :].rearrange("p (c h) -> p c h", h=16)
            T.op("vector", lambda e: e.tensor_tensor(out=c4(sm["dtx"]), in0=c4(smb[:, 0:64]), in1=b16(dtb_b), op=ALU.add),
                 reads=[rbank[B_SM], rC], writes=[rsm["dtx"]])
            T.op("scalar", lambda e: e.activation(out=sm["dtx"][:], in_=sm["dtx"][:], func=AF.Exp), reads=[], writes=[rsm["dtx"]])
            T.op("scalar", lambda e: e.activation(out=sm["dt4"][:], in_=sm["dtx"][:], func=AF.Ln, bias=1.0),
                 reads=[rsm["dtx"]], writes=[rsm["dt4"]])
            T.op("vector", lambda e: e.tensor_tensor(out=c4(sm["adt4"]), in0=c4(sm["dt4"]), in1=b16(a_b), op=ALU.mult),
                 reads=[rsm["dt4"], rC], writes=[rsm["adt4"]])
            mms = []
            for ci in range(4):
                mms.append((smb[:, 64 + 16 * ci:64 + 16 * (ci + 1)], tri[:], sm["adt4"][:, 16 * ci:16 * (ci + 1)], False, True))
                mms.append((smb[:, 128 + 16 * ci:128 + 16 * (ci + 1)], onesF[:], sm["adt4"][:, 16 * ci:16 * (ci + 1)], False, True))
            T.mm_group(mms, reads=[rsm["adt4"], rC], writes=[rbank[B_SM]])
            T.op("vector", lambda e: e.tensor_scalar(out=sm["nacs4"][:], in0=smb[:, 64:128], scalar1=-1.0, scalar2=None, op0=ALU.mult),
                 reads=[rbank[B_SM]], writes=[rsm["nacs4"]])
            T.op("vector", lambda e: e.tensor_copy(out=sm["cdb4"][:], in_=smb[:, 128:192]), reads=[rbank[B_SM]], writes=[rsm["cdb4"]])
            T.op("vector", lambda e: e.tensor_tensor(out=sm["dte4"][:], in0=sm["cdb4"][:], in1=sm["nacs4"][:], op=ALU.add),
                 reads=[rsm["cdb4"], rsm["nacs4"]], writes=[rsm["dte4"]])
            T.op("scalar", lambda e: e.activation(out=sm["D04"][:], in_=sm["nacs4"][:], func=AF.Exp, scale=-1.0),
                 reads=[rsm["nacs4"]], writes=[rsm["D04"]])
            T.op("scalar", lambda e: e.activation(out=sm["dte4"][:], in_=sm["dte4"][:], func=AF.Exp), reads=[], writes=[rsm["dte4"]])
            T.op("scalar", lambda e: e.activation(out=sm["cdb4"][:], in_=sm["cdb4"][:], func=AF.Exp), reads=[rsm["dte4"]], writes=[rsm["cdb4"]])
            T.op("vector", lambda e: e.tensor_tensor(out=sm["ddt4"][:], in0=sm["dt4"][:], in1=sm["dte4"][:], op=ALU.mult),
                 reads=[rsm["dt4"], rsm["dte4"]], writes=[rsm["ddt4"]])

            def chunk_fns(ci):
                k = ci % 2
                cc = slice(128 * ci, 128 * ci + 128)
                hs = lambda ap, h0, n: ap[:, 16 * ci + h0:16 * ci + h0 + n]
                bch = lambda ap, h0, n: hs(ap, h0, n).unsqueeze(2).broadcast_to([128, n, 64])
                def prologue(cj):
                    kj = cj % 2
                    for g in range(2):
                        gs = slice(512 * g, 512 * (g + 1))
                        bcj = lambda ap, h0, n: ap[:, 16 * cj + h0:16 * cj + h0 + n].unsqueeze(2).broadcast_to([128, n, 64])
                        T.op("gpsimd", lambda e, o=v3(xg[kj][:, gs]), i_=v3(xs_tok[:, cj, gs]), b_=bcj(sm["dt4"], 8 * g, 8):
                             e.tensor_tensor(out=o, in0=i_, in1=b_, op=ALU.mult), reads=[rxs, rsm["dt4"]], writes=[rxg[kj]])
                        T.op("gpsimd", lambda e, o=v3(xgd[kj][:, gs]), i_=v3(xs_tok[:, cj, gs]), b_=bcj(sm["ddt4"], 8 * g, 8):
                             e.tensor_tensor(out=o, in0=i_, in1=b_, op=ALU.mult), reads=[rxs, rsm["ddt4"]], writes=[rxgd[kj]])
                        T.op("gpsimd", lambda e, o=v3(tsk[2 * kj + g][:, :]), i_=v3(xs_tok[:, cj, gs]), b_=bc64(dsk_b, 8 * g, 8):
                             e.tensor_tensor(out=o, in0=i_, in1=b_, op=ALU.mult), reads=[rxs, rC], writes=[rtsk[2 * kj + g]])

                def emit_R(q):
                    g, qd = divmod(q, 2)
                    h0 = 8 * g + 4 * qd
                    bk = B_R0 + (q % 2)
                    mms = [(banks[bk][:, :], identB[:], negm4[:], True, False)]
                    for j in range(4):
                        mms.append((banks[bk][:, 128 * j:128 * (j + 1)],
                                    hs(sm["adt4"], h0 + j, 1).broadcast_to([128, 128]), tri[:], False, j == 3))
                    T.mm_group(mms, reads=[rsm["adt4"], rC], writes=[rbank[bk]])

                def emit_exp(q):
                    g, qd = divmod(q, 2)
                    h0 = 8 * g + 4 * qd
                    bk = B_R0 + (q % 2)
                    for j in range(4):
                        T.op("scalar", lambda e, o=Dm[q % 2][:, 128 * j:128 * (j + 1)], i_=banks[bk][:, 128 * j:128 * (j + 1)],
                             b_=hs(sm["nacs4"], h0 + j, 1): e.activation(out=o, in_=i_, func=AF.Exp, bias=b_),
                             reads=[rbank[bk], rsm["nacs4"]], writes=[rDm[q % 2]])

                def emit_MT(q):
                    g, qd = divmod(q, 2)
                    T.op("vector", lambda e, o=MT[q % 2][:].rearrange("p (j l) -> p j l", l=128),
                         i_=Dm[q % 2][:].rearrange("p (j l) -> p j l", l=128),
                         b_=Gm[:, g, ci, :].unsqueeze(1).broadcast_to([128, 4, 128]):
                         e.tensor_tensor(out=o, in0=i_, in1=b_, op=ALU.mult), reads=[rDm[q % 2], rGm], writes=[rMT[q % 2]])

                def emit_ydiag(q):
                    g, qd = divmod(q, 2)
                    h0 = 8 * g + 4 * qd
                    ybk = B_Y if g == 0 else B_PB0
                    mms = []
                    for j in range(4):
                        hh = 4 * qd + j
                        mms.append((banks[ybk][:, 64 * hh:64 * (hh + 1)], MT[q % 2][:, 128 * j:128 * (j + 1)],
                                    xg[k][:, 64 * (h0 + j):64 * (h0 + j + 1)], (qd == 0 and j == 0), False))
                    T.mm_group(mms, reads=[rMT[q % 2], rxg[k]], writes=[rbank[ybk]])

                def emit_yoff(g):
                    obk = B_OS if g == 0 else B_PB1
                    T.mm_group([(banks[obk][:, :], CT[:, g, cc], Hbf[:, g, :], True, True)], reads=[rCT, rHbf[g]], writes=[rbank[obk]])

                def emit_comb(g):
                    gs = slice(512 * g, 512 * (g + 1))
                    obk = B_OS if g == 0 else B_PB1
                    ybk = B_Y if g == 0 else B_PB0
                    T.op("vector", lambda e, o=v3(yo[g][:, :]), i_=v3(banks[obk][:, :]), b_=bch(sm["D04"], 8 * g, 8):
                         e.tensor_tensor(out=o, in0=i_, in1=b_, op=ALU.mult), reads=[rbank[obk], rsm["D04"]], writes=[ryo[g]])
                    T.op("vector", lambda e, o=y[0][:, gs], i_=banks[ybk][:, :]: e.tensor_tensor(out=o, in0=i_, in1=yo[g][:, :], op=ALU.add),
                         reads=[rbank[ybk], ryo[g]], writes=[ry[0]])
                    T.op("gpsimd", lambda e, o=y[0][:, gs], t_=tsk[2 * k + g][:, :]: e.tensor_tensor(out=o, in0=o, in1=t_, op=ALU.add),
                         reads=[rtsk[2 * k + g]], writes=[ry[0]])

                def emit_state(g):
                    gs = slice(512 * g, 512 * (g + 1))
                    obk = B_OS if g == 0 else B_PB1
                    T.mm_group([(banks[obk][:, :], Btok[:, ci, g, :], xgd[k][:, gs], True, True)],
                               reads=[rBtok, rxgd[k]], writes=[rbank[obk]])
                    T.op("vector", lambda e, o=v3(H[:, g, :]), b_=bch(sm["cdb4"], 8 * g, 8):
                         e.tensor_tensor(out=o, in0=o, in1=b_, op=ALU.mult), reads=[rsm["cdb4"]], writes=[rH[g]])
                    T.op("vector", lambda e, o=H[:, g, :], i_=banks[obk][:, :]: e.tensor_tensor(out=o, in0=i_, in1=o, op=ALU.add),
                         reads=[rbank[obk]], writes=[rH[g]])
                    T.op("scalar", lambda e, o=Hbf[:, g, :], i_=H[:, g, :]: e.activation(out=o, in_=i_, func=AF.Copy),
                         reads=[rH[g]], writes=[rHbf[g]])

                def early():
                    emit_R(0); emit_R(1)
                    emit_yoff(0); emit_yoff(1)
                    emit_exp(0); emit_exp(1)
                    emit_MT(0); emit_MT(1)
                    emit_R(2); emit_R(3)
                    emit_ydiag(0); emit_ydiag(1)
                    emit_exp(2); emit_exp(3)
                    emit_MT(2); emit_MT(3)
                    emit_ydiag(2); emit_ydiag(3)

                def mid():
                    if ci + 1 < 4:
                        prologue(ci + 1)
                    emit_comb(0)
                    emit_state(0)
                    emit_comb(1)
                    emit_state(1)

                def late():
                    yk, ssk, rsk, mxk = y[0], ssq[k], rstd[k], mixn[k]
                    T.op("vector", lambda e, yk=yk, z_=zs[:, ci, :]: e.tensor_tensor(out=yk[:], in0=yk[:], in1=z_, op=ALU.mult),
                         reads=[rzs], writes=[ry[0]])
                    T.op("vector", lambda e, ssk=ssk: e.memset(ssk[:], 0.0), reads=[], writes=[rssq[k]])
                    T.op("scalar", lambda e, yk=yk, ssk=ssk, mxk=mxk: e.activation(out=mxk[:], in_=yk[:], func=AF.Square, accum_out=ssk[:]),
                         reads=[ry[0]], writes=[rmixn[k], rssq[k]])
                    T.op("scalar", lambda e, ssk=ssk, rsk=rsk: e.activation(out=rsk[:], in_=ssk[:], func=AF.Ln, bias=EPS, scale=1.0 / 1024),
                         reads=[rssq[k]], writes=[rrstd[k]])
                    T.op("scalar", lambda e, rsk=rsk: e.activation(out=rsk[:], in_=rsk[:], func=AF.Exp, scale=-0.5),
                         reads=[], writes=[rrstd[k]])
                    T.op("vector", lambda e, yk=yk, rsk=rsk, mxk=mxk: e.tensor_scalar(out=mxk[:], in0=yk[:], scalar1=rsk[:, 0:1], scalar2=None, op0=ALU.mult),
                         reads=[ry[0], rrstd[k]], writes=[rmixn[k]])
                    mt = mixT[0]
                    for half in range(2):
                        ev = None
                        for j in range(4):
                            ft = 4 * half + j
                            ev = P.op("tensor", lambda e, o=trb_b[:, 128 * j:128 * (j + 1)], i_=mixn[k][:, 128 * ft:128 * (ft + 1)]:
                                      e.transpose(o, i_, identB[:]), T.deps([rmixn[k], rC], [rbank[B_TR]]) if j == 0 else ())
                        T.mark(ev, [rmixn[k], rC], [rbank[B_TR]])
                        src = trb_b[:, 0:512].rearrange("p (j t) -> p j t", t=128)
                        if half == 0:
                            T.op("vector", lambda e, o=mt[:, 0:4, cc], i_=src: e.tensor_copy(out=o, in_=i_),
                                 reads=[rbank[B_TR]], writes=[rmixT[0][0]])
                        else:
                            T.op("scalar", lambda e, o=mt[:, 4:8, cc], i_=src: e.activation(out=o, in_=i_, func=AF.Copy),
                                 reads=[rbank[B_TR]], writes=[rmixT[0][1]])
                return prologue, early, mid, late

            fns = [chunk_fns(ci) for ci in range(4)]
            fns[0][0](0)
            fns[0][1]()
            for ci in range(4):
                fns[ci][2]()
                if ci + 1 < 4:
                    fns[ci + 1][1]()
                fns[ci][3]()
            T.dma("gpsimd", U[1024:2048, T0:T0 + 512].rearrange("(f p) t -> p f t", p=128), mixT[0][:, :, :], "st_uB",
                  reads=rmixT[0])
        barrier(P)
```

```python
import contextlib
import os
_SKIP = set(os.environ.get('KSKIP', '').split(','))
import numpy as np
import ml_dtypes
import concourse.bass as bass
import concourse.mybir as mybir
from concourse.bass_utils import run_bass_kernel_spmd

F32 = mybir.dt.float32
BF16 = mybir.dt.bfloat16
AF = mybir.ActivationFunctionType
ALU = mybir.AluOpType
AX = mybir.AxisListType

S = 4096
D = 1024
NH = 16
HD = 64
DIL = (1, 4, 16)
D_IN = 6672
OFF_Q, OFF_K, OFF_V, OFF_ZA, OFF_ZS, OFF_XBC, OFF_DT = 0, 1024, 2048, 3072, 4096, 5120, 6656
EPS = 1e-5


class Ev:
    __slots__ = ("eng", "idx", "sem", "val")

    def __init__(self, eng, idx, sem=None, val=None):
        self.eng, self.idx, self.sem, self.val = eng, idx, sem, val


class Prog:
    ENGS = ("sync", "scalar", "vector", "gpsimd", "tensor")

    def __init__(self, nc):
        self.nc = nc
        self.q = {e: [] for e in self.ENGS}
        self.dma_cnt = {}

    def op(self, eng, fn, deps=()):
        lst = self.q[eng]
        ev = Ev(eng, len(lst))
        lst.append([fn, [d for d in deps if d is not None], ev, False])
        return ev

    def dma(self, eng, out, in_, slot, deps=()):
        self.dma_cnt[slot] = self.dma_cnt.get(slot, 0) + 16
        ev = Ev(eng, len(self.q[eng]), sem=slot, val=self.dma_cnt[slot])
        self.q[eng].append([lambda e, o=out, i=in_: e.dma_start(out=o, in_=i),
                            [d for d in deps if d is not None], ev, True])
        return ev

    def emit(self, final_waits):
        nc = self.nc
        ref = {e: set() for e in self.ENGS}
        for e in self.ENGS:
            for fn, deps, ev, is_dma in self.q[e]:
                for d in deps:
                    if d.sem is None or d.sem.startswith("e_"):
                        ref[d.eng].add(d.idx)
        for d in final_waits:
            if d.sem is None:
                ref[d.eng].add(d.idx)
        for e in self.ENGS:
            c = 0
            for i, item in enumerate(self.q[e]):
                if item[3]:
                    continue
                if i in ref[e]:
                    c += 1
                    item[2].sem = "e_" + e
                    item[2].val = c
            assert c < 60000, (e, c)
        for s, v in self.dma_cnt.items():
            assert v < 60000, (s, v)
        names = ["e_" + e for e in self.ENGS] + sorted(self.dma_cnt)
        with contextlib.ExitStack() as st:
            sems = {n: st.enter_context(nc.semaphore(n)) for n in names}
            block = st.enter_context(nc.Block())
            for e in self.ENGS:
                items = self.q[e]
                fw = final_waits if e == "sync" else ()

                def body(eng, items=items, fw=fw):
                    seen = {}
                    for fn, deps, ev, is_dma in items:
                        need = {}
                        for d in deps:
                            assert d.sem is not None
                            if need.get(d.sem, 0) < d.val:
                                need[d.sem] = d.val
                        for sname, v in need.items():
                            if seen.get(sname, 0) < v:
                                eng.wait_ge(sems[sname], v)
                                seen[sname] = v
                        ins = fn(eng)
                        if is_dma:
                            ins.then_inc(sems[ev.sem], 16)
                        elif ev.sem is not None:
                            ins.then_inc(sems[ev.sem], 1)
                    for d in fw:
                        if seen.get(d.sem, 0) < d.val:
                            eng.wait_ge(sems[d.sem], d.val)
                            seen[d.sem] = d.val

                getattr(block, e)(body)


def latest(*evs):
    return [e for e in evs if e is not None]


def _bf16(a):
    return np.asarray(a, np.float32).astype(ml_dtypes.bfloat16)


def make_constants():
    c = {}
    slopes = 2.0 ** (-8.0 * np.arange(1, NH + 1) / NH)
    t = np.arange(S)
    hi_pos = (t >> 7).astype(np.float32)
    lo_pos = (t & 127).astype(np.float32)
    aug = np.zeros((NH, 2, 12, S), np.float32)
    for h in range(NH):
        cc = np.float64(slopes[h])
        c1 = np.float64(_bf16(cc).astype(np.float64))
        c2 = np.float64(_bf16(cc - c1).astype(np.float64))
        c3 = np.float64(_bf16(cc - c1 - c2).astype(np.float64))
        for j, cj in enumerate((c1, c2, c3)):
            aug[h, 0, j] = 128.0 * cj
            aug[h, 0, 3 + j] = cj
            aug[h, 0, 6 + j] = hi_pos
            aug[h, 0, 9 + j] = lo_pos
            aug[h, 1, j] = hi_pos
            aug[h, 1, 3 + j] = lo_pos
            aug[h, 1, 6 + j] = -128.0 * cj
            aug[h, 1, 9 + j] = -cj
    c["aug"] = _bf16(aug)
    ki = np.arange(128)[:, None]
    qi = np.arange(128)[None, :]
    mprev = np.where(ki >= qi, 0.0, -30000.0).astype(np.float32)
    mcur = np.where(ki <= qi, 0.0, -30000.0).astype(np.float32)
    c["mask4"] = np.concatenate([mprev, mcur, mprev, mcur], axis=1).astype(np.float32)
    c["tri"] = np.triu(np.ones((128, 128), np.float32))
    c["ident"] = np.eye(128, dtype=np.float32)
    c["identb"] = _bf16(np.eye(128, dtype=np.float32))
    si = np.arange(128)[:, None]
    li = np.arange(128)[None, :]
    c["negm4"] = _bf16(np.tile(np.where(si > li, -30000.0, 0.0), (1, 4)))
    return c


def I_act(P, out, in_, func, deps=(), bias=None, scale=None, accum_out=None, eng="scalar"):
    kw = {}
    if bias is not None:
        kw["bias"] = bias
    if scale is not None:
        kw["scale"] = scale
    if accum_out is not None:
        kw["accum_out"] = accum_out
    return P.op(eng, lambda e: e.activation(out=out, in_=in_, func=func, **kw), deps)


def I_copy(P, eng, out, in_, deps=()):
    if eng == "scalar":
        return P.op(eng, lambda e: e.activation(out=out, in_=in_, func=AF.Copy), deps)
    return P.op(eng, lambda e: e.tensor_copy(out=out, in_=in_), deps)


def I_tt(P, eng, out, in0, in1, op, deps=()):
    return P.op(eng, lambda e: e.tensor_tensor(out=out, in0=in0, in1=in1, op=op), deps)


def I_ts(P, eng, out, in0, s1, s2, op0, op1=None, deps=(), accum_out=None):
    kw = {}
    if op1 is not None:
        kw["op1"] = op1
    if accum_out is not None:
        kw["accum_out"] = accum_out
    return P.op(eng, lambda e: e.tensor_scalar(out=out, in0=in0, scalar1=s1, scalar2=s2, op0=op0, **kw), deps)


def I_stt(P, eng, out, in0, scalar, in1, op0, op1, deps=()):
    return P.op(eng, lambda e: e.scalar_tensor_tensor(out=out, in0=in0, scalar=scalar, in1=in1, op0=op0, op1=op1), deps)


def I_mm(P, out, lhsT, rhs, start, stop, deps=(), skip=True):
    return P.op("tensor", lambda e: e.matmul(out, lhsT=lhsT, rhs=rhs, start=start, stop=stop,
                                             skip_group_check=skip), deps)


def I_memset(P, eng, ap, val, deps=()):
    return P.op(eng, lambda e: e.memset(ap, val), deps)


def barrier(P):
    evs = []
    for e in P.ENGS:
        for item in reversed(P.q[e]):
            if not item[3]:
                evs.append(item[2])
                break
    last = {}
    for e in P.ENGS:
        for item in P.q[e]:
            if item[3]:
                last[item[2].sem] = item[2]
    evs += list(last.values())
    out = []
    for e in P.ENGS:
        out.append(P.op(e, lambda eng: eng.nop(), evs))
    return out


def tok_ap(t, d, r, n, cnt=128, lo=0, hi=None):
    start = d * (128 * n + lo) + r
    stop = start + d * (cnt - 1) + 1
    return slice(start, stop, d)


def build_program(debug_u=False, pairs=tuple(range(8)), phases=("A", "B", "C"), dbg=3, n_sc=8):
    nc = bass.Bass("TRN2", target_bir_lowering=False)
    xT = nc.dram_tensor("xT", [D, S], F32, kind="ExternalInput").ap()
    w_in = nc.dram_tensor("w_in", [D, D_IN], F32, kind="ExternalInput").ap()
    aug = nc.dram_tensor("aug", [NH, 2, 12, S], BF16, kind="ExternalInput").ap()
    mask4_d = nc.dram_tensor("mask4", [128, 512], F32, kind="ExternalInput").ap()
    dr = {}
    for name, shape, dt in (("tri", [128, 128], F32), ("ident", [128, 128], F32), ("identb", [128, 128], BF16), ("negm4", [128, 512], BF16),
                            ("conv_wT", [128, 12, 4], F32), ("conv_b2", [128, 12], F32), ("dt_bias", [16], F32),
                            ("a_log", [16], F32), ("d_skip", [16], F32), ("ssm_norm_g", [1024], F32),
                            ("att_norm_g2", [128, 8], F32), ("ssm_norm_g2", [128, 8], F32), ("ln_g", [1024], F32), ("ln_b", [1024], F32),
                            ("w_out", [2048, D], F32), ("x", [S, D], F32)):
        dr[name] = nc.dram_tensor(name, shape, dt, kind="ExternalInput").ap()
    U = nc.dram_tensor("U", [2048, S], BF16, kind="ExternalOutput" if debug_u else "Internal").ap()
    out_d = nc.dram_tensor("out", [S, D], F32, kind="ExternalOutput").ap()

    P = Prog(nc)
    final_waits = []
    with contextlib.ExitStack() as st:
        def sb(name, shape, dt, stack=st):
            return stack.enter_context(nc.sbuf_tensor(name, shape, dt))

        banks = [st.enter_context(nc.psum_tensor(f"bank{i}", [128, 512], F32)) for i in range(8)]
        w_in_v = w_in.rearrange("(kc p) c -> p kc c", p=128)
        with contextlib.ExitStack() as stx:
            XT = sb("XT", [128, 8, S], BF16, stx)
            with contextlib.ExitStack() as st0:
                xstage = [sb(f"xstage{i}", [128, S], F32, st0) for i in range(2)]
                free = [[], []]
                for kc in range(8):
                    b = kc % 2
                    ld = P.dma("sync", xstage[b][:], xT[kc * 128:(kc + 1) * 128, :], f"ld_x{b}", deps=free[b])
                    e1 = I_copy(P, "vector", XT[:, kc, 0:1536], xstage[b][:, 0:1536], [ld])
                    e2 = I_copy(P, "scalar", XT[:, kc, 1536:3072], xstage[b][:, 1536:3072], [ld])
                    e3 = I_copy(P, "gpsimd", XT[:, kc, 3072:4096], xstage[b][:, 3072:4096], [ld])
                    free[b] = [e1, e2, e3]
                barrier(P)

            if "A" in phases:
                phase_A(nc, P, st, banks, XT, w_in_v, aug, mask4_d, U, pairs, dbg, dr)
                barrier(P)
            if "B" in phases:
                (phase_B if 'oldB' in _SKIP else phase_B2)(nc, P, banks, XT, w_in_v, dr, U, n_sc)
                barrier(P)
        if "C" in phases:
            phase_C(nc, P, banks, dr, U, out_d)
            barrier(P)
        last = {}
        for e in P.ENGS:
            for item in P.q[e]:
                if item[3]:
                    last[item[2].sem] = item[2]
        final_waits = list(last.values())
        P.emit(final_waits)
    return nc


def phase_C(nc, P, banks, dr, U, out_d, n_tg=8):
    T = Trk(P)
    ALPHA = 2.0 ** 0.25
    with contextlib.ExitStack() as st:
        def sb(name, shape, dt, stack=None):
            return (stack or st).enter_context(nc.sbuf_tensor("C_" + name, shape, dt))

        Wo = sb("Wo", [128, 16, D], BF16)
        gA = sb("gA", [128, 16], F32)
        lg_b = sb("lg_b", [128, D], F32)
        lb_b = sb("lb_b", [128, D], F32)
        onesB = sb("onesB", [128, 1], BF16)
        rC = Res(); rWo = Res()
        T.dma("sync", gA[:, 0:8], dr["att_norm_g2"][:, :], "ld_d1", writes=[rC])
        T.dma("sync", gA[:, 8:16], dr["ssm_norm_g2"][:, :], "ld_d1b", writes=[rC])
        T.dma("sync", lg_b[:], dr["ln_g"].partition_broadcast(128), "ld_d2", writes=[rC])
        T.dma("sync", lb_b[:], dr["ln_b"].partition_broadcast(128), "ld_d3", writes=[rC])
        I_memset(P, "vector", onesB[:], 1.0)
        wo_v = dr["w_out"].rearrange("(f p) d -> p f d", p=128)
        with contextlib.ExitStack() as stw:
            wst = [sb(f"wstC{i}", [128, 2, D], F32, stw) for i in range(2)]
            rws = [Res(), Res()]
            for i in range(8):
                b = i % 2
                T.dma("sync", wst[b][:], wo_v[:, 2 * i:2 * i + 2, :], f"ld_wo{b}", writes=[rws[b]])
                for j in range(2):
                    f = 2 * i + j
                    if j == 0:
                        T.op("vector", lambda e, o=Wo[:, f, :], i_=wst[b][:, j, :], s_=gA[:, f:f + 1]:
                             e.tensor_scalar(out=o, in0=i_, scalar1=s_, scalar2=None, op0=ALU.mult),
                             reads=[rws[b], rC], writes=[])
                    else:
                        T.op("scalar", lambda e, o=Wo[:, f, :], i_=wst[b][:, j, :], s_=gA[:, f:f + 1]:
                             e.activation(out=o, in_=i_, func=AF.Copy, scale=s_),
                             reads=[rws[b], rC], writes=[])
            barrier(P)

        Ub = [sb(f"Ub{i}", [128, 16, 512], BF16) for i in range(2)]
        xt = [sb(f"xt{i}", [128, 4, D], F32) for i in range(2)]
        sq = [sb(f"sq{i}", [128, 8, 128], BF16) for i in range(2)]
        rr = [sb(f"rr{i}", [128, D], F32) for i in range(2)]
        ot = [sb(f"ot{i}", [128, D], F32) for i in range(2)]
        st6 = [sb(f"st6{i}", [128, 2, 6], F32) for i in range(2)]
        mv = [sb(f"mv{i}", [128, 2], F32) for i in range(2)]
        ra = [sb(f"ra{i}", [128, 1], F32) for i in range(2)]
        rl = [sb(f"rl{i}", [128, 1], F32) for i in range(2)]
        rUa = [Res(), Res()]; rUs = [Res(), Res()]; rxt = [[Res() for _ in range(4)] for _ in range(2)]; rot = [Res(), Res()]
        r = [{k: Res() for k in ("sq", "rr", "st6", "mv", "ra", "rl")} for _ in range(2)]
        slots = [(banks[0], banks[1]), (banks[2], banks[3]), (banks[4], banks[5])]
        rslot = [[Res(), Res()] for _ in range(3)]
        ssb = [banks[6], banks[7]]; rss = [Res(), Res()]
        U_v = U.rearrange("(f p) t -> p f t", p=128)
        x_v = dr["x"].rearrange("(c p) d -> p c d", p=128)
        slot_i = 0
        cnt = 0
        for tg in range(n_tg):
            b = tg % 2
            T0 = 512 * tg
            T.dma("sync", Ub[b][:, 0:8, :], U_v[:, 0:8, T0:T0 + 512], f"ld_ua{b}", writes=[rUa[b]])
            T.dma("sync", Ub[b][:, 8:16, :], U_v[:, 8:16, T0:T0 + 512], f"ld_us{b}", writes=[rUs[b]])
            T.dma("sync", xt[b][:], x_v[:, 4 * tg:4 * tg + 4, :], f"ld_xt{b}", writes=rxt[b])
            for ci in range(4):
                cc = slice(128 * ci, 128 * ci + 128)
                k = cnt % 2
                rk = r[k]
                T.op("scalar", lambda e, o=xt[b][:, ci, :]: e.activation(out=o, in_=o, func=AF.Copy, scale=ALPHA),
                     reads=[], writes=[rxt[b][ci]])
                T.op("scalar", lambda e, o=sq[k][:], i_=Ub[b][:, 0:8, cc]: e.activation(out=o, in_=i_, func=AF.Square),
                     reads=[rUa[b]], writes=[rk["sq"]])
                halves = []
                for half in range(2):
                    cols = slice(512 * half, 512 * (half + 1))
                    sl = slot_i; slot_i = (slot_i + 1) % 3
                    bkA, bkB = slots[sl]
                    T.mm_group([(bkA[:, :], Ub[b][:, f, cc], Wo[:, f, cols], f == 0, f == 7) for f in range(8)],
                               reads=[rUa[b]], writes=[rslot[sl][0]])
                    T.mm_group([(bkB[:, :], Ub[b][:, 8 + f, cc], Wo[:, 8 + f, cols], f == 0, f == 7) for f in range(8)],
                               reads=[rUs[b]], writes=[rslot[sl][1]])
                    halves.append((sl, cols))
                T.mm_group([(ssb[k][:, 0:1], sq[k][:, f, :], onesB[:], f == 0, f == 7) for f in range(8)],
                           reads=[rk["sq"]], writes=[rss[k]])
                T.op("scalar", lambda e, o=ra[k][:], i_=ssb[k][:, 0:1]: e.activation(out=o, in_=i_, func=AF.Ln, bias=EPS, scale=1.0 / 1024),
                     reads=[rss[k]], writes=[rk["ra"]])
                T.op("scalar", lambda e, o=ra[k][:]: e.activation(out=o, in_=o, func=AF.Exp, scale=-0.5), reads=[], writes=[rk["ra"]])
                for sl, cols in halves:
                    bkA, bkB = slots[sl]
                    T.op("vector", lambda e, o=rr[k][:, cols], i_=bkA[:, :], x_=xt[b][:, ci, cols], s_=ra[k][:, 0:1]:
                         e.scalar_tensor_tensor(out=o, in0=i_, scalar=s_, in1=x_, op0=ALU.mult, op1=ALU.add),
                         reads=[rslot[sl][0], rk["ra"], rxt[b][ci]], writes=[rk["rr"]])
                    T.op("vector", lambda e, o=rr[k][:, cols], i_=bkB[:, :]: e.tensor_tensor(out=o, in0=i_, in1=o, op=ALU.add),
                         reads=[rslot[sl][1]], writes=[rk["rr"]])
                for half in range(2):
                    T.op("vector", lambda e, o=st6[k][:, half, :], i_=rr[k][:, 512 * half:512 * (half + 1)]: e.bn_stats(out=o, in_=i_),
                         reads=[rk["rr"]], writes=[rk["st6"]])
                T.op("vector", lambda e, o=mv[k][:], i_=st6[k][:]: e.bn_aggr(out=o, in_=i_), reads=[rk["st6"]], writes=[rk["mv"]])
                T.op("scalar", lambda e, o=rl[k][:], i_=mv[k][:, 1:2]: e.activation(out=o, in_=i_, func=AF.Ln, bias=EPS),
                     reads=[rk["mv"]], writes=[rk["rl"]])
                T.op("scalar", lambda e, o=rl[k][:]: e.activation(out=o, in_=o, func=AF.Exp, scale=-0.5), reads=[], writes=[rk["rl"]])
                T.op("vector", lambda e, o=ot[k][:], i_=rr[k][:], m_=mv[k][:, 0:1], s_=rl[k][:, 0:1]:
                     e.tensor_scalar(out=o, in0=i_, scalar1=m_, scalar2=s_, op0=ALU.subtract, op1=ALU.mult),
                     reads=[rk["rr"], rk["mv"], rk["rl"]], writes=[rot[k]])
                T.op("gpsimd", lambda e, o=ot[k][:]: e.tensor_tensor(out=o, in0=o, in1=lg_b[:], op=ALU.mult),
                     reads=[rC], writes=[rot[k]])
                T.op("gpsimd", lambda e, o=ot[k][:]: e.tensor_tensor(out=o, in0=o, in1=lb_b[:], op=ALU.add),
                     reads=[rC], writes=[rot[k]])
                T.dma("gpsimd", out_d[T0 + 128 * ci:T0 + 128 * ci + 128, :], ot[k][:], f"st_o{k}", reads=[rot[k]])
                cnt += 1


def phase_A(nc, P, st_outer, banks, XT, w_in_v, aug, mask4_d, U, pairs, dbg=3, dr=None):
    with contextlib.ExitStack() as st:
        def sb(name, shape, dt):
            return st.enter_context(nc.sbuf_tensor(name, shape, dt))

        mask4 = sb("mask4s", [128, 512], F32)
        stmp = [sb(f"stmp{i}", [128, 512], F32) for i in range(4)]
        wstage = sb("wstage", [128, 8, 512], F32)
        wbf = sb("wbf", [128, 8, 512], BF16)
        qk = [[sb(f"qk{w}{h}", [128, S], BF16) for h in range(2)] for w in range(2)]
        sz = sb("sz", [128, S], BF16)
        V = sb("V", [128, 3, 32, 192], BF16)
        NPT = 6
        PT = [sb(f"PT{i}", [128, 512], BF16) for i in range(NPT)]
        rc = [sb(f"rc{i}", [128, 512], F32) for i in range(2)]
        t1 = [sb(f"t1{i}", [128, 512], F32) for i in range(2)]
        uT = sb("uT", [128, S], BF16)
        vT = sb("vT", [128, S], BF16)
        identBa = sb("identBa", [128, 128], BF16)

        acc = banks[0:4]
        sbank = banks[4:6]
        pbank = banks[6:8]

        ev_mask = P.dma("sync", mask4[:], mask4_d[:, :], "ld_c")
        ev_id = P.dma("sync", identBa[:], dr["identb"][:, :], "ld_cid")
        vT_free = []
        ev_ones = I_memset(P, "gpsimd", V[:, :, :, 64:128], 1.0) if "ones" not in _SKIP else None

        pb_free = [[], []]
        pb_i = 0
        sb_free = [None, None]
        pt_free = [None] * 6
        st_free = [None] * 4
        acc_free = [None] * 4
        w_free = []
        qk_free = [[[], []], [[], []]]
        sz_free = []
        V_free = []
        uT_free = None
        g_ctr = 0

        def load_w(hp_, deps_):
            out_ = []
            for wi, off in enumerate((OFF_Q, OFF_K, OFF_V, OFF_ZA)):
                out_.append(P.dma("sync", wstage[:, :, wi * 128:(wi + 1) * 128],
                                  w_in_v[:, :, off + hp_ * 128: off + (hp_ + 1) * 128], "ld_w", deps=deps_))
            return out_

        lds = load_w(pairs[0], [])
        w_casts = None
        ws_b = wstage[:].rearrange("p a b -> p (a b)").bitcast(BF16)
        qk16 = [ws_b[:, 0:S], ws_b[:, S:2 * S]]
        qk16_free = []
        rc_free = [None, None]
        ev_ctr = 0
        pending_evac = []
        for hp in pairs:
            hA, hB = 2 * hp, 2 * hp + 1
            if w_casts is None:
                c1 = I_copy(P, "vector", wbf[:, 0:4, :], wstage[:, 0:4, :], lds)
                c2 = I_copy(P, "scalar", wbf[:, 4:8, :], wstage[:, 4:8, :], lds)
                w_casts = [c1, c2]
            w_ready = w_casts
            nxt = pairs.index(hp) + 1
            if nxt < len(pairs):
                lds = load_w(pairs[nxt], w_casts + qk16_free)
            aug_ev = [[None, None], [None, None]]
            for w in range(2):
                for hh, h in enumerate((hA, hB)):
                    if "aug" in _SKIP:
                        continue
                    aug_ev[w][hh] = P.dma("sync", qk[w][hh][64:76, :], aug[h, w, :, :], f"ld_aug{w}{hh}",
                                          deps=qk_free[w][hh])
            w_last = []
            qk_ready = [[[], []], [[], []]]
            sz_ready = []
            vT_ready = []
            for wi in (0, 1, 2, 3):
                for tt in range(8):
                    pbk = pbank[pb_i]
                    deps = list(w_ready) + list(pb_free[pb_i])
                    for kc in range(8):
                        mm = I_mm(P, pbk[:, :], wbf[:, kc, wi * 128:(wi + 1) * 128],
                                  XT[:, kc, tt * 512:(tt + 1) * 512], kc == 0, kc == 7,
                                  deps if kc == 0 else ())
                    cols = slice(tt * 512, (tt + 1) * 512)
                    if wi == 3:
                        e = I_act(P, sz[:, cols], pbk[:, :], AF.Silu, [mm] + sz_free)
                        sz_ready.append(e)
                        pb_free[pb_i] = [e]
                    elif wi == 2:
                        e = I_copy(P, "vector" if tt % 2 == 0 else "scalar", vT[:, cols], pbk[:, :], [mm] + vT_free)
                        vT_ready.append(e)
                        pb_free[pb_i] = [e]
                    else:
                        w = wi
                        sc = 0.125 if w == 0 else 1.0
                        eA = I_act(P, qk[w][0][0:64, cols], pbk[0:64, :], AF.Copy, [mm] + qk_free[w][0], scale=sc)
                        if "dveB" in _SKIP:
                            eB = I_act(P, qk[w][1][0:64, cols], pbk[64:128, :], AF.Copy, [mm] + qk_free[w][1], scale=sc)
                        else:
                            eB = I_ts(P, "vector", qk[w][1][0:64, cols], pbk[64:128, :], sc, None, ALU.mult,
                                      deps=[mm] + qk_free[w][1])
                        qk_ready[w][0].append(eA)
                        qk_ready[w][1].append(eB)
                        pb_free[pb_i] = [eA, eB]
                    pb_i ^= 1
                    w_last = [mm]
            V_ready = {}
            tr_last = None
            for di, d in enumerate(DIL if dbg >= 2 else ()):
                nb = 32 // d
                for c0 in range(0, 32, 8):
                    pbk = pbank[pb_i]
                    pbk_b = pbk[:].bitcast(BF16)
                    for j in range(8):
                        r_, m_ = divmod(c0 + j, nb)
                        deps = ()
                        if j == 0:
                            deps = vT_ready + [ev_id] + list(pb_free[pb_i])
                        tr_last = P.op("tensor", lambda e, o=pbk_b[:, 128 * j:128 * (j + 1)], i_=vT[:, tok_ap(None, d, r_, m_)]:
                                       e.transpose(o, i_, identBa[:]), deps)
                    src = pbk_b[:, :].rearrange("p (c f) -> p c f", f=128)
                    eA = I_copy(P, "vector", V[:, di, c0:c0 + 8, 0:64], src[:, :, 0:64], [tr_last] + V_free)
                    eB = I_copy(P, "scalar", V[:, di, c0:c0 + 8, 128:192], src[:, :, 64:128], [tr_last, eA] + V_free)
                    for j in range(8):
                        V_ready[(di, c0 + j)] = [eA, eB]
                    pb_free[pb_i] = [eA, eB]
                    pb_i ^= 1
            vT_free = [tr_last] if tr_last is not None else []
            w_free = w_last
            if nxt < len(pairs):
                c1 = I_copy(P, "vector", wbf[:, 0:4, :], wstage[:, 0:4, :], lds + w_last)
                c2 = I_copy(P, "scalar", wbf[:, 4:8, :], wstage[:, 4:8, :], lds + w_last)
                w_casts = [c1, c2]
            V_free = []
            qk_free = [[[], []], [[], []]]
            sz_free = []

            u_written = []
            for hh, h in enumerate((hA, hB) if dbg >= 3 else ()):
                qT, kT = qk[0][hh], qk[1][hh]
                q_dep = qk_ready[0][hh] + [aug_ev[0][hh]]
                k_dep = qk_ready[1][hh] + [aug_ev[1][hh]]
                vcol = slice(0, 128) if hh == 0 else slice(64, 192)
                cdeps = (w_casts if nxt < len(pairs) else []) + qk16_free
                c16 = []
                for w in range(2):
                    src = qk[w][hh][0:76, :].rearrange("p (j r) -> p r j", r=16)
                    dst = qk16[w][0:76, :].rearrange("p (r j) -> p r j", r=16)
                    dd = (q_dep if w == 0 else k_dep) + cdeps
                    if w == 0:
                        c16.append(I_copy(P, "vector", dst, src, dd))
                    else:
                        c16.append(I_copy(P, "scalar", dst, src, dd))
                last16 = None
                for sbk in range(2):
                    qbs = []
                    for di, d in enumerate(DIL):
                        nb = 32 // d
                        per_sb = nb // 2
                        for r in range(d):
                            for n in range(sbk * per_sb, (sbk + 1) * per_sb):
                                qbs.append((di, d, r, n))
                    groups = [qbs[i:i + 2] for i in range(0, len(qbs), 2)]
                    acc_started = [False] * 4
                    acc_last_mm = [None] * 4
                    pend = []

                    def do_pv(item):
                        ptb, grp, ev_mask_mul = item
                        last_mm = None
                        for j, (di, d, r, n) in enumerate(grp):
                            nb = 32 // d
                            for role in range(2):
                                m = n - 1 + role
                                if m < 0:
                                    continue
                                tile_cols = (2 * j + role) * 128
                                cid = r * nb + m
                                lhsT = V[:, di, cid, vcol]
                                if d == 16:
                                    pieces = [(pc, 32) for pc in range(4)]
                                else:
                                    pieces = [(0, 128)]
                                for pc, cnt in pieces:
                                    i0 = pc * 32 if d == 16 else 0
                                    t0 = d * (128 * n + i0) + r
                                    col0 = t0 - 2048 * sbk
                                    bk = col0 // 512
                                    c0 = col0 % 512
                                    out = acc[bk][:, c0: c0 + d * (cnt - 1) + 1: d]
                                    rhs = PT[ptb][:, tile_cols + i0: tile_cols + i0 + cnt]
                                    deps = [ev_mask_mul] + V_ready[(di, cid)] + latest(ev_ones)
                                    if not acc_started[bk]:
                                        deps = deps + latest(acc_free[bk])
                                    last_mm = I_mm(P, out, lhsT, rhs, not acc_started[bk], False, deps)
                                    acc_started[bk] = True
                                    acc_last_mm[bk] = last_mm
                        pt_free[ptb] = last_mm

                    for gi, grp in enumerate(groups):
                        if gi % 2 == 0 and pending_evac:
                            pending_evac.pop(0)()
                        sbi = g_ctr % 2
                        pti = g_ctr % 6
                        sti = g_ctr % 4
                        g_ctr += 1
                        sbb = sbank[sbi]
                        first = True
                        mm = None
                        for j, (di, d, r, n) in enumerate(grp):
                            for role in range(2):
                                m = n - 1 + role
                                nq = n
                                if m < 0:
                                    m, nq = 0, 1
                                tile_cols = (2 * j + role) * 128
                                deps = ()
                                if first:
                                    deps = q_dep + k_dep + latest(sb_free[sbi])
                                    first = False
                                if d == 16:
                                    kop = qk16[1][0:76, r * 256 + 128 * m: r * 256 + 128 * m + 128]
                                    qop = qk16[0][0:76, r * 256 + 128 * nq: r * 256 + 128 * nq + 128]
                                    mm = I_mm(P, sbb[:, tile_cols:tile_cols + 128], kop, qop, True, True, list(deps) + c16)
                                    last16 = mm
                                else:
                                    mm = I_mm(P, sbb[:, tile_cols:tile_cols + 128],
                                              kT[0:76, tok_ap(None, d, r, m)], qT[0:76, tok_ap(None, d, r, nq)],
                                              True, True, deps)
                        mk0 = I_tt(P, "vector", stmp[sti][:, :], sbb[:, :], mask4[:, :], ALU.add,
                                   [mm, ev_mask] + latest(st_free[sti]))
                        sb_free[sbi] = mk0
                        mk = I_act(P, PT[pti][:, :], stmp[sti][:, :], AF.Exp, [mk0] + latest(pt_free[pti]))
                        st_free[sti] = mk
                        pend.append((pti, grp, mk))
                        if len(pend) > 4:
                            do_pv(pend.pop(0))
                    while pend:
                        do_pv(pend.pop(0))
                    def make_evac(bk, hh=hh, sbk=sbk, alm=acc_last_mm, szr=sz_ready):
                        def evac():
                            nonlocal ev_ctr, sz_free
                            cols = slice(2048 * sbk + 512 * bk, 2048 * sbk + 512 * (bk + 1))
                            if hh == 0:
                                o_rows, s_rows = slice(0, 64), slice(64, 128)
                            else:
                                o_rows, s_rows = slice(64, 128), slice(0, 64)
                            ei = ev_ctr % 2
                            ev_ctr += 1
                            a1 = I_act(P, rc[ei][o_rows, :], acc[bk][s_rows, :], AF.Ln, [alm[bk]] + latest(rc_free[ei]))
                            a2 = I_copy(P, "scalar", t1[ei][o_rows, :], acc[bk][o_rows, :], [alm[bk]] + latest(rc_free[ei]))
                            acc_free[bk] = a2
                            e1 = I_act(P, rc[ei][o_rows, :], rc[ei][o_rows, :], AF.Exp, [a1], scale=-1.0)
                            e2 = I_tt(P, "gpsimd", t1[ei][o_rows, :], t1[ei][o_rows, :], rc[ei][o_rows, :], ALU.mult, [e1, a2])
                            e3 = I_tt(P, "gpsimd", uT[o_rows, cols], t1[ei][o_rows, :], sz[o_rows, cols], ALU.mult,
                                      [e2] + szr + latest(uT_free))
                            rc_free[ei] = e3
                            u_written.append(e3)
                            sz_free = [e3]
                        return evac
                    pending_evac.extend(make_evac(bk) for bk in range(4))
                qk_free[0][hh] = [mm]
                qk_free[1][hh] = [mm]
                qk16_free = [last16]
            if dbg < 3:
                continue
            while pending_evac:
                pending_evac.pop(0)()
            V_free = latest(*[acc_last_mm[b] for b in range(4)])
            uT_free = P.dma("gpsimd", U[hp * 128:(hp + 1) * 128, :], uT[:, :], "st_u", deps=u_written)
            sz_free = u_written[-1:]


class Res:
    __slots__ = ("w", "r")

    def __init__(self):
        self.w = None
        self.r = {}


class Trk:
    def __init__(self, P):
        self.P = P

    def deps(self, reads, writes):
        d = []
        for t in reads:
            if t.w is not None:
                d.append(t.w)
        for t in writes:
            if t.w is not None:
                d.append(t.w)
            d.extend(t.r.values())
        return d

    def mark(self, ev, reads, writes):
        for t in reads:
            t.r[ev.sem if ev.sem is not None else ev.eng] = ev
        for t in writes:
            t.w = ev
            t.r = {}

    def op(self, eng, fn, reads=(), writes=(), extra=()):
        ev = self.P.op(eng, fn, self.deps(reads, writes) + list(extra))
        self.mark(ev, reads, writes)
        return ev

    def dma(self, eng, out, in_, slot, reads=(), writes=(), extra=()):
        ev = self.P.dma(eng, out, in_, slot, self.deps(reads, writes) + list(extra))
        self.mark(ev, reads, writes)
        return ev

    def mm_group(self, mms, reads, writes, extra=()):
        d = self.deps(reads, writes) + list(extra)
        ev = None
        for i, (out, lhsT, rhs, start, stop) in enumerate(mms):
            ev = I_mm(self.P, out, lhsT, rhs, start, stop, d if i == 0 else ())
        self.mark(ev, reads, writes)
        return ev


def bc64(ap16, h0, nh):
    return ap16[:, h0:h0 + nh].unsqueeze(2).broadcast_to([128, nh, 64])


def phase_B(nc, P, banks, XT, w_in_v, dr, U, n_sc=8, dbg_out=None):
    T = Trk(P)
    with contextlib.ExitStack() as st:
        def sb(name, shape, dt, stack=None):
            return (stack or st).enter_context(nc.sbuf_tensor("B_" + name, shape, dt))

        rXT = Res()
        Wz = sb("Wz", [128, 8, 1024], BF16)
        Wx = sb("Wx", [128, 8, 1536], BF16)
        Wdt = sb("Wdt", [128, 8, 16], BF16)
        tri = sb("tri", [128, 128], F32)
        onesF = sb("onesF", [128, 128], F32)
        identF = sb("identF", [128, 128], F32)
        identB = sb("identB", [128, 128], BF16)
        negm4 = sb("negm4", [128, 512], BF16)
        cw = sb("cw", [128, 12, 4], F32)
        cb = sb("cb", [128, 12], F32)
        dtb_b = sb("dtb_b", [128, 16], F32)
        a_b = sb("a_b", [128, 16], F32)
        dsk_b = sb("dsk_b", [128, 16], F32)
        gs_b = sb("gs_b", [128, 1024], F32)
        hal = sb("hal", [128, 12, 3], F32)
        rW = Res(); rC = Res(); rHal = Res()

        with contextlib.ExitStack() as stw:
            wst = sb("wstB", [128, 8, 512], F32, stw)
            rwst = Res()
            pieces = [(OFF_ZS, 0, 512, Wz), (OFF_ZS + 512, 512, 512, Wz),
                      (OFF_XBC, 0, 512, Wx), (OFF_XBC + 512, 512, 512, Wx), (OFF_XBC + 1024, 1024, 512, Wx),
                      (OFF_DT, 0, 16, Wdt)]
            for i, (off, dst0, n, Wt) in enumerate(pieces):
                T.dma("sync", wst[:, :, 0:n], w_in_v[:, :, off:off + n], "ld_wB", writes=[rwst])
                T.op("vector", lambda e, o=Wt[:, 0:4, dst0:dst0 + n], i_=wst[:, 0:4, 0:n]: e.tensor_copy(out=o, in_=i_),
                     reads=[rwst], writes=[rW])
                T.op("gpsimd", lambda e, o=Wt[:, 4:8, dst0:dst0 + n], i_=wst[:, 4:8, 0:n]: e.tensor_copy(out=o, in_=i_),
                     reads=[rwst], writes=[])
            T.dma("sync", tri[:], dr["tri"][:, :], "ld_c1", writes=[rC])
            T.dma("sync", identF[:], dr["ident"][:, :], "ld_c2", writes=[rC])
            T.dma("sync", negm4[:], dr["negm4"][:, :], "ld_c3", writes=[rC])
            T.dma("sync", cw[:], dr["conv_wT"][:, :, :], "ld_c4", writes=[rC])
            T.dma("sync", cb[:], dr["conv_b2"][:, :], "ld_c5", writes=[rC])
            T.dma("sync", dtb_b[:], dr["dt_bias"].partition_broadcast(128), "ld_c6", writes=[rC])
            T.dma("sync", a_b[:], dr["a_log"].partition_broadcast(128), "ld_c7", writes=[rC])
            T.dma("sync", dsk_b[:], dr["d_skip"].partition_broadcast(128), "ld_c8", writes=[rC])
            T.dma("sync", gs_b[:], dr["ssm_norm_g"].partition_broadcast(128), "ld_c9", writes=[rC])
            barrier(P)
            I_memset(P, "vector", onesF[:], 1.0)
            I_memset(P, "vector", hal[:], 0.0)
            I_copy(P, "vector", identB[:], identF[:])
            I_act(P, a_b[:], a_b[:], AF.Exp)
            barrier(P)
            I_ts(P, "vector", a_b[:], a_b[:], -1.0, None, ALU.mult)
            barrier(P)

        xb = [sb(f"xb{i}", [128, 515], F32) for i in range(2)]
        cvb = [sb(f"cvb{i}", [128, 512], F32) for i in range(2)]
        xsT = sb("xsT", [128, 8, 512], F32)
        BT = sb("BT", [128, 2, 512], BF16)
        CT = sb("CT", [128, 2, 512], BF16)
        dtx = sb("dtx", [128, 16], F32)
        dt_t = sb("dt_t", [128, 16], F32)
        adt = sb("adt", [128, 16], F32)
        nacs = sb("nacs", [128, 16], F32)
        D0 = sb("D0", [128, 16], F32)
        dte = sb("dte", [128, 16], F32)
        ddt = sb("ddt", [128, 16], F32)
        cdb = sb("cdb", [128, 16], F32)
        Btok = sb("Btok", [128, 2, 128], BF16)
        xs_tok = sb("xs_tok", [128, 1024], F32)
        xg = sb("xg", [128, 1024], BF16)
        xgd = sb("xgd", [128, 1024], BF16)
        Gm = sb("Gm", [128, 2, 128], BF16)
        Dm = [sb(f"Dm{i}", [128, 512], BF16) for i in range(2)]
        MT = [sb(f"MT{i}", [128, 512], BF16) for i in range(2)]
        H = sb("H", [128, 2, 512], F32)
        Hbf = sb("Hbf", [128, 2, 512], BF16)
        yo = sb("yo", [128, 512], F32)
        tsk = sb("tsk", [128, 512], F32)
        y = sb("y", [128, 1024], F32)
        zs = sb("zs", [128, 1024], F32)
        junk = sb("junkB", [128, 1024], F32)
        ssq = sb("ssq", [128, 1], F32)
        rstd = sb("rstd", [128, 1], F32)
        mixn = sb("mixn", [128, 1024], BF16)
        mixT = sb("mixT", [128, 8, 512], BF16)

        r = {k: Res() for k in ("xsT", "BT", "CT", "dtx", "dt", "adt", "nacs", "D0", "dte", "ddt", "cdb", "Btok",
                                "xs_tok", "xg", "xgd", "Gm", "H", "Hbf", "yo", "tsk", "y", "zs", "junk", "ssq", "rstd",
                                "mixn", "mixT")}
        rxb = [Res(), Res()]; rxbh = [Res(), Res()]; rcv = [Res(), Res()]; rmixT = [Res(), Res()]; rDm = [Res(), Res()]; rMT = [Res(), Res()]
        rhal = [Res() for _ in range(12)]
        pb = [banks[0], banks[1]]; rpb = [Res(), Res()]
        smb = banks[2]; rsmb = Res()
        trb = banks[3]; rtr = Res()
        Rb = [banks[4], banks[5]]; rRb = [Res(), Res()]
        Yb = banks[6]; rY = Res()
        osb = banks[7]; ros = Res()

        I_memset(P, "vector", H[:], 0.0)
        I_memset(P, "vector", Hbf[:], 0.0)
        barrier(P)

        pbi = 0
        quad_ctr = 0
        for sc in range(n_sc):
            T0 = 512 * sc
            for ft in range(12):
                b = pbi; pbi ^= 1
                xbi = ft % 2
                T.mm_group([(pb[b][:, :], Wx[:, kc, ft * 128:(ft + 1) * 128], XT[:, kc, T0:T0 + 512], kc == 0, kc == 7)
                            for kc in range(8)], reads=[rW, rXT], writes=[rpb[b]])
                T.op("scalar", lambda e, o=xb[xbi][:, 3:515], i_=pb[b][:, :]: e.activation(out=o, in_=i_, func=AF.Copy),
                     reads=[rpb[b]], writes=[rxb[xbi]])
                T.op("gpsimd", lambda e, o=xb[xbi][:, 0:3], i_=hal[:, ft, :]: e.tensor_copy(out=o, in_=i_),
                     reads=[rhal[ft]], writes=[rxbh[xbi]])
                T.op("gpsimd", lambda e, o=hal[:, ft, :], i_=xb[xbi][:, 512:515]: e.tensor_copy(out=o, in_=i_),
                     reads=[rxb[xbi]], writes=[rhal[ft]])
                ceng = "vector"
                cv = cvb[xbi]
                T.op(ceng, lambda e, o=cv[:, :], i_=xb[xbi][:, 0:512], s_=cw[:, ft, 0:1]:
                     e.tensor_scalar(out=o, in0=i_, scalar1=s_, scalar2=None, op0=ALU.mult),
                     reads=[rxb[xbi], rxbh[xbi], rC], writes=[rcv[xbi]])
                for j in range(1, 4):
                    T.op(ceng, lambda e, o=cv[:, :], i_=xb[xbi][:, j:j + 512], s_=cw[:, ft, j:j + 1]:
                         e.scalar_tensor_tensor(out=o, in0=i_, scalar=s_, in1=o, op0=ALU.mult, op1=ALU.add),
                         reads=[rxb[xbi], rxbh[xbi], rcv[xbi]], writes=[rcv[xbi]])
                if ft < 8:
                    dst, rd = xsT[:, ft, :], r["xsT"]
                elif ft < 10:
                    dst, rd = BT[:, ft - 8, :], r["BT"]
                else:
                    dst, rd = CT[:, ft - 10, :], r["CT"]
                T.op("scalar", lambda e, o=dst, i_=cv[:, :], b_=cb[:, ft:ft + 1]:
                     e.activation(out=o, in_=i_, func=AF.Silu, bias=b_), reads=[rcv[xbi], rC], writes=[rd])

            for ci in range(4):
                t0 = T0 + 128 * ci
                cc = slice(128 * ci, 128 * ci + 128)
                T.mm_group([(smb[:, 0:16], XT[:, kc, t0:t0 + 128], Wdt[:, kc, :], kc == 0, kc == 7) for kc in range(8)],
                           reads=[rW, rXT], writes=[rsmb])
                for half in range(2):
                    b = pbi; pbi ^= 1
                    T.mm_group([(pb[b][:, :], XT[:, kc, t0:t0 + 128], Wz[:, kc, 512 * half:512 * (half + 1)], kc == 0, kc == 7)
                                for kc in range(8)], reads=[rW, rXT], writes=[rpb[b]])
                    T.op("scalar", lambda e, o=zs[:, 512 * half:512 * (half + 1)], i_=pb[b][:, :]:
                         e.activation(out=o, in_=i_, func=AF.Silu), reads=[rpb[b]], writes=[r["zs"]])
                T.op("vector", lambda e: e.tensor_tensor(out=dtx[:], in0=smb[:, 0:16], in1=dtb_b[:], op=ALU.add),
                     reads=[rsmb, rC], writes=[r["dtx"]])
                T.op("scalar", lambda e: e.activation(out=dtx[:], in_=dtx[:], func=AF.Exp), reads=[], writes=[r["dtx"]])
                T.op("scalar", lambda e: e.activation(out=dt_t[:], in_=dtx[:], func=AF.Ln, bias=1.0),
                     reads=[r["dtx"]], writes=[r["dt"]])
                T.op("vector", lambda e: e.tensor_tensor(out=adt[:], in0=dt_t[:], in1=a_b[:], op=ALU.mult),
                     reads=[r["dt"], rC], writes=[r["adt"]])
                T.mm_group([(smb[:, 16:32], tri[:], adt[:], True, True)], reads=[r["adt"], rC], writes=[rsmb])
                T.mm_group([(smb[:, 32:48], onesF[:], adt[:], True, True)], reads=[r["adt"], rC], writes=[rsmb])
                T.op("vector", lambda e: e.tensor_scalar(out=nacs[:], in0=smb[:, 16:32], scalar1=-1.0, scalar2=None, op0=ALU.mult),
                     reads=[rsmb], writes=[r["nacs"]])
                T.op("vector", lambda e: e.tensor_copy(out=cdb[:], in_=smb[:, 32:48]), reads=[rsmb], writes=[r["cdb"]])
                T.op("vector", lambda e: e.tensor_tensor(out=dte[:], in0=cdb[:], in1=nacs[:], op=ALU.add),
                     reads=[r["cdb"], r["nacs"]], writes=[r["dte"]])
                T.op("scalar", lambda e: e.activation(out=D0[:], in_=nacs[:], func=AF.Exp, scale=-1.0),
                     reads=[r["nacs"]], writes=[r["D0"]])
                T.op("scalar", lambda e: e.activation(out=dte[:], in_=dte[:], func=AF.Exp), reads=[], writes=[r["dte"]])
                T.op("scalar", lambda e: e.activation(out=cdb[:], in_=cdb[:], func=AF.Exp), reads=[r["dte"]], writes=[r["cdb"]])
                T.op("vector", lambda e: e.tensor_tensor(out=ddt[:], in0=dt_t[:], in1=dte[:], op=ALU.mult),
                     reads=[r["dt"], r["dte"]], writes=[r["ddt"]])
                for g in range(2):
                    ev = None
                    for j in range(4):
                        ev = P.op("tensor", lambda e, o=trb[:, 128 * j:128 * (j + 1)], i_=xsT[:, 4 * g + j, cc]:
                                  e.transpose(o, i_, identF[:]), T.deps([r["xsT"], rC], [rtr]) if j == 0 else ())
                    T.mark(ev, [r["xsT"], rC], [rtr])
                    T.op("scalar", lambda e, o=xs_tok[:, 512 * g:512 * (g + 1)], i_=trb[:, :]: e.activation(out=o, in_=i_, func=AF.Copy),
                         reads=[rtr], writes=[r["xs_tok"]])
                for g in range(2):
                    v3 = lambda ap: ap[:, 512 * g:512 * (g + 1)].rearrange("p (h e) -> p h e", e=64)
                    T.op("vector", lambda e, o=v3(xg), i_=v3(xs_tok), b_=bc64(dt_t, 8 * g, 8):
                         e.tensor_tensor(out=o, in0=i_, in1=b_, op=ALU.mult), reads=[r["xs_tok"], r["dt"]], writes=[r["xg"]])
                    T.op("gpsimd", lambda e, o=v3(xgd), i_=v3(xs_tok), b_=bc64(ddt, 8 * g, 8):
                         e.tensor_tensor(out=o, in0=i_, in1=b_, op=ALU.mult), reads=[r["xs_tok"], r["ddt"]], writes=[r["xgd"]])
                trb_b = trb[:].bitcast(BF16)
                ev = None
                for g in range(2):
                    ev = P.op("tensor", lambda e, o=trb_b[:, 128 * g:128 * (g + 1)], i_=BT[:, g, cc]:
                              e.transpose(o, i_, identB[:]), T.deps([r["BT"], rC], [rtr]) if g == 0 else ())
                T.mark(ev, [r["BT"], rC], [rtr])
                T.op("vector", lambda e: e.tensor_copy(out=Btok[:].rearrange("p g n -> p (g n)"), in_=trb_b[:, 0:256]),
                     reads=[rtr], writes=[r["Btok"]])
                for g in range(2):
                    T.mm_group([(smb[:, 128 * (g + 1):128 * (g + 2)], BT[:, g, cc], CT[:, g, cc], True, True)],
                               reads=[r["BT"], r["CT"]], writes=[rsmb])
                    T.op("vector", lambda e, o=Gm[:, g, :], i_=smb[:, 128 * (g + 1):128 * (g + 2)]: e.tensor_copy(out=o, in_=i_),
                         reads=[rsmb], writes=[r["Gm"]])
                for g in range(2):
                    for qd in range(2):
                        qi = quad_ctr % 2; quad_ctr += 1
                        h0 = 8 * g + 4 * qd
                        mms = [(Rb[qi][:, :], identB[:], negm4[:], True, False)]
                        for j in range(4):
                            mms.append((Rb[qi][:, 128 * j:128 * (j + 1)], adt[:, h0 + j:h0 + j + 1].broadcast_to([128, 128]),
                                        tri[:], False, j == 3))
                        T.mm_group(mms, reads=[r["adt"], rC], writes=[rRb[qi]])
                        for j in range(4):
                            T.op("scalar", lambda e, o=Dm[qi][:, 128 * j:128 * (j + 1)], i_=Rb[qi][:, 128 * j:128 * (j + 1)],
                                 b_=nacs[:, h0 + j:h0 + j + 1]: e.activation(out=o, in_=i_, func=AF.Exp, bias=b_),
                                 reads=[rRb[qi], r["nacs"]], writes=[rDm[qi]])
                        T.op("vector", lambda e, o=MT[qi][:].rearrange("p (j l) -> p j l", l=128),
                             i_=Dm[qi][:].rearrange("p (j l) -> p j l", l=128),
                             b_=Gm[:, g, :].unsqueeze(1).broadcast_to([128, 4, 128]):
                             e.tensor_tensor(out=o, in0=i_, in1=b_, op=ALU.mult),
                             reads=[rDm[qi], r["Gm"]], writes=[rMT[qi]])
                        mms = []
                        for j in range(4):
                            hh = 4 * qd + j
                            mms.append((Yb[:, 64 * hh:64 * (hh + 1)], MT[qi][:, 128 * j:128 * (j + 1)],
                                        xg[:, 64 * (h0 + j):64 * (h0 + j + 1)], (qd == 0 and j == 0), False))
                        T.mm_group(mms, reads=[rMT[qi], r["xg"]], writes=[rY])
                    T.mm_group([(osb[:, :], CT[:, g, cc], Hbf[:, g, :], True, True)], reads=[r["CT"], r["Hbf"]], writes=[ros])
                    v3 = lambda ap: ap.rearrange("p (h e) -> p h e", e=64)
                    T.op("vector", lambda e, o=v3(yo[:, :]), i_=v3(osb[:, :]), b_=bc64(D0, 8 * g, 8):
                         e.tensor_tensor(out=o, in0=i_, in1=b_, op=ALU.mult), reads=[ros, r["D0"]], writes=[r["yo"]])
                    T.op("vector", lambda e, o=y[:, 512 * g:512 * (g + 1)], i_=Yb[:, :]:
                         e.tensor_tensor(out=o, in0=i_, in1=yo[:, :], op=ALU.add), reads=[rY, r["yo"]], writes=[r["y"]])
                    T.op("gpsimd", lambda e, o=v3(tsk[:, :]), i_=v3(xs_tok[:, 512 * g:512 * (g + 1)]), b_=bc64(dsk_b, 8 * g, 8):
                         e.tensor_tensor(out=o, in0=i_, in1=b_, op=ALU.mult), reads=[r["xs_tok"], rC], writes=[r["tsk"]])
                    T.op("gpsimd", lambda e, o=y[:, 512 * g:512 * (g + 1)]: e.tensor_tensor(out=o, in0=o, in1=tsk[:, :], op=ALU.add),
                         reads=[r["tsk"]], writes=[r["y"]])
                    T.mm_group([(osb[:, :], Btok[:, g, :], xgd[:, 512 * g:512 * (g + 1)], True, True)],
                               reads=[r["Btok"], r["xgd"]], writes=[ros])
                    T.op("vector", lambda e, o=v3(H[:, g, :]), b_=bc64(cdb, 8 * g, 8):
                         e.tensor_tensor(out=o, in0=o, in1=b_, op=ALU.mult), reads=[r["cdb"]], writes=[r["H"]])
                    T.op("vector", lambda e, o=H[:, g, :]: e.tensor_tensor(out=o, in0=osb[:, :], in1=o, op=ALU.add),
                         reads=[ros], writes=[r["H"]])
                    T.op("gpsimd", lambda e, o=Hbf[:, g, :], i_=H[:, g, :]: e.tensor_copy(out=o, in_=i_),
                         reads=[r["H"]], writes=[r["Hbf"]])
                T.op("vector", lambda e: e.tensor_tensor(out=y[:], in0=y[:], in1=zs[:], op=ALU.mult),
                     reads=[r["zs"]], writes=[r["y"]])
                T.op("vector", lambda e: e.memset(ssq[:], 0.0), reads=[], writes=[r["ssq"]])
                T.op("scalar", lambda e: e.activation(out=junk[:], in_=y[:], func=AF.Square, accum_out=ssq[:]),
                     reads=[r["y"]], writes=[r["junk"], r["ssq"]])
                T.op("scalar", lambda e: e.activation(out=rstd[:], in_=ssq[:], func=AF.Ln, bias=EPS, scale=1.0 / 1024),
                     reads=[r["ssq"]], writes=[r["rstd"]])
                T.op("scalar", lambda e: e.activation(out=rstd[:], in_=rstd[:], func=AF.Exp, scale=-0.5),
                     reads=[], writes=[r["rstd"]])
                T.op("vector", lambda e: e.scalar_tensor_tensor(out=mixn[:], in0=y[:], scalar=rstd[:, 0:1], in1=gs_b[:],
                                                                op0=ALU.mult, op1=ALU.mult),
                     reads=[r["y"], r["rstd"], rC], writes=[r["mixn"]])
                for half in range(2):
                    ev = None
                    for j in range(4):
                        ft = 4 * half + j
                        ev = P.op("tensor", lambda e, o=trb_b[:, 128 * j:128 * (j + 1)], i_=mixn[:, 128 * ft:128 * (ft + 1)]:
                                  e.transpose(o, i_, identB[:]), T.deps([r["mixn"], rC], [rtr]) if j == 0 else ())
                    T.mark(ev, [r["mixn"], rC], [rtr])
                    T.op("vector" if half == 0 else "scalar",
                         (lambda e, o=mixT[:, 4 * half:4 * half + 4, cc], i_=trb_b[:, 0:512].rearrange("p (j t) -> p j t", t=128):
                          e.tensor_copy(out=o, in_=i_)) if half == 0 else
                         (lambda e, o=mixT[:, 4 * half:4 * half + 4, cc], i_=trb_b[:, 0:512].rearrange("p (j t) -> p j t", t=128):
                          e.activation(out=o, in_=i_, func=AF.Copy)),
                         reads=[rtr], writes=[rmixT[half]])
            T.dma("gpsimd", U[1024:2048, T0:T0 + 512].rearrange("(f p) t -> p f t", p=128), mixT[:, :, :], "st_uB",
                  reads=[rmixT[0], rmixT[1]])
        barrier(P)


_NC_CACHE = {}


def _host_inputs(xb, p, c):
    return {
        "xT": np.ascontiguousarray(xb.T), "x": np.ascontiguousarray(xb),
        "w_in": p["w_in"], "w_out": p["w_out"],
        "aug": c["aug"], "mask4": c["mask4"], "tri": c["tri"], "ident": c["ident"], "identb": c["identb"], "negm4": c["negm4"],
        "conv_wT": np.ascontiguousarray(p["conv_w"].T.reshape(12, 128, 4).transpose(1, 0, 2)),
        "conv_b2": np.ascontiguousarray(p["conv_b"].reshape(12, 128).T),
        "dt_bias": p["dt_bias"], "a_log": p["a_log"], "d_skip": p["d_skip"], "ssm_norm_g": p["ssm_norm_g"],
        "att_norm_g2": np.ascontiguousarray(p["att_norm_g"].reshape(8, 128).T),
        "ssm_norm_g2": np.ascontiguousarray(p["ssm_norm_g"].reshape(8, 128).T),
        "ln_g": p["ln_g"], "ln_b": p["ln_b"],
    }


def kernel(x, w_in, conv_w, conv_b, dt_bias, a_log, d_skip, att_norm_g, ssm_norm_g, w_out, ln_g, ln_b):
    x = np.asarray(x, np.float32)
    p = {"w_in": w_in, "conv_w": conv_w, "conv_b": conv_b, "dt_bias": dt_bias, "a_log": a_log, "d_skip": d_skip,
         "att_norm_g": att_norm_g, "ssm_norm_g": ssm_norm_g, "w_out": w_out, "ln_g": ln_g, "ln_b": ln_b}
    p = {k: np.ascontiguousarray(np.asarray(v, np.float32)[0]) for k, v in p.items()}
    c = make_constants()
    n = x.shape[0]
    if "nc" not in _NC_CACHE:
        _NC_CACHE["nc"] = build_program()
    nc = _NC_CACHE["nc"]
    in_maps = [_host_inputs(x[b], p, c) for b in range(n)]
    res = run_bass_kernel_spmd(nc, in_maps, core_ids=list(range(n)))
    return np.stack([np.asarray(r["out"], np.float32) for r in res.results], axis=0)


def phase_B2(nc, P, banks, XT, w_in_v, dr, U, n_sc=8):
    T = Trk(P)
    with contextlib.ExitStack() as st:
        def sb(name, shape, dt, stack=None):
            return (stack or st).enter_context(nc.sbuf_tensor("B_" + name, shape, dt))

        rXT = Res()
        Wz = sb("Wz", [128, 8, 1024], BF16)
        Wx = sb("Wx", [128, 8, 1536], BF16)
        Wdt = sb("Wdt", [128, 8, 16], BF16)
        tri = sb("tri", [128, 128], F32)
        onesF = sb("onesF", [128, 128], F32)
        identF = sb("identF", [128, 128], F32)
        identB = sb("identB", [128, 128], BF16)
        negm4 = sb("negm4", [128, 512], BF16)
        cw = sb("cw", [128, 12, 4], F32)
        cb = sb("cb", [128, 12], F32)
        dtb_b = sb("dtb_b", [128, 16], F32)
        a_b = sb("a_b", [128, 16], F32)
        dsk_b = sb("dsk_b", [128, 16], F32)
        hal = sb("hal", [128, 12, 3], F32)
        rW = Res(); rC = Res()

        with contextlib.ExitStack() as stw:
            wst = sb("wstB", [128, 8, 512], F32, stw)
            rwst = Res()
            pieces = [(OFF_ZS, 0, 512, Wz), (OFF_ZS + 512, 512, 512, Wz),
                      (OFF_XBC, 0, 512, Wx), (OFF_XBC + 512, 512, 512, Wx), (OFF_XBC + 1024, 1024, 512, Wx),
                      (OFF_DT, 0, 16, Wdt)]
            for i, (off, dst0, n, Wt) in enumerate(pieces):
                T.dma("sync", wst[:, :, 0:n], w_in_v[:, :, off:off + n], "ld_wB", writes=[rwst])
                T.op("vector", lambda e, o=Wt[:, 0:4, dst0:dst0 + n], i_=wst[:, 0:4, 0:n]: e.tensor_copy(out=o, in_=i_),
                     reads=[rwst], writes=[rW])
                T.op("scalar", lambda e, o=Wt[:, 4:8, dst0:dst0 + n], i_=wst[:, 4:8, 0:n]: e.activation(out=o, in_=i_, func=AF.Copy),
                     reads=[rwst], writes=[])
            T.dma("sync", tri[:], dr["tri"][:, :], "ld_c1", writes=[rC])
            T.dma("sync", identF[:], dr["ident"][:, :], "ld_c2", writes=[rC])
            T.dma("sync", negm4[:], dr["negm4"][:, :], "ld_c3", writes=[rC])
            T.dma("sync", cw[:], dr["conv_wT"][:, :, :], "ld_c4", writes=[rC])
            T.dma("sync", cb[:], dr["conv_b2"][:, :], "ld_c5", writes=[rC])
            T.dma("sync", dtb_b[:], dr["dt_bias"].partition_broadcast(128), "ld_c6", writes=[rC])
            T.dma("sync", a_b[:], dr["a_log"].partition_broadcast(128), "ld_c7", writes=[rC])
            T.dma("sync", dsk_b[:], dr["d_skip"].partition_broadcast(128), "ld_c8", writes=[rC])
            barrier(P)
            I_memset(P, "vector", onesF[:], 1.0)
            I_memset(P, "vector", hal[:], 0.0)
            I_copy(P, "vector", identB[:], identF[:])
            I_act(P, a_b[:], a_b[:], AF.Exp)
            barrier(P)
            I_ts(P, "vector", a_b[:], a_b[:], -1.0, None, ALU.mult)
            barrier(P)

        xb = [sb(f"xb{i}", [128, 515], F32) for i in range(2)]
        cvb = [sb(f"cvb{i}", [128, 512], F32) for i in range(2)]
        xtmp = [sb(f"xtmp{i}", [128, 512], F32) for i in range(2)]
        BT = sb("BT", [128, 2, 512], BF16)
        CT = sb("CT", [128, 2, 512], BF16)
        xs_tok = sb("xs_tok", [128, 4, 1024], F32)
        zs = sb("zs", [128, 4, 1024], F32)
        Btok = sb("Btok", [128, 4, 2, 128], BF16)
        Gm = sb("Gm", [128, 2, 4, 128], BF16)
        sm = {k: sb(k, [128, 64], F32) for k in ("dtx", "dt4", "adt4", "nacs4", "D04", "dte4", "ddt4", "cdb4")}
        xg = [sb(f"xg{i}", [128, 1024], BF16) for i in range(2)]
        xgd = [sb(f"xgd{i}", [128, 1024], BF16) for i in range(2)]
        Dm = [sb(f"Dm{i}", [128, 512], BF16) for i in range(2)]
        MT = [sb(f"MT{i}", [128, 512], BF16) for i in range(2)]
        H = sb("H", [128, 2, 512], F32)
        Hbf = sb("Hbf", [128, 2, 512], BF16)
        yo = [sb(f"yo{i}", [128, 512], F32) for i in range(2)]
        tsk = [sb(f"tsk{i}", [128, 512], F32) for i in range(4)]
        y = [sb(f"y{i}", [128, 1024], F32) for i in range(1)]
        ssq = [sb(f"ssq{i}", [128, 1], F32) for i in range(2)]
        rstd = [sb(f"rstd{i}", [128, 1], F32) for i in range(2)]
        mixn = [sb(f"mixn{i}", [128, 1024], BF16) for i in range(2)]
        mixT = [sb(f"mixT{i}", [128, 8, 512], BF16) for i in range(1)]

        R_ = lambda n: [Res() for _ in range(n)]
        rxb, rxbh, rcv, rxtmp = R_(2), R_(2), R_(2), R_(2)
        rhal = R_(12)
        rBT, rCT, rxs, rzs, rBtok, rGm = Res(), Res(), Res(), Res(), Res(), Res()
        rsm = {k: Res() for k in sm}
        rxg, rxgd, rDm, rMT, ryo, rtsk, ry, rssq, rrstd, rmixn = R_(2), R_(2), R_(2), R_(2), R_(2), R_(4), R_(1), R_(2), R_(2), R_(2)
        rmixT = [R_(2)]
        rH, rHbf = R_(2), R_(2)
        rbank = R_(8)
        pb0, pb1, smb, trb, Rb0, Rb1, Yb, osb = banks
        B_PB0, B_PB1, B_SM, B_TR, B_R0, B_R1, B_Y, B_OS = range(8)
        trb_b = trb[:].bitcast(BF16)

        I_memset(P, "vector", H[:], 0.0)
        I_memset(P, "vector", Hbf[:], 0.0)
        barrier(P)

        def v3(ap):
            return ap.rearrange("p (h e) -> p h e", e=64)

        pbi = 0
        for sc in range(n_sc):
            T0 = 512 * sc
            def stage_A(ft):
                nonlocal pbi
                b = pbi; pbi ^= 1
                pbk = banks[b]
                xbi = ft % 2
                T.mm_group([(pbk[:, :], Wx[:, kc, ft * 128:(ft + 1) * 128], XT[:, kc, T0:T0 + 512], kc == 0, kc == 7)
                            for kc in range(8)], reads=[rW, rXT], writes=[rbank[b]])
                T.op("scalar", lambda e, o=xb[xbi][:, 3:515], i_=pbk[:, :]: e.activation(out=o, in_=i_, func=AF.Copy),
                     reads=[rbank[b]], writes=[rxb[xbi]])
                cv = cvb[xbi]
                T.op("scalar", lambda e, o=cv[:, :], i_=pbk[:, :], s_=cw[:, ft, 3:4]: e.activation(out=o, in_=i_, func=AF.Copy, scale=s_),
                     reads=[rbank[b], rC], writes=[rcv[xbi]])
                T.op("gpsimd", lambda e, o=xb[xbi][:, 0:3], i_=hal[:, ft, :]: e.tensor_copy(out=o, in_=i_),
                     reads=[rhal[ft]], writes=[rxbh[xbi]])
                T.op("gpsimd", lambda e, o=hal[:, ft, :], i_=xb[xbi][:, 512:515]: e.tensor_copy(out=o, in_=i_),
                     reads=[rxb[xbi]], writes=[rhal[ft]])

            def stage_B(ft):
                xbi = ft % 2
                cv = cvb[xbi]
                for j in range(3):
                    T.op("vector", lambda e, o=cv[:, :], i_=xb[xbi][:, j:j + 512], s_=cw[:, ft, j:j + 1]:
                         e.scalar_tensor_tensor(out=o, in0=i_, scalar=s_, in1=o, op0=ALU.mult, op1=ALU.add),
                         reads=[rxb[xbi], rxbh[xbi]], writes=[rcv[xbi]])
                if ft < 8:
                    xi = ft % 2
                    T.op("scalar", lambda e, o=xtmp[xi][:, :], i_=cv[:, :], b_=cb[:, ft:ft + 1]:
                         e.activation(out=o, in_=i_, func=AF.Silu, bias=b_), reads=[rcv[xbi], rC], writes=[rxtmp[xi]])
                elif ft < 10:
                    T.op("scalar", lambda e, o=BT[:, ft - 8, :], i_=cv[:, :], b_=cb[:, ft:ft + 1]:
                         e.activation(out=o, in_=i_, func=AF.Silu, bias=b_), reads=[rcv[xbi], rC], writes=[rBT])
                else:
                    T.op("scalar", lambda e, o=CT[:, ft - 10, :], i_=cv[:, :], b_=cb[:, ft:ft + 1]:
                         e.activation(out=o, in_=i_, func=AF.Silu, bias=b_), reads=[rcv[xbi], rC], writes=[rCT])
            def stage_T(ft):
                if ft < 0 or ft >= 8:
                    return
                xi = ft % 2
                tbk = B_TR if ft % 2 == 0 else B_Y
                ev = None
                for ci in range(4):
                    ev = P.op("tensor", lambda e, o=banks[tbk][:, 128 * ci:128 * (ci + 1)], i_=xtmp[xi][:, 128 * ci:128 * (ci + 1)]:
                              e.transpose(o, i_, identF[:]), T.deps([rxtmp[xi], rC], [rbank[tbk]]) if ci == 0 else ())
                T.mark(ev, [rxtmp[xi], rC], [rbank[tbk]])

            def stage_C(ft):
                if ft < 0 or ft >= 8:
                    return
                tbk = B_TR if ft % 2 == 0 else B_Y
                T.op("scalar", lambda e, o=xs_tok[:, :, ft * 128:(ft + 1) * 128], i_=banks[tbk][:, :].rearrange("p (c f) -> p c f", f=128):
                     e.activation(out=o, in_=i_, func=AF.Copy), reads=[rbank[tbk]], writes=[rxs])

            def z_group(zi):
                nonlocal pbi
                ci, half = divmod(zi, 2)
                t0 = T0 + 128 * ci
                b = pbi; pbi ^= 1
                T.mm_group([(banks[b][:, :], XT[:, kc, t0:t0 + 128], Wz[:, kc, 512 * half:512 * (half + 1)], kc == 0, kc == 7)
                            for kc in range(8)], reads=[rW, rXT], writes=[rbank[b]])
                T.op("scalar", lambda e, o=zs[:, ci, 512 * half:512 * (half + 1)], i_=banks[b][:, :]:
                     e.activation(out=o, in_=i_, func=AF.Silu), reads=[rbank[b]], writes=[rzs])

            stage_A(0)
            for ft in range(13):
                if ft + 1 < 12:
                    stage_A(ft + 1)
                stage_T(ft - 1)
                if ft < 12:
                    stage_B(ft)
                stage_C(ft - 1)
                if 2 <= ft < 10:
                    z_group(ft - 2)
            ev = None
            for ci in range(4):
                for g in range(2):
                    j = 2 * ci + g
                    ev = P.op("tensor", lambda e, o=trb_b[:, 128 * j:128 * (j + 1)], i_=BT[:, g, 128 * ci:128 * (ci + 1)]:
                              e.transpose(o, i_, identB[:]), T.deps([rBT, rC], [rbank[B_TR]]) if j == 0 else ())
            T.mark(ev, [rBT, rC], [rbank[B_TR]])
            T.op("vector", lambda e: e.tensor_copy(out=Btok[:].rearrange("p c g n -> p (c g n)"), in_=trb_b[:, 0:1024]),
                 reads=[rbank[B_TR]], writes=[rBtok])
            for g in range(2):
                bk = B_R0 + g
                T.mm_group([(banks[bk][:, 128 * ci:128 * (ci + 1)], BT[:, g, 128 * ci:128 * (ci + 1)], CT[:, g, 128 * ci:128 * (ci + 1)],
                             True, True) for ci in range(4)], reads=[rBT, rCT], writes=[rbank[bk]])
                T.op("vector", lambda e, o=Gm[:, g, :, :].rearrange("p c l -> p (c l)"), i_=banks[bk][:, :]: e.tensor_copy(out=o, in_=i_),
                     reads=[rbank[bk]], writes=[rGm])
            mms = []
            for ci in range(4):
                t0 = T0 + 128 * ci
                for kc in range(8):
                    mms.append((smb[:, 16 * ci:16 * (ci + 1)], XT[:, kc, t0:t0 + 128], Wdt[:, kc, :], kc == 0 and ci == 0, kc == 7))
            T.mm_group(mms, reads=[rW, rXT], writes=[rbank[B_SM]])
            b16 = lambda ap: ap[:, :].unsqueeze(1).broadcast_to([128, 4, 16])
            c4 = lambda ap: ap[:, :].rearrange("p (c h) -> p c h", h=16)
            T.op("vector", lambda e: e.tensor_tensor(out=c4(sm["dtx"]), in0=c4(smb[:, 0:64]), in1=b16(dtb_b), op=ALU.add),
                 reads=[rbank[B_SM], rC], writes=[rsm["dtx"]])
            T.op("scalar", lambda e: e.activation(out=sm["dtx"][:], in_=sm["dtx"][:], func=AF.Exp), reads=[], writes=[rsm["dtx"]])
            T.op("scalar", lambda e: e.activation(out=sm["dt4"][:], in_=sm["dtx"][:], func=AF.Ln, bias=1.0),
                 reads=[rsm["dtx"]], writes=[rsm["dt4"]])
            T.op("vector", lambda e: e.tensor_tensor(out=c4(sm["adt4"]), in0=c4(sm["dt4"]), in1=b16(a_b), op=ALU.mult),
                 reads=[rsm["dt4"], rC], writes=[rsm["adt4"]])
            mms = []
            for ci in range(4):
                mms.append((smb[:, 64 + 16 * ci:64 + 16 * (ci + 1)], tri[:], sm["adt4"][:, 16 * ci:16 * (ci + 1)], False, True))
                mms.append((smb[:, 128 + 16 * ci:128 + 16 * (ci + 1)], onesF[:], sm["adt4"][:, 16 * ci:16 * (ci + 1)], False, True))
            T.mm_group(mms, reads=[rsm["adt4"], rC], writes=[rbank[B_SM]])
            T.op("vector", lambda e: e.tensor_scalar(out=sm["nacs4"][:], in0=smb[:, 64:128], scalar1=-1.0, scalar2=None, op0=ALU.mult),
                 reads=[rbank[B_SM]], writes=[rsm["nacs4"]])
            T.op("vector", lambda e: e.tensor_copy(out=sm["cdb4"][:], in_=smb[:, 128:192]), reads=[rbank[B_SM]], writes=[rsm["cdb4"]])
            T.op("vector", lambda e: e.tensor_tensor(out=sm["dte4"][:], in0=sm["cdb4"][:], in1=sm["nacs4"][:], op=ALU.add),
                 reads=[rsm["cdb4"], rsm["nacs4"]], writes=[rsm["dte4"]])
            T.op("scalar", lambda e: e.activation(out=sm["D04"][:], in_=sm["nacs4"][:], func=AF.Exp, scale=-1.0),
                 reads=[rsm["nacs4"]], writes=[rsm["D04"]])
            T.op("scalar", lambda e: e.activation(out=sm["dte4"][:], in_=sm["dte4"][:], func=AF.Exp), reads=[], writes=[rsm["dte4"]])
            T.op("scalar", lambda e: e.activation(out=sm["cdb4"][:], in_=sm["cdb4"][:], func=AF.Exp), reads=[rsm["dte4"]], writes=[rsm["cdb4"]])
            T.op("vector", lambda e: e.tensor_tensor(out=sm["ddt4"][:], in0=sm["dt4"][:], in1=sm["dte4"][:], op=ALU.mult),
                 reads=[rsm["dt4"], rsm["dte4"]], writes=[rsm["ddt4"]])

            def chunk_fns(ci):
                k = ci % 2
                cc = slice(128 * ci, 128 * ci + 128)
                hs = lambda ap, h0, n: ap[:, 16 * ci + h0:16 * ci + h0 + n]
                bch = lambda ap, h0, n: hs(ap, h0, n).unsqueeze(2).broadcast_to([128, n, 64])
                def prologue(cj):
                    kj = cj % 2
                    for g in range(2):
                        gs = slice(512 * g, 512 * (g + 1))
                        bcj = lambda ap, h0, n: ap[:, 16 * cj + h0:16 * cj + h0 + n].unsqueeze(2).broadcast_to([128, n, 64])
                        T.op("gpsimd", lambda e, o=v3(xg[kj][:, gs]), i_=v3(xs_tok[:, cj, gs]), b_=bcj(sm["dt4"], 8 * g, 8):
                             e.tensor_tensor(out=o, in0=i_, in1=b_, op=ALU.mult), reads=[rxs, rsm["dt4"]], writes=[rxg[kj]])
                        T.op("gpsimd", lambda e, o=v3(xgd[kj][:, gs]), i_=v3(xs_tok[:, cj, gs]), b_=bcj(sm["ddt4"], 8 * g, 8):
                             e.tensor_tensor(out=o, in0=i_, in1=b_, op=ALU.mult), reads=[rxs, rsm["ddt4"]], writes=[rxgd[kj]])
                        T.op("gpsimd", lambda e, o=v3(tsk[2 * kj + g][:, :]), i_=v3(xs_tok[:, cj, gs]), b_=bc64(dsk_b, 8 * g, 8):
                             e.tensor_tensor(out=o, in0=i_, in1=b_, op=ALU.mult), reads=[rxs, rC], writes=[rtsk[2 * kj + g]])

                def emit_R(q):
                    g, qd = divmod(q, 2)
                    h0 = 8 * g + 4 * qd
                    bk = B_R0 + (q % 2)
                    mms = [(banks[bk][:, :], identB[:], negm4[:], True, False)]
                    for j in range(4):
                        mms.append((banks[bk][:, 128 * j:128 * (j + 1)],
                                    hs(sm["adt4"], h0 + j, 1).broadcast_to([128, 128]), tri[:], False, j == 3))
                    T.mm_group(mms, reads=[rsm["adt4"], rC], writes=[rbank[bk]])

                def emit_exp(q):
                    g, qd = divmod(q, 2)
                    h0 = 8 * g + 4 * qd
                    bk = B_R0 + (q % 2)
                    for j in range(4):
                        T.op("scalar", lambda e, o=Dm[q % 2][:, 128 * j:128 * (j + 1)], i_=banks[bk][:, 128 * j:128 * (j + 1)],
                             b_=hs(sm["nacs4"], h0 + j, 1): e.activation(out=o, in_=i_, func=AF.Exp, bias=b_),
                             reads=[rbank[bk], rsm["nacs4"]], writes=[rDm[q % 2]])

                def emit_MT(q):
                    g, qd = divmod(q, 2)
                    T.op("vector", lambda e, o=MT[q % 2][:].rearrange("p (j l) -> p j l", l=128),
                         i_=Dm[q % 2][:].rearrange("p (j l) -> p j l", l=128),
                         b_=Gm[:, g, ci, :].unsqueeze(1).broadcast_to([128, 4, 128]):
                         e.tensor_tensor(out=o, in0=i_, in1=b_, op=ALU.mult), reads=[rDm[q % 2], rGm], writes=[rMT[q % 2]])

                def emit_ydiag(q):
                    g, qd = divmod(q, 2)
                    h0 = 8 * g + 4 * qd
                    ybk = B_Y if g == 0 else B_PB0
                    mms = []
                    for j in range(4):
                        hh = 4 * qd + j
                        mms.append((banks[ybk][:, 64 * hh:64 * (hh + 1)], MT[q % 2][:, 128 * j:128 * (j + 1)],
                                    xg[k][:, 64 * (h0 + j):64 * (h0 + j + 1)], (qd == 0 and j == 0), False))
                    T.mm_group(mms, reads=[rMT[q % 2], rxg[k]], writes=[rbank[ybk]])

                def emit_yoff(g):
                    obk = B_OS if g == 0 else B_PB1
                    T.mm_group([(banks[obk][:, :], CT[:, g, cc], Hbf[:, g, :], True, True)], reads=[rCT, rHbf[g]], writes=[rbank[obk]])

                def emit_comb(g):
                    gs = slice(512 * g, 512 * (g + 1))
                    obk = B_OS if g == 0 else B_PB1
                    ybk = B_Y if g == 0 else B_PB0
                    T.op("vector", lambda e, o=v3(yo[g][:, :]), i_=v3(banks[obk][:, :]), b_=bch(sm["D04"], 8 * g, 8):
                         e.tensor_tensor(out=o, in0=i_, in1=b_, op=ALU.mult), reads=[rbank[obk], rsm["D04"]], writes=[ryo[g]])
                    T.op("vector", lambda e, o=y[0][:, gs], i_=banks[ybk][:, :]: e.tensor_tensor(out=o, in0=i_, in1=yo[g][:, :], op=ALU.add),
                         reads=[rbank[ybk], ryo[g]], writes=[ry[0]])
                    T.op("gpsimd", lambda e, o=y[0][:, gs], t_=tsk[2 * k + g][:, :]: e.tensor_tensor(out=o, in0=o, in1=t_, op=ALU.add),
                         reads=[rtsk[2 * k + g]], writes=[ry[0]])

                def emit_state(g):
                    gs = slice(512 * g, 512 * (g + 1))
                    obk = B_OS if g == 0 else B_PB1
                    T.mm_group([(banks[obk][:, :], Btok[:, ci, g, :], xgd[k][:, gs], True, True)],
                               reads=[rBtok, rxgd[k]], writes=[rbank[obk]])
                    T.op("vector", lambda e, o=v3(H[:, g, :]), b_=bch(sm["cdb4"], 8 * g, 8):
                         e.tensor_tensor(out=o, in0=o, in1=b_, op=ALU.mult), reads=[rsm["cdb4"]], writes=[rH[g]])
                    T.op("vector", lambda e, o=H[:, g, :], i_=banks[obk][:, :]: e.tensor_tensor(out=o, in0=i_, in1=o, op=ALU.add),
                         reads=[rbank[obk]], writes=[rH[g]])
                    T.op("scalar", lambda e, o=Hbf[:, g, :], i_=H[:, g, :]: e.activation(out=o, in_=i_, func=AF.Copy),
                         reads=[rH[g]], writes=[rHbf[g]])

                def early():
                    emit_R(0); emit_R(1)
                    emit_yoff(0); emit_yoff(1)
                    emit_exp(0); emit_exp(1)
                    emit_MT(0); emit_MT(1)
                    emit_R(2); emit_R(3)
                    emit_ydiag(0); emit_ydiag(1)
                    emit_exp(2); emit_exp(3)
                    emit_MT(2); emit_MT(3)
                    emit_ydiag(2); emit_ydiag(3)

                def mid():
                    if ci + 1 < 4:
                        prologue(ci + 1)
                    emit_comb(0)
                    emit_state(0)
                    emit_comb(1)
                    emit_state(1)

                def late():
                    yk, ssk, rsk, mxk = y[0], ssq[k], rstd[k], mixn[k]
                    T.op("vector", lambda e, yk=yk, z_=zs[:, ci, :]: e.tensor_tensor(out=yk[:], in0=yk[:], in1=z_, op=ALU.mult),
                         reads=[rzs], writes=[ry[0]])
                    T.op("vector", lambda e, ssk=ssk: e.memset(ssk[:], 0.0), reads=[], writes=[rssq[k]])
                    T.op("scalar", lambda e, yk=yk, ssk=ssk, mxk=mxk: e.activation(out=mxk[:], in_=yk[:], func=AF.Square, accum_out=ssk[:]),
                         reads=[ry[0]], writes=[rmixn[k], rssq[k]])
                    T.op("scalar", lambda e, ssk=ssk, rsk=rsk: e.activation(out=rsk[:], in_=ssk[:], func=AF.Ln, bias=EPS, scale=1.0 / 1024),
                         reads=[rssq[k]], writes=[rrstd[k]])
                    T.op("scalar", lambda e, rsk=rsk: e.activation(out=rsk[:], in_=rsk[:], func=AF.Exp, scale=-0.5),
                         reads=[], writes=[rrstd[k]])
                    T.op("vector", lambda e, yk=yk, rsk=rsk, mxk=mxk: e.tensor_scalar(out=mxk[:], in0=yk[:], scalar1=rsk[:, 0:1], scalar2=None, op0=ALU.mult),
                         reads=[ry[0], rrstd[k]], writes=[rmixn[k]])
                    mt = mixT[0]
                    for half in range(2):
                        ev = None
                        for j in range(4):
                            ft = 4 * half + j
                            ev = P.op("tensor", lambda e, o=trb_b[:, 128 * j:128 * (j + 1)], i_=mixn[k][:, 128 * ft:128 * (ft + 1)]:
                                      e.transpose(o, i_, identB[:]), T.deps([rmixn[k], rC], [rbank[B_TR]]) if j == 0 else ())
                        T.mark(ev, [rmixn[k], rC], [rbank[B_TR]])
                        src = trb_b[:, 0:512].rearrange("p (j t) -> p j t", t=128)
                        if half == 0:
                            T.op("vector", lambda e, o=mt[:, 0:4, cc], i_=src: e.tensor_copy(out=o, in_=i_),
                                 reads=[rbank[B_TR]], writes=[rmixT[0][0]])
                        else:
                            T.op("scalar", lambda e, o=mt[:, 4:8, cc], i_=src: e.activation(out=o, in_=i_, func=AF.Copy),
                                 reads=[rbank[B_TR]], writes=[rmixT[0][1]])
                return prologue, early, mid, late

            fns = [chunk_fns(ci) for ci in range(4)]
            fns[0][0](0)
            fns[0][1]()
            for ci in range(4):
                fns[ci][2]()
                if ci + 1 < 4:
                    fns[ci + 1][1]()
                fns[ci][3]()
            T.dma("gpsimd", U[1024:2048, T0:T0 + 512].rearrange("(f p) t -> p f t", p=128), mixT[0][:, :, :], "st_uB",
                  reads=rmixT[0])
        barrier(P)
```

```python
import contextlib
import os
_SKIP = set(os.environ.get('KSKIP', '').split(','))
import numpy as np
import ml_dtypes
import concourse.bass as bass
import concourse.mybir as mybir
from concourse.bass_utils import run_bass_kernel_spmd

F32 = mybir.dt.float32
BF16 = mybir.dt.bfloat16
AF = mybir.ActivationFunctionType
ALU = mybir.AluOpType
AX = mybir.AxisListType

S = 4096
D = 1024
NH = 16
HD = 64
DIL = (1, 4, 16)
D_IN = 6672
OFF_Q, OFF_K, OFF_V, OFF_ZA, OFF_ZS, OFF_XBC, OFF_DT = 0, 1024, 2048, 3072, 4096, 5120, 6656
EPS = 1e-5


class Ev:
    __slots__ = ("eng", "idx", "sem", "val")

    def __init__(self, eng, idx, sem=None, val=None):
        self.eng, self.idx, self.sem, self.val = eng, idx, sem, val


class Prog:
    ENGS = ("sync", "scalar", "vector", "gpsimd", "tensor")

    def __init__(self, nc):
        self.nc = nc
        self.q = {e: [] for e in self.ENGS}
        self.dma_cnt = {}

    def op(self, eng, fn, deps=()):
        lst = self.q[eng]
        ev = Ev(eng, len(lst))
        lst.append([fn, [d for d in deps if d is not None], ev, False])
        return ev

    def dma(self, eng, out, in_, slot, deps=()):
        self.dma_cnt[slot] = self.dma_cnt.get(slot, 0) + 16
        ev = Ev(eng, len(self.q[eng]), sem=slot, val=self.dma_cnt[slot])
        self.q[eng].append([lambda e, o=out, i=in_: e.dma_start(out=o, in_=i),
                            [d for d in deps if d is not None], ev, True])
        return ev

    def emit(self, final_waits):
        nc = self.nc
        ref = {e: set() for e in self.ENGS}
        for e in self.ENGS:
            for fn, deps, ev, is_dma in self.q[e]:
                for d in deps:
                    if d.sem is None or d.sem.startswith("e_"):
                        ref[d.eng].add(d.idx)
        for d in final_waits:
            if d.sem is None:
                ref[d.eng].add(d.idx)
        for e in self.ENGS:
            c = 0
            for i, item in enumerate(self.q[e]):
                if item[3]:
                    continue
                if i in ref[e]:
                    c += 1
                    item[2].sem = "e_" + e
                    item[2].val = c
            assert c < 60000, (e, c)
        for s, v in self.dma_cnt.items():
            assert v < 60000, (s, v)
        names = ["e_" + e for e in self.ENGS] + sorted(self.dma_cnt)
        with contextlib.ExitStack() as st:
            sems = {n: st.enter_context(nc.semaphore(n)) for n in names}
            block = st.enter_context(nc.Block())
            for e in self.ENGS:
                items = self.q[e]
                fw = final_waits if e == "sync" else ()

                def body(eng, items=items, fw=fw):
                    seen = {}
                    for fn, deps, ev, is_dma in items:
                        need = {}
                        for d in deps:
                            assert d.sem is not None
                            if need.get(d.sem, 0) < d.val:
                                need[d.sem] = d.val
                        for sname, v in need.items():
                            if seen.get(sname, 0) < v:
                                eng.wait_ge(sems[sname], v)
                                seen[sname] = v
                        ins = fn(eng)
                        if is_dma:
                            ins.then_inc(sems[ev.sem], 16)
                        elif ev.sem is not None:
                            ins.then_inc(sems[ev.sem], 1)
                    for d in fw:
                        if seen.get(d.sem, 0) < d.val:
                            eng.wait_ge(sems[d.sem], d.val)
                            seen[d.sem] = d.val

                getattr(block, e)(body)


def latest(*evs):
    return [e for e in evs if e is not None]


def _bf16(a):
    return np.asarray(a, np.float32).astype(ml_dtypes.bfloat16)


def make_constants():
    c = {}
    slopes = 2.0 ** (-8.0 * np.arange(1, NH + 1) / NH)
    t = np.arange(S)
    hi_pos = (t >> 7).astype(np.float32)
    lo_pos = (t & 127).astype(np.float32)
    aug = np.zeros((NH, 2, 12, S), np.float32)
    for h in range(NH):
        cc = np.float64(slopes[h])
        c1 = np.float64(_bf16(cc).astype(np.float64))
        c2 = np.float64(_bf16(cc - c1).astype(np.float64))
        c3 = np.float64(_bf16(cc - c1 - c2).astype(np.float64))
        for j, cj in enumerate((c1, c2, c3)):
            aug[h, 0, j] = 128.0 * cj
            aug[h, 0, 3 + j] = cj
            aug[h, 0, 6 + j] = hi_pos
            aug[h, 0, 9 + j] = lo_pos
            aug[h, 1, j] = hi_pos
            aug[h, 1, 3 + j] = lo_pos
            aug[h, 1, 6 + j] = -128.0 * cj
            aug[h, 1, 9 + j] = -cj
    c["aug"] = _bf16(aug)
    ki = np.arange(128)[:, None]
    qi = np.arange(128)[None, :]
    mprev = np.where(ki >= qi, 0.0, -30000.0).astype(np.float32)
    mcur = np.where(ki <= qi, 0.0, -30000.0).astype(np.float32)
    c["mask4"] = np.concatenate([mprev, mcur, mprev, mcur], axis=1).astype(np.float32)
    c["tri"] = np.triu(np.ones((128, 128), np.float32))
    c["ident"] = np.eye(128, dtype=np.float32)
    c["identb"] = _bf16(np.eye(128, dtype=np.float32))
    si = np.arange(128)[:, None]
    li = np.arange(128)[None, :]
    c["negm4"] = _bf16(np.tile(np.where(si > li, -30000.0, 0.0), (1, 4)))
    return c


def I_act(P, out, in_, func, deps=(), bias=None, scale=None, accum_out=None, eng="scalar"):
    kw = {}
    if bias is not None:
        kw["bias"] = bias
    if scale is not None:
        kw["scale"] = scale
    if accum_out is not None:
        kw["accum_out"] = accum_out
    return P.op(eng, lambda e: e.activation(out=out, in_=in_, func=func, **kw), deps)


def I_copy(P, eng, out, in_, deps=()):
    if eng == "scalar":
        return P.op(eng, lambda e: e.activation(out=out, in_=in_, func=AF.Copy), deps)
    return P.op(eng, lambda e: e.tensor_copy(out=out, in_=in_), deps)


def I_tt(P, eng, out, in0, in1, op, deps=()):
    return P.op(eng, lambda e: e.tensor_tensor(out=out, in0=in0, in1=in1, op=op), deps)


def I_ts(P, eng, out, in0, s1, s2, op0, op1=None, deps=(), accum_out=None):
    kw = {}
    if op1 is not None:
        kw["op1"] = op1
    if accum_out is not None:
        kw["accum_out"] = accum_out
    return P.op(eng, lambda e: e.tensor_scalar(out=out, in0=in0, scalar1=s1, scalar2=s2, op0=op0, **kw), deps)


def I_stt(P, eng, out, in0, scalar, in1, op0, op1, deps=()):
    return P.op(eng, lambda e: e.scalar_tensor_tensor(out=out, in0=in0, scalar=scalar, in1=in1, op0=op0, op1=op1), deps)


def I_mm(P, out, lhsT, rhs, start, stop, deps=(), skip=True):
    return P.op("tensor", lambda e: e.matmul(out, lhsT=lhsT, rhs=rhs, start=start, stop=stop,
                                             skip_group_check=skip), deps)


def I_memset(P, eng, ap, val, deps=()):
    return P.op(eng, lambda e: e.memset(ap, val), deps)


def barrier(P):
    evs = []
    for e in P.ENGS:
        for item in reversed(P.q[e]):
            if not item[3]:
                evs.append(item[2])
                break
    last = {}
    for e in P.ENGS:
        for item in P.q[e]:
            if item[3]:
                last[item[2].sem] = item[2]
    evs += list(last.values())
    out = []
    for e in P.ENGS:
        out.append(P.op(e, lambda eng: eng.nop(), evs))
    return out


def tok_ap(t, d, r, n, cnt=128, lo=0, hi=None):
    start = d * (128 * n + lo) + r
    stop = start + d * (cnt - 1) + 1
    return slice(start, stop, d)


def build_program(debug_u=False, pairs=tuple(range(8)), phases=("A", "B", "C"), dbg=3, n_sc=8):
    nc = bass.Bass("TRN2", target_bir_lowering=False)
    xT = nc.dram_tensor("xT", [D, S], F32, kind="ExternalInput").ap()
    w_in = nc.dram_tensor("w_in", [D, D_IN], F32, kind="ExternalInput").ap()
    aug = nc.dram_tensor("aug", [NH, 2, 12, S], BF16, kind="ExternalInput").ap()
    mask4_d = nc.dram_tensor("mask4", [128, 512], F32, kind="ExternalInput").ap()
    dr = {}
    for name, shape, dt in (("tri", [128, 128], F32), ("ident", [128, 128], F32), ("identb", [128, 128], BF16), ("negm4", [128, 512], BF16),
                            ("conv_wT", [128, 12, 4], F32), ("conv_b2", [128, 12], F32), ("dt_bias", [16], F32),
                            ("a_log", [16], F32), ("d_skip", [16], F32), ("ssm_norm_g", [1024], F32),
                            ("att_norm_g2", [128, 8], F32), ("ssm_norm_g2", [128, 8], F32), ("ln_g", [1024], F32), ("ln_b", [1024], F32),
                            ("w_out", [2048, D], F32), ("x", [S, D], F32)):
        dr[name] = nc.dram_tensor(name, shape, dt, kind="ExternalInput").ap()
    U = nc.dram_tensor("U", [2048, S], BF16, kind="ExternalOutput" if debug_u else "Internal").ap()
    out_d = nc.dram_tensor("out", [S, D], F32, kind="ExternalOutput").ap()

    P = Prog(nc)
    final_waits = []
    with contextlib.ExitStack() as st:
        def sb(name, shape, dt, stack=st):
            return stack.enter_context(nc.sbuf_tensor(name, shape, dt))

        banks = [st.enter_context(nc.psum_tensor(f"bank{i}", [128, 512], F32)) for i in range(8)]
        w_in_v = w_in.rearrange("(kc p) c -> p kc c", p=128)
        with contextlib.ExitStack() as stx:
            XT = sb("XT", [128, 8, S], BF16, stx)
            with contextlib.ExitStack() as st0:
                xstage = [sb(f"xstage{i}", [128, S], F32, st0) for i in range(2)]
                free = [[], []]
                for kc in range(8):
                    b = kc % 2
                    ld = P.dma("sync", xstage[b][:], xT[kc * 128:(kc + 1) * 128, :], f"ld_x{b}", deps=free[b])
                    e1 = I_copy(P, "vector", XT[:, kc, 0:1536], xstage[b][:, 0:1536], [ld])
                    e2 = I_copy(P, "scalar", XT[:, kc, 1536:3072], xstage[b][:, 1536:3072], [ld])
                    e3 = I_copy(P, "gpsimd", XT[:, kc, 3072:4096], xstage[b][:, 3072:4096], [ld])
                    free[b] = [e1, e2, e3]
                barrier(P)

            if "A" in phases:
                phase_A(nc, P, st, banks, XT, w_in_v, aug, mask4_d, U, pairs, dbg, dr)
                barrier(P)
            if "B" in phases:
                (phase_B if 'oldB' in _SKIP else phase_B2)(nc, P, banks, XT, w_in_v, dr, U, n_sc)
                barrier(P)
        if "C" in phases:
            phase_C(nc, P, banks, dr, U, out_d)
            barrier(P)
        last = {}
        for e in P.ENGS:
            for item in P.q[e]:
                if item[3]:
                    last[item[2].sem] = item[2]
        final_waits = list(last.values())
        P.emit(final_waits)
    return nc


def phase_C(nc, P, banks, dr, U, out_d, n_tg=8):
    T = Trk(P)
    ALPHA = 2.0 ** 0.25
    with contextlib.ExitStack() as st:
        def sb(name, shape, dt, stack=None):
            return (stack or st).enter_context(nc.sbuf_tensor("C_" + name, shape, dt))

        Wo = sb("Wo", [128, 16, D], BF16)
        gA = sb("gA", [128, 16], F32)
        lg_b = sb("lg_b", [128, D], F32)
        lb_b = sb("lb_b", [128, D], F32)
        onesB = sb("onesB", [128, 1], BF16)
        rC = Res(); rWo = Res()
        T.dma("sync", gA[:, 0:8], dr["att_norm_g2"][:, :], "ld_d1", writes=[rC])
        T.dma("sync", gA[:, 8:16], dr["ssm_norm_g2"][:, :], "ld_d1b", writes=[rC])
        T.dma("sync", lg_b[:], dr["ln_g"].partition_broadcast(128), "ld_d2", writes=[rC])
        T.dma("sync", lb_b[:], dr["ln_b"].partition_broadcast(128), "ld_d3", writes=[rC])
        I_memset(P, "vector", onesB[:], 1.0)
        wo_v = dr["w_out"].rearrange("(f p) d -> p f d", p=128)
        with contextlib.ExitStack() as stw:
            wst = [sb(f"wstC{i}", [128, 2, D], F32, stw) for i in range(4)]
            rws = [Res() for _ in range(4)]
            for i in range(8):
                b = i % 4
                T.dma("sync" if i % 2 == 0 else "gpsimd", wst[b][:], wo_v[:, 2 * i:2 * i + 2, :], f"ld_wo{b}", writes=[rws[b]])
                for j in range(2):
                    f = 2 * i + j
                    if j == 0:
                        T.op("vector", lambda e, o=Wo[:, f, :], i_=wst[b][:, j, :], s_=gA[:, f:f + 1]:
                             e.tensor_scalar(out=o, in0=i_, scalar1=s_, scalar2=None, op0=ALU.mult),
                             reads=[rws[b], rC], writes=[])
                    else:
                        T.op("scalar", lambda e, o=Wo[:, f, :], i_=wst[b][:, j, :], s_=gA[:, f:f + 1]:
                             e.activation(out=o, in_=i_, func=AF.Copy, scale=s_),
                             reads=[rws[b], rC], writes=[])
            barrier(P)

        Ub = [sb(f"Ub{i}", [128, 16, 512], BF16) for i in range(2)]
        xt = [sb(f"xt{i}", [128, 4, D], F32) for i in range(2)]
        sq = [sb(f"sq{i}", [128, 8, 128], BF16) for i in range(2)]
        rr = [sb(f"rr{i}", [128, D], F32) for i in range(2)]
        ot = [sb(f"ot{i}", [128, D], F32) for i in range(2)]
        st6 = [sb(f"st6{i}", [128, 2, 6], F32) for i in range(2)]
        mv = [sb(f"mv{i}", [128, 2], F32) for i in range(2)]
        ra = [sb(f"ra{i}", [128, 1], F32) for i in range(2)]
        rl = [sb(f"rl{i}", [128, 1], F32) for i in range(2)]
        rUa = [Res(), Res()]; rUs = [Res(), Res()]; rxt = [[Res() for _ in range(4)] for _ in range(2)]; rot = [Res(), Res()]
        r = [{k: Res() for k in ("sq", "rr", "st6", "mv", "ra", "rl")} for _ in range(2)]
        slots = [(banks[0], banks[1]), (banks[2], banks[3]), (banks[4], banks[5])]
        rslot = [[Res(), Res()] for _ in range(3)]
        ssb = [banks[6], banks[7]]; rss = [Res(), Res()]
        U_v = U.rearrange("(f p) t -> p f t", p=128)
        x_v = dr["x"].rearrange("(c p) d -> p c d", p=128)
        slot_i = 0
        cnt = 0
        for tg in range(n_tg):
            b = tg % 2
            T0 = 512 * tg
            T.dma("sync", Ub[b][:, 0:8, :], U_v[:, 0:8, T0:T0 + 512], f"ld_ua{b}", writes=[rUa[b]])
            T.dma("sync", Ub[b][:, 8:16, :], U_v[:, 8:16, T0:T0 + 512], f"ld_us{b}", writes=[rUs[b]])
            T.dma("sync", xt[b][:], x_v[:, 4 * tg:4 * tg + 4, :], f"ld_xt{b}", writes=rxt[b])
            for ci in range(4):
                cc = slice(128 * ci, 128 * ci + 128)
                k = cnt % 2
                rk = r[k]
                T.op("scalar", lambda e, o=xt[b][:, ci, :]: e.activation(out=o, in_=o, func=AF.Copy, scale=ALPHA),
                     reads=[], writes=[rxt[b][ci]])
                T.op("scalar", lambda e, o=sq[k][:], i_=Ub[b][:, 0:8, cc]: e.activation(out=o, in_=i_, func=AF.Square),
                     reads=[rUa[b]], writes=[rk["sq"]])
                halves = []
                for half in range(2):
                    cols = slice(512 * half, 512 * (half + 1))
                    sl = slot_i; slot_i = (slot_i + 1) % 3
                    bkA, bkB = slots[sl]
                    T.mm_group([(bkA[:, :], Ub[b][:, f, cc], Wo[:, f, cols], f == 0, f == 7) for f in range(8)],
                               reads=[rUa[b]], writes=[rslot[sl][0]])
                    T.mm_group([(bkB[:, :], Ub[b][:, 8 + f, cc], Wo[:, 8 + f, cols], f == 0, f == 7) for f in range(8)],
                               reads=[rUs[b]], writes=[rslot[sl][1]])
                    halves.append((sl, cols))
                T.mm_group([(ssb[k][:, 0:1], sq[k][:, f, :], onesB[:], f == 0, f == 7) for f in range(8)],
                           reads=[rk["sq"]], writes=[rss[k]])
                T.op("scalar", lambda e, o=ra[k][:], i_=ssb[k][:, 0:1]: e.activation(out=o, in_=i_, func=AF.Ln, bias=EPS, scale=1.0 / 1024),
                     reads=[rss[k]], writes=[rk["ra"]])
                T.op("scalar", lambda e, o=ra[k][:]: e.activation(out=o, in_=o, func=AF.Exp, scale=-0.5), reads=[], writes=[rk["ra"]])
                for sl, cols in halves:
                    bkA, bkB = slots[sl]
                    T.op("vector", lambda e, o=rr[k][:, cols], i_=bkA[:, :], x_=xt[b][:, ci, cols], s_=ra[k][:, 0:1]:
                         e.scalar_tensor_tensor(out=o, in0=i_, scalar=s_, in1=x_, op0=ALU.mult, op1=ALU.add),
                         reads=[rslot[sl][0], rk["ra"], rxt[b][ci]], writes=[rk["rr"]])
                    T.op("vector", lambda e, o=rr[k][:, cols], i_=bkB[:, :]: e.tensor_tensor(out=o, in0=i_, in1=o, op=ALU.add),
                         reads=[rslot[sl][1]], writes=[rk["rr"]])
                for half in range(2):
                    T.op("vector", lambda e, o=st6[k][:, half, :], i_=rr[k][:, 512 * half:512 * (half + 1)]: e.bn_stats(out=o, in_=i_),
                         reads=[rk["rr"]], writes=[rk["st6"]])
                T.op("vector", lambda e, o=mv[k][:], i_=st6[k][:]: e.bn_aggr(out=o, in_=i_), reads=[rk["st6"]], writes=[rk["mv"]])
                T.op("scalar", lambda e, o=rl[k][:], i_=mv[k][:, 1:2]: e.activation(out=o, in_=i_, func=AF.Ln, bias=EPS),
                     reads=[rk["mv"]], writes=[rk["rl"]])
                T.op("scalar", lambda e, o=rl[k][:]: e.activation(out=o, in_=o, func=AF.Exp, scale=-0.5), reads=[], writes=[rk["rl"]])
                T.op("vector", lambda e, o=ot[k][:], i_=rr[k][:], m_=mv[k][:, 0:1], s_=rl[k][:, 0:1]:
                     e.tensor_scalar(out=o, in0=i_, scalar1=m_, scalar2=s_, op0=ALU.subtract, op1=ALU.mult),
                     reads=[rk["rr"], rk["mv"], rk["rl"]], writes=[rot[k]])
                T.op("gpsimd", lambda e, o=ot[k][:]: e.tensor_tensor(out=o, in0=o, in1=lg_b[:], op=ALU.mult),
                     reads=[rC], writes=[rot[k]])
                T.op("gpsimd", lambda e, o=ot[k][:]: e.tensor_tensor(out=o, in0=o, in1=lb_b[:], op=ALU.add),
                     reads=[rC], writes=[rot[k]])
                T.dma("gpsimd", out_d[T0 + 128 * ci:T0 + 128 * ci + 128, :], ot[k][:], f"st_o{k}", reads=[rot[k]])
                cnt += 1


def phase_A(nc, P, st_outer, banks, XT, w_in_v, aug, mask4_d, U, pairs, dbg=3, dr=None):
    with contextlib.ExitStack() as st:
        def sb(name, shape, dt):
            return st.enter_context(nc.sbuf_tensor(name, shape, dt))

        mask4 = sb("mask4s", [128, 512], F32)
        stmp = [sb(f"stmp{i}", [128, 512], F32) for i in range(4)]
        wstage = sb("wstage", [128, 8, 512], F32)
        wbf = sb("wbf", [128, 8, 512], BF16)
        qk = [[sb(f"qk{w}{h}", [128, S], BF16) for h in range(2)] for w in range(2)]
        sz = sb("sz", [128, S], BF16)
        V = sb("V", [128, 3, 32, 192], BF16)
        NPT = 6
        PT = [sb(f"PT{i}", [128, 512], BF16) for i in range(NPT)]
        rc = [sb(f"rc{i}", [128, 512], F32) for i in range(2)]
        t1 = [sb(f"t1{i}", [128, 512], F32) for i in range(2)]
        uT = sb("uT", [128, S], BF16)
        vT = sb("vT", [128, S], BF16)
        identBa = sb("identBa", [128, 128], BF16)

        acc = banks[0:4]
        sbank = banks[4:6]
        pbank = banks[6:8]

        ev_mask = P.dma("sync", mask4[:], mask4_d[:, :], "ld_c")
        ev_id = P.dma("sync", identBa[:], dr["identb"][:, :], "ld_cid")
        vT_free = []
        ev_ones = I_memset(P, "gpsimd", V[:, :, :, 64:128], 1.0) if "ones" not in _SKIP else None

        pb_free = [[], []]
        pb_i = 0
        sb_free = [None, None]
        pt_free = [None] * 6
        st_free = [None] * 4
        acc_free = [None] * 4
        w_free = []
        qk_free = [[[], []], [[], []]]
        sz_free = []
        V_free = []
        uT_free = None
        g_ctr = 0

        def load_w(hp_, deps_):
            out_ = []
            for wi, off in enumerate((OFF_Q, OFF_K, OFF_V, OFF_ZA)):
                out_.append(P.dma("sync", wstage[:, :, wi * 128:(wi + 1) * 128],
                                  w_in_v[:, :, off + hp_ * 128: off + (hp_ + 1) * 128], "ld_w", deps=deps_))
            return out_

        lds = load_w(pairs[0], [])
        w_casts = None
        ws_b = wstage[:].rearrange("p a b -> p (a b)").bitcast(BF16)
        qk16 = [ws_b[:, 0:S], ws_b[:, S:2 * S]]
        qk16_free = []
        rc_free = [None, None]
        ev_ctr = 0
        pending_evac = []
        for hp in pairs:
            hA, hB = 2 * hp, 2 * hp + 1
            if w_casts is None:
                c1 = I_copy(P, "vector", wbf[:, 0:4, :], wstage[:, 0:4, :], lds)
                c2 = I_copy(P, "scalar", wbf[:, 4:8, :], wstage[:, 4:8, :], lds)
                w_casts = [c1, c2]
            w_ready = w_casts
            nxt = pairs.index(hp) + 1
            if nxt < len(pairs):
                lds = load_w(pairs[nxt], w_casts + qk16_free)
            aug_ev = [[None, None], [None, None]]
            for w in range(2):
                for hh, h in enumerate((hA, hB)):
                    if "aug" in _SKIP:
                        continue
                    aug_ev[w][hh] = P.dma("sync", qk[w][hh][64:76, :], aug[h, w, :, :], f"ld_aug{w}{hh}",
                                          deps=qk_free[w][hh])
            w_last = []
            qk_ready = [[[], []], [[], []]]
            sz_ready = []
            vT_ready = []
            for wi in (0, 1, 2, 3):
                for tt in range(8):
                    pbk = pbank[pb_i]
                    deps = list(w_ready) + list(pb_free[pb_i])
                    for kc in range(8):
                        mm = I_mm(P, pbk[:, :], wbf[:, kc, wi * 128:(wi + 1) * 128],
                                  XT[:, kc, tt * 512:(tt + 1) * 512], kc == 0, kc == 7,
                                  deps if kc == 0 else ())
                    cols = slice(tt * 512, (tt + 1) * 512)
                    if wi == 3:
                        e = I_act(P, sz[:, cols], pbk[:, :], AF.Silu, [mm] + sz_free)
                        sz_ready.append(e)
                        pb_free[pb_i] = [e]
                    elif wi == 2:
                        e = I_copy(P, "vector" if tt % 2 == 0 else "scalar", vT[:, cols], pbk[:, :], [mm] + vT_free)
                        vT_ready.append(e)
                        pb_free[pb_i] = [e]
                    else:
                        w = wi
                        sc = 0.125 if w == 0 else 1.0
                        eA = I_act(P, qk[w][0][0:64, cols], pbk[0:64, :], AF.Copy, [mm] + qk_free[w][0], scale=sc)
                        if "dveB" in _SKIP:
                            eB = I_act(P, qk[w][1][0:64, cols], pbk[64:128, :], AF.Copy, [mm] + qk_free[w][1], scale=sc)
                        else:
                            eB = I_ts(P, "vector", qk[w][1][0:64, cols], pbk[64:128, :], sc, None, ALU.mult,
                                      deps=[mm] + qk_free[w][1])
                        qk_ready[w][0].append(eA)
                        qk_ready[w][1].append(eB)
                        pb_free[pb_i] = [eA, eB]
                    pb_i ^= 1
                    w_last = [mm]
            V_ready = {}
            tr_last = None
            for di, d in enumerate(DIL if dbg >= 2 else ()):
                nb = 32 // d
                for c0 in range(0, 32, 8):
                    pbk = pbank[pb_i]
                    pbk_b = pbk[:].bitcast(BF16)
                    for j in range(8):
                        r_, m_ = divmod(c0 + j, nb)
                        deps = ()
                        if j == 0:
                            deps = vT_ready + [ev_id] + list(pb_free[pb_i])
                        tr_last = P.op("tensor", lambda e, o=pbk_b[:, 128 * j:128 * (j + 1)], i_=vT[:, tok_ap(None, d, r_, m_)]:
                                       e.transpose(o, i_, identBa[:]), deps)
                    src = pbk_b[:, :].rearrange("p (c f) -> p c f", f=128)
                    eA = I_copy(P, "vector", V[:, di, c0:c0 + 8, 0:64], src[:, :, 0:64], [tr_last] + V_free)
                    eB = I_copy(P, "scalar", V[:, di, c0:c0 + 8, 128:192], src[:, :, 64:128], [tr_last, eA] + V_free)
                    for j in range(8):
                        V_ready[(di, c0 + j)] = [eA, eB]
                    pb_free[pb_i] = [eA, eB]
                    pb_i ^= 1
            vT_free = [tr_last] if tr_last is not None else []
            w_free = w_last
            if nxt < len(pairs):
                c1 = I_copy(P, "vector", wbf[:, 0:4, :], wstage[:, 0:4, :], lds + w_last)
                c2 = I_copy(P, "scalar", wbf[:, 4:8, :], wstage[:, 4:8, :], lds + w_last)
                w_casts = [c1, c2]
            V_free = []
            qk_free = [[[], []], [[], []]]
            sz_free = []

            u_written = []
            for hh, h in enumerate((hA, hB) if dbg >= 3 else ()):
                qT, kT = qk[0][hh], qk[1][hh]
                q_dep = qk_ready[0][hh] + [aug_ev[0][hh]]
                k_dep = qk_ready[1][hh] + [aug_ev[1][hh]]
                vcol = slice(0, 128) if hh == 0 else slice(64, 192)
                cdeps = (w_casts if nxt < len(pairs) else []) + qk16_free
                c16 = []
                for w in range(2):
                    src = qk[w][hh][0:76, :].rearrange("p (j r) -> p r j", r=16)
                    dst = qk16[w][0:76, :].rearrange("p (r j) -> p r j", r=16)
                    dd = (q_dep if w == 0 else k_dep) + cdeps
                    if w == 0:
                        c16.append(I_copy(P, "vector", dst, src, dd))
                    else:
                        c16.append(I_copy(P, "scalar", dst, src, dd))
                last16 = None
                for sbk in range(2):
                    qbs = []
                    for di, d in enumerate(DIL):
                        nb = 32 // d
                        per_sb = nb // 2
                        for r in range(d):
                            for n in range(sbk * per_sb, (sbk + 1) * per_sb):
                                qbs.append((di, d, r, n))
                    groups = [qbs[i:i + 2] for i in range(0, len(qbs), 2)]
                    acc_started = [False] * 4
                    acc_last_mm = [None] * 4
                    pend = []

                    def do_pv(item):
                        ptb, grp, ev_mask_mul = item
                        last_mm = None
                        for j, (di, d, r, n) in enumerate(grp):
                            nb = 32 // d
                            for role in range(2):
                                m = n - 1 + role
                                if m < 0:
                                    continue
                                tile_cols = (2 * j + role) * 128
                                cid = r * nb + m
                                lhsT = V[:, di, cid, vcol]
                                if d == 16:
                                    pieces = [(pc, 32) for pc in range(4)]
                                else:
                                    pieces = [(0, 128)]
                                for pc, cnt in pieces:
                                    i0 = pc * 32 if d == 16 else 0
                                    t0 = d * (128 * n + i0) + r
                                    col0 = t0 - 2048 * sbk
                                    bk = col0 // 512
                                    c0 = col0 % 512
                                    out = acc[bk][:, c0: c0 + d * (cnt - 1) + 1: d]
                                    rhs = PT[ptb][:, tile_cols + i0: tile_cols + i0 + cnt]
                                    deps = [ev_mask_mul] + V_ready[(di, cid)] + latest(ev_ones)
                                    if not acc_started[bk]:
                                        deps = deps + latest(acc_free[bk])
                                    last_mm = I_mm(P, out, lhsT, rhs, not acc_started[bk], False, deps)
                                    acc_started[bk] = True
                                    acc_last_mm[bk] = last_mm
                        pt_free[ptb] = last_mm

                    for gi, grp in enumerate(groups):
                        if gi % 2 == 0 and pending_evac:
                            pending_evac.pop(0)()
                        sbi = g_ctr % 2
                        pti = g_ctr % 6
                        sti = g_ctr % 4
                        g_ctr += 1
                        sbb = sbank[sbi]
                        first = True
                        mm = None
                        merged = (len(grp) == 2 and grp[0][1] != 16 and grp[1][0:3] == grp[0][0:3]
                                  and grp[1][3] == grp[0][3] + 1)
                        for j, (di, d, r, n) in enumerate(grp):
                            for role in range(2):
                                if merged and (j, role) == (1, 0):
                                    continue
                                m = n - 1 + role
                                nq = n
                                if m < 0:
                                    m, nq = 0, 1
                                tile_cols = (2 * j + role) * 128
                                deps = ()
                                if first:
                                    deps = q_dep + k_dep + latest(sb_free[sbi])
                                    first = False
                                if d == 16:
                                    kop = qk16[1][0:76, r * 256 + 128 * m: r * 256 + 128 * m + 128]
                                    qop = qk16[0][0:76, r * 256 + 128 * nq: r * 256 + 128 * nq + 128]
                                    mm = I_mm(P, sbb[:, tile_cols:tile_cols + 128], kop, qop, True, True, list(deps) + c16)
                                    last16 = mm
                                elif merged and (j, role) == (0, 1):
                                    mm = I_mm(P, sbb[:, tile_cols:tile_cols + 256],
                                              kT[0:76, tok_ap(None, d, r, m)], qT[0:76, tok_ap(None, d, r, n, cnt=256)],
                                              True, True, deps)
                                else:
                                    mm = I_mm(P, sbb[:, tile_cols:tile_cols + 128],
                                              kT[0:76, tok_ap(None, d, r, m)], qT[0:76, tok_ap(None, d, r, nq)],
                                              True, True, deps)
                        mk0 = I_tt(P, "vector", stmp[sti][:, :], sbb[:, :], mask4[:, :], ALU.add,
                                   [mm, ev_mask] + latest(st_free[sti]))
                        sb_free[sbi] = mk0
                        mk = I_act(P, PT[pti][:, :], stmp[sti][:, :], AF.Exp, [mk0] + latest(pt_free[pti]))
                        st_free[sti] = mk
                        pend.append((pti, grp, mk))
                        if len(pend) > 4:
                            do_pv(pend.pop(0))
                    while pend:
                        do_pv(pend.pop(0))
                    def make_evac(bk, hh=hh, sbk=sbk, alm=acc_last_mm, szr=sz_ready):
                        def evac():
                            nonlocal ev_ctr, sz_free
                            cols = slice(2048 * sbk + 512 * bk, 2048 * sbk + 512 * (bk + 1))
                            if hh == 0:
                                o_rows, s_rows = slice(0, 64), slice(64, 128)
                            else:
                                o_rows, s_rows = slice(64, 128), slice(0, 64)
                            ei = ev_ctr % 2
                            ev_ctr += 1
                            a1 = I_act(P, rc[ei][o_rows, :], acc[bk][s_rows, :], AF.Ln, [alm[bk]] + latest(rc_free[ei]))
                            a2 = I_copy(P, "scalar", t1[ei][o_rows, :], acc[bk][o_rows, :], [alm[bk]] + latest(rc_free[ei]))
                            acc_free[bk] = a2
                            e1 = I_act(P, rc[ei][o_rows, :], rc[ei][o_rows, :], AF.Exp, [a1], scale=-1.0)
                            e2 = I_tt(P, "gpsimd", t1[ei][o_rows, :], t1[ei][o_rows, :], rc[ei][o_rows, :], ALU.mult, [e1, a2])
                            e3 = I_tt(P, "gpsimd", uT[o_rows, cols], t1[ei][o_rows, :], sz[o_rows, cols], ALU.mult,
                                      [e2] + szr + latest(uT_free))
                            rc_free[ei] = e3
                            u_written.append(e3)
                            sz_free = [e3]
                        return evac
                    pending_evac.extend(make_evac(bk) for bk in range(4))
                qk_free[0][hh] = [mm]
                qk_free[1][hh] = [mm]
                qk16_free = [last16]
            if dbg < 3:
                continue
            while pending_evac:
                pending_evac.pop(0)()
            V_free = latest(*[acc_last_mm[b] for b in range(4)])
            uT_free = P.dma("gpsimd", U[hp * 128:(hp + 1) * 128, :], uT[:, :], "st_u", deps=u_written)
            sz_free = u_written[-1:]


class Res:
    __slots__ = ("w", "r")

    def __init__(self):
        self.w = None
        self.r = {}


class Trk:
    def __init__(self, P):
        self.P = P

    def deps(self, reads, writes):
        d = []
        for t in reads:
            if t.w is not None:
                d.append(t.w)
        for t in writes:
            if t.w is not None:
                d.append(t.w)
            d.extend(t.r.values())
        return d

    def mark(self, ev, reads, writes):
        for t in reads:
            t.r[ev.sem if ev.sem is not None else ev.eng] = ev
        for t in writes:
            t.w = ev
            t.r = {}

    def op(self, eng, fn, reads=(), writes=(), extra=()):
        ev = self.P.op(eng, fn, self.deps(reads, writes) + list(extra))
        self.mark(ev, reads, writes)
        return ev

    def dma(self, eng, out, in_, slot, reads=(), writes=(), extra=()):
        ev = self.P.dma(eng, out, in_, slot, self.deps(reads, writes) + list(extra))
        self.mark(ev, reads, writes)
        return ev

    def mm_group(self, mms, reads, writes, extra=()):
        d = self.deps(reads, writes) + list(extra)
        ev = None
        for i, (out, lhsT, rhs, start, stop) in enumerate(mms):
            ev = I_mm(self.P, out, lhsT, rhs, start, stop, d if i == 0 else ())
        self.mark(ev, reads, writes)
        return ev


def bc64(ap16, h0, nh):
    return ap16[:, h0:h0 + nh].unsqueeze(2).broadcast_to([128, nh, 64])


def phase_B(nc, P, banks, XT, w_in_v, dr, U, n_sc=8, dbg_out=None):
    T = Trk(P)
    with contextlib.ExitStack() as st:
        def sb(name, shape, dt, stack=None):
            return (stack or st).enter_context(nc.sbuf_tensor("B_" + name, shape, dt))

        rXT = Res()
        Wz = sb("Wz", [128, 8, 1024], BF16)
        Wx = sb("Wx", [128, 8, 1536], BF16)
        Wdt = sb("Wdt", [128, 8, 16], BF16)
        tri = sb("tri", [128, 128], F32)
        onesF = sb("onesF", [128, 128], F32)
        identF = sb("identF", [128, 128], F32)
        identB = sb("identB", [128, 128], BF16)
        negm4 = sb("negm4", [128, 512], BF16)
        cw = sb("cw", [128, 12, 4], F32)
        cb = sb("cb", [128, 12], F32)
        dtb_b = sb("dtb_b", [128, 16], F32)
        a_b = sb("a_b", [128, 16], F32)
        dsk_b = sb("dsk_b", [128, 16], F32)
        gs_b = sb("gs_b", [128, 1024], F32)
        hal = sb("hal", [128, 12, 3], F32)
        rW = Res(); rC = Res(); rHal = Res()

        with contextlib.ExitStack() as stw:
            wst = sb("wstB", [128, 8, 512], F32, stw)
            rwst = Res()
            pieces = [(OFF_ZS, 0, 512, Wz), (OFF_ZS + 512, 512, 512, Wz),
                      (OFF_XBC, 0, 512, Wx), (OFF_XBC + 512, 512, 512, Wx), (OFF_XBC + 1024, 1024, 512, Wx),
                      (OFF_DT, 0, 16, Wdt)]
            for i, (off, dst0, n, Wt) in enumerate(pieces):
                T.dma("sync", wst[:, :, 0:n], w_in_v[:, :, off:off + n], "ld_wB", writes=[rwst])
                T.op("vector", lambda e, o=Wt[:, 0:4, dst0:dst0 + n], i_=wst[:, 0:4, 0:n]: e.tensor_copy(out=o, in_=i_),
                     reads=[rwst], writes=[rW])
                T.op("gpsimd", lambda e, o=Wt[:, 4:8, dst0:dst0 + n], i_=wst[:, 4:8, 0:n]: e.tensor_copy(out=o, in_=i_),
                     reads=[rwst], writes=[])
            T.dma("sync", tri[:], dr["tri"][:, :], "ld_c1", writes=[rC])
            T.dma("sync", identF[:], dr["ident"][:, :], "ld_c2", writes=[rC])
            T.dma("sync", negm4[:], dr["negm4"][:, :], "ld_c3", writes=[rC])
            T.dma("sync", cw[:], dr["conv_wT"][:, :, :], "ld_c4", writes=[rC])
            T.dma("sync", cb[:], dr["conv_b2"][:, :], "ld_c5", writes=[rC])
            T.dma("sync", dtb_b[:], dr["dt_bias"].partition_broadcast(128), "ld_c6", writes=[rC])
            T.dma("sync", a_b[:], dr["a_log"].partition_broadcast(128), "ld_c7", writes=[rC])
            T.dma("sync", dsk_b[:], dr["d_skip"].partition_broadcast(128), "ld_c8", writes=[rC])
            T.dma("sync", gs_b[:], dr["ssm_norm_g"].partition_broadcast(128), "ld_c9", writes=[rC])
            barrier(P)
            I_memset(P, "vector", onesF[:], 1.0)
            I_memset(P, "vector", hal[:], 0.0)
            I_copy(P, "vector", identB[:], identF[:])
            I_act(P, a_b[:], a_b[:], AF.Exp)
            barrier(P)
            I_ts(P, "vector", a_b[:], a_b[:], -1.0, None, ALU.mult)
            barrier(P)

        xb = [sb(f"xb{i}", [128, 515], F32) for i in range(2)]
        cvb = [sb(f"cvb{i}", [128, 512], F32) for i in range(2)]
        xsT = sb("xsT", [128, 8, 512], F32)
        BT = sb("BT", [128, 2, 512], BF16)
        CT = sb("CT", [128, 2, 512], BF16)
        dtx = sb("dtx", [128, 16], F32)
        dt_t = sb("dt_t", [128, 16], F32)
        adt = sb("adt", [128, 16], F32)
        nacs = sb("nacs", [128, 16], F32)
        D0 = sb("D0", [128, 16], F32)
        dte = sb("dte", [128, 16], F32)
        ddt = sb("ddt", [128, 16], F32)
        cdb = sb("cdb", [128, 16], F32)
        Btok = sb("Btok", [128, 2, 128], BF16)
        xs_tok = sb("xs_tok", [128, 1024], F32)
        xg = sb("xg", [128, 1024], BF16)
        xgd = sb("xgd", [128, 1024], BF16)
        Gm = sb("Gm", [128, 2, 128], BF16)
        Dm = [sb(f"Dm{i}", [128, 512], BF16) for i in range(2)]
        MT = [sb(f"MT{i}", [128, 512], BF16) for i in range(2)]
        H = sb("H", [128, 2, 512], F32)
        Hbf = sb("Hbf", [128, 2, 512], BF16)
        yo = sb("yo", [128, 512], F32)
        tsk = sb("tsk", [128, 512], F32)
        y = sb("y", [128, 1024], F32)
        zs = sb("zs", [128, 1024], F32)
        junk = sb("junkB", [128, 1024], F32)
        ssq = sb("ssq", [128, 1], F32)
        rstd = sb("rstd", [128, 1], F32)
        mixn = sb("mixn", [128, 1024], BF16)
        mixT = sb("mixT", [128, 8, 512], BF16)

        r = {k: Res() for k in ("xsT", "BT", "CT", "dtx", "dt", "adt", "nacs", "D0", "dte", "ddt", "cdb", "Btok",
                                "xs_tok", "xg", "xgd", "Gm", "H", "Hbf", "yo", "tsk", "y", "zs", "junk", "ssq", "rstd",
                                "mixn", "mixT")}
        rxb = [Res(), Res()]; rxbh = [Res(), Res()]; rcv = [Res(), Res()]; rmixT = [Res(), Res()]; rDm = [Res(), Res()]; rMT = [Res(), Res()]
        rhal = [Res() for _ in range(12)]
        pb = [banks[0], banks[1]]; rpb = [Res(), Res()]
        smb = banks[2]; rsmb = Res()
        trb = banks[3]; rtr = Res()
        Rb = [banks[4], banks[5]]; rRb = [Res(), Res()]
        Yb = banks[6]; rY = Res()
        osb = banks[7]; ros = Res()

        I_memset(P, "vector", H[:], 0.0)
        I_memset(P, "vector", Hbf[:], 0.0)
        barrier(P)

        pbi = 0
        quad_ctr = 0
        for sc in range(n_sc):
            T0 = 512 * sc
            for ft in range(12):
                b = pbi; pbi ^= 1
                xbi = ft % 2
                T.mm_group([(pb[b][:, :], Wx[:, kc, ft * 128:(ft + 1) * 128], XT[:, kc, T0:T0 + 512], kc == 0, kc == 7)
                            for kc in range(8)], reads=[rW, rXT], writes=[rpb[b]])
                T.op("scalar", lambda e, o=xb[xbi][:, 3:515], i_=pb[b][:, :]: e.activation(out=o, in_=i_, func=AF.Copy),
                     reads=[rpb[b]], writes=[rxb[xbi]])
                T.op("gpsimd", lambda e, o=xb[xbi][:, 0:3], i_=hal[:, ft, :]: e.tensor_copy(out=o, in_=i_),
                     reads=[rhal[ft]], writes=[rxbh[xbi]])
                T.op("gpsimd", lambda e, o=hal[:, ft, :], i_=xb[xbi][:, 512:515]: e.tensor_copy(out=o, in_=i_),
                     reads=[rxb[xbi]], writes=[rhal[ft]])
                ceng = "vector"
                cv = cvb[xbi]
                T.op(ceng, lambda e, o=cv[:, :], i_=xb[xbi][:, 0:512], s_=cw[:, ft, 0:1]:
                     e.tensor_scalar(out=o, in0=i_, scalar1=s_, scalar2=None, op0=ALU.mult),
                     reads=[rxb[xbi], rxbh[xbi], rC], writes=[rcv[xbi]])
                for j in range(1, 4):
                    T.op(ceng, lambda e, o=cv[:, :], i_=xb[xbi][:, j:j + 512], s_=cw[:, ft, j:j + 1]:
                         e.scalar_tensor_tensor(out=o, in0=i_, scalar=s_, in1=o, op0=ALU.mult, op1=ALU.add),
                         reads=[rxb[xbi], rxbh[xbi], rcv[xbi]], writes=[rcv[xbi]])
                if ft < 8:
                    dst, rd = xsT[:, ft, :], r["xsT"]
                elif ft < 10:
                    dst, rd = BT[:, ft - 8, :], r["BT"]
                else:
                    dst, rd = CT[:, ft - 10, :], r["CT"]
                T.op("scalar", lambda e, o=dst, i_=cv[:, :], b_=cb[:, ft:ft + 1]:
                     e.activation(out=o, in_=i_, func=AF.Silu, bias=b_), reads=[rcv[xbi], rC], writes=[rd])

            for ci in range(4):
                t0 = T0 + 128 * ci
                cc = slice(128 * ci, 128 * ci + 128)
                T.mm_group([(smb[:, 0:16], XT[:, kc, t0:t0 + 128], Wdt[:, kc, :], kc == 0, kc == 7) for kc in range(8)],
                           reads=[rW, rXT], writes=[rsmb])
                for half in range(2):
                    b = pbi; pbi ^= 1
                    T.mm_group([(pb[b][:, :], XT[:, kc, t0:t0 + 128], Wz[:, kc, 512 * half:512 * (half + 1)], kc == 0, kc == 7)
                                for kc in range(8)], reads=[rW, rXT], writes=[rpb[b]])
                    T.op("scalar", lambda e, o=zs[:, 512 * half:512 * (half + 1)], i_=pb[b][:, :]:
                         e.activation(out=o, in_=i_, func=AF.Silu), reads=[rpb[b]], writes=[r["zs"]])
                T.op("vector", lambda e: e.tensor_tensor(out=dtx[:], in0=smb[:, 0:16], in1=dtb_b[:], op=ALU.add),
                     reads=[rsmb, rC], writes=[r["dtx"]])
                T.op("scalar", lambda e: e.activation(out=dtx[:], in_=dtx[:], func=AF.Exp), reads=[], writes=[r["dtx"]])
                T.op("scalar", lambda e: e.activation(out=dt_t[:], in_=dtx[:], func=AF.Ln, bias=1.0),
                     reads=[r["dtx"]], writes=[r["dt"]])
                T.op("vector", lambda e: e.tensor_tensor(out=adt[:], in0=dt_t[:], in1=a_b[:], op=ALU.mult),
                     reads=[r["dt"], rC], writes=[r["adt"]])
                T.mm_group([(smb[:, 16:32], tri[:], adt[:], True, True)], reads=[r["adt"], rC], writes=[rsmb])
                T.mm_group([(smb[:, 32:48], onesF[:], adt[:], True, True)], reads=[r["adt"], rC], writes=[rsmb])
                T.op("vector", lambda e: e.tensor_scalar(out=nacs[:], in0=smb[:, 16:32], scalar1=-1.0, scalar2=None, op0=ALU.mult),
                     reads=[rsmb], writes=[r["nacs"]])
                T.op("vector", lambda e: e.tensor_copy(out=cdb[:], in_=smb[:, 32:48]), reads=[rsmb], writes=[r["cdb"]])
                T.op("vector", lambda e: e.tensor_tensor(out=dte[:], in0=cdb[:], in1=nacs[:], op=ALU.add),
                     reads=[r["cdb"], r["nacs"]], writes=[r["dte"]])
                T.op("scalar", lambda e: e.activation(out=D0[:], in_=nacs[:], func=AF.Exp, scale=-1.0),
                     reads=[r["nacs"]], writes=[r["D0"]])
                T.op("scalar", lambda e: e.activation(out=dte[:], in_=dte[:], func=AF.Exp), reads=[], writes=[r["dte"]])
                T.op("scalar", lambda e: e.activation(out=cdb[:], in_=cdb[:], func=AF.Exp), reads=[r["dte"]], writes=[r["cdb"]])
                T.op("vector", lambda e: e.tensor_tensor(out=ddt[:], in0=dt_t[:], in1=dte[:], op=ALU.mult),
                     reads=[r["dt"], r["dte"]], writes=[r["ddt"]])
                for g in range(2):
                    ev = None
                    for j in range(4):
                        ev = P.op("tensor", lambda e, o=trb[:, 128 * j:128 * (j + 1)], i_=xsT[:, 4 * g + j, cc]:
                                  e.transpose(o, i_, identF[:]), T.deps([r["xsT"], rC], [rtr]) if j == 0 else ())
                    T.mark(ev, [r["xsT"], rC], [rtr])
                    T.op("scalar", lambda e, o=xs_tok[:, 512 * g:512 * (g + 1)], i_=trb[:, :]: e.activation(out=o, in_=i_, func=AF.Copy),
                         reads=[rtr], writes=[r["xs_tok"]])
                for g in range(2):
                    v3 = lambda ap: ap[:, 512 * g:512 * (g + 1)].rearrange("p (h e) -> p h e", e=64)
                    T.op("vector", lambda e, o=v3(xg), i_=v3(xs_tok), b_=bc64(dt_t, 8 * g, 8):
                         e.tensor_tensor(out=o, in0=i_, in1=b_, op=ALU.mult), reads=[r["xs_tok"], r["dt"]], writes=[r["xg"]])
                    T.op("gpsimd", lambda e, o=v3(xgd), i_=v3(xs_tok), b_=bc64(ddt, 8 * g, 8):
                         e.tensor_tensor(out=o, in0=i_, in1=b_, op=ALU.mult), reads=[r["xs_tok"], r["ddt"]], writes=[r["xgd"]])
                trb_b = trb[:].bitcast(BF16)
                ev = None
                for g in range(2):
                    ev = P.op("tensor", lambda e, o=trb_b[:, 128 * g:128 * (g + 1)], i_=BT[:, g, cc]:
                              e.transpose(o, i_, identB[:]), T.deps([r["BT"], rC], [rtr]) if g == 0 else ())
                T.mark(ev, [r["BT"], rC], [rtr])
                T.op("vector", lambda e: e.tensor_copy(out=Btok[:].rearrange("p g n -> p (g n)"), in_=trb_b[:, 0:256]),
                     reads=[rtr], writes=[r["Btok"]])
                for g in range(2):
                    T.mm_group([(smb[:, 128 * (g + 1):128 * (g + 2)], BT[:, g, cc], CT[:, g, cc], True, True)],
                               reads=[r["BT"], r["CT"]], writes=[rsmb])
                    T.op("vector", lambda e, o=Gm[:, g, :], i_=smb[:, 128 * (g + 1):128 * (g + 2)]: e.tensor_copy(out=o, in_=i_),
                         reads=[rsmb], writes=[r["Gm"]])
                for g in range(2):
                    for qd in range(2):
                        qi = quad_ctr % 2; quad_ctr += 1
                        h0 = 8 * g + 4 * qd
                        mms = [(Rb[qi][:, :], identB[:], negm4[:], True, False)]
                        for j in range(4):
                            mms.append((Rb[qi][:, 128 * j:128 * (j + 1)], adt[:, h0 + j:h0 + j + 1].broadcast_to([128, 128]),
                                        tri[:], False, j == 3))
                        T.mm_group(mms, reads=[r["adt"], rC], writes=[rRb[qi]])
                        for j in range(4):
                            T.op("scalar", lambda e, o=Dm[qi][:, 128 * j:128 * (j + 1)], i_=Rb[qi][:, 128 * j:128 * (j + 1)],
                                 b_=nacs[:, h0 + j:h0 + j + 1]: e.activation(out=o, in_=i_, func=AF.Exp, bias=b_),
                                 reads=[rRb[qi], r["nacs"]], writes=[rDm[qi]])
                        T.op("vector", lambda e, o=MT[qi][:].rearrange("p (j l) -> p j l", l=128),
                             i_=Dm[qi][:].rearrange("p (j l) -> p j l", l=128),
                             b_=Gm[:, g, :].unsqueeze(1).broadcast_to([128, 4, 128]):
                             e.tensor_tensor(out=o, in0=i_, in1=b_, op=ALU.mult),
                             reads=[rDm[qi], r["Gm"]], writes=[rMT[qi]])
                        mms = []
                        for j in range(4):
                            hh = 4 * qd + j
                            mms.append((Yb[:, 64 * hh:64 * (hh + 1)], MT[qi][:, 128 * j:128 * (j + 1)],
                                        xg[:, 64 * (h0 + j):64 * (h0 + j + 1)], (qd == 0 and j == 0), False))
                        T.mm_group(mms, reads=[rMT[qi], r["xg"]], writes=[rY])
                    T.mm_group([(osb[:, :], CT[:, g, cc], Hbf[:, g, :], True, True)], reads=[r["CT"], r["Hbf"]], writes=[ros])
                    v3 = lambda ap: ap.rearrange("p (h e) -> p h e", e=64)
                    T.op("vector", lambda e, o=v3(yo[:, :]), i_=v3(osb[:, :]), b_=bc64(D0, 8 * g, 8):
                         e.tensor_tensor(out=o, in0=i_, in1=b_, op=ALU.mult), reads=[ros, r["D0"]], writes=[r["yo"]])
                    T.op("vector", lambda e, o=y[:, 512 * g:512 * (g + 1)], i_=Yb[:, :]:
                         e.tensor_tensor(out=o, in0=i_, in1=yo[:, :], op=ALU.add), reads=[rY, r["yo"]], writes=[r["y"]])
                    T.op("gpsimd", lambda e, o=v3(tsk[:, :]), i_=v3(xs_tok[:, 512 * g:512 * (g + 1)]), b_=bc64(dsk_b, 8 * g, 8):
                         e.tensor_tensor(out=o, in0=i_, in1=b_, op=ALU.mult), reads=[r["xs_tok"], rC], writes=[r["tsk"]])
                    T.op("gpsimd", lambda e, o=y[:, 512 * g:512 * (g + 1)]: e.tensor_tensor(out=o, in0=o, in1=tsk[:, :], op=ALU.add),
                         reads=[r["tsk"]], writes=[r["y"]])
                    T.mm_group([(osb[:, :], Btok[:, g, :], xgd[:, 512 * g:512 * (g + 1)], True, True)],
                               reads=[r["Btok"], r["xgd"]], writes=[ros])
                    T.op("vector", lambda e, o=v3(H[:, g, :]), b_=bc64(cdb, 8 * g, 8):
                         e.tensor_tensor(out=o, in0=o, in1=b_, op=ALU.mult), reads=[r["cdb"]], writes=[r["H"]])
                    T.op("vector", lambda e, o=H[:, g, :]: e.tensor_tensor(out=o, in0=osb[:, :], in1=o, op=ALU.add),
                         reads=[ros], writes=[r["H"]])
                    T.op("gpsimd", lambda e, o=Hbf[:, g, :], i_=H[:, g, :]: e.tensor_copy(out=o, in_=i_),
                         reads=[r["H"]], writes=[r["Hbf"]])
                T.op("vector", lambda e: e.tensor_tensor(out=y[:], in0=y[:], in1=zs[:], op=ALU.mult),
                     reads=[r["zs"]], writes=[r["y"]])
                T.op("vector", lambda e: e.memset(ssq[:], 0.0), reads=[], writes=[r["ssq"]])
                T.op("scalar", lambda e: e.activation(out=junk[:], in_=y[:], func=AF.Square, accum_out=ssq[:]),
                     reads=[r["y"]], writes=[r["junk"], r["ssq"]])
                T.op("scalar", lambda e: e.activation(out=rstd[:], in_=ssq[:], func=AF.Ln, bias=EPS, scale=1.0 / 1024),
                     reads=[r["ssq"]], writes=[r["rstd"]])
                T.op("scalar", lambda e: e.activation(out=rstd[:], in_=rstd[:], func=AF.Exp, scale=-0.5),
                     reads=[], writes=[r["rstd"]])
                T.op("vector", lambda e: e.scalar_tensor_tensor(out=mixn[:], in0=y[:], scalar=rstd[:, 0:1], in1=gs_b[:],
                                                                op0=ALU.mult, op1=ALU.mult),
                     reads=[r["y"], r["rstd"], rC], writes=[r["mixn"]])
                for half in range(2):
                    ev = None
                    for j in range(4):
                        ft = 4 * half + j
                        ev = P.op("tensor", lambda e, o=trb_b[:, 128 * j:128 * (j + 1)], i_=mixn[:, 128 * ft:128 * (ft + 1)]:
                                  e.transpose(o, i_, identB[:]), T.deps([r["mixn"], rC], [rtr]) if j == 0 else ())
                    T.mark(ev, [r["mixn"], rC], [rtr])
                    T.op("vector" if half == 0 else "scalar",
                         (lambda e, o=mixT[:, 4 * half:4 * half + 4, cc], i_=trb_b[:, 0:512].rearrange("p (j t) -> p j t", t=128):
                          e.tensor_copy(out=o, in_=i_)) if half == 0 else
                         (lambda e, o=mixT[:, 4 * half:4 * half + 4, cc], i_=trb_b[:, 0:512].rearrange("p (j t) -> p j t", t=128):
                          e.activation(out=o, in_=i_, func=AF.Copy)),
                         reads=[rtr], writes=[rmixT[half]])
            T.dma("gpsimd", U[1024:2048, T0:T0 + 512].rearrange("(f p) t -> p f t", p=128), mixT[:, :, :], "st_uB",
                  reads=[rmixT[0], rmixT[1]])
        barrier(P)


_NC_CACHE = {}


def _host_inputs(xb, p, c):
    return {
        "xT": np.ascontiguousarray(xb.T), "x": np.ascontiguousarray(xb),
        "w_in": p["w_in"], "w_out": p["w_out"],
        "aug": c["aug"], "mask4": c["mask4"], "tri": c["tri"], "ident": c["ident"], "identb": c["identb"], "negm4": c["negm4"],
        "conv_wT": np.ascontiguousarray(p["conv_w"].T.reshape(12, 128, 4).transpose(1, 0, 2)),
        "conv_b2": np.ascontiguousarray(p["conv_b"].reshape(12, 128).T),
        "dt_bias": p["dt_bias"], "a_log": p["a_log"], "d_skip": p["d_skip"], "ssm_norm_g": p["ssm_norm_g"],
        "att_norm_g2": np.ascontiguousarray(p["att_norm_g"].reshape(8, 128).T),
        "ssm_norm_g2": np.ascontiguousarray(p["ssm_norm_g"].reshape(8, 128).T),
        "ln_g": p["ln_g"], "ln_b": p["ln_b"],
    }


def kernel(x, w_in, conv_w, conv_b, dt_bias, a_log, d_skip, att_norm_g, ssm_norm_g, w_out, ln_g, ln_b):
    x = np.asarray(x, np.float32)
    p = {"w_in": w_in, "conv_w": conv_w, "conv_b": conv_b, "dt_bias": dt_bias, "a_log": a_log, "d_skip": d_skip,
         "att_norm_g": att_norm_g, "ssm_norm_g": ssm_norm_g, "w_out": w_out, "ln_g": ln_g, "ln_b": ln_b}
    p = {k: np.ascontiguousarray(np.asarray(v, np.float32)[0]) for k, v in p.items()}
    c = make_constants()
    n = x.shape[0]
    if "nc" not in _NC_CACHE:
        _NC_CACHE["nc"] = build_program()
    nc = _NC_CACHE["nc"]
    in_maps = [_host_inputs(x[b], p, c) for b in range(n)]
    res = run_bass_kernel_spmd(nc, in_maps, core_ids=list(range(n)))
    return np.stack([np.asarray(r["out"], np.float32) for r in res.results], axis=0)


def phase_B2(nc, P, banks, XT, w_in_v, dr, U, n_sc=8):
    T = Trk(P)
    with contextlib.ExitStack() as st:
        def sb(name, shape, dt, stack=None):
            return (stack or st).enter_context(nc.sbuf_tensor("B_" + name, shape, dt))

        rXT = Res()
        Wz = sb("Wz", [128, 8, 1024], BF16)
        Wx = sb("Wx", [128, 8, 1536], BF16)
        Wdt = sb("Wdt", [128, 8, 16], BF16)
        tri = sb("tri", [128, 128], F32)
        onesF = sb("onesF", [128, 128], F32)
        identF = sb("identF", [128, 128], F32)
        identB = sb("identB", [128, 128], BF16)
        negm4 = sb("negm4", [128, 512], BF16)
        cw = sb("cw", [128, 12, 4], F32)
        cb = sb("cb", [128, 12], F32)
        dtb_b = sb("dtb_b", [128, 16], F32)
        a_b = sb("a_b", [128, 16], F32)
        dsk_b = sb("dsk_b", [128, 16], F32)
        hal = sb("hal", [128, 12, 3], F32)
        rW = Res(); rC = Res()

        with contextlib.ExitStack() as stw:
            wsts = [sb(f"wstB{i}", [128, 8, 512], F32, stw) for i in range(3)]
            rwsts = [Res(), Res(), Res()]
            pieces = [(OFF_ZS, 0, 512, Wz), (OFF_ZS + 512, 512, 512, Wz),
                      (OFF_XBC, 0, 512, Wx), (OFF_XBC + 512, 512, 512, Wx), (OFF_XBC + 1024, 1024, 512, Wx),
                      (OFF_DT, 0, 16, Wdt)]
            for i, (off, dst0, n, Wt) in enumerate(pieces):
                wst, rwst = wsts[i % 3], rwsts[i % 3]
                T.dma("sync" if i % 2 == 0 else "gpsimd", wst[:, :, 0:n], w_in_v[:, :, off:off + n], f"ld_wB{i % 3}", writes=[rwst])
                T.op("vector", lambda e, o=Wt[:, 0:4, dst0:dst0 + n], i_=wst[:, 0:4, 0:n]: e.tensor_copy(out=o, in_=i_),
                     reads=[rwst], writes=[rW])
                T.op("scalar", lambda e, o=Wt[:, 4:8, dst0:dst0 + n], i_=wst[:, 4:8, 0:n]: e.activation(out=o, in_=i_, func=AF.Copy),
                     reads=[rwst], writes=[])
            T.dma("sync", tri[:], dr["tri"][:, :], "ld_c1", writes=[rC])
            T.dma("sync", identF[:], dr["ident"][:, :], "ld_c2", writes=[rC])
            T.dma("sync", negm4[:], dr["negm4"][:, :], "ld_c3", writes=[rC])
            T.dma("sync", cw[:], dr["conv_wT"][:, :, :], "ld_c4", writes=[rC])
            T.dma("sync", cb[:], dr["conv_b2"][:, :], "ld_c5", writes=[rC])
            T.dma("sync", dtb_b[:], dr["dt_bias"].partition_broadcast(128), "ld_c6", writes=[rC])
            T.dma("sync", a_b[:], dr["a_log"].partition_broadcast(128), "ld_c7", writes=[rC])
            T.dma("sync", dsk_b[:], dr["d_skip"].partition_broadcast(128), "ld_c8", writes=[rC])
            barrier(P)
            I_memset(P, "vector", onesF[:], 1.0)
            I_memset(P, "vector", hal[:], 0.0)
            I_copy(P, "vector", identB[:], identF[:])
            I_act(P, a_b[:], a_b[:], AF.Exp)
            barrier(P)
            I_ts(P, "vector", a_b[:], a_b[:], -1.0, None, ALU.mult)
            barrier(P)

        xb = [sb(f"xb{i}", [128, 515], F32) for i in range(2)]
        cvb = [sb(f"cvb{i}", [128, 512], F32) for i in range(2)]
        xtmp = [sb(f"xtmp{i}", [128, 512], F32) for i in range(2)]
        BT = sb("BT", [128, 2, 512], BF16)
        CT = sb("CT", [128, 2, 512], BF16)
        xs_tok = sb("xs_tok", [128, 4, 1024], F32)
        zs = sb("zs", [128, 4, 1024], F32)
        Btok = sb("Btok", [128, 4, 2, 128], BF16)
        Gm = sb("Gm", [128, 2, 4, 128], BF16)
        sm = {k: sb(k, [128, 64], F32) for k in ("dtx", "dt4", "adt4", "nacs4", "D04", "dte4", "ddt4", "cdb4")}
        xg = [sb(f"xg{i}", [128, 1024], BF16) for i in range(2)]
        xgd = [sb(f"xgd{i}", [128, 1024], BF16) for i in range(2)]
        Dm = [sb(f"Dm{i}", [128, 512], BF16) for i in range(2)]
        MT = [sb(f"MT{i}", [128, 512], BF16) for i in range(2)]
        H = sb("H", [128, 2, 512], F32)
        Hbf = sb("Hbf", [128, 2, 512], BF16)
        yo = [sb(f"yo{i}", [128, 512], F32) for i in range(2)]
        tsk = [sb(f"tsk{i}", [128, 512], F32) for i in range(4)]
        y = [sb(f"y{i}", [128, 1024], F32) for i in range(1)]
        ssq = [sb(f"ssq{i}", [128, 1], F32) for i in range(2)]
        rstd = [sb(f"rstd{i}", [128, 1], F32) for i in range(2)]
        mixn = [sb(f"mixn{i}", [128, 1024], BF16) for i in range(2)]
        mixT = [sb(f"mixT{i}", [128, 8, 512], BF16) for i in range(1)]

        R_ = lambda n: [Res() for _ in range(n)]
        rxb, rxbh, rcv, rxtmp = R_(2), R_(2), R_(2), R_(2)
        rhal = R_(12)
        rBT, rCT, rxs, rzs, rBtok, rGm = Res(), Res(), Res(), Res(), Res(), Res()
        rsm = {k: Res() for k in sm}
        rxg, rxgd, rDm, rMT, ryo, rtsk, ry, rssq, rrstd, rmixn = R_(2), R_(2), R_(2), R_(2), R_(2), R_(4), R_(1), R_(2), R_(2), R_(2)
        rmixT = [R_(2)]
        rH, rHbf = R_(2), R_(2)
        rbank = R_(8)
        pb0, pb1, smb, trb, Rb0, Rb1, Yb, osb = banks
        B_PB0, B_PB1, B_SM, B_TR, B_R0, B_R1, B_Y, B_OS = range(8)
        trb_b = trb[:].bitcast(BF16)

        I_memset(P, "vector", H[:], 0.0)
        I_memset(P, "vector", Hbf[:], 0.0)
        barrier(P)

        def v3(ap):
            return ap.rearrange("p (h e) -> p h e", e=64)

        pbi = 0
        for sc in range(n_sc):
            T0 = 512 * sc
            def stage_A(ft):
                nonlocal pbi
                b = pbi; pbi ^= 1
                pbk = banks[b]
                xbi = ft % 2
                T.mm_group([(pbk[:, :], Wx[:, kc, ft * 128:(ft + 1) * 128], XT[:, kc, T0:T0 + 512], kc == 0, kc == 7)
                            for kc in range(8)], reads=[rW, rXT], writes=[rbank[b]])
                T.op("scalar", lambda e, o=xb[xbi][:, 3:515], i_=pbk[:, :]: e.activation(out=o, in_=i_, func=AF.Copy),
                     reads=[rbank[b]], writes=[rxb[xbi]])
                cv = cvb[xbi]
                T.op("scalar", lambda e, o=cv[:, :], i_=pbk[:, :], s_=cw[:, ft, 3:4]: e.activation(out=o, in_=i_, func=AF.Copy, scale=s_),
                     reads=[rbank[b], rC], writes=[rcv[xbi]])
                T.op("gpsimd", lambda e, o=xb[xbi][:, 0:3], i_=hal[:, ft, :]: e.tensor_copy(out=o, in_=i_),
                     reads=[rhal[ft]], writes=[rxbh[xbi]])
                T.op("gpsimd", lambda e, o=hal[:, ft, :], i_=xb[xbi][:, 512:515]: e.tensor_copy(out=o, in_=i_),
                     reads=[rxb[xbi]], writes=[rhal[ft]])

            def stage_B(ft):
                xbi = ft % 2
                cv = cvb[xbi]
                for j in range(3):
                    T.op("vector", lambda e, o=cv[:, :], i_=xb[xbi][:, j:j + 512], s_=cw[:, ft, j:j + 1]:
                         e.scalar_tensor_tensor(out=o, in0=i_, scalar=s_, in1=o, op0=ALU.mult, op1=ALU.add),
                         reads=[rxb[xbi], rxbh[xbi]], writes=[rcv[xbi]])
                if ft < 8:
                    xi = ft % 2
                    T.op("scalar", lambda e, o=xtmp[xi][:, :], i_=cv[:, :], b_=cb[:, ft:ft + 1]:
                         e.activation(out=o, in_=i_, func=AF.Silu, bias=b_), reads=[rcv[xbi], rC], writes=[rxtmp[xi]])
                elif ft < 10:
                    T.op("scalar", lambda e, o=BT[:, ft - 8, :], i_=cv[:, :], b_=cb[:, ft:ft + 1]:
                         e.activation(out=o, in_=i_, func=AF.Silu, bias=b_), reads=[rcv[xbi], rC], writes=[rBT])
                else:
                    T.op("scalar", lambda e, o=CT[:, ft - 10, :], i_=cv[:, :], b_=cb[:, ft:ft + 1]:
                         e.activation(out=o, in_=i_, func=AF.Silu, bias=b_), reads=[rcv[xbi], rC], writes=[rCT])
            def stage_T(ft):
                if ft < 0 or ft >= 8:
                    return
                xi = ft % 2
                tbk = B_TR if ft % 2 == 0 else B_Y
                ev = None
                for ci in range(4):
                    ev = P.op("tensor", lambda e, o=banks[tbk][:, 128 * ci:128 * (ci + 1)], i_=xtmp[xi][:, 128 * ci:128 * (ci + 1)]:
                              e.transpose(o, i_, identF[:]), T.deps([rxtmp[xi], rC], [rbank[tbk]]) if ci == 0 else ())
                T.mark(ev, [rxtmp[xi], rC], [rbank[tbk]])

            def stage_C(ft):
                if ft < 0 or ft >= 8:
                    return
                tbk = B_TR if ft % 2 == 0 else B_Y
                T.op("scalar", lambda e, o=xs_tok[:, :, ft * 128:(ft + 1) * 128], i_=banks[tbk][:, :].rearrange("p (c f) -> p c f", f=128):
                     e.activation(out=o, in_=i_, func=AF.Copy), reads=[rbank[tbk]], writes=[rxs])

            def z_group(zi):
                nonlocal pbi
                ci, half = divmod(zi, 2)
                t0 = T0 + 128 * ci
                b = pbi; pbi ^= 1
                T.mm_group([(banks[b][:, :], XT[:, kc, t0:t0 + 128], Wz[:, kc, 512 * half:512 * (half + 1)], kc == 0, kc == 7)
                            for kc in range(8)], reads=[rW, rXT], writes=[rbank[b]])
                T.op("scalar", lambda e, o=zs[:, ci, 512 * half:512 * (half + 1)], i_=banks[b][:, :]:
                     e.activation(out=o, in_=i_, func=AF.Silu), reads=[rbank[b]], writes=[rzs])

            stage_A(0)
            for ft in range(13):
                if ft + 1 < 12:
                    stage_A(ft + 1)
                stage_T(ft - 1)
                if ft < 12:
                    stage_B(ft)
                stage_C(ft - 1)
                if 2 <= ft < 10:
                    z_group(ft - 2)
            ev = None
            for ci in range(4):
                for g in range(2):
                    j = 2 * ci + g
                    ev = P.op("tensor", lambda e, o=trb_b[:, 128 * j:128 * (j + 1)], i_=BT[:, g, 128 * ci:128 * (ci + 1)]:
                              e.transpose(o, i_, identB[:]), T.deps([rBT, rC], [rbank[B_TR]]) if j == 0 else ())
            T.mark(ev, [rBT, rC], [rbank[B_TR]])
            T.op("vector", lambda e: e.tensor_copy(out=Btok[:].rearrange("p c g n -> p (c g n)"), in_=trb_b[:, 0:1024]),
                 reads=[rbank[B_TR]], writes=[rBtok])
            for g in range(2):
                bk = B_R0 + g
                T.mm_group([(banks[bk][:, 128 * ci:128 * (ci + 1)], BT[:, g, 128 * ci:128 * (ci + 1)], CT[:, g, 128 * ci:128 * (ci + 1)],
                             True, True) for ci in range(4)], reads=[rBT, rCT], writes=[rbank[bk]])
                T.op("vector", lambda e, o=Gm[:, g, :, :].rearrange("p c l -> p (c l)"), i_=banks[bk][:, :]: e.tensor_copy(out=o, in_=i_),
                     reads=[rbank[bk]], writes=[rGm])
            mms = []
            for ci in range(4):
                t0 = T0 + 128 * ci
                for kc in range(8):
                    mms.append((smb[:, 16 * ci:16 * (ci + 1)], XT[:, kc, t0:t0 + 128], Wdt[:, kc, :], kc == 0 and ci == 0, kc == 7))
            T.mm_group(mms, reads=[rW, rXT], writes=[rbank[B_SM]])
            b16 = lambda ap: ap[:, :].unsqueeze(1).broadcast_to([128, 4, 16])
            c4 = lambda ap: ap[:, :].rearrange("p (c h) -> p c h", h=16)
            T.op("vector", lambda e: e.tensor_tensor(out=c4(sm["dtx"]), in0=c4(smb[:, 0:64]), in1=b16(dtb_b), op=ALU.add),
                 reads=[rbank[B_SM], rC], writes=[rsm["dtx"]])
            T.op("scalar", lambda e: e.activation(out=sm["dtx"][:], in_=sm["dtx"][:], func=AF.Exp), reads=[], writes=[rsm["dtx"]])
            T.op("scalar", lambda e: e.activation(out=sm["dt4"][:], in_=sm["dtx"][:], func=AF.Ln, bias=1.0),
                 reads=[rsm["dtx"]], writes=[rsm["dt4"]])
            T.op("vector", lambda e: e.tensor_tensor(out=c4(sm["adt4"]), in0=c4(sm["dt4"]), in1=b16(a_b), op=ALU.mult),
                 reads=[rsm["dt4"], rC], writes=[rsm["adt4"]])
            mms = []
            for ci in range(4):
                mms.append((smb[:, 64 + 16 * ci:64 + 16 * (ci + 1)], tri[:], sm["adt4"][:, 16 * ci:16 * (ci + 1)], False, True))
                mms.append((smb[:, 128 + 16 * ci:128 + 16 * (ci + 1)], onesF[:], sm["adt4"][:, 16 * ci:16 * (ci + 1)], False, True))
            T.mm_group(mms, reads=[rsm["adt4"], rC], writes=[rbank[B_SM]])
            T.op("vector", lambda e: e.tensor_scalar(out=sm["nacs4"][:], in0=smb[:, 64:128], scalar1=-1.0, scalar2=None, op0=ALU.mult),
                 reads=[rbank[B_SM]], writes=[rsm["nacs4"]])
            T.op("vector", lambda e: e.tensor_copy(out=sm["cdb4"][:], in_=smb[:, 128:192]), reads=[rbank[B_SM]], writes=[rsm["cdb4"]])
            T.op("vector", lambda e: e.tensor_tensor(out=sm["dte4"][:], in0=sm["cdb4"][:], in1=sm["nacs4"][:], op=ALU.add),
                 reads=[rsm["cdb4"], rsm["nacs4"]], writes=[rsm["dte4"]])
            T.op("scalar", lambda e: e.activation(out=sm["D04"][:], in_=sm["nacs4"][:], func=AF.Exp, scale=-1.0),
                 reads=[rsm["nacs4"]], writes=[rsm["D04"]])
            T.op("scalar", lambda e: e.activation(out=sm["dte4"][:], in_=sm["dte4"][:], func=AF.Exp), reads=[], writes=[rsm["dte4"]])
            T.op("scalar", lambda e: e.activation(out=sm["cdb4"][:], in_=sm["cdb4"][:], func=AF.Exp), reads=[rsm["dte4"]], writes=[rsm["cdb4"]])
            T.op("vector", lambda e: e.tensor_tensor(out=sm["ddt4"][:], in0=sm["dt4"][:], in1=sm["dte4"][:], op=ALU.mult),
                 reads=[rsm["dt4"], rsm["dte4"]], writes=[rsm["ddt4"]])

            def chunk_fns(ci):
                k = ci % 2
                cc = slice(128 * ci, 128 * ci + 128)
                hs = lambda ap, h0, n: ap[:, 16 * ci + h0:16 * ci + h0 + n]
                bch = lambda ap, h0, n: hs(ap, h0, n).unsqueeze(2).broadcast_to([128, n, 64])
                def prologue(cj):
                    kj = cj % 2
                    for g in range(2):
                        gs = slice(512 * g, 512 * (g + 1))
                        bcj = lambda ap, h0, n: ap[:, 16 * cj + h0:16 * cj + h0 + n].unsqueeze(2).broadcast_to([128, n, 64])
                        T.op("gpsimd", lambda e, o=v3(xg[kj][:, gs]), i_=v3(xs_tok[:, cj, gs]), b_=bcj(sm["dt4"], 8 * g, 8):
                             e.tensor_tensor(out=o, in0=i_, in1=b_, op=ALU.mult), reads=[rxs, rsm["dt4"]], writes=[rxg[kj]])
                        T.op("gpsimd", lambda e, o=v3(xgd[kj][:, gs]), i_=v3(xs_tok[:, cj, gs]), b_=bcj(sm["ddt4"], 8 * g, 8):
                             e.tensor_tensor(out=o, in0=i_, in1=b_, op=ALU.mult), reads=[rxs, rsm["ddt4"]], writes=[rxgd[kj]])
                        T.op("gpsimd", lambda e, o=v3(tsk[2 * kj + g][:, :]), i_=v3(xs_tok[:, cj, gs]), b_=bc64(dsk_b, 8 * g, 8):
                             e.tensor_tensor(out=o, in0=i_, in1=b_, op=ALU.mult), reads=[rxs, rC], writes=[rtsk[2 * kj + g]])

                def emit_R(q):
                    g, qd = divmod(q, 2)
                    h0 = 8 * g + 4 * qd
                    bk = B_R0 + (q % 2)
                    mms = [(banks[bk][:, :], identB[:], negm4[:], True, False)]
                    for j in range(4):
                        mms.append((banks[bk][:, 128 * j:128 * (j + 1)],
                                    hs(sm["adt4"], h0 + j, 1).broadcast_to([128, 128]), tri[:], False, j == 3))
                    T.mm_group(mms, reads=[rsm["adt4"], rC], writes=[rbank[bk]])

                def emit_exp(q):
                    g, qd = divmod(q, 2)
                    h0 = 8 * g + 4 * qd
                    bk = B_R0 + (q % 2)
                    for j in range(4):
                        T.op("scalar", lambda e, o=Dm[q % 2][:, 128 * j:128 * (j + 1)], i_=banks[bk][:, 128 * j:128 * (j + 1)],
                             b_=hs(sm["nacs4"], h0 + j, 1): e.activation(out=o, in_=i_, func=AF.Exp, bias=b_),
                             reads=[rbank[bk], rsm["nacs4"]], writes=[rDm[q % 2]])

                def emit_MT(q):
                    g, qd = divmod(q, 2)
                    T.op("vector", lambda e, o=MT[q % 2][:].rearrange("p (j l) -> p j l", l=128),
                         i_=Dm[q % 2][:].rearrange("p (j l) -> p j l", l=128),
                         b_=Gm[:, g, ci, :].unsqueeze(1).broadcast_to([128, 4, 128]):
                         e.tensor_tensor(out=o, in0=i_, in1=b_, op=ALU.mult), reads=[rDm[q % 2], rGm], writes=[rMT[q % 2]])

                def emit_ydiag(q):
                    g, qd = divmod(q, 2)
                    h0 = 8 * g + 4 * qd
                    ybk = B_Y if g == 0 else B_PB0
                    mms = []
                    for j in range(4):
                        hh = 4 * qd + j
                        mms.append((banks[ybk][:, 64 * hh:64 * (hh + 1)], MT[q % 2][:, 128 * j:128 * (j + 1)],
                                    xg[k][:, 64 * (h0 + j):64 * (h0 + j + 1)], (qd == 0 and j == 0), False))
                    T.mm_group(mms, reads=[rMT[q % 2], rxg[k]], writes=[rbank[ybk]])

                def emit_yoff(g):
                    obk = B_OS if g == 0 else B_PB1
                    T.mm_group([(banks[obk][:, :], CT[:, g, cc], Hbf[:, g, :], True, True)], reads=[rCT, rHbf[g]], writes=[rbank[obk]])

                def emit_comb(g):
                    gs = slice(512 * g, 512 * (g + 1))
                    obk = B_OS if g == 0 else B_PB1
                    ybk = B_Y if g == 0 else B_PB0
                    T.op("vector", lambda e, o=v3(yo[g][:, :]), i_=v3(banks[obk][:, :]), b_=bch(sm["D04"], 8 * g, 8):
                         e.tensor_tensor(out=o, in0=i_, in1=b_, op=ALU.mult), reads=[rbank[obk], rsm["D04"]], writes=[ryo[g]])
                    T.op("vector", lambda e, o=y[0][:, gs], i_=banks[ybk][:, :]: e.tensor_tensor(out=o, in0=i_, in1=yo[g][:, :], op=ALU.add),
                         reads=[rbank[ybk], ryo[g]], writes=[ry[0]])
                    T.op("gpsimd", lambda e, o=y[0][:, gs], t_=tsk[2 * k + g][:, :]: e.tensor_tensor(out=o, in0=o, in1=t_, op=ALU.add),
                         reads=[rtsk[2 * k + g]], writes=[ry[0]])

                def emit_state(g):
                    gs = slice(512 * g, 512 * (g + 1))
                    obk = B_OS if g == 0 else B_PB1
                    T.mm_group([(banks[obk][:, :], Btok[:, ci, g, :], xgd[k][:, gs], True, True)],
                               reads=[rBtok, rxgd[k]], writes=[rbank[obk]])
                    T.op("vector", lambda e, o=v3(H[:, g, :]), b_=bch(sm["cdb4"], 8 * g, 8):
                         e.tensor_tensor(out=o, in0=o, in1=b_, op=ALU.mult), reads=[rsm["cdb4"]], writes=[rH[g]])
                    T.op("vector", lambda e, o=H[:, g, :], i_=banks[obk][:, :]: e.tensor_tensor(out=o, in0=i_, in1=o, op=ALU.add),
                         reads=[rbank[obk]], writes=[rH[g]])
                    T.op("scalar", lambda e, o=Hbf[:, g, :], i_=H[:, g, :]: e.activation(out=o, in_=i_, func=AF.Copy),
                         reads=[rH[g]], writes=[rHbf[g]])

                def early():
                    emit_R(0); emit_R(1)
                    emit_yoff(0); emit_yoff(1)
                    emit_exp(0); emit_exp(1)
                    emit_MT(0); emit_MT(1)
                    emit_R(2); emit_R(3)
                    emit_ydiag(0); emit_ydiag(1)
                    emit_exp(2); emit_exp(3)
                    emit_MT(2); emit_MT(3)
                    emit_ydiag(2); emit_ydiag(3)

                def mid():
                    if ci + 1 < 4:
                        prologue(ci + 1)
                    emit_comb(0)
                    emit_state(0)
                    emit_comb(1)
                    emit_state(1)

                def late():
                    yk, ssk, rsk, mxk = y[0], ssq[k], rstd[k], mixn[k]
                    T.op("vector", lambda e, yk=yk, z_=zs[:, ci, :]: e.tensor_tensor(out=yk[:], in0=yk[:], in1=z_, op=ALU.mult),
                         reads=[rzs], writes=[ry[0]])
                    T.op("vector", lambda e, ssk=ssk: e.memset(ssk[:], 0.0), reads=[], writes=[rssq[k]])
                    T.op("scalar", lambda e, yk=yk, ssk=ssk, mxk=mxk: e.activation(out=mxk[:], in_=yk[:], func=AF.Square, accum_out=ssk[:]),
                         reads=[ry[0]], writes=[rmixn[k], rssq[k]])
                    T.op("scalar", lambda e, ssk=ssk, rsk=rsk: e.activation(out=rsk[:], in_=ssk[:], func=AF.Ln, bias=EPS, scale=1.0 / 1024),
                         reads=[rssq[k]], writes=[rrstd[k]])
                    T.op("scalar", lambda e, rsk=rsk: e.activation(out=rsk[:], in_=rsk[:], func=AF.Exp, scale=-0.5),
                         reads=[], writes=[rrstd[k]])
                    T.op("vector", lambda e, yk=yk, rsk=rsk, mxk=mxk: e.tensor_scalar(out=mxk[:], in0=yk[:], scalar1=rsk[:, 0:1], scalar2=None, op0=ALU.mult),
                         reads=[ry[0], rrstd[k]], writes=[rmixn[k]])
                    mt = mixT[0]
                    for half in range(2):
                        ev = None
                        for j in range(4):
                            ft = 4 * half + j
                            ev = P.op("tensor", lambda e, o=trb_b[:, 128 * j:128 * (j + 1)], i_=mixn[k][:, 128 * ft:128 * (ft + 1)]:
                                      e.transpose(o, i_, identB[:]), T.deps([rmixn[k], rC], [rbank[B_TR]]) if j == 0 else ())
                        T.mark(ev, [rmixn[k], rC], [rbank[B_TR]])
                        src = trb_b[:, 0:512].rearrange("p (j t) -> p j t", t=128)
                        if half == 0:
                            T.op("vector", lambda e, o=mt[:, 0:4, cc], i_=src: e.tensor_copy(out=o, in_=i_),
                                 reads=[rbank[B_TR]], writes=[rmixT[0][0]])
                        else:
                            T.op("scalar", lambda e, o=mt[:, 4:8, cc], i_=src: e.activation(out=o, in_=i_, func=AF.Copy),
                                 reads=[rbank[B_TR]], writes=[rmixT[0][1]])
                return prologue, early, mid, late

            fns = [chunk_fns(ci) for ci in range(4)]
            fns[0][0](0)
            fns[0][1]()
            for ci in range(4):
                fns[ci][2]()
                if ci + 1 < 4:
                    fns[ci + 1][1]()
                fns[ci][3]()
            T.dma("gpsimd", U[1024:2048, T0:T0 + 512].rearrange("(f p) t -> p f t", p=128), mixT[0][:, :, :], "st_uB",
                  reads=rmixT[0])
        barrier(P)
```

```python
import contextlib
import os
_SKIP = set(os.environ.get('KSKIP', '').split(','))
import numpy as np
import ml_dtypes
import concourse.bass as bass
import concourse.mybir as mybir
from concourse.bass_utils import run_bass_kernel_spmd

F32 = mybir.dt.float32
BF16 = mybir.dt.bfloat16
AF = mybir.ActivationFunctionType
ALU = mybir.AluOpType
AX = mybir.AxisListType

S = 4096
D = 1024
NH = 16
HD = 64
DIL = (1, 4, 16)
D_IN = 6672
OFF_Q, OFF_K, OFF_V, OFF_ZA, OFF_ZS, OFF_XBC, OFF_DT = 0, 1024, 2048, 3072, 4096, 5120, 6656
EPS = 1e-5


class Ev:
    __slots__ = ("eng", "idx", "sem", "val")

    def __init__(self, eng, idx, sem=None, val=None):
        self.eng, self.idx, self.sem, self.val = eng, idx, sem, val


class Prog:
    ENGS = ("sync", "scalar", "vector", "gpsimd", "tensor")

    def __init__(self, nc):
        self.nc = nc
        self.q = {e: [] for e in self.ENGS}
        self.dma_cnt = {}

    def op(self, eng, fn, deps=()):
        lst = self.q[eng]
        ev = Ev(eng, len(lst))
        lst.append([fn, [d for d in deps if d is not None], ev, False])
        return ev

    def dma(self, eng, out, in_, slot, deps=()):
        self.dma_cnt[slot] = self.dma_cnt.get(slot, 0) + 16
        ev = Ev(eng, len(self.q[eng]), sem=slot, val=self.dma_cnt[slot])
        self.q[eng].append([lambda e, o=out, i=in_: e.dma_start(out=o, in_=i),
                            [d for d in deps if d is not None], ev, True])
        return ev

    def emit(self, final_waits):
        nc = self.nc
        ref = {e: set() for e in self.ENGS}
        for e in self.ENGS:
            for fn, deps, ev, is_dma in self.q[e]:
                for d in deps:
                    if d.sem is None or d.sem.startswith("e_"):
                        ref[d.eng].add(d.idx)
        for d in final_waits:
            if d.sem is None:
                ref[d.eng].add(d.idx)
        for e in self.ENGS:
            c = 0
            for i, item in enumerate(self.q[e]):
                if item[3]:
                    continue
                if i in ref[e]:
                    c += 1
                    item[2].sem = "e_" + e
                    item[2].val = c
            assert c < 60000, (e, c)
        for s, v in self.dma_cnt.items():
            assert v < 60000, (s, v)
        names = ["e_" + e for e in self.ENGS] + sorted(self.dma_cnt)
        with contextlib.ExitStack() as st:
            sems = {n: st.enter_context(nc.semaphore(n)) for n in names}
            block = st.enter_context(nc.Block())
            for e in self.ENGS:
                items = self.q[e]
                fw = final_waits if e == "sync" else ()

                def body(eng, items=items, fw=fw):
                    seen = {}
                    for fn, deps, ev, is_dma in items:
                        need = {}
                        for d in deps:
                            assert d.sem is not None
                            if need.get(d.sem, 0) < d.val:
                                need[d.sem] = d.val
                        for sname, v in need.items():
                            if seen.get(sname, 0) < v:
                                eng.wait_ge(sems[sname], v)
                                seen[sname] = v
                        ins = fn(eng)
                        if is_dma:
                            ins.then_inc(sems[ev.sem], 16)
                        elif ev.sem is not None:
                            ins.then_inc(sems[ev.sem], 1)
                    for d in fw:
                        if seen.get(d.sem, 0) < d.val:
                            eng.wait_ge(sems[d.sem], d.val)
                            seen[d.sem] = d.val

                getattr(block, e)(body)


def latest(*evs):
    return [e for e in evs if e is not None]


def _bf16(a):
    return np.asarray(a, np.float32).astype(ml_dtypes.bfloat16)


def make_constants():
    c = {}
    slopes = 2.0 ** (-8.0 * np.arange(1, NH + 1) / NH)
    t = np.arange(S)
    hi_pos = (t >> 7).astype(np.float32)
    lo_pos = (t & 127).astype(np.float32)
    aug = np.zeros((NH, 2, 12, S), np.float32)
    for h in range(NH):
        cc = np.float64(slopes[h])
        c1 = np.float64(_bf16(cc).astype(np.float64))
        c2 = np.float64(_bf16(cc - c1).astype(np.float64))
        c3 = np.float64(_bf16(cc - c1 - c2).astype(np.float64))
        for j, cj in enumerate((c1, c2, c3)):
            aug[h, 0, j] = 128.0 * cj
            aug[h, 0, 3 + j] = cj
            aug[h, 0, 6 + j] = hi_pos
            aug[h, 0, 9 + j] = lo_pos
            aug[h, 1, j] = hi_pos
            aug[h, 1, 3 + j] = lo_pos
            aug[h, 1, 6 + j] = -128.0 * cj
            aug[h, 1, 9 + j] = -cj
    c["aug"] = _bf16(aug)
    ki = np.arange(128)[:, None]
    qi = np.arange(128)[None, :]
    mprev = np.where(ki >= qi, 0.0, -30000.0).astype(np.float32)
    mcur = np.where(ki <= qi, 0.0, -30000.0).astype(np.float32)
    c["mask4"] = np.concatenate([mprev, mcur, mprev, mcur], axis=1).astype(np.float32)
    c["tri"] = np.triu(np.ones((128, 128), np.float32))
    c["ident"] = np.eye(128, dtype=np.float32)
    c["identb"] = _bf16(np.eye(128, dtype=np.float32))
    si = np.arange(128)[:, None]
    li = np.arange(128)[None, :]
    c["negm4"] = _bf16(np.tile(np.where(si > li, -30000.0, 0.0), (1, 4)))
    return c


def I_act(P, out, in_, func, deps=(), bias=None, scale=None, accum_out=None, eng="scalar"):
    kw = {}
    if bias is not None:
        kw["bias"] = bias
    if scale is not None:
        kw["scale"] = scale
    if accum_out is not None:
        kw["accum_out"] = accum_out
    return P.op(eng, lambda e: e.activation(out=out, in_=in_, func=func, **kw), deps)


def I_copy(P, eng, out, in_, deps=()):
    if eng == "scalar":
        return P.op(eng, lambda e: e.activation(out=out, in_=in_, func=AF.Copy), deps)
    return P.op(eng, lambda e: e.tensor_copy(out=out, in_=in_), deps)


def I_tt(P, eng, out, in0, in1, op, deps=()):
    return P.op(eng, lambda e: e.tensor_tensor(out=out, in0=in0, in1=in1, op=op), deps)


def I_ts(P, eng, out, in0, s1, s2, op0, op1=None, deps=(), accum_out=None):
    kw = {}
    if op1 is not None:
        kw["op1"] = op1
    if accum_out is not None:
        kw["accum_out"] = accum_out
    return P.op(eng, lambda e: e.tensor_scalar(out=out, in0=in0, scalar1=s1, scalar2=s2, op0=op0, **kw), deps)


def I_stt(P, eng, out, in0, scalar, in1, op0, op1, deps=()):
    return P.op(eng, lambda e: e.scalar_tensor_tensor(out=out, in0=in0, scalar=scalar, in1=in1, op0=op0, op1=op1), deps)


def I_mm(P, out, lhsT, rhs, start, stop, deps=(), skip=True):
    return P.op("tensor", lambda e: e.matmul(out, lhsT=lhsT, rhs=rhs, start=start, stop=stop,
                                             skip_group_check=skip), deps)


def I_memset(P, eng, ap, val, deps=()):
    return P.op(eng, lambda e: e.memset(ap, val), deps)


def barrier(P):
    evs = []
    for e in P.ENGS:
        for item in reversed(P.q[e]):
            if not item[3]:
                evs.append(item[2])
                break
    last = {}
    for e in P.ENGS:
        for item in P.q[e]:
            if item[3]:
                last[item[2].sem] = item[2]
    evs += list(last.values())
    out = []
    for e in P.ENGS:
        out.append(P.op(e, lambda eng: eng.nop(), evs))
    return out


def tok_ap(t, d, r, n, cnt=128, lo=0, hi=None):
    start = d * (128 * n + lo) + r
    stop = start + d * (cnt - 1) + 1
    return slice(start, stop, d)


def build_program(debug_u=False, pairs=tuple(range(8)), phases=("A", "B", "C"), dbg=3, n_sc=8):
    nc = bass.Bass("TRN2", target_bir_lowering=False)
    xT = nc.dram_tensor("xT", [D, S], F32, kind="ExternalInput").ap()
    w_in = nc.dram_tensor("w_in", [D, D_IN], F32, kind="ExternalInput").ap()
    aug = nc.dram_tensor("aug", [NH, 2, 12, S], BF16, kind="ExternalInput").ap()
    mask4_d = nc.dram_tensor("mask4", [128, 512], F32, kind="ExternalInput").ap()
    dr = {}
    for name, shape, dt in (("tri", [128, 128], F32), ("ident", [128, 128], F32), ("identb", [128, 128], BF16), ("negm4", [128, 512], BF16),
                            ("conv_wT", [128, 12, 4], F32), ("conv_b2", [128, 12], F32), ("dt_bias", [16], F32),
                            ("a_log", [16], F32), ("d_skip", [16], F32), ("ssm_norm_g", [1024], F32),
                            ("att_norm_g2", [128, 8], F32), ("ssm_norm_g2", [128, 8], F32), ("ln_g", [1024], F32), ("ln_b", [1024], F32),
                            ("w_out", [2048, D], F32), ("x", [S, D], F32)):
        dr[name] = nc.dram_tensor(name, shape, dt, kind="ExternalInput").ap()
    U = nc.dram_tensor("U", [2048, S], BF16, kind="ExternalOutput" if debug_u else "Internal").ap()
    out_d = nc.dram_tensor("out", [S, D], F32, kind="ExternalOutput").ap()

    P = Prog(nc)
    final_waits = []
    with contextlib.ExitStack() as st:
        def sb(name, shape, dt, stack=st):
            return stack.enter_context(nc.sbuf_tensor(name, shape, dt))

        banks = [st.enter_context(nc.psum_tensor(f"bank{i}", [128, 512], F32)) for i in range(8)]
        w_in_v = w_in.rearrange("(kc p) c -> p kc c", p=128)
        with contextlib.ExitStack() as stx:
            XT = sb("XT", [128, 8, S], BF16, stx)
            with contextlib.ExitStack() as st0:
                xstage = [sb(f"xstage{i}", [128, S], F32, st0) for i in range(2)]
                free = [[], []]
                for kc in range(8):
                    b = kc % 2
                    ld = P.dma("sync", xstage[b][:], xT[kc * 128:(kc + 1) * 128, :], f"ld_x{b}", deps=free[b])
                    e1 = I_copy(P, "vector", XT[:, kc, 0:1536], xstage[b][:, 0:1536], [ld])
                    e2 = I_copy(P, "scalar", XT[:, kc, 1536:3072], xstage[b][:, 1536:3072], [ld])
                    e3 = I_copy(P, "gpsimd", XT[:, kc, 3072:4096], xstage[b][:, 3072:4096], [ld])
                    free[b] = [e1, e2, e3]
                barrier(P)

            if "A" in phases:
                phase_A(nc, P, st, banks, XT, w_in_v, aug, mask4_d, U, pairs, dbg, dr)
                barrier(P)
            if "B" in phases:
                (phase_B if 'oldB' in _SKIP else phase_B2)(nc, P, banks, XT, w_in_v, dr, U, n_sc)
                barrier(P)
        if "C" in phases:
            phase_C(nc, P, banks, dr, U, out_d)
            barrier(P)
        last = {}
        for e in P.ENGS:
            for item in P.q[e]:
                if item[3]:
                    last[item[2].sem] = item[2]
        final_waits = list(last.values())
        P.emit(final_waits)
    return nc


def phase_C(nc, P, banks, dr, U, out_d, n_tg=8):
    T = Trk(P)
    ALPHA = 2.0 ** 0.25
    with contextlib.ExitStack() as st:
        def sb(name, shape, dt, stack=None):
            return (stack or st).enter_context(nc.sbuf_tensor("C_" + name, shape, dt))

        Wo = sb("Wo", [128, 16, D], BF16)
        gA = sb("gA", [128, 16], F32)
        lg_b = sb("lg_b", [128, D], F32)
        lb_b = sb("lb_b", [128, D], F32)
        onesB = sb("onesB", [128, 1], BF16)
        rC = Res(); rWo = Res()
        T.dma("sync", gA[:, 0:8], dr["att_norm_g2"][:, :], "ld_d1", writes=[rC])
        T.dma("sync", gA[:, 8:16], dr["ssm_norm_g2"][:, :], "ld_d1b", writes=[rC])
        T.dma("sync", lg_b[:], dr["ln_g"].partition_broadcast(128), "ld_d2", writes=[rC])
        T.dma("sync", lb_b[:], dr["ln_b"].partition_broadcast(128), "ld_d3", writes=[rC])
        I_memset(P, "vector", onesB[:], 1.0)
        wo_v = dr["w_out"].rearrange("(f p) d -> p f d", p=128)
        with contextlib.ExitStack() as stw:
            wst = [sb(f"wstC{i}", [128, 2, D], F32, stw) for i in range(4)]
            rws = [Res() for _ in range(4)]
            for i in range(8):
                b = i % 4
                T.dma("sync" if i % 2 == 0 else "gpsimd", wst[b][:], wo_v[:, 2 * i:2 * i + 2, :], f"ld_wo{b}", writes=[rws[b]])
                for j in range(2):
                    f = 2 * i + j
                    if j == 0:
                        T.op("vector", lambda e, o=Wo[:, f, :], i_=wst[b][:, j, :], s_=gA[:, f:f + 1]:
                             e.tensor_scalar(out=o, in0=i_, scalar1=s_, scalar2=None, op0=ALU.mult),
                             reads=[rws[b], rC], writes=[])
                    else:
                        T.op("scalar", lambda e, o=Wo[:, f, :], i_=wst[b][:, j, :], s_=gA[:, f:f + 1]:
                             e.activation(out=o, in_=i_, func=AF.Copy, scale=s_),
                             reads=[rws[b], rC], writes=[])
            barrier(P)

        Ub = [sb(f"Ub{i}", [128, 16, 512], BF16) for i in range(2)]
        xt = [sb(f"xt{i}", [128, 4, D], F32) for i in range(2)]
        sq = [sb(f"sq{i}", [128, 8, 128], BF16) for i in range(2)]
        rr = [sb(f"rr{i}", [128, D], F32) for i in range(2)]
        ot = [sb(f"ot{i}", [128, D], F32) for i in range(2)]
        st6 = [sb(f"st6{i}", [128, 2, 6], F32) for i in range(2)]
        mv = [sb(f"mv{i}", [128, 2], F32) for i in range(2)]
        ra = [sb(f"ra{i}", [128, 1], F32) for i in range(2)]
        rl = [sb(f"rl{i}", [128, 1], F32) for i in range(2)]
        rUa = [Res(), Res()]; rUs = [Res(), Res()]; rxt = [[Res() for _ in range(4)] for _ in range(2)]; rot = [Res(), Res()]
        r = [{k: Res() for k in ("sq", "rr", "st6", "mv", "ra", "rl")} for _ in range(2)]
        slots = [(banks[0], banks[1]), (banks[2], banks[3]), (banks[4], banks[5])]
        rslot = [[Res(), Res()] for _ in range(3)]
        ssb = [banks[6], banks[7]]; rss = [Res(), Res()]
        U_v = U.rearrange("(f p) t -> p f t", p=128)
        x_v = dr["x"].rearrange("(c p) d -> p c d", p=128)
        slot_i = 0
        cnt = 0
        for tg in range(n_tg):
            b = tg % 2
            T0 = 512 * tg
            T.dma("sync", Ub[b][:, 0:8, :], U_v[:, 0:8, T0:T0 + 512], f"ld_ua{b}", writes=[rUa[b]])
            T.dma("sync", Ub[b][:, 8:16, :], U_v[:, 8:16, T0:T0 + 512], f"ld_us{b}", writes=[rUs[b]])
            T.dma("sync", xt[b][:], x_v[:, 4 * tg:4 * tg + 4, :], f"ld_xt{b}", writes=rxt[b])
            for ci in range(4):
                cc = slice(128 * ci, 128 * ci + 128)
                k = cnt % 2
                rk = r[k]
                T.op("scalar", lambda e, o=xt[b][:, ci, :]: e.activation(out=o, in_=o, func=AF.Copy, scale=ALPHA),
                     reads=[], writes=[rxt[b][ci]])
                T.op("scalar", lambda e, o=sq[k][:], i_=Ub[b][:, 0:8, cc]: e.activation(out=o, in_=i_, func=AF.Square),
                     reads=[rUa[b]], writes=[rk["sq"]])
                halves = []
                for half in range(2):
                    cols = slice(512 * half, 512 * (half + 1))
                    sl = slot_i; slot_i = (slot_i + 1) % 3
                    bkA, bkB = slots[sl]
                    T.mm_group([(bkA[:, :], Ub[b][:, f, cc], Wo[:, f, cols], f == 0, f == 7) for f in range(8)],
                               reads=[rUa[b]], writes=[rslot[sl][0]])
                    T.mm_group([(bkB[:, :], Ub[b][:, 8 + f, cc], Wo[:, 8 + f, cols], f == 0, f == 7) for f in range(8)],
                               reads=[rUs[b]], writes=[rslot[sl][1]])
                    halves.append((sl, cols))
                T.mm_group([(ssb[k][:, 0:1], sq[k][:, f, :], onesB[:], f == 0, f == 7) for f in range(8)],
                           reads=[rk["sq"]], writes=[rss[k]])
                T.op("scalar", lambda e, o=ra[k][:], i_=ssb[k][:, 0:1]: e.activation(out=o, in_=i_, func=AF.Ln, bias=EPS, scale=1.0 / 1024),
                     reads=[rss[k]], writes=[rk["ra"]])
                T.op("scalar", lambda e, o=ra[k][:]: e.activation(out=o, in_=o, func=AF.Exp, scale=-0.5), reads=[], writes=[rk["ra"]])
                for sl, cols in halves:
                    bkA, bkB = slots[sl]
                    T.op("vector", lambda e, o=rr[k][:, cols], i_=bkA[:, :], x_=xt[b][:, ci, cols], s_=ra[k][:, 0:1]:
                         e.scalar_tensor_tensor(out=o, in0=i_, scalar=s_, in1=x_, op0=ALU.mult, op1=ALU.add),
                         reads=[rslot[sl][0], rk["ra"], rxt[b][ci]], writes=[rk["rr"]])
                    T.op("vector", lambda e, o=rr[k][:, cols], i_=bkB[:, :]: e.tensor_tensor(out=o, in0=i_, in1=o, op=ALU.add),
                         reads=[rslot[sl][1]], writes=[rk["rr"]])
                for half in range(2):
                    T.op("vector", lambda e, o=st6[k][:, half, :], i_=rr[k][:, 512 * half:512 * (half + 1)]: e.bn_stats(out=o, in_=i_),
                         reads=[rk["rr"]], writes=[rk["st6"]])
                T.op("vector", lambda e, o=mv[k][:], i_=st6[k][:]: e.bn_aggr(out=o, in_=i_), reads=[rk["st6"]], writes=[rk["mv"]])
                T.op("scalar", lambda e, o=rl[k][:], i_=mv[k][:, 1:2]: e.activation(out=o, in_=i_, func=AF.Ln, bias=EPS),
                     reads=[rk["mv"]], writes=[rk["rl"]])
                T.op("scalar", lambda e, o=rl[k][:]: e.activation(out=o, in_=o, func=AF.Exp, scale=-0.5), reads=[], writes=[rk["rl"]])
                T.op("vector", lambda e, o=ot[k][:], i_=rr[k][:], m_=mv[k][:, 0:1], s_=rl[k][:, 0:1]:
                     e.tensor_scalar(out=o, in0=i_, scalar1=m_, scalar2=s_, op0=ALU.subtract, op1=ALU.mult),
                     reads=[rk["rr"], rk["mv"], rk["rl"]], writes=[rot[k]])
                T.op("gpsimd", lambda e, o=ot[k][:]: e.tensor_tensor(out=o, in0=o, in1=lg_b[:], op=ALU.mult),
                     reads=[rC], writes=[rot[k]])
                T.op("gpsimd", lambda e, o=ot[k][:]: e.tensor_tensor(out=o, in0=o, in1=lb_b[:], op=ALU.add),
                     reads=[rC], writes=[rot[k]])
                T.dma("gpsimd", out_d[T0 + 128 * ci:T0 + 128 * ci + 128, :], ot[k][:], f"st_o{k}", reads=[rot[k]])
                cnt += 1


def phase_A(nc, P, st_outer, banks, XT, w_in_v, aug, mask4_d, U, pairs, dbg=3, dr=None):
    with contextlib.ExitStack() as st:
        def sb(name, shape, dt):
            return st.enter_context(nc.sbuf_tensor(name, shape, dt))

        mask4 = sb("mask4s", [128, 512], F32)
        stmp = [sb(f"stmp{i}", [128, 512], F32) for i in range(4)]
        wstage = sb("wstage", [128, 8, 512], F32)
        wbf = sb("wbf", [128, 8, 512], BF16)
        qk = [[sb(f"qk{w}{h}", [128, S], BF16) for h in range(2)] for w in range(2)]
        sz = sb("sz", [128, S], BF16)
        V = sb("V", [128, 3, 32, 192], BF16)
        NPT = 6
        PT = [sb(f"PT{i}", [128, 512], BF16) for i in range(NPT)]
        rc = [sb(f"rc{i}", [128, 512], F32) for i in range(2)]
        t1 = [sb(f"t1{i}", [128, 512], F32) for i in range(2)]
        uT = sb("uT", [128, S], BF16)
        vT = sb("vT", [128, S], BF16)
        identBa = sb("identBa", [128, 128], BF16)

        acc = banks[0:4]
        sbank = banks[4:6]
        pbank = banks[6:8]

        ev_mask = P.dma("sync", mask4[:], mask4_d[:, :], "ld_c")
        ev_id = P.dma("sync", identBa[:], dr["identb"][:, :], "ld_cid")
        vT_free = []
        ev_ones = I_memset(P, "gpsimd", V[:, :, :, 64:128], 1.0) if "ones" not in _SKIP else None

        pb_free = [[], []]
        pb_i = 0
        sb_free = [None, None]
        pt_free = [None] * 6
        st_free = [None] * 4
        acc_free = [None] * 4
        w_free = []
        qk_free = [[[], []], [[], []]]
        sz_free = []
        V_free = []
        uT_free = None
        g_ctr = 0

        def load_w(hp_, deps_):
            out_ = []
            for wi, off in enumerate((OFF_Q, OFF_K, OFF_V, OFF_ZA)):
                out_.append(P.dma("sync", wstage[:, :, wi * 128:(wi + 1) * 128],
                                  w_in_v[:, :, off + hp_ * 128: off + (hp_ + 1) * 128], "ld_w", deps=deps_))
            return out_

        lds = load_w(pairs[0], [])
        w_casts = None
        ws_b = wstage[:].rearrange("p a b -> p (a b)").bitcast(BF16)
        qk16 = [ws_b[:, 0:S], ws_b[:, S:2 * S]]
        qk16_free = []
        rc_free = [None, None]
        ev_ctr = 0
        pending_evac = []
        for hp in pairs:
            hA, hB = 2 * hp, 2 * hp + 1
            if w_casts is None:
                c1 = I_copy(P, "vector", wbf[:, 0:4, :], wstage[:, 0:4, :], lds)
                c2 = I_copy(P, "scalar", wbf[:, 4:8, :], wstage[:, 4:8, :], lds)
                w_casts = [c1, c2]
            w_ready = w_casts
            nxt = pairs.index(hp) + 1
            if nxt < len(pairs):
                lds = load_w(pairs[nxt], w_casts + qk16_free)
            aug_ev = [[None, None], [None, None]]
            for w in range(2):
                for hh, h in enumerate((hA, hB)):
                    if "aug" in _SKIP:
                        continue
                    aug_ev[w][hh] = P.dma("sync", qk[w][hh][64:76, :], aug[h, w, :, :], f"ld_aug{w}{hh}",
                                          deps=qk_free[w][hh])
            w_last = []
            qk_ready = [[[], []], [[], []]]
            sz_ready = []
            vT_ready = []
            for wi in (0, 1, 2, 3):
                for tt in range(8):
                    pbk = pbank[pb_i]
                    deps = list(w_ready) + list(pb_free[pb_i])
                    for kc in range(8):
                        mm = I_mm(P, pbk[:, :], wbf[:, kc, wi * 128:(wi + 1) * 128],
                                  XT[:, kc, tt * 512:(tt + 1) * 512], kc == 0, kc == 7,
                                  deps if kc == 0 else ())
                    cols = slice(tt * 512, (tt + 1) * 512)
                    if wi == 3:
                        e = I_act(P, sz[:, cols], pbk[:, :], AF.Silu, [mm] + sz_free)
                        sz_ready.append(e)
                        pb_free[pb_i] = [e]
                    elif wi == 2:
                        e = I_copy(P, "vector" if tt % 2 == 0 else "scalar", vT[:, cols], pbk[:, :], [mm] + vT_free)
                        vT_ready.append(e)
                        pb_free[pb_i] = [e]
                    else:
                        w = wi
                        sc = 0.125 if w == 0 else 1.0
                        eA = I_act(P, qk[w][0][0:64, cols], pbk[0:64, :], AF.Copy, [mm] + qk_free[w][0], scale=sc)
                        if "dveB" in _SKIP:
                            eB = I_act(P, qk[w][1][0:64, cols], pbk[64:128, :], AF.Copy, [mm] + qk_free[w][1], scale=sc)
                        else:
                            eB = I_ts(P, "vector", qk[w][1][0:64, cols], pbk[64:128, :], sc, None, ALU.mult,
                                      deps=[mm] + qk_free[w][1])
                        qk_ready[w][0].append(eA)
                        qk_ready[w][1].append(eB)
                        pb_free[pb_i] = [eA, eB]
                    pb_i ^= 1
                    w_last = [mm]
            V_ready = {}
            tr_last = None
            for di, d in enumerate(DIL if dbg >= 2 else ()):
                nb = 32 // d
                for c0 in range(0, 32, 8):
                    pbk = pbank[pb_i]
                    pbk_b = pbk[:].bitcast(BF16)
                    for j in range(8):
                        r_, m_ = divmod(c0 + j, nb)
                        deps = ()
                        if j == 0:
                            deps = vT_ready + [ev_id] + list(pb_free[pb_i])
                        tr_last = P.op("tensor", lambda e, o=pbk_b[:, 128 * j:128 * (j + 1)], i_=vT[:, tok_ap(None, d, r_, m_)]:
                                       e.transpose(o, i_, identBa[:]), deps)
                    src = pbk_b[:, :].rearrange("p (c f) -> p c f", f=128)
                    eA = I_copy(P, "vector", V[:, di, c0:c0 + 8, 0:64], src[:, :, 0:64], [tr_last] + V_free)
                    eB = I_copy(P, "scalar", V[:, di, c0:c0 + 8, 128:192], src[:, :, 64:128], [tr_last, eA] + V_free)
                    for j in range(8):
                        V_ready[(di, c0 + j)] = [eA, eB]
                    pb_free[pb_i] = [eA, eB]
                    pb_i ^= 1
            vT_free = [tr_last] if tr_last is not None else []
            w_free = w_last
            if nxt < len(pairs):
                c1 = I_copy(P, "vector", wbf[:, 0:4, :], wstage[:, 0:4, :], lds + w_last)
                c2 = I_copy(P, "scalar", wbf[:, 4:8, :], wstage[:, 4:8, :], lds + w_last)
                w_casts = [c1, c2]
            V_free = []
            qk_free = [[[], []], [[], []]]
            sz_free = []

            u_written = []
            for hh, h in enumerate((hA, hB) if dbg >= 3 else ()):
                qT, kT = qk[0][hh], qk[1][hh]
                q_dep = qk_ready[0][hh] + [aug_ev[0][hh]]
                k_dep = qk_ready[1][hh] + [aug_ev[1][hh]]
                vcol = slice(0, 128) if hh == 0 else slice(64, 192)
                cdeps = (w_casts if nxt < len(pairs) else []) + qk16_free
                c16 = []
                for w in range(2):
                    src = qk[w][hh][0:76, :].rearrange("p (j r) -> p r j", r=16)
                    dst = qk16[w][0:76, :].rearrange("p (r j) -> p r j", r=16)
                    dd = (q_dep if w == 0 else k_dep) + cdeps
                    if w == 0:
                        c16.append(I_copy(P, "vector", dst, src, dd))
                    else:
                        c16.append(I_copy(P, "scalar", dst, src, dd))
                last16 = None
                for sbk in range(2):
                    qbs = []
                    for di, d in enumerate(DIL):
                        nb = 32 // d
                        per_sb = nb // 2
                        for r in range(d):
                            for n in range(sbk * per_sb, (sbk + 1) * per_sb):
                                qbs.append((di, d, r, n))
                    groups = [qbs[i:i + 2] for i in range(0, len(qbs), 2)]
                    acc_started = [False] * 4
                    acc_last_mm = [None] * 4
                    pend = []

                    def do_pv(item):
                        ptb, grp, ev_mask_mul = item
                        last_mm = None
                        for j, (di, d, r, n) in enumerate(grp):
                            nb = 32 // d
                            for role in range(2):
                                m = n - 1 + role
                                if m < 0:
                                    continue
                                tile_cols = (2 * j + role) * 128
                                cid = r * nb + m
                                lhsT = V[:, di, cid, vcol]
                                if d == 16:
                                    pieces = [(pc, 32) for pc in range(4)]
                                else:
                                    pieces = [(0, 128)]
                                for pc, cnt in pieces:
                                    i0 = pc * 32 if d == 16 else 0
                                    t0 = d * (128 * n + i0) + r
                                    col0 = t0 - 2048 * sbk
                                    bk = col0 // 512
                                    c0 = col0 % 512
                                    out = acc[bk][:, c0: c0 + d * (cnt - 1) + 1: d]
                                    rhs = PT[ptb][:, tile_cols + i0: tile_cols + i0 + cnt]
                                    deps = [ev_mask_mul] + V_ready[(di, cid)] + latest(ev_ones)
                                    if not acc_started[bk]:
                                        deps = deps + latest(acc_free[bk])
                                    last_mm = I_mm(P, out, lhsT, rhs, not acc_started[bk], False, deps)
                                    acc_started[bk] = True
                                    acc_last_mm[bk] = last_mm
                        pt_free[ptb] = last_mm

                    for gi, grp in enumerate(groups):
                        if gi % 2 == 0 and pending_evac:
                            pending_evac.pop(0)()
                        sbi = g_ctr % 2
                        pti = g_ctr % 6
                        sti = g_ctr % 4
                        g_ctr += 1
                        sbb = sbank[sbi]
                        first = True
                        mm = None
                        merged = (len(grp) == 2 and grp[0][1] != 16 and grp[1][0:3] == grp[0][0:3]
                                  and grp[1][3] == grp[0][3] + 1)
                        for j, (di, d, r, n) in enumerate(grp):
                            for role in range(2):
                                if merged and (j, role) == (1, 0):
                                    continue
                                m = n - 1 + role
                                nq = n
                                if m < 0:
                                    m, nq = 0, 1
                                tile_cols = (2 * j + role) * 128
                                deps = ()
                                if first:
                                    deps = q_dep + k_dep + latest(sb_free[sbi])
                                    first = False
                                if d == 16:
                                    kop = qk16[1][0:76, r * 256 + 128 * m: r * 256 + 128 * m + 128]
                                    qop = qk16[0][0:76, r * 256 + 128 * nq: r * 256 + 128 * nq + 128]
                                    mm = I_mm(P, sbb[:, tile_cols:tile_cols + 128], kop, qop, True, True, list(deps) + c16)
                                    last16 = mm
                                elif merged and (j, role) == (0, 1):
                                    mm = I_mm(P, sbb[:, tile_cols:tile_cols + 256],
                                              kT[0:76, tok_ap(None, d, r, m)], qT[0:76, tok_ap(None, d, r, n, cnt=256)],
                                              True, True, deps)
                                else:
                                    mm = I_mm(P, sbb[:, tile_cols:tile_cols + 128],
                                              kT[0:76, tok_ap(None, d, r, m)], qT[0:76, tok_ap(None, d, r, nq)],
                                              True, True, deps)
                        mk0 = I_tt(P, "vector", stmp[sti][:, :], sbb[:, :], mask4[:, :], ALU.add,
                                   [mm, ev_mask] + latest(st_free[sti]))
                        sb_free[sbi] = mk0
                        mk = I_act(P, PT[pti][:, :], stmp[sti][:, :], AF.Exp, [mk0] + latest(pt_free[pti]))
                        st_free[sti] = mk
                        pend.append((pti, grp, mk))
                        if len(pend) > 4:
                            do_pv(pend.pop(0))
                    while pend:
                        do_pv(pend.pop(0))
                    def make_evac(bk, hh=hh, sbk=sbk, alm=acc_last_mm, szr=sz_ready):
                        def evac():
                            nonlocal ev_ctr, sz_free
                            cols = slice(2048 * sbk + 512 * bk, 2048 * sbk + 512 * (bk + 1))
                            if hh == 0:
                                o_rows, s_rows = slice(0, 64), slice(64, 128)
                            else:
                                o_rows, s_rows = slice(64, 128), slice(0, 64)
                            ei = ev_ctr % 2
                            ev_ctr += 1
                            a1 = I_act(P, rc[ei][o_rows, :], acc[bk][s_rows, :], AF.Ln, [alm[bk]] + latest(rc_free[ei]))
                            a2 = I_copy(P, "scalar", t1[ei][o_rows, :], acc[bk][o_rows, :], [alm[bk]] + latest(rc_free[ei]))
                            acc_free[bk] = a2
                            e1 = I_act(P, rc[ei][o_rows, :], rc[ei][o_rows, :], AF.Exp, [a1], scale=-1.0)
                            e2 = I_tt(P, "gpsimd", t1[ei][o_rows, :], t1[ei][o_rows, :], rc[ei][o_rows, :], ALU.mult, [e1, a2])
                            e3 = I_tt(P, "gpsimd", uT[o_rows, cols], t1[ei][o_rows, :], sz[o_rows, cols], ALU.mult,
                                      [e2] + szr + latest(uT_free))
                            rc_free[ei] = e3
                            u_written.append(e3)
                            sz_free = [e3]
                        return evac
                    pending_evac.extend(make_evac(bk) for bk in range(4))
                qk_free[0][hh] = [mm]
                qk_free[1][hh] = [mm]
                qk16_free = [last16]
            if dbg < 3:
                continue
            while pending_evac:
                pending_evac.pop(0)()
            V_free = latest(*[acc_last_mm[b] for b in range(4)])
            uT_free = P.dma("gpsimd", U[hp * 128:(hp + 1) * 128, :], uT[:, :], "st_u", deps=u_written)
            sz_free = u_written[-1:]


class Res:
    __slots__ = ("w", "r")

    def __init__(self):
        self.w = None
        self.r = {}


class Trk:
    def __init__(self, P):
        self.P = P

    def deps(self, reads, writes):
        d = []
        for t in reads:
            if t.w is not None:
                d.append(t.w)
        for t in writes:
            if t.w is not None:
                d.append(t.w)
            d.extend(t.r.values())
        return d

    def mark(self, ev, reads, writes):
        for t in reads:
            t.r[ev.sem if ev.sem is not None else ev.eng] = ev
        for t in writes:
            t.w = ev
            t.r = {}

    def op(self, eng, fn, reads=(), writes=(), extra=()):
        ev = self.P.op(eng, fn, self.deps(reads, writes) + list(extra))
        self.mark(ev, reads, writes)
        return ev

    def dma(self, eng, out, in_, slot, reads=(), writes=(), extra=()):
        ev = self.P.dma(eng, out, in_, slot, self.deps(reads, writes) + list(extra))
        self.mark(ev, reads, writes)
        return ev

    def mm_group(self, mms, reads, writes, extra=()):
        d = self.deps(reads, writes) + list(extra)
        ev = None
        for i, (out, lhsT, rhs, start, stop) in enumerate(mms):
            ev = I_mm(self.P, out, lhsT, rhs, start, stop, d if i == 0 else ())
        self.mark(ev, reads, writes)
        return ev


def bc64(ap16, h0, nh):
    return ap16[:, h0:h0 + nh].unsqueeze(2).broadcast_to([128, nh, 64])


def phase_B(nc, P, banks, XT, w_in_v, dr, U, n_sc=8, dbg_out=None):
    T = Trk(P)
    with contextlib.ExitStack() as st:
        def sb(name, shape, dt, stack=None):
            return (stack or st).enter_context(nc.sbuf_tensor("B_" + name, shape, dt))

        rXT = Res()
        Wz = sb("Wz", [128, 8, 1024], BF16)
        Wx = sb("Wx", [128, 8, 1536], BF16)
        Wdt = sb("Wdt", [128, 8, 16], BF16)
        tri = sb("tri", [128, 128], F32)
        onesF = sb("onesF", [128, 128], F32)
        identF = sb("identF", [128, 128], F32)
        identB = sb("identB", [128, 128], BF16)
        negm4 = sb("negm4", [128, 512], BF16)
        cw = sb("cw", [128, 12, 4], F32)
        cb = sb("cb", [128, 12], F32)
        dtb_b = sb("dtb_b", [128, 16], F32)
        a_b = sb("a_b", [128, 16], F32)
        dsk_b = sb("dsk_b", [128, 16], F32)
        gs_b = sb("gs_b", [128, 1024], F32)
        hal = sb("hal", [128, 12, 3], F32)
        rW = Res(); rC = Res(); rHal = Res()

        with contextlib.ExitStack() as stw:
            wst = sb("wstB", [128, 8, 512], F32, stw)
            rwst = Res()
            pieces = [(OFF_ZS, 0, 512, Wz), (OFF_ZS + 512, 512, 512, Wz),
                      (OFF_XBC, 0, 512, Wx), (OFF_XBC + 512, 512, 512, Wx), (OFF_XBC + 1024, 1024, 512, Wx),
                      (OFF_DT, 0, 16, Wdt)]
            for i, (off, dst0, n, Wt) in enumerate(pieces):
                T.dma("sync", wst[:, :, 0:n], w_in_v[:, :, off:off + n], "ld_wB", writes=[rwst])
                T.op("vector", lambda e, o=Wt[:, 0:4, dst0:dst0 + n], i_=wst[:, 0:4, 0:n]: e.tensor_copy(out=o, in_=i_),
                     reads=[rwst], writes=[rW])
                T.op("gpsimd", lambda e, o=Wt[:, 4:8, dst0:dst0 + n], i_=wst[:, 4:8, 0:n]: e.tensor_copy(out=o, in_=i_),
                     reads=[rwst], writes=[])
            T.dma("sync", tri[:], dr["tri"][:, :], "ld_c1", writes=[rC])
            T.dma("sync", identF[:], dr["ident"][:, :], "ld_c2", writes=[rC])
            T.dma("sync", negm4[:], dr["negm4"][:, :], "ld_c3", writes=[rC])
            T.dma("sync", cw[:], dr["conv_wT"][:, :, :], "ld_c4", writes=[rC])
            T.dma("sync", cb[:], dr["conv_b2"][:, :], "ld_c5", writes=[rC])
            T.dma("sync", dtb_b[:], dr["dt_bias"].partition_broadcast(128), "ld_c6", writes=[rC])
            T.dma("sync", a_b[:], dr["a_log"].partition_broadcast(128), "ld_c7", writes=[rC])
            T.dma("sync", dsk_b[:], dr["d_skip"].partition_broadcast(128), "ld_c8", writes=[rC])
            T.dma("sync", gs_b[:], dr["ssm_norm_g"].partition_broadcast(128), "ld_c9", writes=[rC])
            barrier(P)
            I_memset(P, "vector", onesF[:], 1.0)
            I_memset(P, "vector", hal[:], 0.0)
            I_copy(P, "vector", identB[:], identF[:])
            I_act(P, a_b[:], a_b[:], AF.Exp)
            barrier(P)
            I_ts(P, "vector", a_b[:], a_b[:], -1.0, None, ALU.mult)
            barrier(P)

        xb = [sb(f"xb{i}", [128, 515], F32) for i in range(2)]
        cvb = [sb(f"cvb{i}", [128, 512], F32) for i in range(2)]
        xsT = sb("xsT", [128, 8, 512], F32)
        BT = sb("BT", [128, 2, 512], BF16)
        CT = sb("CT", [128, 2, 512], BF16)
        dtx = sb("dtx", [128, 16], F32)
        dt_t = sb("dt_t", [128, 16], F32)
        adt = sb("adt", [128, 16], F32)
        nacs = sb("nacs", [128, 16], F32)
        D0 = sb("D0", [128, 16], F32)
        dte = sb("dte", [128, 16], F32)
        ddt = sb("ddt", [128, 16], F32)
        cdb = sb("cdb", [128, 16], F32)
        Btok = sb("Btok", [128, 2, 128], BF16)
        xs_tok = sb("xs_tok", [128, 1024], F32)
        xg = sb("xg", [128, 1024], BF16)
        xgd = sb("xgd", [128, 1024], BF16)
        Gm = sb("Gm", [128, 2, 128], BF16)
        Dm = [sb(f"Dm{i}", [128, 512], BF16) for i in range(2)]
        MT = [sb(f"MT{i}", [128, 512], BF16) for i in range(2)]
        H = sb("H", [128, 2, 512], F32)
        Hbf = sb("Hbf", [128, 2, 512], BF16)
        yo = sb("yo", [128, 512], F32)
        tsk = sb("tsk", [128, 512], F32)
        y = sb("y", [128, 1024], F32)
        zs = sb("zs", [128, 1024], F32)
        junk = sb("junkB", [128, 1024], F32)
        ssq = sb("ssq", [128, 1], F32)
        rstd = sb("rstd", [128, 1], F32)
        mixn = sb("mixn", [128, 1024], BF16)
        mixT = sb("mixT", [128, 8, 512], BF16)

        r = {k: Res() for k in ("xsT", "BT", "CT", "dtx", "dt", "adt", "nacs", "D0", "dte", "ddt", "cdb", "Btok",
                                "xs_tok", "xg", "xgd", "Gm", "H", "Hbf", "yo", "tsk", "y", "zs", "junk", "ssq", "rstd",
                                "mixn", "mixT")}
        rxb = [Res(), Res()]; rxbh = [Res(), Res()]; rcv = [Res(), Res()]; rmixT = [Res(), Res()]; rDm = [Res(), Res()]; rMT = [Res(), Res()]
        rhal = [Res() for _ in range(12)]
        pb = [banks[0], banks[1]]; rpb = [Res(), Res()]
        smb = banks[2]; rsmb = Res()
        trb = banks[3]; rtr = Res()
        Rb = [banks[4], banks[5]]; rRb = [Res(), Res()]
        Yb = banks[6]; rY = Res()
        osb = banks[7]; ros = Res()

        I_memset(P, "vector", H[:], 0.0)
        I_memset(P, "vector", Hbf[:], 0.0)
        barrier(P)

        pbi = 0
        quad_ctr = 0
        for sc in range(n_sc):
            T0 = 512 * sc
            for ft in range(12):
                b = pbi; pbi ^= 1
                xbi = ft % 2
                T.mm_group([(pb[b][:, :], Wx[:, kc, ft * 128:(ft + 1) * 128], XT[:, kc, T0:T0 + 512], kc == 0, kc == 7)
                            for kc in range(8)], reads=[rW, rXT], writes=[rpb[b]])
                T.op("scalar", lambda e, o=xb[xbi][:, 3:515], i_=pb[b][:, :]: e.activation(out=o, in_=i_, func=AF.Copy),
                     reads=[rpb[b]], writes=[rxb[xbi]])
                T.op("gpsimd", lambda e, o=xb[xbi][:, 0:3], i_=hal[:, ft, :]: e.tensor_copy(out=o, in_=i_),
                     reads=[rhal[ft]], writes=[rxbh[xbi]])
                T.op("gpsimd", lambda e, o=hal[:, ft, :], i_=xb[xbi][:, 512:515]: e.tensor_copy(out=o, in_=i_),
                     reads=[rxb[xbi]], writes=[rhal[ft]])
                ceng = "vector"
                cv = cvb[xbi]
                T.op(ceng, lambda e, o=cv[:, :], i_=xb[xbi][:, 0:512], s_=cw[:, ft, 0:1]:
                     e.tensor_scalar(out=o, in0=i_, scalar1=s_, scalar2=None, op0=ALU.mult),
                     reads=[rxb[xbi], rxbh[xbi], rC], writes=[rcv[xbi]])
                for j in range(1, 4):
                    T.op(ceng, lambda e, o=cv[:, :], i_=xb[xbi][:, j:j + 512], s_=cw[:, ft, j:j + 1]:
                         e.scalar_tensor_tensor(out=o, in0=i_, scalar=s_, in1=o, op0=ALU.mult, op1=ALU.add),
                         reads=[rxb[xbi], rxbh[xbi], rcv[xbi]], writes=[rcv[xbi]])
                if ft < 8:
                    dst, rd = xsT[:, ft, :], r["xsT"]
                elif ft < 10:
                    dst, rd = BT[:, ft - 8, :], r["BT"]
                else:
                    dst, rd = CT[:, ft - 10, :], r["CT"]
                T.op("scalar", lambda e, o=dst, i_=cv[:, :], b_=cb[:, ft:ft + 1]:
                     e.activation(out=o, in_=i_, func=AF.Silu, bias=b_), reads=[rcv[xbi], rC], writes=[rd])

            for ci in range(4):
                t0 = T0 + 128 * ci
                cc = slice(128 * ci, 128 * ci + 128)
                T.mm_group([(smb[:, 0:16], XT[:, kc, t0:t0 + 128], Wdt[:, kc, :], kc == 0, kc == 7) for kc in range(8)],
                           reads=[rW, rXT], writes=[rsmb])
                for half in range(2):
                    b = pbi; pbi ^= 1
                    T.mm_group([(pb[b][:, :], XT[:, kc, t0:t0 + 128], Wz[:, kc, 512 * half:512 * (half + 1)], kc == 0, kc == 7)
                                for kc in range(8)], reads=[rW, rXT], writes=[rpb[b]])
                    T.op("scalar", lambda e, o=zs[:, 512 * half:512 * (half + 1)], i_=pb[b][:, :]:
                         e.activation(out=o, in_=i_, func=AF.Silu), reads=[rpb[b]], writes=[r["zs"]])
                T.op("vector", lambda e: e.tensor_tensor(out=dtx[:], in0=smb[:, 0:16], in1=dtb_b[:], op=ALU.add),
                     reads=[rsmb, rC], writes=[r["dtx"]])
                T.op("scalar", lambda e: e.activation(out=dtx[:], in_=dtx[:], func=AF.Exp), reads=[], writes=[r["dtx"]])
                T.op("scalar", lambda e: e.activation(out=dt_t[:], in_=dtx[:], func=AF.Ln, bias=1.0),
                     reads=[r["dtx"]], writes=[r["dt"]])
                T.op("vector", lambda e: e.tensor_tensor(out=adt[:], in0=dt_t[:], in1=a_b[:], op=ALU.mult),
                     reads=[r["dt"], rC], writes=[r["adt"]])
                T.mm_group([(smb[:, 16:32], tri[:], adt[:], True, True)], reads=[r["adt"], rC], writes=[rsmb])
                T.mm_group([(smb[:, 32:48], onesF[:], adt[:], True, True)], reads=[r["adt"], rC], writes=[rsmb])
                T.op("vector", lambda e: e.tensor_scalar(out=nacs[:], in0=smb[:, 16:32], scalar1=-1.0, scalar2=None, op0=ALU.mult),
                     reads=[rsmb], writes=[r["nacs"]])
                T.op("vector", lambda e: e.tensor_copy(out=cdb[:], in_=smb[:, 32:48]), reads=[rsmb], writes=[r["cdb"]])
                T.op("vector", lambda e: e.tensor_tensor(out=dte[:], in0=cdb[:], in1=nacs[:], op=ALU.add),
                     reads=[r["cdb"], r["nacs"]], writes=[r["dte"]])
                T.op("scalar", lambda e: e.activation(out=D0[:], in_=nacs[:], func=AF.Exp, scale=-1.0),
                     reads=[r["nacs"]], writes=[r["D0"]])
                T.op("scalar", lambda e: e.activation(out=dte[:], in_=dte[:], func=AF.Exp), reads=[], writes=[r["dte"]])
                T.op("scalar", lambda e: e.activation(out=cdb[:], in_=cdb[:], func=AF.Exp), reads=[r["dte"]], writes=[r["cdb"]])
                T.op("vector", lambda e: e.tensor_tensor(out=ddt[:], in0=dt_t[:], in1=dte[:], op=ALU.mult),
                     reads=[r["dt"], r["dte"]], writes=[r["ddt"]])
                for g in range(2):
                    ev = None
                    for j in range(4):
                        ev = P.op("tensor", lambda e, o=trb[:, 128 * j:128 * (j + 1)], i_=xsT[:, 4 * g + j, cc]:
                                  e.transpose(o, i_, identF[:]), T.deps([r["xsT"], rC], [rtr]) if j == 0 else ())
                    T.mark(ev, [r["xsT"], rC], [rtr])
                    T.op("scalar", lambda e, o=xs_tok[:, 512 * g:512 * (g + 1)], i_=trb[:, :]: e.activation(out=o, in_=i_, func=AF.Copy),
                         reads=[rtr], writes=[r["xs_tok"]])
                for g in range(2):
                    v3 = lambda ap: ap[:, 512 * g:512 * (g + 1)].rearrange("p (h e) -> p h e", e=64)
                    T.op("vector", lambda e, o=v3(xg), i_=v3(xs_tok), b_=bc64(dt_t, 8 * g, 8):
                         e.tensor_tensor(out=o, in0=i_, in1=b_, op=ALU.mult), reads=[r["xs_tok"], r["dt"]], writes=[r["xg"]])
                    T.op("gpsimd", lambda e, o=v3(xgd), i_=v3(xs_tok), b_=bc64(ddt, 8 * g, 8):
                         e.tensor_tensor(out=o, in0=i_, in1=b_, op=ALU.mult), reads=[r["xs_tok"], r["ddt"]], writes=[r["xgd"]])
                trb_b = trb[:].bitcast(BF16)
                ev = None
                for g in range(2):
                    ev = P.op("tensor", lambda e, o=trb_b[:, 128 * g:128 * (g + 1)], i_=BT[:, g, cc]:
                              e.transpose(o, i_, identB[:]), T.deps([r["BT"], rC], [rtr]) if g == 0 else ())
                T.mark(ev, [r["BT"], rC], [rtr])
                T.op("vector", lambda e: e.tensor_copy(out=Btok[:].rearrange("p g n -> p (g n)"), in_=trb_b[:, 0:256]),
                     reads=[rtr], writes=[r["Btok"]])
                for g in range(2):
                    T.mm_group([(smb[:, 128 * (g + 1):128 * (g + 2)], BT[:, g, cc], CT[:, g, cc], True, True)],
                               reads=[r["BT"], r["CT"]], writes=[rsmb])
                    T.op("vector", lambda e, o=Gm[:, g, :], i_=smb[:, 128 * (g + 1):128 * (g + 2)]: e.tensor_copy(out=o, in_=i_),
                         reads=[rsmb], writes=[r["Gm"]])
                for g in range(2):
                    for qd in range(2):
                        qi = quad_ctr % 2; quad_ctr += 1
                        h0 = 8 * g + 4 * qd
                        mms = [(Rb[qi][:, :], identB[:], negm4[:], True, False)]
                        for j in range(4):
                            mms.append((Rb[qi][:, 128 * j:128 * (j + 1)], adt[:, h0 + j:h0 + j + 1].broadcast_to([128, 128]),
                                        tri[:], False, j == 3))
                        T.mm_group(mms, reads=[r["adt"], rC], writes=[rRb[qi]])
                        for j in range(4):
                            T.op("scalar", lambda e, o=Dm[qi][:, 128 * j:128 * (j + 1)], i_=Rb[qi][:, 128 * j:128 * (j + 1)],
                                 b_=nacs[:, h0 + j:h0 + j + 1]: e.activation(out=o, in_=i_, func=AF.Exp, bias=b_),
                                 reads=[rRb[qi], r["nacs"]], writes=[rDm[qi]])
                        T.op("vector", lambda e, o=MT[qi][:].rearrange("p (j l) -> p j l", l=128),
                             i_=Dm[qi][:].rearrange("p (j l) -> p j l", l=128),
                             b_=Gm[:, g, :].unsqueeze(1).broadcast_to([128, 4, 128]):
                             e.tensor_tensor(out=o, in0=i_, in1=b_, op=ALU.mult),
                             reads=[rDm[qi], r["Gm"]], writes=[rMT[qi]])
                        mms = []
                        for j in range(4):
                            hh = 4 * qd + j
                            mms.append((Yb[:, 64 * hh:64 * (hh + 1)], MT[qi][:, 128 * j:128 * (j + 1)],
                                        xg[:, 64 * (h0 + j):64 * (h0 + j + 1)], (qd == 0 and j == 0), False))
                        T.mm_group(mms, reads=[rMT[qi], r["xg"]], writes=[rY])
                    T.mm_group([(osb[:, :], CT[:, g, cc], Hbf[:, g, :], True, True)], reads=[r["CT"], r["Hbf"]], writes=[ros])
                    v3 = lambda ap: ap.rearrange("p (h e) -> p h e", e=64)
                    T.op("vector", lambda e, o=v3(yo[:, :]), i_=v3(osb[:, :]), b_=bc64(D0, 8 * g, 8):
                         e.tensor_tensor(out=o, in0=i_, in1=b_, op=ALU.mult), reads=[ros, r["D0"]], writes=[r["yo"]])
                    T.op("vector", lambda e, o=y[:, 512 * g:512 * (g + 1)], i_=Yb[:, :]:
                         e.tensor_tensor(out=o, in0=i_, in1=yo[:, :], op=ALU.add), reads=[rY, r["yo"]], writes=[r["y"]])
                    T.op("gpsimd", lambda e, o=v3(tsk[:, :]), i_=v3(xs_tok[:, 512 * g:512 * (g + 1)]), b_=bc64(dsk_b, 8 * g, 8):
                         e.tensor_tensor(out=o, in0=i_, in1=b_, op=ALU.mult), reads=[r["xs_tok"], rC], writes=[r["tsk"]])
                    T.op("gpsimd", lambda e, o=y[:, 512 * g:512 * (g + 1)]: e.tensor_tensor(out=o, in0=o, in1=tsk[:, :], op=ALU.add),
                         reads=[r["tsk"]], writes=[r["y"]])
                    T.mm_group([(osb[:, :], Btok[:, g, :], xgd[:, 512 * g:512 * (g + 1)], True, True)],
                               reads=[r["Btok"], r["xgd"]], writes=[ros])
                    T.op("vector", lambda e, o=v3(H[:, g, :]), b_=bc64(cdb, 8 * g, 8):
                         e.tensor_tensor(out=o, in0=o, in1=b_, op=ALU.mult), reads=[r["cdb"]], writes=[r["H"]])
                    T.op("vector", lambda e, o=H[:, g, :]: e.tensor_tensor(out=o, in0=osb[:, :], in1=o, op=ALU.add),
                         reads=[ros], writes=[r["H"]])
                    T.op("gpsimd", lambda e, o=Hbf[:, g, :], i_=H[:, g, :]: e.tensor_copy(out=o, in_=i_),
                         reads=[r["H"]], writes=[r["Hbf"]])
                T.op("vector", lambda e: e.tensor_tensor(out=y[:], in0=y[:], in1=zs[:], op=ALU.mult),
                     reads=[r["zs"]], writes=[r["y"]])
                T.op("vector", lambda e: e.memset(ssq[:], 0.0), reads=[], writes=[r["ssq"]])
                T.op("scalar", lambda e: e.activation(out=junk[:], in_=y[:], func=AF.Square, accum_out=ssq[:]),
                     reads=[r["y"]], writes=[r["junk"], r["ssq"]])
                T.op("scalar", lambda e: e.activation(out=rstd[:], in_=ssq[:], func=AF.Ln, bias=EPS, scale=1.0 / 1024),
                     reads=[r["ssq"]], writes=[r["rstd"]])
                T.op("scalar", lambda e: e.activation(out=rstd[:], in_=rstd[:], func=AF.Exp, scale=-0.5),
                     reads=[], writes=[r["rstd"]])
                T.op("vector", lambda e: e.scalar_tensor_tensor(out=mixn[:], in0=y[:], scalar=rstd[:, 0:1], in1=gs_b[:],
                                                                op0=ALU.mult, op1=ALU.mult),
                     reads=[r["y"], r["rstd"], rC], writes=[r["mixn"]])
                for half in range(2):
                    ev = None
                    for j in range(4):
                        ft = 4 * half + j
                        ev = P.op("tensor", lambda e, o=trb_b[:, 128 * j:128 * (j + 1)], i_=mixn[:, 128 * ft:128 * (ft + 1)]:
                                  e.transpose(o, i_, identB[:]), T.deps([r["mixn"], rC], [rtr]) if j == 0 else ())
                    T.mark(ev, [r["mixn"], rC], [rtr])
                    T.op("vector" if half == 0 else "scalar",
                         (lambda e, o=mixT[:, 4 * half:4 * half + 4, cc], i_=trb_b[:, 0:512].rearrange("p (j t) -> p j t", t=128):
                          e.tensor_copy(out=o, in_=i_)) if half == 0 else
                         (lambda e, o=mixT[:, 4 * half:4 * half + 4, cc], i_=trb_b[:, 0:512].rearrange("p (j t) -> p j t", t=128):
                          e.activation(out=o, in_=i_, func=AF.Copy)),
                         reads=[rtr], writes=[rmixT[half]])
            T.dma("gpsimd", U[1024:2048, T0:T0 + 512].rearrange("(f p) t -> p f t", p=128), mixT[:, :, :], "st_uB",
                  reads=[rmixT[0], rmixT[1]])
        barrier(P)


_NC_CACHE = {}


def _host_inputs(xb, p, c):
    return {
        "xT": np.ascontiguousarray(xb.T), "x": np.ascontiguousarray(xb),
        "w_in": p["w_in"], "w_out": p["w_out"],
        "aug": c["aug"], "mask4": c["mask4"], "tri": c["tri"], "ident": c["ident"], "identb": c["identb"], "negm4": c["negm4"],
        "conv_wT": np.ascontiguousarray(p["conv_w"].T.reshape(12, 128, 4).transpose(1, 0, 2)),
        "conv_b2": np.ascontiguousarray(p["conv_b"].reshape(12, 128).T),
        "dt_bias": p["dt_bias"], "a_log": p["a_log"], "d_skip": p["d_skip"], "ssm_norm_g": p["ssm_norm_g"],
        "att_norm_g2": np.ascontiguousarray(p["att_norm_g"].reshape(8, 128).T),
        "ssm_norm_g2": np.ascontiguousarray(p["ssm_norm_g"].reshape(8, 128).T),
        "ln_g": p["ln_g"], "ln_b": p["ln_b"],
    }


def kernel(x, w_in, conv_w, conv_b, dt_bias, a_log, d_skip, att_norm_g, ssm_norm_g, w_out, ln_g, ln_b):
    x = np.asarray(x, np.float32)
    p = {"w_in": w_in, "conv_w": conv_w, "conv_b": conv_b, "dt_bias": dt_bias, "a_log": a_log, "d_skip": d_skip,
         "att_norm_g": att_norm_g, "ssm_norm_g": ssm_norm_g, "w_out": w_out, "ln_g": ln_g, "ln_b": ln_b}
    p = {k: np.ascontiguousarray(np.asarray(v, np.float32)[0]) for k, v in p.items()}
    c = make_constants()
    n = x.shape[0]
    if "nc" not in _NC_CACHE:
        _NC_CACHE["nc"] = build_program()
    nc = _NC_CACHE["nc"]
    in_maps = [_host_inputs(x[b], p, c) for b in range(n)]
    res = run_bass_kernel_spmd(nc, in_maps, core_ids=list(range(n)))
    return np.stack([np.asarray(r["out"], np.float32) for r in res.results], axis=0)


def phase_B2(nc, P, banks, XT, w_in_v, dr, U, n_sc=8):
    T = Trk(P)
    with contextlib.ExitStack() as st:
        def sb(name, shape, dt, stack=None):
            return (stack or st).enter_context(nc.sbuf_tensor("B_" + name, shape, dt))

        rXT = Res()
        Wz = sb("Wz", [128, 8, 1024], BF16)
        Wx = sb("Wx", [128, 8, 1536], BF16)
        Wdt = sb("Wdt", [128, 8, 16], BF16)
        tri = sb("tri", [128, 128], F32)
        onesF = sb("onesF", [128, 128], F32)
        identF = sb("identF", [128, 128], F32)
        identB = sb("identB", [128, 128], BF16)
        negm4 = sb("negm4", [128, 512], BF16)
        cw = sb("cw", [128, 12, 4], F32)
        cb = sb("cb", [128, 12], F32)
        dtb_b = sb("dtb_b", [128, 16], F32)
        a_b = sb("a_b", [128, 16], F32)
        dsk_b = sb("dsk_b", [128, 16], F32)
        hal = sb("hal", [128, 12, 3], F32)
        rW = Res(); rC = Res()

        with contextlib.ExitStack() as stw:
            wsts = [sb(f"wstB{i}", [128, 8, 512], F32, stw) for i in range(3)]
            rwsts = [Res(), Res(), Res()]
            pieces = [(OFF_ZS, 0, 512, Wz), (OFF_ZS + 512, 512, 512, Wz),
                      (OFF_XBC, 0, 512, Wx), (OFF_XBC + 512, 512, 512, Wx), (OFF_XBC + 1024, 1024, 512, Wx),
                      (OFF_DT, 0, 16, Wdt)]
            for i, (off, dst0, n, Wt) in enumerate(pieces):
                wst, rwst = wsts[i % 3], rwsts[i % 3]
                T.dma("sync" if i % 2 == 0 else "gpsimd", wst[:, :, 0:n], w_in_v[:, :, off:off + n], f"ld_wB{i % 3}{i % 2}", writes=[rwst])
                T.op("vector", lambda e, o=Wt[:, 0:4, dst0:dst0 + n], i_=wst[:, 0:4, 0:n]: e.tensor_copy(out=o, in_=i_),
                     reads=[rwst], writes=[rW])
                T.op("scalar", lambda e, o=Wt[:, 4:8, dst0:dst0 + n], i_=wst[:, 4:8, 0:n]: e.activation(out=o, in_=i_, func=AF.Copy),
                     reads=[rwst], writes=[])
            T.dma("sync", tri[:], dr["tri"][:, :], "ld_c1", writes=[rC])
            T.dma("sync", identF[:], dr["ident"][:, :], "ld_c2", writes=[rC])
            T.dma("sync", negm4[:], dr["negm4"][:, :], "ld_c3", writes=[rC])
            T.dma("sync", cw[:], dr["conv_wT"][:, :, :], "ld_c4", writes=[rC])
            T.dma("sync", cb[:], dr["conv_b2"][:, :], "ld_c5", writes=[rC])
            T.dma("sync", dtb_b[:], dr["dt_bias"].partition_broadcast(128), "ld_c6", writes=[rC])
            T.dma("sync", a_b[:], dr["a_log"].partition_broadcast(128), "ld_c7", writes=[rC])
            T.dma("sync", dsk_b[:], dr["d_skip"].partition_broadcast(128), "ld_c8", writes=[rC])
            barrier(P)
            I_memset(P, "vector", onesF[:], 1.0)
            I_memset(P, "vector", hal[:], 0.0)
            I_copy(P, "vector", identB[:], identF[:])
            I_act(P, a_b[:], a_b[:], AF.Exp)
            barrier(P)
            I_ts(P, "vector", a_b[:], a_b[:], -1.0, None, ALU.mult)
            barrier(P)

        xb = [sb(f"xb{i}", [128, 515], F32) for i in range(2)]
        cvb = [sb(f"cvb{i}", [128, 512], F32) for i in range(2)]
        xtmp = [sb(f"xtmp{i}", [128, 512], F32) for i in range(2)]
        BT = sb("BT", [128, 2, 512], BF16)
        CT = sb("CT", [128, 2, 512], BF16)
        xs_tok = sb("xs_tok", [128, 4, 1024], F32)
        zs = sb("zs", [128, 4, 1024], F32)
        Btok = sb("Btok", [128, 4, 2, 128], BF16)
        Gm = sb("Gm", [128, 2, 4, 128], BF16)
        sm = {k: sb(k, [128, 64], F32) for k in ("dtx", "dt4", "adt4", "nacs4", "D04", "dte4", "ddt4", "cdb4")}
        xg = [sb(f"xg{i}", [128, 1024], BF16) for i in range(2)]
        xgd = [sb(f"xgd{i}", [128, 1024], BF16) for i in range(2)]
        Dm = [sb(f"Dm{i}", [128, 512], BF16) for i in range(2)]
        MT = [sb(f"MT{i}", [128, 512], BF16) for i in range(2)]
        H = sb("H", [128, 2, 512], F32)
        Hbf = sb("Hbf", [128, 2, 512], BF16)
        yo = [sb(f"yo{i}", [128, 512], F32) for i in range(2)]
        tsk = [sb(f"tsk{i}", [128, 512], F32) for i in range(4)]
        y = [sb(f"y{i}", [128, 1024], F32) for i in range(1)]
        ssq = [sb(f"ssq{i}", [128, 1], F32) for i in range(2)]
        rstd = [sb(f"rstd{i}", [128, 1], F32) for i in range(2)]
        mixn = [sb(f"mixn{i}", [128, 1024], BF16) for i in range(2)]
        mixT = [sb(f"mixT{i}", [128, 8, 512], BF16) for i in range(1)]

        R_ = lambda n: [Res() for _ in range(n)]
        rxb, rxbh, rcv, rxtmp = R_(2), R_(2), R_(2), R_(2)
        rhal = R_(12)
        rBT, rCT, rxs, rzs, rBtok, rGm = Res(), Res(), Res(), Res(), Res(), Res()
        rsm = {k: Res() for k in sm}
        rxg, rxgd, rDm, rMT, ryo, rtsk, ry, rssq, rrstd, rmixn = R_(2), R_(2), R_(2), R_(2), R_(2), R_(4), R_(1), R_(2), R_(2), R_(2)
        rmixT = [R_(2)]
        rH, rHbf = R_(2), R_(2)
        rbank = R_(8)
        pb0, pb1, smb, trb, Rb0, Rb1, Yb, osb = banks
        B_PB0, B_PB1, B_SM, B_TR, B_R0, B_R1, B_Y, B_OS = range(8)
        trb_b = trb[:].bitcast(BF16)

        I_memset(P, "vector", H[:], 0.0)
        I_memset(P, "vector", Hbf[:], 0.0)
        barrier(P)

        def v3(ap):
            return ap.rearrange("p (h e) -> p h e", e=64)

        pbi = 0
        for sc in range(n_sc):
            T0 = 512 * sc
            def stage_A(ft):
                nonlocal pbi
                b = pbi; pbi ^= 1
                pbk = banks[b]
                xbi = ft % 2
                T.mm_group([(pbk[:, :], Wx[:, kc, ft * 128:(ft + 1) * 128], XT[:, kc, T0:T0 + 512], kc == 0, kc == 7)
                            for kc in range(8)], reads=[rW, rXT], writes=[rbank[b]])
                T.op("scalar", lambda e, o=xb[xbi][:, 3:515], i_=pbk[:, :]: e.activation(out=o, in_=i_, func=AF.Copy),
                     reads=[rbank[b]], writes=[rxb[xbi]])
                cv = cvb[xbi]
                T.op("scalar", lambda e, o=cv[:, :], i_=pbk[:, :], s_=cw[:, ft, 3:4]: e.activation(out=o, in_=i_, func=AF.Copy, scale=s_),
                     reads=[rbank[b], rC], writes=[rcv[xbi]])
                T.op("gpsimd", lambda e, o=xb[xbi][:, 0:3], i_=hal[:, ft, :]: e.tensor_copy(out=o, in_=i_),
                     reads=[rhal[ft]], writes=[rxbh[xbi]])
                T.op("gpsimd", lambda e, o=hal[:, ft, :], i_=xb[xbi][:, 512:515]: e.tensor_copy(out=o, in_=i_),
                     reads=[rxb[xbi]], writes=[rhal[ft]])

            def stage_B(ft):
                xbi = ft % 2
                cv = cvb[xbi]
                for j in range(3):
                    T.op("vector", lambda e, o=cv[:, :], i_=xb[xbi][:, j:j + 512], s_=cw[:, ft, j:j + 1]:
                         e.scalar_tensor_tensor(out=o, in0=i_, scalar=s_, in1=o, op0=ALU.mult, op1=ALU.add),
                         reads=[rxb[xbi], rxbh[xbi]], writes=[rcv[xbi]])
                if ft < 8:
                    xi = ft % 2
                    T.op("scalar", lambda e, o=xtmp[xi][:, :], i_=cv[:, :], b_=cb[:, ft:ft + 1]:
                         e.activation(out=o, in_=i_, func=AF.Silu, bias=b_), reads=[rcv[xbi], rC], writes=[rxtmp[xi]])
                elif ft < 10:
                    T.op("scalar", lambda e, o=BT[:, ft - 8, :], i_=cv[:, :], b_=cb[:, ft:ft + 1]:
                         e.activation(out=o, in_=i_, func=AF.Silu, bias=b_), reads=[rcv[xbi], rC], writes=[rBT])
                else:
                    T.op("scalar", lambda e, o=CT[:, ft - 10, :], i_=cv[:, :], b_=cb[:, ft:ft + 1]:
                         e.activation(out=o, in_=i_, func=AF.Silu, bias=b_), reads=[rcv[xbi], rC], writes=[rCT])
            def stage_T(ft):
                if ft < 0 or ft >= 8:
                    return
                xi = ft % 2
                tbk = B_TR if ft % 2 == 0 else B_Y
                ev = None
                for ci in range(4):
                    ev = P.op("tensor", lambda e, o=banks[tbk][:, 128 * ci:128 * (ci + 1)], i_=xtmp[xi][:, 128 * ci:128 * (ci + 1)]:
                              e.transpose(o, i_, identF[:]), T.deps([rxtmp[xi], rC], [rbank[tbk]]) if ci == 0 else ())
                T.mark(ev, [rxtmp[xi], rC], [rbank[tbk]])

            def stage_C(ft):
                if ft < 0 or ft >= 8:
                    return
                tbk = B_TR if ft % 2 == 0 else B_Y
                T.op("scalar", lambda e, o=xs_tok[:, :, ft * 128:(ft + 1) * 128], i_=banks[tbk][:, :].rearrange("p (c f) -> p c f", f=128):
                     e.activation(out=o, in_=i_, func=AF.Copy), reads=[rbank[tbk]], writes=[rxs])

            def z_group(zi):
                nonlocal pbi
                ci, half = divmod(zi, 2)
                t0 = T0 + 128 * ci
                b = pbi; pbi ^= 1
                T.mm_group([(banks[b][:, :], XT[:, kc, t0:t0 + 128], Wz[:, kc, 512 * half:512 * (half + 1)], kc == 0, kc == 7)
                            for kc in range(8)], reads=[rW, rXT], writes=[rbank[b]])
                T.op("scalar", lambda e, o=zs[:, ci, 512 * half:512 * (half + 1)], i_=banks[b][:, :]:
                     e.activation(out=o, in_=i_, func=AF.Silu), reads=[rbank[b]], writes=[rzs])

            stage_A(0)
            for ft in range(13):
                if ft + 1 < 12:
                    stage_A(ft + 1)
                stage_T(ft - 1)
                if ft < 12:
                    stage_B(ft)
                stage_C(ft - 1)
                if 2 <= ft < 10:
                    z_group(ft - 2)
            ev = None
            for ci in range(4):
                for g in range(2):
                    j = 2 * ci + g
                    ev = P.op("tensor", lambda e, o=trb_b[:, 128 * j:128 * (j + 1)], i_=BT[:, g, 128 * ci:128 * (ci + 1)]:
                              e.transpose(o, i_, identB[:]), T.deps([rBT, rC], [rbank[B_TR]]) if j == 0 else ())
            T.mark(ev, [rBT, rC], [rbank[B_TR]])
            T.op("vector", lambda e: e.tensor_copy(out=Btok[:].rearrange("p c g n -> p (c g n)"), in_=trb_b[:, 0:1024]),
                 reads=[rbank[B_TR]], writes=[rBtok])
            for g in range(2):
                bk = B_R0 + g
                T.mm_group([(banks[bk][:, 128 * ci:128 * (ci + 1)], BT[:, g, 128 * ci:128 * (ci + 1)], CT[:, g, 128 * ci:128 * (ci + 1)],
                             True, True) for ci in range(4)], reads=[rBT, rCT], writes=[rbank[bk]])
                T.op("vector", lambda e, o=Gm[:, g, :, :].rearrange("p c l -> p (c l)"), i_=banks[bk][:, :]: e.tensor_copy(out=o, in_=i_),
                     reads=[rbank[bk]], writes=[rGm])
            mms = []
            for ci in range(4):
                t0 = T0 + 128 * ci
                for kc in range(8):
                    mms.append((smb[:, 16 * ci:16 * (ci + 1)], XT[:, kc, t0:t0 + 128], Wdt[:, kc, :], kc == 0 and ci == 0, kc == 7))
            T.mm_group(mms, reads=[rW, rXT], writes=[rbank[B_SM]])
            b16 = lambda ap: ap[:, :].unsqueeze(1).broadcast_to([128, 4, 16])
            c4 = lambda ap: ap[:, :].rearrange("p (c h) -> p c h", h=16)
            T.op("vector", lambda e: e.tensor_tensor(out=c4(sm["dtx"]), in0=c4(smb[:, 0:64]), in1=b16(dtb_b), op=ALU.add),
                 reads=[rbank[B_SM], rC], writes=[rsm["dtx"]])
            T.op("scalar", lambda e: e.activation(out=sm["dtx"][:], in_=sm["dtx"][:], func=AF.Exp), reads=[], writes=[rsm["dtx"]])
            T.op("scalar", lambda e: e.activation(out=sm["dt4"][:], in_=sm["dtx"][:], func=AF.Ln, bias=1.0),
                 reads=[rsm["dtx"]], writes=[rsm["dt4"]])
            T.op("vector", lambda e: e.tensor_tensor(out=c4(sm["adt4"]), in0=c4(sm["dt4"]), in1=b16(a_b), op=ALU.mult),
                 reads=[rsm["dt4"], rC], writes=[rsm["adt4"]])
            mms = []
            for ci in range(4):
                mms.append((smb[:, 64 + 16 * ci:64 + 16 * (ci + 1)], tri[:], sm["adt4"][:, 16 * ci:16 * (ci + 1)], False, True))
                mms.append((smb[:, 128 + 16 * ci:128 + 16 * (ci + 1)], onesF[:], sm["adt4"][:, 16 * ci:16 * (ci + 1)], False, True))
            T.mm_group(mms, reads=[rsm["adt4"], rC], writes=[rbank[B_SM]])
            T.op("vector", lambda e: e.tensor_scalar(out=sm["nacs4"][:], in0=smb[:, 64:128], scalar1=-1.0, scalar2=None, op0=ALU.mult),
                 reads=[rbank[B_SM]], writes=[rsm["nacs4"]])
            T.op("vector", lambda e: e.tensor_copy(out=sm["cdb4"][:], in_=smb[:, 128:192]), reads=[rbank[B_SM]], writes=[rsm["cdb4"]])
            T.op("vector", lambda e: e.tensor_tensor(out=sm["dte4"][:], in0=sm["cdb4"][:], in1=sm["nacs4"][:], op=ALU.add),
                 reads=[rsm["cdb4"], rsm["nacs4"]], writes=[rsm["dte4"]])
            T.op("scalar", lambda e: e.activation(out=sm["D04"][:], in_=sm["nacs4"][:], func=AF.Exp, scale=-1.0),
                 reads=[rsm["nacs4"]], writes=[rsm["D04"]])
            T.op("scalar", lambda e: e.activation(out=sm["dte4"][:], in_=sm["dte4"][:], func=AF.Exp), reads=[], writes=[rsm["dte4"]])
            T.op("scalar", lambda e: e.activation(out=sm["cdb4"][:], in_=sm["cdb4"][:], func=AF.Exp), reads=[rsm["dte4"]], writes=[rsm["cdb4"]])
            T.op("vector", lambda e: e.tensor_tensor(out=sm["ddt4"][:], in0=sm["dt4"][:], in1=sm["dte4"][:], op=ALU.mult),
                 reads=[rsm["dt4"], rsm["dte4"]], writes=[rsm["ddt4"]])

            def chunk_fns(ci):
                k = ci % 2
                cc = slice(128 * ci, 128 * ci + 128)
                hs = lambda ap, h0, n: ap[:, 16 * ci + h0:16 * ci + h0 + n]
                bch = lambda ap, h0, n: hs(ap, h0, n).unsqueeze(2).broadcast_to([128, n, 64])
                def prologue(cj):
                    kj = cj % 2
                    for g in range(2):
                        gs = slice(512 * g, 512 * (g + 1))
                        bcj = lambda ap, h0, n: ap[:, 16 * cj + h0:16 * cj + h0 + n].unsqueeze(2).broadcast_to([128, n, 64])
                        T.op("gpsimd", lambda e, o=v3(xg[kj][:, gs]), i_=v3(xs_tok[:, cj, gs]), b_=bcj(sm["dt4"], 8 * g, 8):
                             e.tensor_tensor(out=o, in0=i_, in1=b_, op=ALU.mult), reads=[rxs, rsm["dt4"]], writes=[rxg[kj]])
                        T.op("gpsimd", lambda e, o=v3(xgd[kj][:, gs]), i_=v3(xs_tok[:, cj, gs]), b_=bcj(sm["ddt4"], 8 * g, 8):
                             e.tensor_tensor(out=o, in0=i_, in1=b_, op=ALU.mult), reads=[rxs, rsm["ddt4"]], writes=[rxgd[kj]])
                        T.op("gpsimd", lambda e, o=v3(tsk[2 * kj + g][:, :]), i_=v3(xs_tok[:, cj, gs]), b_=bc64(dsk_b, 8 * g, 8):
                             e.tensor_tensor(out=o, in0=i_, in1=b_, op=ALU.mult), reads=[rxs, rC], writes=[rtsk[2 * kj + g]])

                def emit_R(q):
                    g, qd = divmod(q, 2)
                    h0 = 8 * g + 4 * qd
                    bk = B_R0 + (q % 2)
                    mms = [(banks[bk][:, :], identB[:], negm4[:], True, False)]
                    for j in range(4):
                        mms.append((banks[bk][:, 128 * j:128 * (j + 1)],
                                    hs(sm["adt4"], h0 + j, 1).broadcast_to([128, 128]), tri[:], False, j == 3))
                    T.mm_group(mms, reads=[rsm["adt4"], rC], writes=[rbank[bk]])

                def emit_exp(q):
                    g, qd = divmod(q, 2)
                    h0 = 8 * g + 4 * qd
                    bk = B_R0 + (q % 2)
                    for j in range(4):
                        T.op("scalar", lambda e, o=Dm[q % 2][:, 128 * j:128 * (j + 1)], i_=banks[bk][:, 128 * j:128 * (j + 1)],
                             b_=hs(sm["nacs4"], h0 + j, 1): e.activation(out=o, in_=i_, func=AF.Exp, bias=b_),
                             reads=[rbank[bk], rsm["nacs4"]], writes=[rDm[q % 2]])

                def emit_MT(q):
                    g, qd = divmod(q, 2)
                    T.op("vector", lambda e, o=MT[q % 2][:].rearrange("p (j l) -> p j l", l=128),
                         i_=Dm[q % 2][:].rearrange("p (j l) -> p j l", l=128),
                         b_=Gm[:, g, ci, :].unsqueeze(1).broadcast_to([128, 4, 128]):
                         e.tensor_tensor(out=o, in0=i_, in1=b_, op=ALU.mult), reads=[rDm[q % 2], rGm], writes=[rMT[q % 2]])

                def emit_ydiag(q):
                    g, qd = divmod(q, 2)
                    h0 = 8 * g + 4 * qd
                    ybk = B_Y if g == 0 else B_PB0
                    mms = []
                    for j in range(4):
                        hh = 4 * qd + j
                        mms.append((banks[ybk][:, 64 * hh:64 * (hh + 1)], MT[q % 2][:, 128 * j:128 * (j + 1)],
                                    xg[k][:, 64 * (h0 + j):64 * (h0 + j + 1)], (qd == 0 and j == 0), False))
                    T.mm_group(mms, reads=[rMT[q % 2], rxg[k]], writes=[rbank[ybk]])

                def emit_yoff(g):
                    obk = B_OS if g == 0 else B_PB1
                    T.mm_group([(banks[obk][:, :], CT[:, g, cc], Hbf[:, g, :], True, True)], reads=[rCT, rHbf[g]], writes=[rbank[obk]])

                def emit_comb(g):
                    gs = slice(512 * g, 512 * (g + 1))
                    obk = B_OS if g == 0 else B_PB1
                    ybk = B_Y if g == 0 else B_PB0
                    T.op("vector", lambda e, o=v3(yo[g][:, :]), i_=v3(banks[obk][:, :]), b_=bch(sm["D04"], 8 * g, 8):
                         e.tensor_tensor(out=o, in0=i_, in1=b_, op=ALU.mult), reads=[rbank[obk], rsm["D04"]], writes=[ryo[g]])
                    T.op("vector", lambda e, o=y[0][:, gs], i_=banks[ybk][:, :]: e.tensor_tensor(out=o, in0=i_, in1=yo[g][:, :], op=ALU.add),
                         reads=[rbank[ybk], ryo[g]], writes=[ry[0]])
                    T.op("gpsimd", lambda e, o=y[0][:, gs], t_=tsk[2 * k + g][:, :]: e.tensor_tensor(out=o, in0=o, in1=t_, op=ALU.add),
                         reads=[rtsk[2 * k + g]], writes=[ry[0]])

                def emit_state(g):
                    gs = slice(512 * g, 512 * (g + 1))
                    obk = B_OS if g == 0 else B_PB1
                    T.mm_group([(banks[obk][:, :], Btok[:, ci, g, :], xgd[k][:, gs], True, True)],
                               reads=[rBtok, rxgd[k]], writes=[rbank[obk]])
                    T.op("vector", lambda e, o=v3(H[:, g, :]), b_=bch(sm["cdb4"], 8 * g, 8):
                         e.tensor_tensor(out=o, in0=o, in1=b_, op=ALU.mult), reads=[rsm["cdb4"]], writes=[rH[g]])
                    T.op("vector", lambda e, o=H[:, g, :], i_=banks[obk][:, :]: e.tensor_tensor(out=o, in0=i_, in1=o, op=ALU.add),
                         reads=[rbank[obk]], writes=[rH[g]])
                    T.op("scalar", lambda e, o=Hbf[:, g, :], i_=H[:, g, :]: e.activation(out=o, in_=i_, func=AF.Copy),
                         reads=[rH[g]], writes=[rHbf[g]])

                def early():
                    emit_R(0); emit_R(1)
                    emit_yoff(0); emit_yoff(1)
                    emit_exp(0); emit_exp(1)
                    emit_MT(0); emit_MT(1)
                    emit_R(2); emit_R(3)
                    emit_ydiag(0); emit_ydiag(1)
                    emit_exp(2); emit_exp(3)
                    emit_MT(2); emit_MT(3)
                    emit_ydiag(2); emit_ydiag(3)

                def mid():
                    if ci + 1 < 4:
                        prologue(ci + 1)
                    emit_comb(0)
                    emit_state(0)
                    emit_comb(1)
                    emit_state(1)

                def mid_g(g):
                    emit_comb(g)
                    emit_state(g)

                def exps_g(g):
                    emit_R(2 * g); emit_R(2 * g + 1)
                    emit_yoff(g)
                    emit_exp(2 * g); emit_exp(2 * g + 1)

                def mt_g(g):
                    emit_MT(2 * g); emit_MT(2 * g + 1)
                    emit_ydiag(2 * g); emit_ydiag(2 * g + 1)

                def late():
                    yk, ssk, rsk, mxk = y[0], ssq[k], rstd[k], mixn[k]
                    T.op("vector", lambda e, yk=yk, z_=zs[:, ci, :]: e.tensor_tensor(out=yk[:], in0=yk[:], in1=z_, op=ALU.mult),
                         reads=[rzs], writes=[ry[0]])
                    T.op("vector", lambda e, ssk=ssk: e.memset(ssk[:], 0.0), reads=[], writes=[rssq[k]])
                    T.op("scalar", lambda e, yk=yk, ssk=ssk, mxk=mxk: e.activation(out=mxk[:], in_=yk[:], func=AF.Square, accum_out=ssk[:]),
                         reads=[ry[0]], writes=[rmixn[k], rssq[k]])
                    T.op("scalar", lambda e, ssk=ssk, rsk=rsk: e.activation(out=rsk[:], in_=ssk[:], func=AF.Ln, bias=EPS, scale=1.0 / 1024),
                         reads=[rssq[k]], writes=[rrstd[k]])
                    T.op("scalar", lambda e, rsk=rsk: e.activation(out=rsk[:], in_=rsk[:], func=AF.Exp, scale=-0.5),
                         reads=[], writes=[rrstd[k]])
                    T.op("vector", lambda e, yk=yk, rsk=rsk, mxk=mxk: e.tensor_scalar(out=mxk[:], in0=yk[:], scalar1=rsk[:, 0:1], scalar2=None, op0=ALU.mult),
                         reads=[ry[0], rrstd[k]], writes=[rmixn[k]])
                    mt = mixT[0]
                    for half in range(2):
                        ev = None
                        for j in range(4):
                            ft = 4 * half + j
                            ev = P.op("tensor", lambda e, o=trb_b[:, 128 * j:128 * (j + 1)], i_=mixn[k][:, 128 * ft:128 * (ft + 1)]:
                                      e.transpose(o, i_, identB[:]), T.deps([rmixn[k], rC], [rbank[B_TR]]) if j == 0 else ())
                        T.mark(ev, [rmixn[k], rC], [rbank[B_TR]])
                        src = trb_b[:, 0:512].rearrange("p (j t) -> p j t", t=128)
                        if half == 0:
                            T.op("vector", lambda e, o=mt[:, 0:4, cc], i_=src: e.tensor_copy(out=o, in_=i_),
                                 reads=[rbank[B_TR]], writes=[rmixT[0][0]])
                        else:
                            T.op("scalar", lambda e, o=mt[:, 4:8, cc], i_=src: e.activation(out=o, in_=i_, func=AF.Copy),
                                 reads=[rbank[B_TR]], writes=[rmixT[0][1]])
                return prologue, early, mid, late, mid_g, exps_g, mt_g

            fns = [chunk_fns(ci) for ci in range(4)]
            fns[0][0](0)
            fns[0][1]()
            for ci in range(4):
                if ci + 1 < 4:
                    nx = fns[ci + 1]
                    fns[ci][0](ci + 1)
                    fns[ci][4](0)
                    nx[5](0)
                    fns[ci][4](1)
                    nx[6](0)
                    nx[5](1)
                    fns[ci][3]()
                    nx[6](1)
                else:
                    fns[ci][2]()
                    fns[ci][3]()
            T.dma("gpsimd", U[1024:2048, T0:T0 + 512].rearrange("(f p) t -> p f t", p=128), mixT[0][:, :, :], "st_uB",
                  reads=rmixT[0])
        barrier(P)
```

```python
import contextlib
import os
_SKIP = set(os.environ.get('KSKIP', '').split(','))
import numpy as np
import ml_dtypes
import concourse.bass as bass
import concourse.mybir as mybir
from concourse.bass_utils import run_bass_kernel_spmd

F32 = mybir.dt.float32
BF16 = mybir.dt.bfloat16
AF = mybir.ActivationFunctionType
ALU = mybir.AluOpType
AX = mybir.AxisListType

S = 4096
D = 1024
NH = 16
HD = 64
DIL = (1, 4, 16)
D_IN = 6672
OFF_Q, OFF_K, OFF_V, OFF_ZA, OFF_ZS, OFF_XBC, OFF_DT = 0, 1024, 2048, 3072, 4096, 5120, 6656
EPS = 1e-5


class Ev:
    __slots__ = ("eng", "idx", "sem", "val")

    def __init__(self, eng, idx, sem=None, val=None):
        self.eng, self.idx, self.sem, self.val = eng, idx, sem, val


class Prog:
    ENGS = ("sync", "scalar", "vector", "gpsimd", "tensor")

    def __init__(self, nc):
        self.nc = nc
        self.q = {e: [] for e in self.ENGS}
        self.dma_cnt = {}

    def op(self, eng, fn, deps=()):
        lst = self.q[eng]
        ev = Ev(eng, len(lst))
        lst.append([fn, [d for d in deps if d is not None], ev, False])
        return ev

    def dma(self, eng, out, in_, slot, deps=()):
        self.dma_cnt[slot] = self.dma_cnt.get(slot, 0) + 16
        ev = Ev(eng, len(self.q[eng]), sem=slot, val=self.dma_cnt[slot])
        self.q[eng].append([lambda e, o=out, i=in_: e.dma_start(out=o, in_=i),
                            [d for d in deps if d is not None], ev, True])
        return ev

    def emit(self, final_waits):
        nc = self.nc
        ref = {e: set() for e in self.ENGS}
        for e in self.ENGS:
            for fn, deps, ev, is_dma in self.q[e]:
                for d in deps:
                    if d.sem is None or d.sem.startswith("e_"):
                        ref[d.eng].add(d.idx)
        for d in final_waits:
            if d.sem is None:
                ref[d.eng].add(d.idx)
        for e in self.ENGS:
            c = 0
            for i, item in enumerate(self.q[e]):
                if item[3]:
                    continue
                if i in ref[e]:
                    c += 1
                    item[2].sem = "e_" + e
                    item[2].val = c
            assert c < 60000, (e, c)
        for s, v in self.dma_cnt.items():
            assert v < 60000, (s, v)
        names = ["e_" + e for e in self.ENGS] + sorted(self.dma_cnt)
        with contextlib.ExitStack() as st:
            sems = {n: st.enter_context(nc.semaphore(n)) for n in names}
            block = st.enter_context(nc.Block())
            for e in self.ENGS:
                items = self.q[e]
                fw = final_waits if e == "sync" else ()

                def body(eng, items=items, fw=fw):
                    seen = {}
                    for fn, deps, ev, is_dma in items:
                        need = {}
                        for d in deps:
                            assert d.sem is not None
                            if need.get(d.sem, 0) < d.val:
                                need[d.sem] = d.val
                        for sname, v in need.items():
                            if seen.get(sname, 0) < v:
                                eng.wait_ge(sems[sname], v)
                                seen[sname] = v
                        ins = fn(eng)
                        if is_dma:
                            ins.then_inc(sems[ev.sem], 16)
                        elif ev.sem is not None:
                            ins.then_inc(sems[ev.sem], 1)
                    for d in fw:
                        if seen.get(d.sem, 0) < d.val:
                            eng.wait_ge(sems[d.sem], d.val)
                            seen[d.sem] = d.val

                getattr(block, e)(body)


def latest(*evs):
    return [e for e in evs if e is not None]


def _bf16(a):
    return np.asarray(a, np.float32).astype(ml_dtypes.bfloat16)


def make_constants():
    c = {}
    slopes = 2.0 ** (-8.0 * np.arange(1, NH + 1) / NH)
    t = np.arange(S)
    hi_pos = (t >> 7).astype(np.float32)
    lo_pos = (t & 127).astype(np.float32)
    aug = np.zeros((NH, 2, 12, S), np.float32)
    for h in range(NH):
        cc = np.float64(slopes[h])
        c1 = np.float64(_bf16(cc).astype(np.float64))
        c2 = np.float64(_bf16(cc - c1).astype(np.float64))
        c3 = np.float64(_bf16(cc - c1 - c2).astype(np.float64))
        for j, cj in enumerate((c1, c2, c3)):
            aug[h, 0, j] = 128.0 * cj
            aug[h, 0, 3 + j] = cj
            aug[h, 0, 6 + j] = hi_pos
            aug[h, 0, 9 + j] = lo_pos
            aug[h, 1, j] = hi_pos
            aug[h, 1, 3 + j] = lo_pos
            aug[h, 1, 6 + j] = -128.0 * cj
            aug[h, 1, 9 + j] = -cj
    c["aug"] = _bf16(aug)
    ki = np.arange(128)[:, None]
    qi = np.arange(128)[None, :]
    mprev = np.where(ki >= qi, 0.0, -30000.0).astype(np.float32)
    mcur = np.where(ki <= qi, 0.0, -30000.0).astype(np.float32)
    c["mask4"] = np.concatenate([mprev, mcur, mprev, mcur], axis=1).astype(np.float32)
    c["tri"] = np.triu(np.ones((128, 128), np.float32))
    c["ident"] = np.eye(128, dtype=np.float32)
    c["identb"] = _bf16(np.eye(128, dtype=np.float32))
    si = np.arange(128)[:, None]
    li = np.arange(128)[None, :]
    c["negm4"] = _bf16(np.tile(np.where(si > li, -30000.0, 0.0), (1, 4)))
    return c


def I_act(P, out, in_, func, deps=(), bias=None, scale=None, accum_out=None, eng="scalar"):
    kw = {}
    if bias is not None:
        kw["bias"] = bias
    if scale is not None:
        kw["scale"] = scale
    if accum_out is not None:
        kw["accum_out"] = accum_out
    return P.op(eng, lambda e: e.activation(out=out, in_=in_, func=func, **kw), deps)


def I_copy(P, eng, out, in_, deps=()):
    if eng == "scalar":
        return P.op(eng, lambda e: e.activation(out=out, in_=in_, func=AF.Copy), deps)
    return P.op(eng, lambda e: e.tensor_copy(out=out, in_=in_), deps)


def I_tt(P, eng, out, in0, in1, op, deps=()):
    return P.op(eng, lambda e: e.tensor_tensor(out=out, in0=in0, in1=in1, op=op), deps)


def I_ts(P, eng, out, in0, s1, s2, op0, op1=None, deps=(), accum_out=None):
    kw = {}
    if op1 is not None:
        kw["op1"] = op1
    if accum_out is not None:
        kw["accum_out"] = accum_out
    return P.op(eng, lambda e: e.tensor_scalar(out=out, in0=in0, scalar1=s1, scalar2=s2, op0=op0, **kw), deps)


def I_stt(P, eng, out, in0, scalar, in1, op0, op1, deps=()):
    return P.op(eng, lambda e: e.scalar_tensor_tensor(out=out, in0=in0, scalar=scalar, in1=in1, op0=op0, op1=op1), deps)


def I_mm(P, out, lhsT, rhs, start, stop, deps=(), skip=True):
    return P.op("tensor", lambda e: e.matmul(out, lhsT=lhsT, rhs=rhs, start=start, stop=stop,
                                             skip_group_check=skip), deps)


def I_memset(P, eng, ap, val, deps=()):
    return P.op(eng, lambda e: e.memset(ap, val), deps)


def barrier(P):
    evs = []
    for e in P.ENGS:
        for item in reversed(P.q[e]):
            if not item[3]:
                evs.append(item[2])
                break
    last = {}
    for e in P.ENGS:
        for item in P.q[e]:
            if item[3]:
                last[item[2].sem] = item[2]
    evs += list(last.values())
    out = []
    for e in P.ENGS:
        out.append(P.op(e, lambda eng: eng.nop(), evs))
    return out


def tok_ap(t, d, r, n, cnt=128, lo=0, hi=None):
    start = d * (128 * n + lo) + r
    stop = start + d * (cnt - 1) + 1
    return slice(start, stop, d)


def build_program(debug_u=False, pairs=tuple(range(8)), phases=("A", "B", "C"), dbg=3, n_sc=8):
    nc = bass.Bass("TRN2", target_bir_lowering=False)
    xT = nc.dram_tensor("xT", [D, S], F32, kind="ExternalInput").ap()
    w_in = nc.dram_tensor("w_in", [D, D_IN], F32, kind="ExternalInput").ap()
    aug = nc.dram_tensor("aug", [NH, 2, 12, S], BF16, kind="ExternalInput").ap()
    mask4_d = nc.dram_tensor("mask4", [128, 512], F32, kind="ExternalInput").ap()
    dr = {}
    for name, shape, dt in (("tri", [128, 128], F32), ("ident", [128, 128], F32), ("identb", [128, 128], BF16), ("negm4", [128, 512], BF16),
                            ("conv_wT", [128, 12, 4], F32), ("conv_b2", [128, 12], F32), ("dt_bias", [16], F32),
                            ("a_log", [16], F32), ("d_skip", [16], F32), ("ssm_norm_g", [1024], F32),
                            ("att_norm_g2", [128, 8], F32), ("ssm_norm_g2", [128, 8], F32), ("ln_g", [1024], F32), ("ln_b", [1024], F32),
                            ("w_out", [2048, D], F32), ("x", [S, D], F32)):
        dr[name] = nc.dram_tensor(name, shape, dt, kind="ExternalInput").ap()
    U = nc.dram_tensor("U", [2048, S], BF16, kind="ExternalOutput" if debug_u else "Internal").ap()
    out_d = nc.dram_tensor("out", [S, D], F32, kind="ExternalOutput").ap()

    P = Prog(nc)
    final_waits = []
    with contextlib.ExitStack() as st:
        def sb(name, shape, dt, stack=st):
            return stack.enter_context(nc.sbuf_tensor(name, shape, dt))

        banks = [st.enter_context(nc.psum_tensor(f"bank{i}", [128, 512], F32)) for i in range(8)]
        w_in_v = w_in.rearrange("(kc p) c -> p kc c", p=128)
        with contextlib.ExitStack() as stx:
            XT = sb("XT", [128, 8, S], BF16, stx)
            with contextlib.ExitStack() as st0:
                xstage = [sb(f"xstage{i}", [128, S], F32, st0) for i in range(2)]
                free = [[], []]
                for kc in range(8):
                    b = kc % 2
                    ld = P.dma("sync", xstage[b][:], xT[kc * 128:(kc + 1) * 128, :], f"ld_x{b}", deps=free[b])
                    e1 = I_copy(P, "vector", XT[:, kc, 0:1536], xstage[b][:, 0:1536], [ld])
                    e2 = I_copy(P, "scalar", XT[:, kc, 1536:3072], xstage[b][:, 1536:3072], [ld])
                    e3 = I_copy(P, "gpsimd", XT[:, kc, 3072:4096], xstage[b][:, 3072:4096], [ld])
                    free[b] = [e1, e2, e3]
                barrier(P)

            if "A" in phases:
                phase_A(nc, P, st, banks, XT, w_in_v, aug, mask4_d, U, pairs, dbg, dr)
                barrier(P)
            if "B" in phases:
                (phase_B if 'oldB' in _SKIP else phase_B2)(nc, P, banks, XT, w_in_v, dr, U, n_sc)
                barrier(P)
        if "C" in phases:
            phase_C(nc, P, banks, dr, U, out_d)
            barrier(P)
        last = {}
        for e in P.ENGS:
            for item in P.q[e]:
                if item[3]:
                    last[item[2].sem] = item[2]
        final_waits = list(last.values())
        P.emit(final_waits)
    return nc


def phase_C(nc, P, banks, dr, U, out_d, n_tg=8):
    T = Trk(P)
    ALPHA = 2.0 ** 0.25
    with contextlib.ExitStack() as st:
        def sb(name, shape, dt, stack=None):
            return (stack or st).enter_context(nc.sbuf_tensor("C_" + name, shape, dt))

        Wo = sb("Wo", [128, 16, D], BF16)
        gA = sb("gA", [128, 16], F32)
        lg_b = sb("lg_b", [128, D], F32)
        lb_b = sb("lb_b", [128, D], F32)
        onesB = sb("onesB", [128, 1], BF16)
        rC = Res(); rWo = Res()
        T.dma("sync", gA[:, 0:8], dr["att_norm_g2"][:, :], "ld_d1", writes=[rC])
        T.dma("sync", gA[:, 8:16], dr["ssm_norm_g2"][:, :], "ld_d1b", writes=[rC])
        T.dma("sync", lg_b[:], dr["ln_g"].partition_broadcast(128), "ld_d2", writes=[rC])
        T.dma("sync", lb_b[:], dr["ln_b"].partition_broadcast(128), "ld_d3", writes=[rC])
        I_memset(P, "vector", onesB[:], 1.0)
        wo_v = dr["w_out"].rearrange("(f p) d -> p f d", p=128)
        with contextlib.ExitStack() as stw:
            wst = [sb(f"wstC{i}", [128, 2, D], F32, stw) for i in range(4)]
            rws = [Res() for _ in range(4)]
            for i in range(8):
                b = i % 4
                T.dma("sync" if i % 2 == 0 else "gpsimd", wst[b][:], wo_v[:, 2 * i:2 * i + 2, :], f"ld_wo{b}", writes=[rws[b]])
                for j in range(2):
                    f = 2 * i + j
                    if j == 0:
                        T.op("vector", lambda e, o=Wo[:, f, :], i_=wst[b][:, j, :], s_=gA[:, f:f + 1]:
                             e.tensor_scalar(out=o, in0=i_, scalar1=s_, scalar2=None, op0=ALU.mult),
                             reads=[rws[b], rC], writes=[])
                    else:
                        T.op("scalar", lambda e, o=Wo[:, f, :], i_=wst[b][:, j, :], s_=gA[:, f:f + 1]:
                             e.activation(out=o, in_=i_, func=AF.Copy, scale=s_),
                             reads=[rws[b], rC], writes=[])
            barrier(P)

        Ub = [sb(f"Ub{i}", [128, 16, 512], BF16) for i in range(2)]
        xt = [sb(f"xt{i}", [128, 4, D], F32) for i in range(2)]
        sq = [sb(f"sq{i}", [128, 8, 128], BF16) for i in range(2)]
        rr = [sb(f"rr{i}", [128, D], F32) for i in range(2)]
        ot = [sb(f"ot{i}", [128, D], F32) for i in range(2)]
        st6 = [sb(f"st6{i}", [128, 2, 6], F32) for i in range(2)]
        mv = [sb(f"mv{i}", [128, 2], F32) for i in range(2)]
        ra = [sb(f"ra{i}", [128, 1], F32) for i in range(2)]
        rl = [sb(f"rl{i}", [128, 1], F32) for i in range(2)]
        rUa = [Res(), Res()]; rUs = [Res(), Res()]; rxt = [[Res() for _ in range(4)] for _ in range(2)]; rot = [Res(), Res()]
        r = [{k: Res() for k in ("sq", "rr", "st6", "mv", "ra", "rl")} for _ in range(2)]
        slots = [(banks[0], banks[1]), (banks[2], banks[3]), (banks[4], banks[5])]
        rslot = [[Res(), Res()] for _ in range(3)]
        ssb = [banks[6], banks[7]]; rss = [Res(), Res()]
        U_v = U.rearrange("(f p) t -> p f t", p=128)
        x_v = dr["x"].rearrange("(c p) d -> p c d", p=128)
        slot_i = 0
        cnt = 0
        for tg in range(n_tg):
            b = tg % 2
            T0 = 512 * tg
            T.dma("sync", Ub[b][:, 0:8, :], U_v[:, 0:8, T0:T0 + 512], f"ld_ua{b}", writes=[rUa[b]])
            T.dma("sync", Ub[b][:, 8:16, :], U_v[:, 8:16, T0:T0 + 512], f"ld_us{b}", writes=[rUs[b]])
            T.dma("sync", xt[b][:], x_v[:, 4 * tg:4 * tg + 4, :], f"ld_xt{b}", writes=rxt[b])
            for ci in range(4):
                cc = slice(128 * ci, 128 * ci + 128)
                k = cnt % 2
                rk = r[k]
                T.op("scalar", lambda e, o=xt[b][:, ci, :]: e.activation(out=o, in_=o, func=AF.Copy, scale=ALPHA),
                     reads=[], writes=[rxt[b][ci]])
                T.op("scalar", lambda e, o=sq[k][:], i_=Ub[b][:, 0:8, cc]: e.activation(out=o, in_=i_, func=AF.Square),
                     reads=[rUa[b]], writes=[rk["sq"]])
                halves = []
                for half in range(2):
                    cols = slice(512 * half, 512 * (half + 1))
                    sl = slot_i; slot_i = (slot_i + 1) % 3
                    bkA, bkB = slots[sl]
                    T.mm_group([(bkA[:, :], Ub[b][:, f, cc], Wo[:, f, cols], f == 0, f == 7) for f in range(8)],
                               reads=[rUa[b]], writes=[rslot[sl][0]])
                    T.mm_group([(bkB[:, :], Ub[b][:, 8 + f, cc], Wo[:, 8 + f, cols], f == 0, f == 7) for f in range(8)],
                               reads=[rUs[b]], writes=[rslot[sl][1]])
                    halves.append((sl, cols))
                T.mm_group([(ssb[k][:, 0:1], sq[k][:, f, :], onesB[:], f == 0, f == 7) for f in range(8)],
                           reads=[rk["sq"]], writes=[rss[k]])
                T.op("scalar", lambda e, o=ra[k][:], i_=ssb[k][:, 0:1]: e.activation(out=o, in_=i_, func=AF.Ln, bias=EPS, scale=1.0 / 1024),
                     reads=[rss[k]], writes=[rk["ra"]])
                T.op("scalar", lambda e, o=ra[k][:]: e.activation(out=o, in_=o, func=AF.Exp, scale=-0.5), reads=[], writes=[rk["ra"]])
                for sl, cols in halves:
                    bkA, bkB = slots[sl]
                    T.op("vector", lambda e, o=rr[k][:, cols], i_=bkA[:, :], x_=xt[b][:, ci, cols], s_=ra[k][:, 0:1]:
                         e.scalar_tensor_tensor(out=o, in0=i_, scalar=s_, in1=x_, op0=ALU.mult, op1=ALU.add),
                         reads=[rslot[sl][0], rk["ra"], rxt[b][ci]], writes=[rk["rr"]])
                    T.op("vector", lambda e, o=rr[k][:, cols], i_=bkB[:, :]: e.tensor_tensor(out=o, in0=i_, in1=o, op=ALU.add),
                         reads=[rslot[sl][1]], writes=[rk["rr"]])
                for half in range(2):
                    T.op("vector", lambda e, o=st6[k][:, half, :], i_=rr[k][:, 512 * half:512 * (half + 1)]: e.bn_stats(out=o, in_=i_),
                         reads=[rk["rr"]], writes=[rk["st6"]])
                T.op("vector", lambda e, o=mv[k][:], i_=st6[k][:]: e.bn_aggr(out=o, in_=i_), reads=[rk["st6"]], writes=[rk["mv"]])
                T.op("scalar", lambda e, o=rl[k][:], i_=mv[k][:, 1:2]: e.activation(out=o, in_=i_, func=AF.Ln, bias=EPS),
                     reads=[rk["mv"]], writes=[rk["rl"]])
                T.op("scalar", lambda e, o=rl[k][:]: e.activation(out=o, in_=o, func=AF.Exp, scale=-0.5), reads=[], writes=[rk["rl"]])
                T.op("vector", lambda e, o=ot[k][:], i_=rr[k][:], m_=mv[k][:, 0:1], s_=rl[k][:, 0:1]:
                     e.tensor_scalar(out=o, in0=i_, scalar1=m_, scalar2=s_, op0=ALU.subtract, op1=ALU.mult),
                     reads=[rk["rr"], rk["mv"], rk["rl"]], writes=[rot[k]])
                T.op("gpsimd", lambda e, o=ot[k][:]: e.tensor_tensor(out=o, in0=o, in1=lg_b[:], op=ALU.mult),
                     reads=[rC], writes=[rot[k]])
                T.op("gpsimd", lambda e, o=ot[k][:]: e.tensor_tensor(out=o, in0=o, in1=lb_b[:], op=ALU.add),
                     reads=[rC], writes=[rot[k]])
                T.dma("gpsimd", out_d[T0 + 128 * ci:T0 + 128 * ci + 128, :], ot[k][:], f"st_o{k}", reads=[rot[k]])
                cnt += 1


def phase_A(nc, P, st_outer, banks, XT, w_in_v, aug, mask4_d, U, pairs, dbg=3, dr=None):
    with contextlib.ExitStack() as st:
        def sb(name, shape, dt):
            return st.enter_context(nc.sbuf_tensor(name, shape, dt))

        mask4 = sb("mask4s", [128, 512], F32)
        stmp = [sb(f"stmp{i}", [128, 512], F32) for i in range(4)]
        wstage = sb("wstage", [128, 8, 512], F32)
        wbf = sb("wbf", [128, 8, 512], BF16)
        qk = [[sb(f"qk{w}{h}", [128, S], BF16) for h in range(2)] for w in range(2)]
        sz = sb("sz", [128, S], BF16)
        V = sb("V", [128, 3, 32, 192], BF16)
        NPT = 6
        PT = [sb(f"PT{i}", [128, 512], BF16) for i in range(NPT)]
        rc = [sb(f"rc{i}", [128, 512], F32) for i in range(2)]
        t1 = [sb(f"t1{i}", [128, 512], F32) for i in range(2)]
        uT = sb("uT", [128, S], BF16)
        vT = sb("vT", [128, S], BF16)
        identBa = sb("identBa", [128, 128], BF16)

        acc = banks[0:4]
        sbank = banks[4:6]
        pbank = banks[6:8]

        ev_mask = P.dma("sync", mask4[:], mask4_d[:, :], "ld_c")
        ev_id = P.dma("sync", identBa[:], dr["identb"][:, :], "ld_cid")
        vT_free = []
        ev_ones = I_memset(P, "gpsimd", V[:, :, :, 64:128], 1.0) if "ones" not in _SKIP else None

        pb_free = [[], []]
        pb_i = 0
        sb_free = [None, None]
        pt_free = [None] * 6
        st_free = [None] * 4
        acc_free = [[] for _ in range(4)]
        w_free = []
        qk_free = [[[], []], [[], []]]
        sz_free = []
        V_free = []
        uT_free = None
        g_ctr = 0

        def load_w(hp_, deps_):
            out_ = []
            for wi, off in enumerate((OFF_Q, OFF_K, OFF_V, OFF_ZA)):
                out_.append(P.dma("sync", wstage[:, :, wi * 128:(wi + 1) * 128],
                                  w_in_v[:, :, off + hp_ * 128: off + (hp_ + 1) * 128], "ld_w", deps=deps_))
            return out_

        lds = load_w(pairs[0], [])
        w_casts = None
        ws_b = wstage[:].rearrange("p a b -> p (a b)").bitcast(BF16)
        qk16 = [ws_b[:, 0:S], ws_b[:, S:2 * S]]
        qk16_free = []
        rc_free = [None, None]
        ev_ctr = 0
        pending_evac = []
        for hp in pairs:
            hA, hB = 2 * hp, 2 * hp + 1
            if w_casts is None:
                c1 = I_copy(P, "vector", wbf[:, 0:4, :], wstage[:, 0:4, :], lds)
                c2 = I_copy(P, "scalar", wbf[:, 4:8, :], wstage[:, 4:8, :], lds)
                w_casts = [c1, c2]
            w_ready = w_casts
            nxt = pairs.index(hp) + 1
            if nxt < len(pairs):
                lds = load_w(pairs[nxt], w_casts + qk16_free)
            aug_ev = [[None, None], [None, None]]
            for w in range(2):
                for hh, h in enumerate((hA, hB)):
                    if "aug" in _SKIP:
                        continue
                    aug_ev[w][hh] = P.dma("sync", qk[w][hh][64:76, :], aug[h, w, :, :], f"ld_aug{w}{hh}",
                                          deps=qk_free[w][hh])
            w_last = []
            qk_ready = [[[], []], [[], []]]
            sz_ready = []
            vT_ready = []
            for wi in (0, 1, 2, 3):
                for tt in range(8):
                    pbk = pbank[pb_i]
                    deps = list(w_ready) + list(pb_free[pb_i])
                    for kc in range(8):
                        mm = I_mm(P, pbk[:, :], wbf[:, kc, wi * 128:(wi + 1) * 128],
                                  XT[:, kc, tt * 512:(tt + 1) * 512], kc == 0, kc == 7,
                                  deps if kc == 0 else ())
                    cols = slice(tt * 512, (tt + 1) * 512)
                    if wi == 3:
                        e = I_act(P, sz[:, cols], pbk[:, :], AF.Silu, [mm] + sz_free)
                        sz_ready.append(e)
                        pb_free[pb_i] = [e]
                    elif wi == 2:
                        e = I_copy(P, "vector" if tt % 2 == 0 else "scalar", vT[:, cols], pbk[:, :], [mm] + vT_free)
                        vT_ready.append(e)
                        pb_free[pb_i] = [e]
                    else:
                        w = wi
                        sc = 0.125 if w == 0 else 1.0
                        eA = I_act(P, qk[w][0][0:64, cols], pbk[0:64, :], AF.Copy, [mm] + qk_free[w][0], scale=sc)
                        if "dveB" in _SKIP:
                            eB = I_act(P, qk[w][1][0:64, cols], pbk[64:128, :], AF.Copy, [mm] + qk_free[w][1], scale=sc)
                        else:
                            eB = I_ts(P, "vector", qk[w][1][0:64, cols], pbk[64:128, :], sc, None, ALU.mult,
                                      deps=[mm] + qk_free[w][1])
                        qk_ready[w][0].append(eA)
                        qk_ready[w][1].append(eB)
                        pb_free[pb_i] = [eA, eB]
                    pb_i ^= 1
                    w_last = [mm]
            V_ready = {}
            tr_last = None
            for di, d in enumerate(DIL if dbg >= 2 else ()):
                nb = 32 // d
                for c0 in range(0, 32, 8):
                    pbk = pbank[pb_i]
                    pbk_b = pbk[:].bitcast(BF16)
                    for j in range(8):
                        r_, m_ = divmod(c0 + j, nb)
                        deps = ()
                        if j == 0:
                            deps = vT_ready + [ev_id] + list(pb_free[pb_i])
                        tr_last = P.op("tensor", lambda e, o=pbk_b[:, 128 * j:128 * (j + 1)], i_=vT[:, tok_ap(None, d, r_, m_)]:
                                       e.transpose(o, i_, identBa[:]), deps)
                    src = pbk_b[:, :].rearrange("p (c f) -> p c f", f=128)
                    eA = I_copy(P, "vector", V[:, di, c0:c0 + 8, 0:64], src[:, :, 0:64], [tr_last] + V_free)
                    eB = I_copy(P, "scalar", V[:, di, c0:c0 + 8, 128:192], src[:, :, 64:128], [tr_last, eA] + V_free)
                    for j in range(8):
                        V_ready[(di, c0 + j)] = [eA, eB]
                    pb_free[pb_i] = [eA, eB]
                    pb_i ^= 1
            vT_free = [tr_last] if tr_last is not None else []
            w_free = w_last
            if nxt < len(pairs):
                c1 = I_copy(P, "vector", wbf[:, 0:4, :], wstage[:, 0:4, :], lds + w_last)
                c2 = I_copy(P, "scalar", wbf[:, 4:8, :], wstage[:, 4:8, :], lds + w_last)
                w_casts = [c1, c2]
            V_free = []
            qk_free = [[[], []], [[], []]]
            sz_free = []

            u_written = []
            for hh, h in enumerate((hA, hB) if dbg >= 3 else ()):
                qT, kT = qk[0][hh], qk[1][hh]
                q_dep = qk_ready[0][hh] + [aug_ev[0][hh]]
                k_dep = qk_ready[1][hh] + [aug_ev[1][hh]]
                vcol = slice(0, 128) if hh == 0 else slice(64, 192)
                cdeps = (w_casts if nxt < len(pairs) else []) + qk16_free
                c16 = []
                for w in range(2):
                    src = qk[w][hh][0:76, :].rearrange("p (j r) -> p r j", r=16)
                    dst = qk16[w][0:76, :].rearrange("p (r j) -> p r j", r=16)
                    dd = (q_dep if w == 0 else k_dep) + cdeps
                    if w == 0:
                        c16.append(I_copy(P, "vector", dst, src, dd))
                    else:
                        c16.append(I_copy(P, "scalar", dst, src, dd))
                last16 = None
                for sbk in range(2):
                    qbs = []
                    for di, d in enumerate(DIL):
                        nb = 32 // d
                        per_sb = nb // 2
                        for r in range(d):
                            for n in range(sbk * per_sb, (sbk + 1) * per_sb):
                                qbs.append((di, d, r, n))
                    groups = [qbs[i:i + 2] for i in range(0, len(qbs), 2)]
                    acc_started = [False] * 4
                    acc_last_mm = [None] * 4
                    pend = []

                    def do_pv(item):
                        ptb, grp, ev_mask_mul = item
                        last_mm = None
                        for j, (di, d, r, n) in enumerate(grp):
                            nb = 32 // d
                            for role in range(2):
                                m = n - 1 + role
                                if m < 0:
                                    continue
                                tile_cols = (2 * j + role) * 128
                                cid = r * nb + m
                                lhsT = V[:, di, cid, vcol]
                                if d == 16:
                                    pieces = [(pc, 32) for pc in range(4)]
                                else:
                                    pieces = [(0, 128)]
                                for pc, cnt in pieces:
                                    i0 = pc * 32 if d == 16 else 0
                                    t0 = d * (128 * n + i0) + r
                                    col0 = t0 - 2048 * sbk
                                    bk = col0 // 512
                                    c0 = col0 % 512
                                    out = acc[bk][:, c0: c0 + d * (cnt - 1) + 1: d]
                                    rhs = PT[ptb][:, tile_cols + i0: tile_cols + i0 + cnt]
                                    deps = [ev_mask_mul] + V_ready[(di, cid)] + latest(ev_ones)
                                    if not acc_started[bk]:
                                        deps = deps + list(acc_free[bk])
                                    last_mm = I_mm(P, out, lhsT, rhs, not acc_started[bk], False, deps)
                                    acc_started[bk] = True
                                    acc_last_mm[bk] = last_mm
                        pt_free[ptb] = last_mm

                    for gi, grp in enumerate(groups):
                        if gi % 2 == 0 and pending_evac:
                            pending_evac.pop(0)()
                        sbi = g_ctr % 2
                        pti = g_ctr % 6
                        sti = g_ctr % 4
                        g_ctr += 1
                        sbb = sbank[sbi]
                        first = True
                        mm = None
                        merged = (len(grp) == 2 and grp[0][1] != 16 and grp[1][0:3] == grp[0][0:3]
                                  and grp[1][3] == grp[0][3] + 1)
                        for j, (di, d, r, n) in enumerate(grp):
                            for role in range(2):
                                if merged and (j, role) == (1, 0):
                                    continue
                                m = n - 1 + role
                                nq = n
                                if m < 0:
                                    m, nq = 0, 1
                                tile_cols = (2 * j + role) * 128
                                deps = ()
                                if first:
                                    deps = q_dep + k_dep + latest(sb_free[sbi])
                                    first = False
                                if d == 16:
                                    kop = qk16[1][0:76, r * 256 + 128 * m: r * 256 + 128 * m + 128]
                                    qop = qk16[0][0:76, r * 256 + 128 * nq: r * 256 + 128 * nq + 128]
                                    mm = I_mm(P, sbb[:, tile_cols:tile_cols + 128], kop, qop, True, True, list(deps) + c16)
                                    last16 = mm
                                elif merged and (j, role) == (0, 1):
                                    mm = I_mm(P, sbb[:, tile_cols:tile_cols + 256],
                                              kT[0:76, tok_ap(None, d, r, m)], qT[0:76, tok_ap(None, d, r, n, cnt=256)],
                                              True, True, deps)
                                else:
                                    mm = I_mm(P, sbb[:, tile_cols:tile_cols + 128],
                                              kT[0:76, tok_ap(None, d, r, m)], qT[0:76, tok_ap(None, d, r, nq)],
                                              True, True, deps)
                        mk0 = I_tt(P, "vector", stmp[sti][:, :], sbb[:, :], mask4[:, :], ALU.add,
                                   [mm, ev_mask] + latest(st_free[sti]))
                        sb_free[sbi] = mk0
                        mk = I_act(P, PT[pti][:, :], stmp[sti][:, :], AF.Exp, [mk0] + latest(pt_free[pti]))
                        st_free[sti] = mk
                        pend.append((pti, grp, mk))
                        if len(pend) > 4:
                            do_pv(pend.pop(0))
                    while pend:
                        do_pv(pend.pop(0))
                    def make_evac(bk, hh=hh, sbk=sbk, alm=acc_last_mm, szr=sz_ready):
                        def evac():
                            nonlocal ev_ctr, sz_free
                            cols = slice(2048 * sbk + 512 * bk, 2048 * sbk + 512 * (bk + 1))
                            if hh == 0:
                                o_rows, s_rows = slice(0, 64), slice(64, 128)
                            else:
                                o_rows, s_rows = slice(64, 128), slice(0, 64)
                            ei = ev_ctr % 2
                            ev_ctr += 1
                            a1 = I_act(P, rc[ei][o_rows, :], acc[bk][s_rows, :], AF.Ln, [alm[bk]] + latest(rc_free[ei]))
                            a2 = I_copy(P, "vector", t1[ei][o_rows, :], acc[bk][o_rows, :], [alm[bk]] + latest(rc_free[ei]))
                            acc_free[bk] = [a1, a2]
                            e1 = I_act(P, rc[ei][o_rows, :], rc[ei][o_rows, :], AF.Exp, [a1], scale=-1.0)
                            e2 = I_tt(P, "gpsimd", t1[ei][o_rows, :], t1[ei][o_rows, :], rc[ei][o_rows, :], ALU.mult, [e1, a2])
                            e3 = I_tt(P, "gpsimd", uT[o_rows, cols], t1[ei][o_rows, :], sz[o_rows, cols], ALU.mult,
                                      [e2] + szr + latest(uT_free))
                            rc_free[ei] = e3
                            u_written.append(e3)
                            sz_free = [e3]
                        return evac
                    pending_evac.extend(make_evac(bk) for bk in range(4))
                qk_free[0][hh] = [mm]
                qk_free[1][hh] = [mm]
                qk16_free = [last16]
            if dbg < 3:
                continue
            while pending_evac:
                pending_evac.pop(0)()
            V_free = latest(*[acc_last_mm[b] for b in range(4)])
            uT_free = P.dma("gpsimd", U[hp * 128:(hp + 1) * 128, :], uT[:, :], "st_u", deps=u_written)
            sz_free = u_written[-1:]


class Res:
    __slots__ = ("w", "r")

    def __init__(self):
        self.w = None
        self.r = {}


class Trk:
    def __init__(self, P):
        self.P = P

    def deps(self, reads, writes):
        d = []
        for t in reads:
            if t.w is not None:
                d.append(t.w)
        for t in writes:
            if t.w is not None:
                d.append(t.w)
            d.extend(t.r.values())
        return d

    def mark(self, ev, reads, writes):
        for t in reads:
            t.r[ev.sem if ev.sem is not None else ev.eng] = ev
        for t in writes:
            t.w = ev
            t.r = {}

    def op(self, eng, fn, reads=(), writes=(), extra=()):
        ev = self.P.op(eng, fn, self.deps(reads, writes) + list(extra))
        self.mark(ev, reads, writes)
        return ev

    def dma(self, eng, out, in_, slot, reads=(), writes=(), extra=()):
        ev = self.P.dma(eng, out, in_, slot, self.deps(reads, writes) + list(extra))
        self.mark(ev, reads, writes)
        return ev

    def mm_group(self, mms, reads, writes, extra=()):
        d = self.deps(reads, writes) + list(extra)
        ev = None
        for i, (out, lhsT, rhs, start, stop) in enumerate(mms):
            ev = I_mm(self.P, out, lhsT, rhs, start, stop, d if i == 0 else ())
        self.mark(ev, reads, writes)
        return ev


def bc64(ap16, h0, nh):
    return ap16[:, h0:h0 + nh].unsqueeze(2).broadcast_to([128, nh, 64])


def phase_B(nc, P, banks, XT, w_in_v, dr, U, n_sc=8, dbg_out=None):
    T = Trk(P)
    with contextlib.ExitStack() as st:
        def sb(name, shape, dt, stack=None):
            return (stack or st).enter_context(nc.sbuf_tensor("B_" + name, shape, dt))

        rXT = Res()
        Wz = sb("Wz", [128, 8, 1024], BF16)
        Wx = sb("Wx", [128, 8, 1536], BF16)
        Wdt = sb("Wdt", [128, 8, 16], BF16)
        tri = sb("tri", [128, 128], F32)
        onesF = sb("onesF", [128, 128], F32)
        identF = sb("identF", [128, 128], F32)
        identB = sb("identB", [128, 128], BF16)
        negm4 = sb("negm4", [128, 512], BF16)
        cw = sb("cw", [128, 12, 4], F32)
        cb = sb("cb", [128, 12], F32)
        dtb_b = sb("dtb_b", [128, 16], F32)
        a_b = sb("a_b", [128, 16], F32)
        dsk_b = sb("dsk_b", [128, 16], F32)
        gs_b = sb("gs_b", [128, 1024], F32)
        hal = sb("hal", [128, 12, 3], F32)
        rW = Res(); rC = Res(); rHal = Res()

        with contextlib.ExitStack() as stw:
            wst = sb("wstB", [128, 8, 512], F32, stw)
            rwst = Res()
            pieces = [(OFF_ZS, 0, 512, Wz), (OFF_ZS + 512, 512, 512, Wz),
                      (OFF_XBC, 0, 512, Wx), (OFF_XBC + 512, 512, 512, Wx), (OFF_XBC + 1024, 1024, 512, Wx),
                      (OFF_DT, 0, 16, Wdt)]
            for i, (off, dst0, n, Wt) in enumerate(pieces):
                T.dma("sync", wst[:, :, 0:n], w_in_v[:, :, off:off + n], "ld_wB", writes=[rwst])
                T.op("vector", lambda e, o=Wt[:, 0:4, dst0:dst0 + n], i_=wst[:, 0:4, 0:n]: e.tensor_copy(out=o, in_=i_),
                     reads=[rwst], writes=[rW])
                T.op("gpsimd", lambda e, o=Wt[:, 4:8, dst0:dst0 + n], i_=wst[:, 4:8, 0:n]: e.tensor_copy(out=o, in_=i_),
                     reads=[rwst], writes=[])
            T.dma("sync", tri[:], dr["tri"][:, :], "ld_c1", writes=[rC])
            T.dma("sync", identF[:], dr["ident"][:, :], "ld_c2", writes=[rC])
            T.dma("sync", negm4[:], dr["negm4"][:, :], "ld_c3", writes=[rC])
            T.dma("sync", cw[:], dr["conv_wT"][:, :, :], "ld_c4", writes=[rC])
            T.dma("sync", cb[:], dr["conv_b2"][:, :], "ld_c5", writes=[rC])
            T.dma("sync", dtb_b[:], dr["dt_bias"].partition_broadcast(128), "ld_c6", writes=[rC])
            T.dma("sync", a_b[:], dr["a_log"].partition_broadcast(128), "ld_c7", writes=[rC])
            T.dma("sync", dsk_b[:], dr["d_skip"].partition_broadcast(128), "ld_c8", writes=[rC])
            T.dma("sync", gs_b[:], dr["ssm_norm_g"].partition_broadcast(128), "ld_c9", writes=[rC])
            barrier(P)
            I_memset(P, "vector", onesF[:], 1.0)
            I_memset(P, "vector", hal[:], 0.0)
            I_copy(P, "vector", identB[:], identF[:])
            I_act(P, a_b[:], a_b[:], AF.Exp)
            barrier(P)
            I_ts(P, "vector", a_b[:], a_b[:], -1.0, None, ALU.mult)
            barrier(P)

        xb = [sb(f"xb{i}", [128, 515], F32) for i in range(2)]
        cvb = [sb(f"cvb{i}", [128, 512], F32) for i in range(2)]
        xsT = sb("xsT", [128, 8, 512], F32)
        BT = sb("BT", [128, 2, 512], BF16)
        CT = sb("CT", [128, 2, 512], BF16)
        dtx = sb("dtx", [128, 16], F32)
        dt_t = sb("dt_t", [128, 16], F32)
        adt = sb("adt", [128, 16], F32)
        nacs = sb("nacs", [128, 16], F32)
        D0 = sb("D0", [128, 16], F32)
        dte = sb("dte", [128, 16], F32)
        ddt = sb("ddt", [128, 16], F32)
        cdb = sb("cdb", [128, 16], F32)
        Btok = sb("Btok", [128, 2, 128], BF16)
        xs_tok = sb("xs_tok", [128, 1024], F32)
        xg = sb("xg", [128, 1024], BF16)
        xgd = sb("xgd", [128, 1024], BF16)
        Gm = sb("Gm", [128, 2, 128], BF16)
        Dm = [sb(f"Dm{i}", [128, 512], BF16) for i in range(2)]
        MT = [sb(f"MT{i}", [128, 512], BF16) for i in range(2)]
        H = sb("H", [128, 2, 512], F32)
        Hbf = sb("Hbf", [128, 2, 512], BF16)
        yo = sb("yo", [128, 512], F32)
        tsk = sb("tsk", [128, 512], F32)
        y = sb("y", [128, 1024], F32)
        zs = sb("zs", [128, 1024], F32)
        junk = sb("junkB", [128, 1024], F32)
        ssq = sb("ssq", [128, 1], F32)
        rstd = sb("rstd", [128, 1], F32)
        mixn = sb("mixn", [128, 1024], BF16)
        mixT = sb("mixT", [128, 8, 512], BF16)

        r = {k: Res() for k in ("xsT", "BT", "CT", "dtx", "dt", "adt", "nacs", "D0", "dte", "ddt", "cdb", "Btok",
                                "xs_tok", "xg", "xgd", "Gm", "H", "Hbf", "yo", "tsk", "y", "zs", "junk", "ssq", "rstd",
                                "mixn", "mixT")}
        rxb = [Res(), Res()]; rxbh = [Res(), Res()]; rcv = [Res(), Res()]; rmixT = [Res(), Res()]; rDm = [Res(), Res()]; rMT = [Res(), Res()]
        rhal = [Res() for _ in range(12)]
        pb = [banks[0], banks[1]]; rpb = [Res(), Res()]
        smb = banks[2]; rsmb = Res()
        trb = banks[3]; rtr = Res()
        Rb = [banks[4], banks[5]]; rRb = [Res(), Res()]
        Yb = banks[6]; rY = Res()
        osb = banks[7]; ros = Res()

        I_memset(P, "vector", H[:], 0.0)
        I_memset(P, "vector", Hbf[:], 0.0)
        barrier(P)

        pbi = 0
        quad_ctr = 0
        for sc in range(n_sc):
            T0 = 512 * sc
            for ft in range(12):
                b = pbi; pbi ^= 1
                xbi = ft % 2
                T.mm_group([(pb[b][:, :], Wx[:, kc, ft * 128:(ft + 1) * 128], XT[:, kc, T0:T0 + 512], kc == 0, kc == 7)
                            for kc in range(8)], reads=[rW, rXT], writes=[rpb[b]])
                T.op("scalar", lambda e, o=xb[xbi][:, 3:515], i_=pb[b][:, :]: e.activation(out=o, in_=i_, func=AF.Copy),
                     reads=[rpb[b]], writes=[rxb[xbi]])
                T.op("gpsimd", lambda e, o=xb[xbi][:, 0:3], i_=hal[:, ft, :]: e.tensor_copy(out=o, in_=i_),
                     reads=[rhal[ft]], writes=[rxbh[xbi]])
                T.op("gpsimd", lambda e, o=hal[:, ft, :], i_=xb[xbi][:, 512:515]: e.tensor_copy(out=o, in_=i_),
                     reads=[rxb[xbi]], writes=[rhal[ft]])
                ceng = "vector"
                cv = cvb[xbi]
                T.op(ceng, lambda e, o=cv[:, :], i_=xb[xbi][:, 0:512], s_=cw[:, ft, 0:1]:
                     e.tensor_scalar(out=o, in0=i_, scalar1=s_, scalar2=None, op0=ALU.mult),
                     reads=[rxb[xbi], rxbh[xbi], rC], writes=[rcv[xbi]])
                for j in range(1, 4):
                    T.op(ceng, lambda e, o=cv[:, :], i_=xb[xbi][:, j:j + 512], s_=cw[:, ft, j:j + 1]:
                         e.scalar_tensor_tensor(out=o, in0=i_, scalar=s_, in1=o, op0=ALU.mult, op1=ALU.add),
                         reads=[rxb[xbi], rxbh[xbi], rcv[xbi]], writes=[rcv[xbi]])
                if ft < 8:
                    dst, rd = xsT[:, ft, :], r["xsT"]
                elif ft < 10:
                    dst, rd = BT[:, ft - 8, :], r["BT"]
                else:
                    dst, rd = CT[:, ft - 10, :], r["CT"]
                T.op("scalar", lambda e, o=dst, i_=cv[:, :], b_=cb[:, ft:ft + 1]:
                     e.activation(out=o, in_=i_, func=AF.Silu, bias=b_), reads=[rcv[xbi], rC], writes=[rd])

            for ci in range(4):
                t0 = T0 + 128 * ci
                cc = slice(128 * ci, 128 * ci + 128)
                T.mm_group([(smb[:, 0:16], XT[:, kc, t0:t0 + 128], Wdt[:, kc, :], kc == 0, kc == 7) for kc in range(8)],
                           reads=[rW, rXT], writes=[rsmb])
                for half in range(2):
                    b = pbi; pbi ^= 1
                    T.mm_group([(pb[b][:, :], XT[:, kc, t0:t0 + 128], Wz[:, kc, 512 * half:512 * (half + 1)], kc == 0, kc == 7)
                                for kc in range(8)], reads=[rW, rXT], writes=[rpb[b]])
                    T.op("scalar", lambda e, o=zs[:, 512 * half:512 * (half + 1)], i_=pb[b][:, :]:
                         e.activation(out=o, in_=i_, func=AF.Silu), reads=[rpb[b]], writes=[r["zs"]])
                T.op("vector", lambda e: e.tensor_tensor(out=dtx[:], in0=smb[:, 0:16], in1=dtb_b[:], op=ALU.add),
                     reads=[rsmb, rC], writes=[r["dtx"]])
                T.op("scalar", lambda e: e.activation(out=dtx[:], in_=dtx[:], func=AF.Exp), reads=[], writes=[r["dtx"]])
                T.op("scalar", lambda e: e.activation(out=dt_t[:], in_=dtx[:], func=AF.Ln, bias=1.0),
                     reads=[r["dtx"]], writes=[r["dt"]])
                T.op("vector", lambda e: e.tensor_tensor(out=adt[:], in0=dt_t[:], in1=a_b[:], op=ALU.mult),
                     reads=[r["dt"], rC], writes=[r["adt"]])
                T.mm_group([(smb[:, 16:32], tri[:], adt[:], True, True)], reads=[r["adt"], rC], writes=[rsmb])
                T.mm_group([(smb[:, 32:48], onesF[:], adt[:], True, True)], reads=[r["adt"], rC], writes=[rsmb])
                T.op("vector", lambda e: e.tensor_scalar(out=nacs[:], in0=smb[:, 16:32], scalar1=-1.0, scalar2=None, op0=ALU.mult),
                     reads=[rsmb], writes=[r["nacs"]])
                T.op("vector", lambda e: e.tensor_copy(out=cdb[:], in_=smb[:, 32:48]), reads=[rsmb], writes=[r["cdb"]])
                T.op("vector", lambda e: e.tensor_tensor(out=dte[:], in0=cdb[:], in1=nacs[:], op=ALU.add),
                     reads=[r["cdb"], r["nacs"]], writes=[r["dte"]])
                T.op("scalar", lambda e: e.activation(out=D0[:], in_=nacs[:], func=AF.Exp, scale=-1.0),
                     reads=[r["nacs"]], writes=[r["D0"]])
                T.op("scalar", lambda e: e.activation(out=dte[:], in_=dte[:], func=AF.Exp), reads=[], writes=[r["dte"]])
                T.op("scalar", lambda e: e.activation(out=cdb[:], in_=cdb[:], func=AF.Exp), reads=[r["dte"]], writes=[r["cdb"]])
                T.op("vector", lambda e: e.tensor_tensor(out=ddt[:], in0=dt_t[:], in1=dte[:], op=ALU.mult),
                     reads=[r["dt"], r["dte"]], writes=[r["ddt"]])
                for g in range(2):
                    ev = None
                    for j in range(4):
                        ev = P.op("tensor", lambda e, o=trb[:, 128 * j:128 * (j + 1)], i_=xsT[:, 4 * g + j, cc]:
                                  e.transpose(o, i_, identF[:]), T.deps([r["xsT"], rC], [rtr]) if j == 0 else ())
                    T.mark(ev, [r["xsT"], rC], [rtr])
                    T.op("scalar", lambda e, o=xs_tok[:, 512 * g:512 * (g + 1)], i_=trb[:, :]: e.activation(out=o, in_=i_, func=AF.Copy),
                         reads=[rtr], writes=[r["xs_tok"]])
                for g in range(2):
                    v3 = lambda ap: ap[:, 512 * g:512 * (g + 1)].rearrange("p (h e) -> p h e", e=64)
                    T.op("vector", lambda e, o=v3(xg), i_=v3(xs_tok), b_=bc64(dt_t, 8 * g, 8):
                         e.tensor_tensor(out=o, in0=i_, in1=b_, op=ALU.mult), reads=[r["xs_tok"], r["dt"]], writes=[r["xg"]])
                    T.op("gpsimd", lambda e, o=v3(xgd), i_=v3(xs_tok), b_=bc64(ddt, 8 * g, 8):
                         e.tensor_tensor(out=o, in0=i_, in1=b_, op=ALU.mult), reads=[r["xs_tok"], r["ddt"]], writes=[r["xgd"]])
                trb_b = trb[:].bitcast(BF16)
                ev = None
                for g in range(2):
                    ev = P.op("tensor", lambda e, o=trb_b[:, 128 * g:128 * (g + 1)], i_=BT[:, g, cc]:
                              e.transpose(o, i_, identB[:]), T.deps([r["BT"], rC], [rtr]) if g == 0 else ())
                T.mark(ev, [r["BT"], rC], [rtr])
                T.op("vector", lambda e: e.tensor_copy(out=Btok[:].rearrange("p g n -> p (g n)"), in_=trb_b[:, 0:256]),
                     reads=[rtr], writes=[r["Btok"]])
                for g in range(2):
                    T.mm_group([(smb[:, 128 * (g + 1):128 * (g + 2)], BT[:, g, cc], CT[:, g, cc], True, True)],
                               reads=[r["BT"], r["CT"]], writes=[rsmb])
                    T.op("vector", lambda e, o=Gm[:, g, :], i_=smb[:, 128 * (g + 1):128 * (g + 2)]: e.tensor_copy(out=o, in_=i_),
                         reads=[rsmb], writes=[r["Gm"]])
                for g in range(2):
                    for qd in range(2):
                        qi = quad_ctr % 2; quad_ctr += 1
                        h0 = 8 * g + 4 * qd
                        mms = [(Rb[qi][:, :], identB[:], negm4[:], True, False)]
                        for j in range(4):
                            mms.append((Rb[qi][:, 128 * j:128 * (j + 1)], adt[:, h0 + j:h0 + j + 1].broadcast_to([128, 128]),
                                        tri[:], False, j == 3))
                        T.mm_group(mms, reads=[r["adt"], rC], writes=[rRb[qi]])
                        for j in range(4):
                            T.op("scalar", lambda e, o=Dm[qi][:, 128 * j:128 * (j + 1)], i_=Rb[qi][:, 128 * j:128 * (j + 1)],
                                 b_=nacs[:, h0 + j:h0 + j + 1]: e.activation(out=o, in_=i_, func=AF.Exp, bias=b_),
                                 reads=[rRb[qi], r["nacs"]], writes=[rDm[qi]])
                        T.op("vector", lambda e, o=MT[qi][:].rearrange("p (j l) -> p j l", l=128),
                             i_=Dm[qi][:].rearrange("p (j l) -> p j l", l=128),
                             b_=Gm[:, g, :].unsqueeze(1).broadcast_to([128, 4, 128]):
                             e.tensor_tensor(out=o, in0=i_, in1=b_, op=ALU.mult),
                             reads=[rDm[qi], r["Gm"]], writes=[rMT[qi]])
                        mms = []
                        for j in range(4):
                            hh = 4 * qd + j
                            mms.append((Yb[:, 64 * hh:64 * (hh + 1)], MT[qi][:, 128 * j:128 * (j + 1)],
                                        xg[:, 64 * (h0 + j):64 * (h0 + j + 1)], (qd == 0 and j == 0), False))
                        T.mm_group(mms, reads=[rMT[qi], r["xg"]], writes=[rY])
                    T.mm_group([(osb[:, :], CT[:, g, cc], Hbf[:, g, :], True, True)], reads=[r["CT"], r["Hbf"]], writes=[ros])
                    v3 = lambda ap: ap.rearrange("p (h e) -> p h e", e=64)
                    T.op("vector", lambda e, o=v3(yo[:, :]), i_=v3(osb[:, :]), b_=bc64(D0, 8 * g, 8):
                         e.tensor_tensor(out=o, in0=i_, in1=b_, op=ALU.mult), reads=[ros, r["D0"]], writes=[r["yo"]])
                    T.op("vector", lambda e, o=y[:, 512 * g:512 * (g + 1)], i_=Yb[:, :]:
                         e.tensor_tensor(out=o, in0=i_, in1=yo[:, :], op=ALU.add), reads=[rY, r["yo"]], writes=[r["y"]])
                    T.op("gpsimd", lambda e, o=v3(tsk[:, :]), i_=v3(xs_tok[:, 512 * g:512 * (g + 1)]), b_=bc64(dsk_b, 8 * g, 8):
                         e.tensor_tensor(out=o, in0=i_, in1=b_, op=ALU.mult), reads=[r["xs_tok"], rC], writes=[r["tsk"]])
                    T.op("gpsimd", lambda e, o=y[:, 512 * g:512 * (g + 1)]: e.tensor_tensor(out=o, in0=o, in1=tsk[:, :], op=ALU.add),
                         reads=[r["tsk"]], writes=[r["y"]])
                    T.mm_group([(osb[:, :], Btok[:, g, :], xgd[:, 512 * g:512 * (g + 1)], True, True)],
                               reads=[r["Btok"], r["xgd"]], writes=[ros])
                    T.op("vector", lambda e, o=v3(H[:, g, :]), b_=bc64(cdb, 8 * g, 8):
                         e.tensor_tensor(out=o, in0=o, in1=b_, op=ALU.mult), reads=[r["cdb"]], writes=[r["H"]])
                    T.op("vector", lambda e, o=H[:, g, :]: e.tensor_tensor(out=o, in0=osb[:, :], in1=o, op=ALU.add),
                         reads=[ros], writes=[r["H"]])
                    T.op("gpsimd", lambda e, o=Hbf[:, g, :], i_=H[:, g, :]: e.tensor_copy(out=o, in_=i_),
                         reads=[r["H"]], writes=[r["Hbf"]])
                T.op("vector", lambda e: e.tensor_tensor(out=y[:], in0=y[:], in1=zs[:], op=ALU.mult),
                     reads=[r["zs"]], writes=[r["y"]])
                T.op("vector", lambda e: e.memset(ssq[:], 0.0), reads=[], writes=[r["ssq"]])
                T.op("scalar", lambda e: e.activation(out=junk[:], in_=y[:], func=AF.Square, accum_out=ssq[:]),
                     reads=[r["y"]], writes=[r["junk"], r["ssq"]])
                T.op("scalar", lambda e: e.activation(out=rstd[:], in_=ssq[:], func=AF.Ln, bias=EPS, scale=1.0 / 1024),
                     reads=[r["ssq"]], writes=[r["rstd"]])
                T.op("scalar", lambda e: e.activation(out=rstd[:], in_=rstd[:], func=AF.Exp, scale=-0.5),
                     reads=[], writes=[r["rstd"]])
                T.op("vector", lambda e: e.scalar_tensor_tensor(out=mixn[:], in0=y[:], scalar=rstd[:, 0:1], in1=gs_b[:],
                                                                op0=ALU.mult, op1=ALU.mult),
                     reads=[r["y"], r["rstd"], rC], writes=[r["mixn"]])
                for half in range(2):
                    ev = None
                    for j in range(4):
                        ft = 4 * half + j
                        ev = P.op("tensor", lambda e, o=trb_b[:, 128 * j:128 * (j + 1)], i_=mixn[:, 128 * ft:128 * (ft + 1)]:
                                  e.transpose(o, i_, identB[:]), T.deps([r["mixn"], rC], [rtr]) if j == 0 else ())
                    T.mark(ev, [r["mixn"], rC], [rtr])
                    T.op("vector" if half == 0 else "scalar",
                         (lambda e, o=mixT[:, 4 * half:4 * half + 4, cc], i_=trb_b[:, 0:512].rearrange("p (j t) -> p j t", t=128):
                          e.tensor_copy(out=o, in_=i_)) if half == 0 else
                         (lambda e, o=mixT[:, 4 * half:4 * half + 4, cc], i_=trb_b[:, 0:512].rearrange("p (j t) -> p j t", t=128):
                          e.activation(out=o, in_=i_, func=AF.Copy)),
                         reads=[rtr], writes=[rmixT[half]])
            T.dma("gpsimd", U[1024:2048, T0:T0 + 512].rearrange("(f p) t -> p f t", p=128), mixT[:, :, :], "st_uB",
                  reads=[rmixT[0], rmixT[1]])
        barrier(P)


_NC_CACHE = {}


def _host_inputs(xb, p, c):
    return {
        "xT": np.ascontiguousarray(xb.T), "x": np.ascontiguousarray(xb),
        "w_in": p["w_in"], "w_out": p["w_out"],
        "aug": c["aug"], "mask4": c["mask4"], "tri": c["tri"], "ident": c["ident"], "identb": c["identb"], "negm4": c["negm4"],
        "conv_wT": np.ascontiguousarray(p["conv_w"].T.reshape(12, 128, 4).transpose(1, 0, 2)),
        "conv_b2": np.ascontiguousarray(p["conv_b"].reshape(12, 128).T),
        "dt_bias": p["dt_bias"], "a_log": p["a_log"], "d_skip": p["d_skip"], "ssm_norm_g": p["ssm_norm_g"],
        "att_norm_g2": np.ascontiguousarray(p["att_norm_g"].reshape(8, 128).T),
        "ssm_norm_g2": np.ascontiguousarray(p["ssm_norm_g"].reshape(8, 128).T),
        "ln_g": p["ln_g"], "ln_b": p["ln_b"],
    }


def kernel(x, w_in, conv_w, conv_b, dt_bias, a_log, d_skip, att_norm_g, ssm_norm_g, w_out, ln_g, ln_b):
    x = np.asarray(x, np.float32)
    p = {"w_in": w_in, "conv_w": conv_w, "conv_b": conv_b, "dt_bias": dt_bias, "a_log": a_log, "d_skip": d_skip,
         "att_norm_g": att_norm_g, "ssm_norm_g": ssm_norm_g, "w_out": w_out, "ln_g": ln_g, "ln_b": ln_b}
    p = {k: np.ascontiguousarray(np.asarray(v, np.float32)[0]) for k, v in p.items()}
    c = make_constants()
    n = x.shape[0]
    if "nc" not in _NC_CACHE:
        _NC_CACHE["nc"] = build_program()
    nc = _NC_CACHE["nc"]
    in_maps = [_host_inputs(x[b], p, c) for b in range(n)]
    res = run_bass_kernel_spmd(nc, in_maps, core_ids=list(range(n)))
    return np.stack([np.asarray(r["out"], np.float32) for r in res.results], axis=0)


def phase_B2(nc, P, banks, XT, w_in_v, dr, U, n_sc=8):
    T = Trk(P)
    with contextlib.ExitStack() as st:
        def sb(name, shape, dt, stack=None):
            return (stack or st).enter_context(nc.sbuf_tensor("B_" + name, shape, dt))

        rXT = Res()
        Wz = sb("Wz", [128, 8, 1024], BF16)
        Wx = sb("Wx", [128, 8, 1536], BF16)
        Wdt = sb("Wdt", [128, 8, 16], BF16)
        tri = sb("tri", [128, 128], F32)
        onesF = sb("onesF", [128, 128], F32)
        identF = sb("identF", [128, 128], F32)
        identB = sb("identB", [128, 128], BF16)
        negm4 = sb("negm4", [128, 512], BF16)
        cw = sb("cw", [128, 12, 4], F32)
        cb = sb("cb", [128, 12], F32)
        dtb_b = sb("dtb_b", [128, 16], F32)
        a_b = sb("a_b", [128, 16], F32)
        dsk_b = sb("dsk_b", [128, 16], F32)
        hal = sb("hal", [128, 12, 3], F32)
        rW = Res(); rC = Res()

        with contextlib.ExitStack() as stw:
            wsts = [sb(f"wstB{i}", [128, 8, 512], F32, stw) for i in range(3)]
            rwsts = [Res(), Res(), Res()]
            pieces = [(OFF_ZS, 0, 512, Wz), (OFF_ZS + 512, 512, 512, Wz),
                      (OFF_XBC, 0, 512, Wx), (OFF_XBC + 512, 512, 512, Wx), (OFF_XBC + 1024, 1024, 512, Wx),
                      (OFF_DT, 0, 16, Wdt)]
            for i, (off, dst0, n, Wt) in enumerate(pieces):
                wst, rwst = wsts[i % 3], rwsts[i % 3]
                T.dma("sync" if i % 2 == 0 else "gpsimd", wst[:, :, 0:n], w_in_v[:, :, off:off + n], f"ld_wB{i % 3}{i % 2}", writes=[rwst])
                T.op("vector", lambda e, o=Wt[:, 0:4, dst0:dst0 + n], i_=wst[:, 0:4, 0:n]: e.tensor_copy(out=o, in_=i_),
                     reads=[rwst], writes=[rW])
                T.op("scalar", lambda e, o=Wt[:, 4:8, dst0:dst0 + n], i_=wst[:, 4:8, 0:n]: e.activation(out=o, in_=i_, func=AF.Copy),
                     reads=[rwst], writes=[])
            T.dma("sync", tri[:], dr["tri"][:, :], "ld_c1", writes=[rC])
            T.dma("sync", identF[:], dr["ident"][:, :], "ld_c2", writes=[rC])
            T.dma("sync", negm4[:], dr["negm4"][:, :], "ld_c3", writes=[rC])
            T.dma("sync", cw[:], dr["conv_wT"][:, :, :], "ld_c4", writes=[rC])
            T.dma("sync", cb[:], dr["conv_b2"][:, :], "ld_c5", writes=[rC])
            T.dma("sync", dtb_b[:], dr["dt_bias"].partition_broadcast(128), "ld_c6", writes=[rC])
            T.dma("sync", a_b[:], dr["a_log"].partition_broadcast(128), "ld_c7", writes=[rC])
            T.dma("sync", dsk_b[:], dr["d_skip"].partition_broadcast(128), "ld_c8", writes=[rC])
            barrier(P)
            I_memset(P, "vector", onesF[:], 1.0)
            I_memset(P, "vector", hal[:], 0.0)
            I_copy(P, "vector", identB[:], identF[:])
            I_act(P, a_b[:], a_b[:], AF.Exp)
            barrier(P)
            I_ts(P, "vector", a_b[:], a_b[:], -1.0, None, ALU.mult)
            barrier(P)

        xb = [sb(f"xb{i}", [128, 515], F32) for i in range(2)]
        cvb = [sb(f"cvb{i}", [128, 512], F32) for i in range(2)]
        xtmp = [sb(f"xtmp{i}", [128, 512], F32) for i in range(2)]
        BT = sb("BT", [128, 2, 512], BF16)
        CT = sb("CT", [128, 2, 512], BF16)
        xs_tok = sb("xs_tok", [128, 4, 1024], F32)
        zs = sb("zs", [128, 4, 1024], F32)
        Btok = sb("Btok", [128, 4, 2, 128], BF16)
        Gm = sb("Gm", [128, 2, 4, 128], BF16)
        sm = {k: sb(k, [128, 64], F32) for k in ("dtx", "dt4", "adt4", "nacs4", "D04", "dte4", "ddt4", "cdb4")}
        xg = [sb(f"xg{i}", [128, 1024], BF16) for i in range(2)]
        xgd = [sb(f"xgd{i}", [128, 1024], BF16) for i in range(2)]
        Dm = [sb(f"Dm{i}", [128, 512], BF16) for i in range(2)]
        MT = [sb(f"MT{i}", [128, 512], BF16) for i in range(2)]
        H = sb("H", [128, 2, 512], F32)
        Hbf = sb("Hbf", [128, 2, 512], BF16)
        yo = [sb(f"yo{i}", [128, 512], F32) for i in range(2)]
        tsk = [sb(f"tsk{i}", [128, 512], F32) for i in range(4)]
        y = [sb(f"y{i}", [128, 1024], F32) for i in range(1)]
        ssq = [sb(f"ssq{i}", [128, 1], F32) for i in range(2)]
        rstd = [sb(f"rstd{i}", [128, 1], F32) for i in range(2)]
        mixn = [sb(f"mixn{i}", [128, 1024], BF16) for i in range(2)]
        mixT = [sb(f"mixT{i}", [128, 8, 512], BF16) for i in range(1)]

        R_ = lambda n: [Res() for _ in range(n)]
        rxb, rxbh, rcv, rxtmp = R_(2), R_(2), R_(2), R_(2)
        rhal = R_(12)
        rBT, rCT, rxs, rzs, rBtok, rGm = Res(), Res(), Res(), Res(), Res(), Res()
        rsm = {k: Res() for k in sm}
        rxg, rxgd, rDm, rMT, ryo, rtsk, ry, rssq, rrstd, rmixn = R_(2), R_(2), R_(2), R_(2), R_(2), R_(4), R_(1), R_(2), R_(2), R_(2)
        rmixT = [R_(2)]
        rH, rHbf = R_(2), R_(2)
        rbank = R_(8)
        pb0, pb1, smb, trb, Rb0, Rb1, Yb, osb = banks
        B_PB0, B_PB1, B_SM, B_TR, B_R0, B_R1, B_Y, B_OS = range(8)
        trb_b = trb[:].bitcast(BF16)

        I_memset(P, "vector", H[:], 0.0)
        I_memset(P, "vector", Hbf[:], 0.0)
        barrier(P)

        def v3(ap):
            return ap.rearrange("p (h e) -> p h e", e=64)

        pbi = 0
        for sc in range(n_sc):
            T0 = 512 * sc
            def stage_A(ft):
                nonlocal pbi
                b = pbi; pbi ^= 1
                pbk = banks[b]
                xbi = ft % 2
                T.mm_group([(pbk[:, :], Wx[:, kc, ft * 128:(ft + 1) * 128], XT[:, kc, T0:T0 + 512], kc == 0, kc == 7)
                            for kc in range(8)], reads=[rW, rXT], writes=[rbank[b]])
                T.op("scalar", lambda e, o=xb[xbi][:, 3:515], i_=pbk[:, :]: e.activation(out=o, in_=i_, func=AF.Copy),
                     reads=[rbank[b]], writes=[rxb[xbi]])
                cv = cvb[xbi]
                T.op("scalar", lambda e, o=cv[:, :], i_=pbk[:, :], s_=cw[:, ft, 3:4]: e.activation(out=o, in_=i_, func=AF.Copy, scale=s_),
                     reads=[rbank[b], rC], writes=[rcv[xbi]])
                T.op("gpsimd", lambda e, o=xb[xbi][:, 0:3], i_=hal[:, ft, :]: e.tensor_copy(out=o, in_=i_),
                     reads=[rhal[ft]], writes=[rxbh[xbi]])
                T.op("gpsimd", lambda e, o=hal[:, ft, :], i_=xb[xbi][:, 512:515]: e.tensor_copy(out=o, in_=i_),
                     reads=[rxb[xbi]], writes=[rhal[ft]])

            def stage_B(ft):
                xbi = ft % 2
                cv = cvb[xbi]
                for j in range(3):
                    T.op("vector", lambda e, o=cv[:, :], i_=xb[xbi][:, j:j + 512], s_=cw[:, ft, j:j + 1]:
                         e.scalar_tensor_tensor(out=o, in0=i_, scalar=s_, in1=o, op0=ALU.mult, op1=ALU.add),
                         reads=[rxb[xbi], rxbh[xbi]], writes=[rcv[xbi]])
                if ft < 8:
                    xi = ft % 2
                    T.op("scalar", lambda e, o=xtmp[xi][:, :], i_=cv[:, :], b_=cb[:, ft:ft + 1]:
                         e.activation(out=o, in_=i_, func=AF.Silu, bias=b_), reads=[rcv[xbi], rC], writes=[rxtmp[xi]])
                elif ft < 10:
                    T.op("scalar", lambda e, o=BT[:, ft - 8, :], i_=cv[:, :], b_=cb[:, ft:ft + 1]:
                         e.activation(out=o, in_=i_, func=AF.Silu, bias=b_), reads=[rcv[xbi], rC], writes=[rBT])
                else:
                    T.op("scalar", lambda e, o=CT[:, ft - 10, :], i_=cv[:, :], b_=cb[:, ft:ft + 1]:
                         e.activation(out=o, in_=i_, func=AF.Silu, bias=b_), reads=[rcv[xbi], rC], writes=[rCT])
            def stage_T(ft):
                if ft < 0 or ft >= 8:
                    return
                xi = ft % 2
                tbk = B_TR if ft % 2 == 0 else B_Y
                ev = None
                for ci in range(4):
                    ev = P.op("tensor", lambda e, o=banks[tbk][:, 128 * ci:128 * (ci + 1)], i_=xtmp[xi][:, 128 * ci:128 * (ci + 1)]:
                              e.transpose(o, i_, identF[:]), T.deps([rxtmp[xi], rC], [rbank[tbk]]) if ci == 0 else ())
                T.mark(ev, [rxtmp[xi], rC], [rbank[tbk]])

            def stage_C(ft):
                if ft < 0 or ft >= 8:
                    return
                tbk = B_TR if ft % 2 == 0 else B_Y
                T.op("scalar", lambda e, o=xs_tok[:, :, ft * 128:(ft + 1) * 128], i_=banks[tbk][:, :].rearrange("p (c f) -> p c f", f=128):
                     e.activation(out=o, in_=i_, func=AF.Copy), reads=[rbank[tbk]], writes=[rxs])

            def z_group(zi):
                nonlocal pbi
                ci, half = divmod(zi, 2)
                t0 = T0 + 128 * ci
                b = pbi; pbi ^= 1
                T.mm_group([(banks[b][:, :], XT[:, kc, t0:t0 + 128], Wz[:, kc, 512 * half:512 * (half + 1)], kc == 0, kc == 7)
                            for kc in range(8)], reads=[rW, rXT], writes=[rbank[b]])
                T.op("scalar", lambda e, o=zs[:, ci, 512 * half:512 * (half + 1)], i_=banks[b][:, :]:
                     e.activation(out=o, in_=i_, func=AF.Silu), reads=[rbank[b]], writes=[rzs])

            stage_A(0)
            for ft in range(13):
                if ft + 1 < 12:
                    stage_A(ft + 1)
                stage_T(ft - 1)
                if ft < 12:
                    stage_B(ft)
                stage_C(ft - 1)
                if 2 <= ft < 10:
                    z_group(ft - 2)
            ev = None
            for ci in range(4):
                for g in range(2):
                    j = 2 * ci + g
                    ev = P.op("tensor", lambda e, o=trb_b[:, 128 * j:128 * (j + 1)], i_=BT[:, g, 128 * ci:128 * (ci + 1)]:
                              e.transpose(o, i_, identB[:]), T.deps([rBT, rC], [rbank[B_TR]]) if j == 0 else ())
            T.mark(ev, [rBT, rC], [rbank[B_TR]])
            T.op("vector", lambda e: e.tensor_copy(out=Btok[:].rearrange("p c g n -> p (c g n)"), in_=trb_b[:, 0:1024]),
                 reads=[rbank[B_TR]], writes=[rBtok])
            for g in range(2):
                bk = B_R0 + g
                T.mm_group([(banks[bk][:, 128 * ci:128 * (ci + 1)], BT[:, g, 128 * ci:128 * (ci + 1)], CT[:, g, 128 * ci:128 * (ci + 1)],
                             True, True) for ci in range(4)], reads=[rBT, rCT], writes=[rbank[bk]])
                T.op("vector", lambda e, o=Gm[:, g, :, :].rearrange("p c l -> p (c l)"), i_=banks[bk][:, :]: e.tensor_copy(out=o, in_=i_),
                     reads=[rbank[bk]], writes=[rGm])
            mms = []
            for ci in range(4):
                t0 = T0 + 128 * ci
                for kc in range(8):
                    mms.append((smb[:, 16 * ci:16 * (ci + 1)], XT[:, kc, t0:t0 + 128], Wdt[:, kc, :], kc == 0 and ci == 0, kc == 7))
            T.mm_group(mms, reads=[rW, rXT], writes=[rbank[B_SM]])
            b16 = lambda ap: ap[:, :].unsqueeze(1).broadcast_to([128, 4, 16])
            c4 = lambda ap: ap[:, :].rearrange("p (c h) -> p c h", h=16)
            T.op("vector", lambda e: e.tensor_tensor(out=c4(sm["dtx"]), in0=c4(smb[:, 0:64]), in1=b16(dtb_b), op=ALU.add),
                 reads=[rbank[B_SM], rC], writes=[rsm["dtx"]])
            T.op("scalar", lambda e: e.activation(out=sm["dtx"][:], in_=sm["dtx"][:], func=AF.Exp), reads=[], writes=[rsm["dtx"]])
            T.op("scalar", lambda e: e.activation(out=sm["dt4"][:], in_=sm["dtx"][:], func=AF.Ln, bias=1.0),
                 reads=[rsm["dtx"]], writes=[rsm["dt4"]])
            T.op("vector", lambda e: e.tensor_tensor(out=c4(sm["adt4"]), in0=c4(sm["dt4"]), in1=b16(a_b), op=ALU.mult),
                 reads=[rsm["dt4"], rC], writes=[rsm["adt4"]])
            mms = []
            for ci in range(4):
                mms.append((smb[:, 64 + 16 * ci:64 + 16 * (ci + 1)], tri[:], sm["adt4"][:, 16 * ci:16 * (ci + 1)], False, True))
                mms.append((smb[:, 128 + 16 * ci:128 + 16 * (ci + 1)], onesF[:], sm["adt4"][:, 16 * ci:16 * (ci + 1)], False, True))
            T.mm_group(mms, reads=[rsm["adt4"], rC], writes=[rbank[B_SM]])
            T.op("vector", lambda e: e.tensor_scalar(out=sm["nacs4"][:], in0=smb[:, 64:128], scalar1=-1.0, scalar2=None, op0=ALU.mult),
                 reads=[rbank[B_SM]], writes=[rsm["nacs4"]])
            T.op("vector", lambda e: e.tensor_copy(out=sm["cdb4"][:], in_=smb[:, 128:192]), reads=[rbank[B_SM]], writes=[rsm["cdb4"]])
            T.op("vector", lambda e: e.tensor_tensor(out=sm["dte4"][:], in0=sm["cdb4"][:], in1=sm["nacs4"][:], op=ALU.add),
                 reads=[rsm["cdb4"], rsm["nacs4"]], writes=[rsm["dte4"]])
            T.op("scalar", lambda e: e.activation(out=sm["D04"][:], in_=sm["nacs4"][:], func=AF.Exp, scale=-1.0),
                 reads=[rsm["nacs4"]], writes=[rsm["D04"]])
            T.op("scalar", lambda e: e.activation(out=sm["dte4"][:], in_=sm["dte4"][:], func=AF.Exp), reads=[], writes=[rsm["dte4"]])
            T.op("scalar", lambda e: e.activation(out=sm["cdb4"][:], in_=sm["cdb4"][:], func=AF.Exp), reads=[rsm["dte4"]], writes=[rsm["cdb4"]])
            T.op("vector", lambda e: e.tensor_tensor(out=sm["ddt4"][:], in0=sm["dt4"][:], in1=sm["dte4"][:], op=ALU.mult),
                 reads=[rsm["dt4"], rsm["dte4"]], writes=[rsm["ddt4"]])

            def chunk_fns(ci):
                k = ci % 2
                cc = slice(128 * ci, 128 * ci + 128)
                hs = lambda ap, h0, n: ap[:, 16 * ci + h0:16 * ci + h0 + n]
                bch = lambda ap, h0, n: hs(ap, h0, n).unsqueeze(2).broadcast_to([128, n, 64])
                def prologue(cj):
                    kj = cj % 2
                    for g in range(2):
                        gs = slice(512 * g, 512 * (g + 1))
                        bcj = lambda ap, h0, n: ap[:, 16 * cj + h0:16 * cj + h0 + n].unsqueeze(2).broadcast_to([128, n, 64])
                        T.op("gpsimd", lambda e, o=v3(xg[kj][:, gs]), i_=v3(xs_tok[:, cj, gs]), b_=bcj(sm["dt4"], 8 * g, 8):
                             e.tensor_tensor(out=o, in0=i_, in1=b_, op=ALU.mult), reads=[rxs, rsm["dt4"]], writes=[rxg[kj]])
                        T.op("gpsimd", lambda e, o=v3(xgd[kj][:, gs]), i_=v3(xs_tok[:, cj, gs]), b_=bcj(sm["ddt4"], 8 * g, 8):
                             e.tensor_tensor(out=o, in0=i_, in1=b_, op=ALU.mult), reads=[rxs, rsm["ddt4"]], writes=[rxgd[kj]])
                        T.op("gpsimd", lambda e, o=v3(tsk[2 * kj + g][:, :]), i_=v3(xs_tok[:, cj, gs]), b_=bc64(dsk_b, 8 * g, 8):
                             e.tensor_tensor(out=o, in0=i_, in1=b_, op=ALU.mult), reads=[rxs, rC], writes=[rtsk[2 * kj + g]])

                def emit_R(q):
                    g, qd = divmod(q, 2)
                    h0 = 8 * g + 4 * qd
                    bk = B_R0 + (q % 2)
                    mms = [(banks[bk][:, :], identB[:], negm4[:], True, False)]
                    for j in range(4):
                        mms.append((banks[bk][:, 128 * j:128 * (j + 1)],
                                    hs(sm["adt4"], h0 + j, 1).broadcast_to([128, 128]), tri[:], False, j == 3))
                    T.mm_group(mms, reads=[rsm["adt4"], rC], writes=[rbank[bk]])

                def emit_exp(q):
                    g, qd = divmod(q, 2)
                    h0 = 8 * g + 4 * qd
                    bk = B_R0 + (q % 2)
                    for j in range(4):
                        T.op("scalar", lambda e, o=Dm[q % 2][:, 128 * j:128 * (j + 1)], i_=banks[bk][:, 128 * j:128 * (j + 1)],
                             b_=hs(sm["nacs4"], h0 + j, 1): e.activation(out=o, in_=i_, func=AF.Exp, bias=b_),
                             reads=[rbank[bk], rsm["nacs4"]], writes=[rDm[q % 2]])

                def emit_MT(q):
                    g, qd = divmod(q, 2)
                    T.op("vector", lambda e, o=MT[q % 2][:].rearrange("p (j l) -> p j l", l=128),
                         i_=Dm[q % 2][:].rearrange("p (j l) -> p j l", l=128),
                         b_=Gm[:, g, ci, :].unsqueeze(1).broadcast_to([128, 4, 128]):
                         e.tensor_tensor(out=o, in0=i_, in1=b_, op=ALU.mult), reads=[rDm[q % 2], rGm], writes=[rMT[q % 2]])

                def emit_ydiag(q):
                    g, qd = divmod(q, 2)
                    h0 = 8 * g + 4 * qd
                    ybk = B_Y if g == 0 else B_PB0
                    mms = []
                    for j in range(4):
                        hh = 4 * qd + j
                        mms.append((banks[ybk][:, 64 * hh:64 * (hh + 1)], MT[q % 2][:, 128 * j:128 * (j + 1)],
                                    xg[k][:, 64 * (h0 + j):64 * (h0 + j + 1)], (qd == 0 and j == 0), False))
                    T.mm_group(mms, reads=[rMT[q % 2], rxg[k]], writes=[rbank[ybk]])

                def emit_yoff(g):
                    obk = B_OS if g == 0 else B_PB1
                    T.mm_group([(banks[obk][:, :], CT[:, g, cc], Hbf[:, g, :], True, True)], reads=[rCT, rHbf[g]], writes=[rbank[obk]])

                def emit_comb(g):
                    gs = slice(512 * g, 512 * (g + 1))
                    obk = B_OS if g == 0 else B_PB1
                    ybk = B_Y if g == 0 else B_PB0
                    T.op("vector", lambda e, o=v3(yo[g][:, :]), i_=v3(banks[obk][:, :]), b_=bch(sm["D04"], 8 * g, 8):
                         e.tensor_tensor(out=o, in0=i_, in1=b_, op=ALU.mult), reads=[rbank[obk], rsm["D04"]], writes=[ryo[g]])
                    T.op("vector", lambda e, o=y[0][:, gs], i_=banks[ybk][:, :]: e.tensor_tensor(out=o, in0=i_, in1=yo[g][:, :], op=ALU.add),
                         reads=[rbank[ybk], ryo[g]], writes=[ry[0]])
                    T.op("gpsimd", lambda e, o=y[0][:, gs], t_=tsk[2 * k + g][:, :]: e.tensor_tensor(out=o, in0=o, in1=t_, op=ALU.add),
                         reads=[rtsk[2 * k + g]], writes=[ry[0]])

                def emit_state(g):
                    gs = slice(512 * g, 512 * (g + 1))
                    obk = B_OS if g == 0 else B_PB1
                    T.mm_group([(banks[obk][:, :], Btok[:, ci, g, :], xgd[k][:, gs], True, True)],
                               reads=[rBtok, rxgd[k]], writes=[rbank[obk]])
                    T.op("vector", lambda e, o=v3(H[:, g, :]), b_=bch(sm["cdb4"], 8 * g, 8):
                         e.tensor_tensor(out=o, in0=o, in1=b_, op=ALU.mult), reads=[rsm["cdb4"]], writes=[rH[g]])
                    T.op("vector", lambda e, o=H[:, g, :], i_=banks[obk][:, :]: e.tensor_tensor(out=o, in0=i_, in1=o, op=ALU.add),
                         reads=[rbank[obk]], writes=[rH[g]])
                    T.op("scalar", lambda e, o=Hbf[:, g, :], i_=H[:, g, :]: e.activation(out=o, in_=i_, func=AF.Copy),
                         reads=[rH[g]], writes=[rHbf[g]])

                def early():
                    emit_R(0); emit_R(1)
                    emit_yoff(0); emit_yoff(1)
                    emit_exp(0); emit_exp(1)
                    emit_MT(0); emit_MT(1)
                    emit_R(2); emit_R(3)
                    emit_ydiag(0); emit_ydiag(1)
                    emit_exp(2); emit_exp(3)
                    emit_MT(2); emit_MT(3)
                    emit_ydiag(2); emit_ydiag(3)

                def mid():
                    if ci + 1 < 4:
                        prologue(ci + 1)
                    emit_comb(0)
                    emit_state(0)
                    emit_comb(1)
                    emit_state(1)

                def mid_g(g):
                    emit_comb(g)
                    emit_state(g)

                def exps_g(g):
                    emit_R(2 * g); emit_R(2 * g + 1)
                    emit_yoff(g)
                    emit_exp(2 * g); emit_exp(2 * g + 1)

                def mt_g(g):
                    emit_MT(2 * g); emit_MT(2 * g + 1)
                    emit_ydiag(2 * g); emit_ydiag(2 * g + 1)

                def late():
                    yk, ssk, rsk, mxk = y[0], ssq[k], rstd[k], mixn[k]
                    T.op("vector", lambda e, yk=yk, z_=zs[:, ci, :]: e.tensor_tensor(out=yk[:], in0=yk[:], in1=z_, op=ALU.mult),
                         reads=[rzs], writes=[ry[0]])
                    T.op("vector", lambda e, ssk=ssk: e.memset(ssk[:], 0.0), reads=[], writes=[rssq[k]])
                    T.op("scalar", lambda e, yk=yk, ssk=ssk, mxk=mxk: e.activation(out=mxk[:], in_=yk[:], func=AF.Square, accum_out=ssk[:]),
                         reads=[ry[0]], writes=[rmixn[k], rssq[k]])
                    T.op("scalar", lambda e, ssk=ssk, rsk=rsk: e.activation(out=rsk[:], in_=ssk[:], func=AF.Ln, bias=EPS, scale=1.0 / 1024),
                         reads=[rssq[k]], writes=[rrstd[k]])
                    T.op("scalar", lambda e, rsk=rsk: e.activation(out=rsk[:], in_=rsk[:], func=AF.Exp, scale=-0.5),
                         reads=[], writes=[rrstd[k]])
                    T.op("vector", lambda e, yk=yk, rsk=rsk, mxk=mxk: e.tensor_scalar(out=mxk[:], in0=yk[:], scalar1=rsk[:, 0:1], scalar2=None, op0=ALU.mult),
                         reads=[ry[0], rrstd[k]], writes=[rmixn[k]])
                    mt = mixT[0]
                    for half in range(2):
                        ev = None
                        for j in range(4):
                            ft = 4 * half + j
                            ev = P.op("tensor", lambda e, o=trb_b[:, 128 * j:128 * (j + 1)], i_=mixn[k][:, 128 * ft:128 * (ft + 1)]:
                                      e.transpose(o, i_, identB[:]), T.deps([rmixn[k], rC], [rbank[B_TR]]) if j == 0 else ())
                        T.mark(ev, [rmixn[k], rC], [rbank[B_TR]])
                        src = trb_b[:, 0:512].rearrange("p (j t) -> p j t", t=128)
                        if half == 0:
                            T.op("vector", lambda e, o=mt[:, 0:4, cc], i_=src: e.tensor_copy(out=o, in_=i_),
                                 reads=[rbank[B_TR]], writes=[rmixT[0][0]])
                        else:
                            T.op("scalar", lambda e, o=mt[:, 4:8, cc], i_=src: e.activation(out=o, in_=i_, func=AF.Copy),
                                 reads=[rbank[B_TR]], writes=[rmixT[0][1]])
                return prologue, early, mid, late, mid_g, exps_g, mt_g

            fns = [chunk_fns(ci) for ci in range(4)]
            fns[0][0](0)
            fns[0][1]()
            for ci in range(4):
                if ci + 1 < 4:
                    nx = fns[ci + 1]
                    fns[ci][0](ci + 1)
                    fns[ci][4](0)
                    nx[5](0)
                    fns[ci][4](1)
                    nx[6](0)
                    nx[5](1)
                    fns[ci][3]()
                    nx[6](1)
                else:
                    fns[ci][2]()
                    fns[ci][3]()
            T.dma("gpsimd", U[1024:2048, T0:T0 + 512].rearrange("(f p) t -> p f t", p=128), mixT[0][:, :, :], "st_uB",
                  reads=rmixT[0])
        barrier(P)
```

```python
import contextlib
import os
_SKIP = set(os.environ.get('KSKIP', '').split(','))
import numpy as np
import ml_dtypes
import concourse.bass as bass
import concourse.mybir as mybir
from concourse.bass_utils import run_bass_kernel_spmd

F32 = mybir.dt.float32
BF16 = mybir.dt.bfloat16
AF = mybir.ActivationFunctionType
ALU = mybir.AluOpType
AX = mybir.AxisListType

S = 4096
D = 1024
NH = 16
HD = 64
DIL = (1, 4, 16)
D_IN = 6672
OFF_Q, OFF_K, OFF_V, OFF_ZA, OFF_ZS, OFF_XBC, OFF_DT = 0, 1024, 2048, 3072, 4096, 5120, 6656
EPS = 1e-5


class Ev:
    __slots__ = ("eng", "idx", "sem", "val")

    def __init__(self, eng, idx, sem=None, val=None):
        self.eng, self.idx, self.sem, self.val = eng, idx, sem, val


class Prog:
    ENGS = ("sync", "scalar", "vector", "gpsimd", "tensor")

    def __init__(self, nc):
        self.nc = nc
        self.q = {e: [] for e in self.ENGS}
        self.dma_cnt = {}

    def op(self, eng, fn, deps=()):
        lst = self.q[eng]
        ev = Ev(eng, len(lst))
        lst.append([fn, [d for d in deps if d is not None], ev, False])
        return ev

    def dma(self, eng, out, in_, slot, deps=()):
        self.dma_cnt[slot] = self.dma_cnt.get(slot, 0) + 16
        ev = Ev(eng, len(self.q[eng]), sem=slot, val=self.dma_cnt[slot])
        self.q[eng].append([lambda e, o=out, i=in_: e.dma_start(out=o, in_=i),
                            [d for d in deps if d is not None], ev, True])
        return ev

    def emit(self, final_waits):
        nc = self.nc
        ref = {e: set() for e in self.ENGS}
        for e in self.ENGS:
            for fn, deps, ev, is_dma in self.q[e]:
                for d in deps:
                    if d.sem is None or d.sem.startswith("e_"):
                        ref[d.eng].add(d.idx)
        for d in final_waits:
            if d.sem is None:
                ref[d.eng].add(d.idx)
        for e in self.ENGS:
            c = 0
            for i, item in enumerate(self.q[e]):
                if item[3]:
                    continue
                if i in ref[e]:
                    c += 1
                    item[2].sem = "e_" + e
                    item[2].val = c
            assert c < 60000, (e, c)
        for s, v in self.dma_cnt.items():
            assert v < 60000, (s, v)
        names = ["e_" + e for e in self.ENGS] + sorted(self.dma_cnt)
        with contextlib.ExitStack() as st:
            sems = {n: st.enter_context(nc.semaphore(n)) for n in names}
            block = st.enter_context(nc.Block())
            for e in self.ENGS:
                items = self.q[e]
                fw = final_waits if e == "sync" else ()

                def body(eng, items=items, fw=fw):
                    seen = {}
                    for fn, deps, ev, is_dma in items:
                        need = {}
                        for d in deps:
                            assert d.sem is not None
                            if need.get(d.sem, 0) < d.val:
                                need[d.sem] = d.val
                        for sname, v in need.items():
                            if seen.get(sname, 0) < v:
                                eng.wait_ge(sems[sname], v)
                                seen[sname] = v
                        ins = fn(eng)
                        if is_dma:
                            ins.then_inc(sems[ev.sem], 16)
                        elif ev.sem is not None:
                            ins.then_inc(sems[ev.sem], 1)
                    for d in fw:
                        if seen.get(d.sem, 0) < d.val:
                            eng.wait_ge(sems[d.sem], d.val)
                            seen[d.sem] = d.val

                getattr(block, e)(body)


def latest(*evs):
    return [e for e in evs if e is not None]


def _bf16(a):
    return np.asarray(a, np.float32).astype(ml_dtypes.bfloat16)


def make_constants():
    c = {}
    slopes = 2.0 ** (-8.0 * np.arange(1, NH + 1) / NH)
    t = np.arange(S)
    hi_pos = (t >> 7).astype(np.float32)
    lo_pos = (t & 127).astype(np.float32)
    aug = np.zeros((NH, 2, 12, S), np.float32)
    for h in range(NH):
        cc = np.float64(slopes[h])
        c1 = np.float64(_bf16(cc).astype(np.float64))
        c2 = np.float64(_bf16(cc - c1).astype(np.float64))
        c3 = np.float64(_bf16(cc - c1 - c2).astype(np.float64))
        for j, cj in enumerate((c1, c2, c3)):
            aug[h, 0, j] = 128.0 * cj
            aug[h, 0, 3 + j] = cj
            aug[h, 0, 6 + j] = hi_pos
            aug[h, 0, 9 + j] = lo_pos
            aug[h, 1, j] = hi_pos
            aug[h, 1, 3 + j] = lo_pos
            aug[h, 1, 6 + j] = -128.0 * cj
            aug[h, 1, 9 + j] = -cj
    c["aug"] = _bf16(aug)
    ki = np.arange(128)[:, None]
    qi = np.arange(128)[None, :]
    mprev = np.where(ki >= qi, 0.0, -30000.0).astype(np.float32)
    mcur = np.where(ki <= qi, 0.0, -30000.0).astype(np.float32)
    c["mask4"] = np.concatenate([mprev, mcur, mprev, mcur], axis=1).astype(np.float32)
    c["tri"] = np.triu(np.ones((128, 128), np.float32))
    c["ident"] = np.eye(128, dtype=np.float32)
    c["identb"] = _bf16(np.eye(128, dtype=np.float32))
    si = np.arange(128)[:, None]
    li = np.arange(128)[None, :]
    c["negm4"] = _bf16(np.tile(np.where(si > li, -30000.0, 0.0), (1, 4)))
    return c


def I_act(P, out, in_, func, deps=(), bias=None, scale=None, accum_out=None, eng="scalar"):
    kw = {}
    if bias is not None:
        kw["bias"] = bias
    if scale is not None:
        kw["scale"] = scale
    if accum_out is not None:
        kw["accum_out"] = accum_out
    return P.op(eng, lambda e: e.activation(out=out, in_=in_, func=func, **kw), deps)


def I_copy(P, eng, out, in_, deps=()):
    if eng == "scalar":
        return P.op(eng, lambda e: e.activation(out=out, in_=in_, func=AF.Copy), deps)
    return P.op(eng, lambda e: e.tensor_copy(out=out, in_=in_), deps)


def I_tt(P, eng, out, in0, in1, op, deps=()):
    return P.op(eng, lambda e: e.tensor_tensor(out=out, in0=in0, in1=in1, op=op), deps)


def I_ts(P, eng, out, in0, s1, s2, op0, op1=None, deps=(), accum_out=None):
    kw = {}
    if op1 is not None:
        kw["op1"] = op1
    if accum_out is not None:
        kw["accum_out"] = accum_out
    return P.op(eng, lambda e: e.tensor_scalar(out=out, in0=in0, scalar1=s1, scalar2=s2, op0=op0, **kw), deps)


def I_stt(P, eng, out, in0, scalar, in1, op0, op1, deps=()):
    return P.op(eng, lambda e: e.scalar_tensor_tensor(out=out, in0=in0, scalar=scalar, in1=in1, op0=op0, op1=op1), deps)


def I_mm(P, out, lhsT, rhs, start, stop, deps=(), skip=True):
    return P.op("tensor", lambda e: e.matmul(out, lhsT=lhsT, rhs=rhs, start=start, stop=stop,
                                             skip_group_check=skip), deps)


def I_memset(P, eng, ap, val, deps=()):
    return P.op(eng, lambda e: e.memset(ap, val), deps)


def barrier(P):
    evs = []
    for e in P.ENGS:
        for item in reversed(P.q[e]):
            if not item[3]:
                evs.append(item[2])
                break
    last = {}
    for e in P.ENGS:
        for item in P.q[e]:
            if item[3]:
                last[item[2].sem] = item[2]
    evs += list(last.values())
    out = []
    for e in P.ENGS:
        out.append(P.op(e, lambda eng: eng.nop(), evs))
    return out


def tok_ap(t, d, r, n, cnt=128, lo=0, hi=None):
    start = d * (128 * n + lo) + r
    stop = start + d * (cnt - 1) + 1
    return slice(start, stop, d)


def build_program(debug_u=False, pairs=tuple(range(8)), phases=("A", "B", "C"), dbg=3, n_sc=8):
    nc = bass.Bass("TRN2", target_bir_lowering=False)
    xT = nc.dram_tensor("xT", [D, S], F32, kind="ExternalInput").ap()
    w_in = nc.dram_tensor("w_in", [D, D_IN], F32, kind="ExternalInput").ap()
    aug = nc.dram_tensor("aug", [NH, 2, 12, S], BF16, kind="ExternalInput").ap()
    mask4_d = nc.dram_tensor("mask4", [128, 512], F32, kind="ExternalInput").ap()
    dr = {}
    for name, shape, dt in (("tri", [128, 128], F32), ("ident", [128, 128], F32), ("identb", [128, 128], BF16), ("negm4", [128, 512], BF16),
                            ("conv_wT", [128, 12, 4], F32), ("conv_b2", [128, 12], F32), ("dt_bias", [16], F32),
                            ("a_log", [16], F32), ("d_skip", [16], F32), ("ssm_norm_g", [1024], F32),
                            ("att_norm_g2", [128, 8], F32), ("ssm_norm_g2", [128, 8], F32), ("ln_g", [1024], F32), ("ln_b", [1024], F32),
                            ("w_out", [2048, D], F32), ("x", [S, D], F32)):
        dr[name] = nc.dram_tensor(name, shape, dt, kind="ExternalInput").ap()
    U = nc.dram_tensor("U", [2048, S], BF16, kind="ExternalOutput" if debug_u else "Internal").ap()
    out_d = nc.dram_tensor("out", [S, D], F32, kind="ExternalOutput").ap()

    P = Prog(nc)
    final_waits = []
    with contextlib.ExitStack() as st:
        def sb(name, shape, dt, stack=st):
            return stack.enter_context(nc.sbuf_tensor(name, shape, dt))

        banks = [st.enter_context(nc.psum_tensor(f"bank{i}", [128, 512], F32)) for i in range(8)]
        w_in_v = w_in.rearrange("(kc p) c -> p kc c", p=128)
        with contextlib.ExitStack() as stx:
            XT = sb("XT", [128, 8, S], BF16, stx)
            with contextlib.ExitStack() as st0:
                xstage = [sb(f"xstage{i}", [128, S], F32, st0) for i in range(2)]
                free = [[], []]
                for kc in range(8):
                    b = kc % 2
                    ld = P.dma("sync", xstage[b][:], xT[kc * 128:(kc + 1) * 128, :], f"ld_x{b}", deps=free[b])
                    e1 = I_copy(P, "vector", XT[:, kc, 0:1920], xstage[b][:, 0:1920], [ld])
                    e2 = I_copy(P, "scalar", XT[:, kc, 1920:3840], xstage[b][:, 1920:3840], [ld])
                    e3 = I_copy(P, "gpsimd", XT[:, kc, 3840:4096], xstage[b][:, 3840:4096], [ld])
                    free[b] = [e1, e2, e3]
                barrier(P)

            if "A" in phases:
                phase_A(nc, P, st, banks, XT, w_in_v, aug, mask4_d, U, pairs, dbg, dr)
                barrier(P)
            if "B" in phases:
                (phase_B if 'oldB' in _SKIP else phase_B2)(nc, P, banks, XT, w_in_v, dr, U, n_sc)
                barrier(P)
        if "C" in phases:
            phase_C(nc, P, banks, dr, U, out_d)
            barrier(P)
        last = {}
        for e in P.ENGS:
            for item in P.q[e]:
                if item[3]:
                    last[item[2].sem] = item[2]
        final_waits = list(last.values())
        P.emit(final_waits)
    return nc


def phase_C(nc, P, banks, dr, U, out_d, n_tg=8):
    T = Trk(P)
    ALPHA = 2.0 ** 0.25
    with contextlib.ExitStack() as st:
        def sb(name, shape, dt, stack=None):
            return (stack or st).enter_context(nc.sbuf_tensor("C_" + name, shape, dt))

        Wo = sb("Wo", [128, 16, D], BF16)
        gA = sb("gA", [128, 16], F32)
        lg_b = sb("lg_b", [128, D], F32)
        lb_b = sb("lb_b", [128, D], F32)
        onesB = sb("onesB", [128, 1], BF16)
        rC = Res(); rWo = Res()
        T.dma("sync", gA[:, 0:8], dr["att_norm_g2"][:, :], "ld_d1", writes=[rC])
        T.dma("sync", gA[:, 8:16], dr["ssm_norm_g2"][:, :], "ld_d1b", writes=[rC])
        T.dma("sync", lg_b[:], dr["ln_g"].partition_broadcast(128), "ld_d2", writes=[rC])
        T.dma("sync", lb_b[:], dr["ln_b"].partition_broadcast(128), "ld_d3", writes=[rC])
        I_memset(P, "vector", onesB[:], 1.0)
        wo_v = dr["w_out"].rearrange("(f p) d -> p f d", p=128)
        with contextlib.ExitStack() as stw:
            wst = [sb(f"wstC{i}", [128, 2, D], F32, stw) for i in range(4)]
            rws = [Res() for _ in range(4)]
            for i in range(8):
                b = i % 4
                T.dma("sync" if i % 2 == 0 else "gpsimd", wst[b][:], wo_v[:, 2 * i:2 * i + 2, :], f"ld_wo{b}", writes=[rws[b]])
                for j in range(2):
                    f = 2 * i + j
                    if j == 0:
                        T.op("vector", lambda e, o=Wo[:, f, :], i_=wst[b][:, j, :], s_=gA[:, f:f + 1]:
                             e.tensor_scalar(out=o, in0=i_, scalar1=s_, scalar2=None, op0=ALU.mult),
                             reads=[rws[b], rC], writes=[])
                    else:
                        T.op("scalar", lambda e, o=Wo[:, f, :], i_=wst[b][:, j, :], s_=gA[:, f:f + 1]:
                             e.activation(out=o, in_=i_, func=AF.Copy, scale=s_),
                             reads=[rws[b], rC], writes=[])
            barrier(P)

        Ub = [sb(f"Ub{i}", [128, 16, 512], BF16) for i in range(2)]
        xt = [sb(f"xt{i}", [128, 4, D], F32) for i in range(2)]
        sq = [sb(f"sq{i}", [128, 8, 128], BF16) for i in range(2)]
        rr = [sb(f"rr{i}", [128, D], F32) for i in range(2)]
        ot = [sb(f"ot{i}", [128, D], F32) for i in range(2)]
        st6 = [sb(f"st6{i}", [128, 2, 6], F32) for i in range(2)]
        mv = [sb(f"mv{i}", [128, 2], F32) for i in range(2)]
        ra = [sb(f"ra{i}", [128, 1], F32) for i in range(2)]
        rl = [sb(f"rl{i}", [128, 1], F32) for i in range(2)]
        rUa = [Res(), Res()]; rUs = [Res(), Res()]; rxt = [[Res() for _ in range(4)] for _ in range(2)]; rot = [Res(), Res()]
        r = [{k: Res() for k in ("sq", "rr", "st6", "mv", "ra", "rl")} for _ in range(2)]
        slots = [(banks[0], banks[1]), (banks[2], banks[3]), (banks[4], banks[5])]
        rslot = [[Res(), Res()] for _ in range(3)]
        ssb = [banks[6], banks[7]]; rss = [Res(), Res()]
        U_v = U.rearrange("(f p) t -> p f t", p=128)
        x_v = dr["x"].rearrange("(c p) d -> p c d", p=128)
        slot_i = 0
        cnt = 0
        for tg in range(n_tg):
            b = tg % 2
            T0 = 512 * tg
            T.dma("sync", Ub[b][:, 0:8, :], U_v[:, 0:8, T0:T0 + 512], f"ld_ua{b}", writes=[rUa[b]])
            T.dma("sync", Ub[b][:, 8:16, :], U_v[:, 8:16, T0:T0 + 512], f"ld_us{b}", writes=[rUs[b]])
            T.dma("sync", xt[b][:], x_v[:, 4 * tg:4 * tg + 4, :], f"ld_xt{b}", writes=rxt[b])
            for ci in range(4):
                cc = slice(128 * ci, 128 * ci + 128)
                k = cnt % 2
                rk = r[k]
                T.op("scalar", lambda e, o=xt[b][:, ci, :]: e.activation(out=o, in_=o, func=AF.Copy, scale=ALPHA),
                     reads=[], writes=[rxt[b][ci]])
                T.op("scalar", lambda e, o=sq[k][:], i_=Ub[b][:, 0:8, cc]: e.activation(out=o, in_=i_, func=AF.Square),
                     reads=[rUa[b]], writes=[rk["sq"]])
                halves = []
                for half in range(2):
                    cols = slice(512 * half, 512 * (half + 1))
                    sl = slot_i; slot_i = (slot_i + 1) % 3
                    bkA, bkB = slots[sl]
                    T.mm_group([(bkA[:, :], Ub[b][:, f, cc], Wo[:, f, cols], f == 0, f == 7) for f in range(8)],
                               reads=[rUa[b]], writes=[rslot[sl][0]])
                    T.mm_group([(bkB[:, :], Ub[b][:, 8 + f, cc], Wo[:, 8 + f, cols], f == 0, f == 7) for f in range(8)],
                               reads=[rUs[b]], writes=[rslot[sl][1]])
                    halves.append((sl, cols))
                T.mm_group([(ssb[k][:, 0:1], sq[k][:, f, :], onesB[:], f == 0, f == 7) for f in range(8)],
                           reads=[rk["sq"]], writes=[rss[k]])
                T.op("scalar", lambda e, o=ra[k][:], i_=ssb[k][:, 0:1]: e.activation(out=o, in_=i_, func=AF.Ln, bias=EPS, scale=1.0 / 1024),
                     reads=[rss[k]], writes=[rk["ra"]])
                T.op("scalar", lambda e, o=ra[k][:]: e.activation(out=o, in_=o, func=AF.Exp, scale=-0.5), reads=[], writes=[rk["ra"]])
                for sl, cols in halves:
                    bkA, bkB = slots[sl]
                    T.op("vector", lambda e, o=rr[k][:, cols], i_=bkA[:, :], x_=xt[b][:, ci, cols], s_=ra[k][:, 0:1]:
                         e.scalar_tensor_tensor(out=o, in0=i_, scalar=s_, in1=x_, op0=ALU.mult, op1=ALU.add),
                         reads=[rslot[sl][0], rk["ra"], rxt[b][ci]], writes=[rk["rr"]])
                    T.op("vector", lambda e, o=rr[k][:, cols], i_=bkB[:, :]: e.tensor_tensor(out=o, in0=i_, in1=o, op=ALU.add),
                         reads=[rslot[sl][1]], writes=[rk["rr"]])
                for half in range(2):
                    T.op("vector", lambda e, o=st6[k][:, half, :], i_=rr[k][:, 512 * half:512 * (half + 1)]: e.bn_stats(out=o, in_=i_),
                         reads=[rk["rr"]], writes=[rk["st6"]])
                T.op("vector", lambda e, o=mv[k][:], i_=st6[k][:]: e.bn_aggr(out=o, in_=i_), reads=[rk["st6"]], writes=[rk["mv"]])
                T.op("scalar", lambda e, o=rl[k][:], i_=mv[k][:, 1:2]: e.activation(out=o, in_=i_, func=AF.Ln, bias=EPS),
                     reads=[rk["mv"]], writes=[rk["rl"]])
                T.op("scalar", lambda e, o=rl[k][:]: e.activation(out=o, in_=o, func=AF.Exp, scale=-0.5), reads=[], writes=[rk["rl"]])
                T.op("vector", lambda e, o=ot[k][:], i_=rr[k][:], m_=mv[k][:, 0:1], s_=rl[k][:, 0:1]:
                     e.tensor_scalar(out=o, in0=i_, scalar1=m_, scalar2=s_, op0=ALU.subtract, op1=ALU.mult),
                     reads=[rk["rr"], rk["mv"], rk["rl"]], writes=[rot[k]])
                T.op("gpsimd", lambda e, o=ot[k][:]: e.tensor_tensor(out=o, in0=o, in1=lg_b[:], op=ALU.mult),
                     reads=[rC], writes=[rot[k]])
                T.op("gpsimd", lambda e, o=ot[k][:]: e.tensor_tensor(out=o, in0=o, in1=lb_b[:], op=ALU.add),
                     reads=[rC], writes=[rot[k]])
                T.dma("gpsimd", out_d[T0 + 128 * ci:T0 + 128 * ci + 128, :], ot[k][:], f"st_o{k}", reads=[rot[k]])
                cnt += 1


def phase_A(nc, P, st_outer, banks, XT, w_in_v, aug, mask4_d, U, pairs, dbg=3, dr=None):
    with contextlib.ExitStack() as st:
        def sb(name, shape, dt):
            return st.enter_context(nc.sbuf_tensor(name, shape, dt))

        mask4 = sb("mask4s", [128, 512], F32)
        stmp = [sb(f"stmp{i}", [128, 512], F32) for i in range(4)]
        wstage = sb("wstage", [128, 8, 512], F32)
        wbf = sb("wbf", [128, 8, 512], BF16)
        qk = [[sb(f"qk{w}{h}", [128, S], BF16) for h in range(2)] for w in range(2)]
        sz = sb("sz", [128, S], BF16)
        V = sb("V", [128, 3, 32, 192], BF16)
        NPT = 6
        PT = [sb(f"PT{i}", [128, 512], BF16) for i in range(NPT)]
        rc = [sb(f"rc{i}", [128, 512], F32) for i in range(2)]
        t1 = [sb(f"t1{i}", [128, 512], F32) for i in range(2)]
        uT = sb("uT", [128, S], BF16)
        vT = sb("vT", [128, S], BF16)
        identBa = sb("identBa", [128, 128], BF16)

        acc = banks[0:4]
        sbank = banks[4:6]
        pbank = banks[6:8]

        ev_mask = P.dma("sync", mask4[:], mask4_d[:, :], "ld_c")
        ev_id = P.dma("sync", identBa[:], dr["identb"][:, :], "ld_cid")
        vT_free = []
        ev_ones = I_memset(P, "gpsimd", V[:, :, :, 64:128], 1.0) if "ones" not in _SKIP else None

        pb_free = [[], []]
        pb_i = 0
        sb_free = [None, None]
        pt_free = [None] * 6
        st_free = [None] * 4
        acc_free = [[] for _ in range(4)]
        w_free = []
        qk_free = [[[], []], [[], []]]
        sz_free = []
        V_free = []
        uT_free = None
        g_ctr = 0

        def load_w(hp_, deps_):
            out_ = []
            for wi, off in enumerate((OFF_Q, OFF_K, OFF_V, OFF_ZA)):
                out_.append(P.dma("sync", wstage[:, :, wi * 128:(wi + 1) * 128],
                                  w_in_v[:, :, off + hp_ * 128: off + (hp_ + 1) * 128], "ld_w", deps=deps_))
            return out_

        lds = load_w(pairs[0], [])
        w_casts = None
        ws_b = wstage[:].rearrange("p a b -> p (a b)").bitcast(BF16)
        qk16 = [ws_b[:, 0:S], ws_b[:, S:2 * S]]
        qk16_free = []
        rc_free = [None, None]
        ev_ctr = 0
        pending_evac = []
        for hp in pairs:
            hA, hB = 2 * hp, 2 * hp + 1
            if w_casts is None:
                c1 = I_copy(P, "vector", wbf[:, 0:4, :], wstage[:, 0:4, :], lds)
                c2 = I_copy(P, "scalar", wbf[:, 4:8, :], wstage[:, 4:8, :], lds)
                w_casts = [c1, c2]
            w_ready = w_casts
            nxt = pairs.index(hp) + 1
            if nxt < len(pairs):
                lds = load_w(pairs[nxt], w_casts + qk16_free)
            aug_ev = [[None, None], [None, None]]
            for w in range(2):
                for hh, h in enumerate((hA, hB)):
                    if "aug" in _SKIP:
                        continue
                    aug_ev[w][hh] = P.dma("sync", qk[w][hh][64:76, :], aug[h, w, :, :], f"ld_aug{w}{hh}",
                                          deps=qk_free[w][hh])
            w_last = []
            qk_ready = [[[], []], [[], []]]
            sz_ready = []
            vT_ready = []
            for wi in (0, 1, 2, 3):
                for tt in range(8):
                    pbk = pbank[pb_i]
                    deps = list(w_ready) + list(pb_free[pb_i])
                    for kc in range(8):
                        mm = I_mm(P, pbk[:, :], wbf[:, kc, wi * 128:(wi + 1) * 128],
                                  XT[:, kc, tt * 512:(tt + 1) * 512], kc == 0, kc == 7,
                                  deps if kc == 0 else ())
                    cols = slice(tt * 512, (tt + 1) * 512)
                    if wi == 3:
                        e = I_act(P, sz[:, cols], pbk[:, :], AF.Silu, [mm] + sz_free)
                        sz_ready.append(e)
                        pb_free[pb_i] = [e]
                    elif wi == 2:
                        e = I_copy(P, "vector" if tt % 2 == 0 else "scalar", vT[:, cols], pbk[:, :], [mm] + vT_free)
                        vT_ready.append(e)
                        pb_free[pb_i] = [e]
                    else:
                        w = wi
                        sc = 0.125 if w == 0 else 1.0
                        eA = I_act(P, qk[w][0][0:64, cols], pbk[0:64, :], AF.Copy, [mm] + qk_free[w][0], scale=sc)
                        if "dveB" in _SKIP:
                            eB = I_act(P, qk[w][1][0:64, cols], pbk[64:128, :], AF.Copy, [mm] + qk_free[w][1], scale=sc)
                        else:
                            eB = I_ts(P, "vector", qk[w][1][0:64, cols], pbk[64:128, :], sc, None, ALU.mult,
                                      deps=[mm] + qk_free[w][1])
                        qk_ready[w][0].append(eA)
                        qk_ready[w][1].append(eB)
                        pb_free[pb_i] = [eA, eB]
                    pb_i ^= 1
                    w_last = [mm]
            V_ready = {}
            tr_last = None
            for di, d in enumerate(DIL if dbg >= 2 else ()):
                nb = 32 // d
                for c0 in range(0, 32, 8):
                    pbk = pbank[pb_i]
                    pbk_b = pbk[:].bitcast(BF16)
                    for j in range(8):
                        r_, m_ = divmod(c0 + j, nb)
                        deps = ()
                        if j == 0:
                            deps = vT_ready + [ev_id] + list(pb_free[pb_i])
                        tr_last = P.op("tensor", lambda e, o=pbk_b[:, 128 * j:128 * (j + 1)], i_=vT[:, tok_ap(None, d, r_, m_)]:
                                       e.transpose(o, i_, identBa[:]), deps)
                    src = pbk_b[:, :].rearrange("p (c f) -> p c f", f=128)
                    eA = I_copy(P, "vector", V[:, di, c0:c0 + 8, 0:64], src[:, :, 0:64], [tr_last] + V_free)
                    eB = I_copy(P, "scalar", V[:, di, c0:c0 + 8, 128:192], src[:, :, 64:128], [tr_last, eA] + V_free)
                    for j in range(8):
                        V_ready[(di, c0 + j)] = [eA, eB]
                    pb_free[pb_i] = [eA, eB]
                    pb_i ^= 1
            vT_free = [tr_last] if tr_last is not None else []
            w_free = w_last
            if nxt < len(pairs):
                c1 = I_copy(P, "vector", wbf[:, 0:4, :], wstage[:, 0:4, :], lds + w_last)
                c2 = I_copy(P, "scalar", wbf[:, 4:8, :], wstage[:, 4:8, :], lds + w_last)
                w_casts = [c1, c2]
            V_free = []
            qk_free = [[[], []], [[], []]]
            sz_free = []

            u_written = []
            for hh, h in enumerate((hA, hB) if dbg >= 3 else ()):
                qT, kT = qk[0][hh], qk[1][hh]
                q_dep = qk_ready[0][hh] + [aug_ev[0][hh]]
                k_dep = qk_ready[1][hh] + [aug_ev[1][hh]]
                vcol = slice(0, 128) if hh == 0 else slice(64, 192)
                cdeps = (w_casts if nxt < len(pairs) else []) + qk16_free
                c16 = []
                for w in range(2):
                    src = qk[w][hh][0:76, :].rearrange("p (j r) -> p r j", r=16)
                    dst = qk16[w][0:76, :].rearrange("p (r j) -> p r j", r=16)
                    dd = (q_dep if w == 0 else k_dep) + cdeps
                    if w == 0:
                        c16.append(I_copy(P, "vector", dst, src, dd))
                    else:
                        c16.append(I_copy(P, "scalar", dst, src, dd))
                last16 = None
                for sbk in range(2):
                    qbs = []
                    for di, d in enumerate(DIL):
                        nb = 32 // d
                        per_sb = nb // 2
                        for r in range(d):
                            for n in range(sbk * per_sb, (sbk + 1) * per_sb):
                                qbs.append((di, d, r, n))
                    groups = [qbs[i:i + 2] for i in range(0, len(qbs), 2)]
                    acc_started = [False] * 4
                    acc_last_mm = [None] * 4
                    pend = []

                    def do_pv(item):
                        ptb, grp, ev_mask_mul = item
                        last_mm = None
                        for j, (di, d, r, n) in enumerate(grp):
                            nb = 32 // d
                            for role in range(2):
                                m = n - 1 + role
                                if m < 0:
                                    continue
                                tile_cols = (2 * j + role) * 128
                                cid = r * nb + m
                                lhsT = V[:, di, cid, vcol]
                                if d == 16:
                                    pieces = [(pc, 32) for pc in range(4)]
                                else:
                                    pieces = [(0, 128)]
                                for pc, cnt in pieces:
                                    i0 = pc * 32 if d == 16 else 0
                                    t0 = d * (128 * n + i0) + r
                                    col0 = t0 - 2048 * sbk
                                    bk = col0 // 512
                                    c0 = col0 % 512
                                    out = acc[bk][:, c0: c0 + d * (cnt - 1) + 1: d]
                                    rhs = PT[ptb][:, tile_cols + i0: tile_cols + i0 + cnt]
                                    deps = [ev_mask_mul] + V_ready[(di, cid)] + latest(ev_ones)
                                    if not acc_started[bk]:
                                        deps = deps + list(acc_free[bk])
                                    last_mm = I_mm(P, out, lhsT, rhs, not acc_started[bk], False, deps)
                                    acc_started[bk] = True
                                    acc_last_mm[bk] = last_mm
                        pt_free[ptb] = last_mm

                    for gi, grp in enumerate(groups):
                        if gi % 2 == 0 and pending_evac:
                            pending_evac.pop(0)()
                        sbi = g_ctr % 2
                        pti = g_ctr % 6
                        sti = g_ctr % 4
                        g_ctr += 1
                        sbb = sbank[sbi]
                        first = True
                        mm = None
                        merged = (len(grp) == 2 and grp[0][1] != 16 and grp[1][0:3] == grp[0][0:3]
                                  and grp[1][3] == grp[0][3] + 1)
                        for j, (di, d, r, n) in enumerate(grp):
                            for role in range(2):
                                if merged and (j, role) == (1, 0):
                                    continue
                                m = n - 1 + role
                                nq = n
                                if m < 0:
                                    m, nq = 0, 1
                                tile_cols = (2 * j + role) * 128
                                deps = ()
                                if first:
                                    deps = q_dep + k_dep + latest(sb_free[sbi])
                                    first = False
                                if d == 16:
                                    kop = qk16[1][0:76, r * 256 + 128 * m: r * 256 + 128 * m + 128]
                                    qop = qk16[0][0:76, r * 256 + 128 * nq: r * 256 + 128 * nq + 128]
                                    mm = I_mm(P, sbb[:, tile_cols:tile_cols + 128], kop, qop, True, True, list(deps) + c16)
                                    last16 = mm
                                elif merged and (j, role) == (0, 1):
                                    mm = I_mm(P, sbb[:, tile_cols:tile_cols + 256],
                                              kT[0:76, tok_ap(None, d, r, m)], qT[0:76, tok_ap(None, d, r, n, cnt=256)],
                                              True, True, deps)
                                else:
                                    mm = I_mm(P, sbb[:, tile_cols:tile_cols + 128],
                                              kT[0:76, tok_ap(None, d, r, m)], qT[0:76, tok_ap(None, d, r, nq)],
                                              True, True, deps)
                        mk0 = I_tt(P, "vector", stmp[sti][:, :], sbb[:, :], mask4[:, :], ALU.add,
                                   [mm, ev_mask] + latest(st_free[sti]))
                        sb_free[sbi] = mk0
                        mk = I_act(P, PT[pti][:, :], stmp[sti][:, :], AF.Exp, [mk0] + latest(pt_free[pti]))
                        st_free[sti] = mk
                        pend.append((pti, grp, mk))
                        if len(pend) > 4:
                            do_pv(pend.pop(0))
                    while pend:
                        do_pv(pend.pop(0))
                    def make_evac(bk, hh=hh, sbk=sbk, alm=acc_last_mm, szr=sz_ready):
                        def evac():
                            nonlocal ev_ctr, sz_free
                            cols = slice(2048 * sbk + 512 * bk, 2048 * sbk + 512 * (bk + 1))
                            if hh == 0:
                                o_rows, s_rows = slice(0, 64), slice(64, 128)
                            else:
                                o_rows, s_rows = slice(64, 128), slice(0, 64)
                            ei = ev_ctr % 2
                            ev_ctr += 1
                            a1 = I_act(P, rc[ei][o_rows, :], acc[bk][s_rows, :], AF.Ln, [alm[bk]] + latest(rc_free[ei]))
                            a2 = I_copy(P, "vector", t1[ei][o_rows, :], acc[bk][o_rows, :], [alm[bk]] + latest(rc_free[ei]))
                            acc_free[bk] = [a1, a2]
                            e1 = I_act(P, rc[ei][o_rows, :], rc[ei][o_rows, :], AF.Exp, [a1], scale=-1.0)
                            e2 = I_tt(P, "gpsimd", t1[ei][o_rows, :], t1[ei][o_rows, :], rc[ei][o_rows, :], ALU.mult, [e1, a2])
                            e3 = I_tt(P, "gpsimd", uT[o_rows, cols], t1[ei][o_rows, :], sz[o_rows, cols], ALU.mult,
                                      [e2] + szr + latest(uT_free))
                            rc_free[ei] = e3
                            u_written.append(e3)
                            sz_free = [e3]
                        return evac
                    pending_evac.extend(make_evac(bk) for bk in range(4))
                qk_free[0][hh] = [mm]
                qk_free[1][hh] = [mm]
                qk16_free = [last16]
            if dbg < 3:
                continue
            while pending_evac:
                pending_evac.pop(0)()
            V_free = latest(*[acc_last_mm[b] for b in range(4)])
            uT_free = P.dma("gpsimd", U[hp * 128:(hp + 1) * 128, :], uT[:, :], "st_u", deps=u_written)
            sz_free = u_written[-1:]


class Res:
    __slots__ = ("w", "r")

    def __init__(self):
        self.w = None
        self.r = {}


class Trk:
    def __init__(self, P):
        self.P = P

    def deps(self, reads, writes):
        d = []
        for t in reads:
            if t.w is not None:
                d.append(t.w)
        for t in writes:
            if t.w is not None:
                d.append(t.w)
            d.extend(t.r.values())
        return d

    def mark(self, ev, reads, writes):
        for t in reads:
            t.r[ev.sem if ev.sem is not None else ev.eng] = ev
        for t in writes:
            t.w = ev
            t.r = {}

    def op(self, eng, fn, reads=(), writes=(), extra=()):
        ev = self.P.op(eng, fn, self.deps(reads, writes) + list(extra))
        self.mark(ev, reads, writes)
        return ev

    def dma(self, eng, out, in_, slot, reads=(), writes=(), extra=()):
        ev = self.P.dma(eng, out, in_, slot, self.deps(reads, writes) + list(extra))
        self.mark(ev, reads, writes)
        return ev

    def mm_group(self, mms, reads, writes, extra=()):
        d = self.deps(reads, writes) + list(extra)
        ev = None
        for i, (out, lhsT, rhs, start, stop) in enumerate(mms):
            ev = I_mm(self.P, out, lhsT, rhs, start, stop, d if i == 0 else ())
        self.mark(ev, reads, writes)
        return ev


def bc64(ap16, h0, nh):
    return ap16[:, h0:h0 + nh].unsqueeze(2).broadcast_to([128, nh, 64])


def phase_B(nc, P, banks, XT, w_in_v, dr, U, n_sc=8, dbg_out=None):
    T = Trk(P)
    with contextlib.ExitStack() as st:
        def sb(name, shape, dt, stack=None):
            return (stack or st).enter_context(nc.sbuf_tensor("B_" + name, shape, dt))

        rXT = Res()
        Wz = sb("Wz", [128, 8, 1024], BF16)
        Wx = sb("Wx", [128, 8, 1536], BF16)
        Wdt = sb("Wdt", [128, 8, 16], BF16)
        tri = sb("tri", [128, 128], F32)
        onesF = sb("onesF", [128, 128], F32)
        identF = sb("identF", [128, 128], F32)
        identB = sb("identB", [128, 128], BF16)
        negm4 = sb("negm4", [128, 512], BF16)
        cw = sb("cw", [128, 12, 4], F32)
        cb = sb("cb", [128, 12], F32)
        dtb_b = sb("dtb_b", [128, 16], F32)
        a_b = sb("a_b", [128, 16], F32)
        dsk_b = sb("dsk_b", [128, 16], F32)
        gs_b = sb("gs_b", [128, 1024], F32)
        hal = sb("hal", [128, 12, 3], F32)
        rW = Res(); rC = Res(); rHal = Res()

        with contextlib.ExitStack() as stw:
            wst = sb("wstB", [128, 8, 512], F32, stw)
            rwst = Res()
            pieces = [(OFF_ZS, 0, 512, Wz), (OFF_ZS + 512, 512, 512, Wz),
                      (OFF_XBC, 0, 512, Wx), (OFF_XBC + 512, 512, 512, Wx), (OFF_XBC + 1024, 1024, 512, Wx),
                      (OFF_DT, 0, 16, Wdt)]
            for i, (off, dst0, n, Wt) in enumerate(pieces):
                T.dma("sync", wst[:, :, 0:n], w_in_v[:, :, off:off + n], "ld_wB", writes=[rwst])
                T.op("vector", lambda e, o=Wt[:, 0:4, dst0:dst0 + n], i_=wst[:, 0:4, 0:n]: e.tensor_copy(out=o, in_=i_),
                     reads=[rwst], writes=[rW])
                T.op("gpsimd", lambda e, o=Wt[:, 4:8, dst0:dst0 + n], i_=wst[:, 4:8, 0:n]: e.tensor_copy(out=o, in_=i_),
                     reads=[rwst], writes=[])
            T.dma("sync", tri[:], dr["tri"][:, :], "ld_c1", writes=[rC])
            T.dma("sync", identF[:], dr["ident"][:, :], "ld_c2", writes=[rC])
            T.dma("sync", negm4[:], dr["negm4"][:, :], "ld_c3", writes=[rC])
            T.dma("sync", cw[:], dr["conv_wT"][:, :, :], "ld_c4", writes=[rC])
            T.dma("sync", cb[:], dr["conv_b2"][:, :], "ld_c5", writes=[rC])
            T.dma("sync", dtb_b[:], dr["dt_bias"].partition_broadcast(128), "ld_c6", writes=[rC])
            T.dma("sync", a_b[:], dr["a_log"].partition_broadcast(128), "ld_c7", writes=[rC])
            T.dma("sync", dsk_b[:], dr["d_skip"].partition_broadcast(128), "ld_c8", writes=[rC])
            T.dma("sync", gs_b[:], dr["ssm_norm_g"].partition_broadcast(128), "ld_c9", writes=[rC])
            barrier(P)
            I_memset(P, "vector", onesF[:], 1.0)
            I_memset(P, "vector", hal[:], 0.0)
            I_copy(P, "vector", identB[:], identF[:])
            I_act(P, a_b[:], a_b[:], AF.Exp)
            barrier(P)
            I_ts(P, "vector", a_b[:], a_b[:], -1.0, None, ALU.mult)
            barrier(P)

        xb = [sb(f"xb{i}", [128, 515], F32) for i in range(2)]
        cvb = [sb(f"cvb{i}", [128, 512], F32) for i in range(2)]
        xsT = sb("xsT", [128, 8, 512], F32)
        BT = sb("BT", [128, 2, 512], BF16)
        CT = sb("CT", [128, 2, 512], BF16)
        dtx = sb("dtx", [128, 16], F32)
        dt_t = sb("dt_t", [128, 16], F32)
        adt = sb("adt", [128, 16], F32)
        nacs = sb("nacs", [128, 16], F32)
        D0 = sb("D0", [128, 16], F32)
        dte = sb("dte", [128, 16], F32)
        ddt = sb("ddt", [128, 16], F32)
        cdb = sb("cdb", [128, 16], F32)
        Btok = sb("Btok", [128, 2, 128], BF16)
        xs_tok = sb("xs_tok", [128, 1024], F32)
        xg = sb("xg", [128, 1024], BF16)
        xgd = sb("xgd", [128, 1024], BF16)
        Gm = sb("Gm", [128, 2, 128], BF16)
        Dm = [sb(f"Dm{i}", [128, 512], BF16) for i in range(2)]
        MT = [sb(f"MT{i}", [128, 512], BF16) for i in range(2)]
        H = sb("H", [128, 2, 512], F32)
        Hbf = sb("Hbf", [128, 2, 512], BF16)
        yo = sb("yo", [128, 512], F32)
        tsk = sb("tsk", [128, 512], F32)
        y = sb("y", [128, 1024], F32)
        zs = sb("zs", [128, 1024], F32)
        junk = sb("junkB", [128, 1024], F32)
        ssq = sb("ssq", [128, 1], F32)
        rstd = sb("rstd", [128, 1], F32)
        mixn = sb("mixn", [128, 1024], BF16)
        mixT = sb("mixT", [128, 8, 512], BF16)

        r = {k: Res() for k in ("xsT", "BT", "CT", "dtx", "dt", "adt", "nacs", "D0", "dte", "ddt", "cdb", "Btok",
                                "xs_tok", "xg", "xgd", "Gm", "H", "Hbf", "yo", "tsk", "y", "zs", "junk", "ssq", "rstd",
                                "mixn", "mixT")}
        rxb = [Res(), Res()]; rxbh = [Res(), Res()]; rcv = [Res(), Res()]; rmixT = [Res(), Res()]; rDm = [Res(), Res()]; rMT = [Res(), Res()]
        rhal = [Res() for _ in range(12)]
        pb = [banks[0], banks[1]]; rpb = [Res(), Res()]
        smb = banks[2]; rsmb = Res()
        trb = banks[3]; rtr = Res()
        Rb = [banks[4], banks[5]]; rRb = [Res(), Res()]
        Yb = banks[6]; rY = Res()
        osb = banks[7]; ros = Res()

        I_memset(P, "vector", H[:], 0.0)
        I_memset(P, "vector", Hbf[:], 0.0)
        barrier(P)

        pbi = 0
        quad_ctr = 0
        for sc in range(n_sc):
            T0 = 512 * sc
            for ft in range(12):
                b = pbi; pbi ^= 1
                xbi = ft % 2
                T.mm_group([(pb[b][:, :], Wx[:, kc, ft * 128:(ft + 1) * 128], XT[:, kc, T0:T0 + 512], kc == 0, kc == 7)
                            for kc in range(8)], reads=[rW, rXT], writes=[rpb[b]])
                T.op("scalar", lambda e, o=xb[xbi][:, 3:515], i_=pb[b][:, :]: e.activation(out=o, in_=i_, func=AF.Copy),
                     reads=[rpb[b]], writes=[rxb[xbi]])
                T.op("gpsimd", lambda e, o=xb[xbi][:, 0:3], i_=hal[:, ft, :]: e.tensor_copy(out=o, in_=i_),
                     reads=[rhal[ft]], writes=[rxbh[xbi]])
                T.op("gpsimd", lambda e, o=hal[:, ft, :], i_=xb[xbi][:, 512:515]: e.tensor_copy(out=o, in_=i_),
                     reads=[rxb[xbi]], writes=[rhal[ft]])
                ceng = "vector"
                cv = cvb[xbi]
                T.op(ceng, lambda e, o=cv[:, :], i_=xb[xbi][:, 0:512], s_=cw[:, ft, 0:1]:
                     e.tensor_scalar(out=o, in0=i_, scalar1=s_, scalar2=None, op0=ALU.mult),
                     reads=[rxb[xbi], rxbh[xbi], rC], writes=[rcv[xbi]])
                for j in range(1, 4):
                    T.op(ceng, lambda e, o=cv[:, :], i_=xb[xbi][:, j:j + 512], s_=cw[:, ft, j:j + 1]:
                         e.scalar_tensor_tensor(out=o, in0=i_, scalar=s_, in1=o, op0=ALU.mult, op1=ALU.add),
                         reads=[rxb[xbi], rxbh[xbi], rcv[xbi]], writes=[rcv[xbi]])
                if ft < 8:
                    dst, rd = xsT[:, ft, :], r["xsT"]
                elif ft < 10:
                    dst, rd = BT[:, ft - 8, :], r["BT"]
                else:
                    dst, rd = CT[:, ft - 10, :], r["CT"]
                T.op("scalar", lambda e, o=dst, i_=cv[:, :], b_=cb[:, ft:ft + 1]:
                     e.activation(out=o, in_=i_, func=AF.Silu, bias=b_), reads=[rcv[xbi], rC], writes=[rd])

            for ci in range(4):
                t0 = T0 + 128 * ci
                cc = slice(128 * ci, 128 * ci + 128)
                T.mm_group([(smb[:, 0:16], XT[:, kc, t0:t0 + 128], Wdt[:, kc, :], kc == 0, kc == 7) for kc in range(8)],
                           reads=[rW, rXT], writes=[rsmb])
                for half in range(2):
                    b = pbi; pbi ^= 1
                    T.mm_group([(pb[b][:, :], XT[:, kc, t0:t0 + 128], Wz[:, kc, 512 * half:512 * (half + 1)], kc == 0, kc == 7)
                                for kc in range(8)], reads=[rW, rXT], writes=[rpb[b]])
                    T.op("scalar", lambda e, o=zs[:, 512 * half:512 * (half + 1)], i_=pb[b][:, :]:
                         e.activation(out=o, in_=i_, func=AF.Silu), reads=[rpb[b]], writes=[r["zs"]])
                T.op("vector", lambda e: e.tensor_tensor(out=dtx[:], in0=smb[:, 0:16], in1=dtb_b[:], op=ALU.add),
                     reads=[rsmb, rC], writes=[r["dtx"]])
                T.op("scalar", lambda e: e.activation(out=dtx[:], in_=dtx[:], func=AF.Exp), reads=[], writes=[r["dtx"]])
                T.op("scalar", lambda e: e.activation(out=dt_t[:], in_=dtx[:], func=AF.Ln, bias=1.0),
                     reads=[r["dtx"]], writes=[r["dt"]])
                T.op("vector", lambda e: e.tensor_tensor(out=adt[:], in0=dt_t[:], in1=a_b[:], op=ALU.mult),
                     reads=[r["dt"], rC], writes=[r["adt"]])
                T.mm_group([(smb[:, 16:32], tri[:], adt[:], True, True)], reads=[r["adt"], rC], writes=[rsmb])
                T.mm_group([(smb[:, 32:48], onesF[:], adt[:], True, True)], reads=[r["adt"], rC], writes=[rsmb])
                T.op("vector", lambda e: e.tensor_scalar(out=nacs[:], in0=smb[:, 16:32], scalar1=-1.0, scalar2=None, op0=ALU.mult),
                     reads=[rsmb], writes=[r["nacs"]])
                T.op("vector", lambda e: e.tensor_copy(out=cdb[:], in_=smb[:, 32:48]), reads=[rsmb], writes=[r["cdb"]])
                T.op("vector", lambda e: e.tensor_tensor(out=dte[:], in0=cdb[:], in1=nacs[:], op=ALU.add),
                     reads=[r["cdb"], r["nacs"]], writes=[r["dte"]])
                T.op("scalar", lambda e: e.activation(out=D0[:], in_=nacs[:], func=AF.Exp, scale=-1.0),
                     reads=[r["nacs"]], writes=[r["D0"]])
                T.op("scalar", lambda e: e.activation(out=dte[:], in_=dte[:], func=AF.Exp), reads=[], writes=[r["dte"]])
                T.op("scalar", lambda e: e.activation(out=cdb[:], in_=cdb[:], func=AF.Exp), reads=[r["dte"]], writes=[r["cdb"]])
                T.op("vector", lambda e: e.tensor_tensor(out=ddt[:], in0=dt_t[:], in1=dte[:], op=ALU.mult),
                     reads=[r["dt"], r["dte"]], writes=[r["ddt"]])
                for g in range(2):
                    ev = None
                    for j in range(4):
                        ev = P.op("tensor", lambda e, o=trb[:, 128 * j:128 * (j + 1)], i_=xsT[:, 4 * g + j, cc]:
                                  e.transpose(o, i_, identF[:]), T.deps([r["xsT"], rC], [rtr]) if j == 0 else ())
                    T.mark(ev, [r["xsT"], rC], [rtr])
                    T.op("scalar", lambda e, o=xs_tok[:, 512 * g:512 * (g + 1)], i_=trb[:, :]: e.activation(out=o, in_=i_, func=AF.Copy),
                         reads=[rtr], writes=[r["xs_tok"]])
                for g in range(2):
                    v3 = lambda ap: ap[:, 512 * g:512 * (g + 1)].rearrange("p (h e) -> p h e", e=64)
                    T.op("vector", lambda e, o=v3(xg), i_=v3(xs_tok), b_=bc64(dt_t, 8 * g, 8):
                         e.tensor_tensor(out=o, in0=i_, in1=b_, op=ALU.mult), reads=[r["xs_tok"], r["dt"]], writes=[r["xg"]])
                    T.op("gpsimd", lambda e, o=v3(xgd), i_=v3(xs_tok), b_=bc64(ddt, 8 * g, 8):
                         e.tensor_tensor(out=o, in0=i_, in1=b_, op=ALU.mult), reads=[r["xs_tok"], r["ddt"]], writes=[r["xgd"]])
                trb_b = trb[:].bitcast(BF16)
                ev = None
                for g in range(2):
                    ev = P.op("tensor", lambda e, o=trb_b[:, 128 * g:128 * (g + 1)], i_=BT[:, g, cc]:
                              e.transpose(o, i_, identB[:]), T.deps([r["BT"], rC], [rtr]) if g == 0 else ())
                T.mark(ev, [r["BT"], rC], [rtr])
                T.op("vector", lambda e: e.tensor_copy(out=Btok[:].rearrange("p g n -> p (g n)"), in_=trb_b[:, 0:256]),
                     reads=[rtr], writes=[r["Btok"]])
                for g in range(2):
                    T.mm_group([(smb[:, 128 * (g + 1):128 * (g + 2)], BT[:, g, cc], CT[:, g, cc], True, True)],
                               reads=[r["BT"], r["CT"]], writes=[rsmb])
                    T.op("vector", lambda e, o=Gm[:, g, :], i_=smb[:, 128 * (g + 1):128 * (g + 2)]: e.tensor_copy(out=o, in_=i_),
                         reads=[rsmb], writes=[r["Gm"]])
                for g in range(2):
                    for qd in range(2):
                        qi = quad_ctr % 2; quad_ctr += 1
                        h0 = 8 * g + 4 * qd
                        mms = [(Rb[qi][:, :], identB[:], negm4[:], True, False)]
                        for j in range(4):
                            mms.append((Rb[qi][:, 128 * j:128 * (j + 1)], adt[:, h0 + j:h0 + j + 1].broadcast_to([128, 128]),
                                        tri[:], False, j == 3))
                        T.mm_group(mms, reads=[r["adt"], rC], writes=[rRb[qi]])
                        for j in range(4):
                            T.op("scalar", lambda e, o=Dm[qi][:, 128 * j:128 * (j + 1)], i_=Rb[qi][:, 128 * j:128 * (j + 1)],
                                 b_=nacs[:, h0 + j:h0 + j + 1]: e.activation(out=o, in_=i_, func=AF.Exp, bias=b_),
                                 reads=[rRb[qi], r["nacs"]], writes=[rDm[qi]])
                        T.op("vector", lambda e, o=MT[qi][:].rearrange("p (j l) -> p j l", l=128),
                             i_=Dm[qi][:].rearrange("p (j l) -> p j l", l=128),
                             b_=Gm[:, g, :].unsqueeze(1).broadcast_to([128, 4, 128]):
                             e.tensor_tensor(out=o, in0=i_, in1=b_, op=ALU.mult),
                             reads=[rDm[qi], r["Gm"]], writes=[rMT[qi]])
                        mms = []
                        for j in range(4):
                            hh = 4 * qd + j
                            mms.append((Yb[:, 64 * hh:64 * (hh + 1)], MT[qi][:, 128 * j:128 * (j + 1)],
                                        xg[:, 64 * (h0 + j):64 * (h0 + j + 1)], (qd == 0 and j == 0), False))
                        T.mm_group(mms, reads=[rMT[qi], r["xg"]], writes=[rY])
                    T.mm_group([(osb[:, :], CT[:, g, cc], Hbf[:, g, :], True, True)], reads=[r["CT"], r["Hbf"]], writes=[ros])
                    v3 = lambda ap: ap.rearrange("p (h e) -> p h e", e=64)
                    T.op("vector", lambda e, o=v3(yo[:, :]), i_=v3(osb[:, :]), b_=bc64(D0, 8 * g, 8):
                         e.tensor_tensor(out=o, in0=i_, in1=b_, op=ALU.mult), reads=[ros, r["D0"]], writes=[r["yo"]])
                    T.op("vector", lambda e, o=y[:, 512 * g:512 * (g + 1)], i_=Yb[:, :]:
                         e.tensor_tensor(out=o, in0=i_, in1=yo[:, :], op=ALU.add), reads=[rY, r["yo"]], writes=[r["y"]])
                    T.op("gpsimd", lambda e, o=v3(tsk[:, :]), i_=v3(xs_tok[:, 512 * g:512 * (g + 1)]), b_=bc64(dsk_b, 8 * g, 8):
                         e.tensor_tensor(out=o, in0=i_, in1=b_, op=ALU.mult), reads=[r["xs_tok"], rC], writes=[r["tsk"]])
                    T.op("gpsimd", lambda e, o=y[:, 512 * g:512 * (g + 1)]: e.tensor_tensor(out=o, in0=o, in1=tsk[:, :], op=ALU.add),
                         reads=[r["tsk"]], writes=[r["y"]])
                    T.mm_group([(osb[:, :], Btok[:, g, :], xgd[:, 512 * g:512 * (g + 1)], True, True)],
                               reads=[r["Btok"], r["xgd"]], writes=[ros])
                    T.op("vector", lambda e, o=v3(H[:, g, :]), b_=bc64(cdb, 8 * g, 8):
                         e.tensor_tensor(out=o, in0=o, in1=b_, op=ALU.mult), reads=[r["cdb"]], writes=[r["H"]])
                    T.op("vector", lambda e, o=H[:, g, :]: e.tensor_tensor(out=o, in0=osb[:, :], in1=o, op=ALU.add),
                         reads=[ros], writes=[r["H"]])
                    T.op("gpsimd", lambda e, o=Hbf[:, g, :], i_=H[:, g, :]: e.tensor_copy(out=o, in_=i_),
                         reads=[r["H"]], writes=[r["Hbf"]])
                T.op("vector", lambda e: e.tensor_tensor(out=y[:], in0=y[:], in1=zs[:], op=ALU.mult),
                     reads=[r["zs"]], writes=[r["y"]])
                T.op("vector", lambda e: e.memset(ssq[:], 0.0), reads=[], writes=[r["ssq"]])
                T.op("scalar", lambda e: e.activation(out=junk[:], in_=y[:], func=AF.Square, accum_out=ssq[:]),
                     reads=[r["y"]], writes=[r["junk"], r["ssq"]])
                T.op("scalar", lambda e: e.activation(out=rstd[:], in_=ssq[:], func=AF.Ln, bias=EPS, scale=1.0 / 1024),
                     reads=[r["ssq"]], writes=[r["rstd"]])
                T.op("scalar", lambda e: e.activation(out=rstd[:], in_=rstd[:], func=AF.Exp, scale=-0.5),
                     reads=[], writes=[r["rstd"]])
                T.op("vector", lambda e: e.scalar_tensor_tensor(out=mixn[:], in0=y[:], scalar=rstd[:, 0:1], in1=gs_b[:],
                                                                op0=ALU.mult, op1=ALU.mult),
                     reads=[r["y"], r["rstd"], rC], writes=[r["mixn"]])
                for half in range(2):
                    ev = None
                    for j in range(4):
                        ft = 4 * half + j
                        ev = P.op("tensor", lambda e, o=trb_b[:, 128 * j:128 * (j + 1)], i_=mixn[:, 128 * ft:128 * (ft + 1)]:
                                  e.transpose(o, i_, identB[:]), T.deps([r["mixn"], rC], [rtr]) if j == 0 else ())
                    T.mark(ev, [r["mixn"], rC], [rtr])
                    T.op("vector" if half == 0 else "scalar",
                         (lambda e, o=mixT[:, 4 * half:4 * half + 4, cc], i_=trb_b[:, 0:512].rearrange("p (j t) -> p j t", t=128):
                          e.tensor_copy(out=o, in_=i_)) if half == 0 else
                         (lambda e, o=mixT[:, 4 * half:4 * half + 4, cc], i_=trb_b[:, 0:512].rearrange("p (j t) -> p j t", t=128):
                          e.activation(out=o, in_=i_, func=AF.Copy)),
                         reads=[rtr], writes=[rmixT[half]])
            T.dma("gpsimd", U[1024:2048, T0:T0 + 512].rearrange("(f p) t -> p f t", p=128), mixT[:, :, :], "st_uB",
                  reads=[rmixT[0], rmixT[1]])
        barrier(P)


_NC_CACHE = {}


def _host_inputs(xb, p, c):
    return {
        "xT": np.ascontiguousarray(xb.T), "x": np.ascontiguousarray(xb),
        "w_in": p["w_in"], "w_out": p["w_out"],
        "aug": c["aug"], "mask4": c["mask4"], "tri": c["tri"], "ident": c["ident"], "identb": c["identb"], "negm4": c["negm4"],
        "conv_wT": np.ascontiguousarray(p["conv_w"].T.reshape(12, 128, 4).transpose(1, 0, 2)),
        "conv_b2": np.ascontiguousarray(p["conv_b"].reshape(12, 128).T),
        "dt_bias": p["dt_bias"], "a_log": p["a_log"], "d_skip": p["d_skip"], "ssm_norm_g": p["ssm_norm_g"],
        "att_norm_g2": np.ascontiguousarray(p["att_norm_g"].reshape(8, 128).T),
        "ssm_norm_g2": np.ascontiguousarray(p["ssm_norm_g"].reshape(8, 128).T),
        "ln_g": p["ln_g"], "ln_b": p["ln_b"],
    }


def kernel(x, w_in, conv_w, conv_b, dt_bias, a_log, d_skip, att_norm_g, ssm_norm_g, w_out, ln_g, ln_b):
    x = np.asarray(x, np.float32)
    p = {"w_in": w_in, "conv_w": conv_w, "conv_b": conv_b, "dt_bias": dt_bias, "a_log": a_log, "d_skip": d_skip,
         "att_norm_g": att_norm_g, "ssm_norm_g": ssm_norm_g, "w_out": w_out, "ln_g": ln_g, "ln_b": ln_b}
    p = {k: np.ascontiguousarray(np.asarray(v, np.float32)[0]) for k, v in p.items()}
    c = make_constants()
    n = x.shape[0]
    if "nc" not in _NC_CACHE:
        _NC_CACHE["nc"] = build_program()
    nc = _NC_CACHE["nc"]
    in_maps = [_host_inputs(x[b], p, c) for b in range(n)]
    res = run_bass_kernel_spmd(nc, in_maps, core_ids=list(range(n)))
    return np.stack([np.asarray(r["out"], np.float32) for r in res.results], axis=0)


def phase_B2(nc, P, banks, XT, w_in_v, dr, U, n_sc=8):
    T = Trk(P)
    with contextlib.ExitStack() as st:
        def sb(name, shape, dt, stack=None):
            return (stack or st).enter_context(nc.sbuf_tensor("B_" + name, shape, dt))

        rXT = Res()
        Wz = sb("Wz", [128, 8, 1024], BF16)
        Wx = sb("Wx", [128, 8, 1536], BF16)
        Wdt = sb("Wdt", [128, 8, 16], BF16)
        tri = sb("tri", [128, 128], F32)
        onesF = sb("onesF", [128, 128], F32)
        identF = sb("identF", [128, 128], F32)
        identB = sb("identB", [128, 128], BF16)
        negm4 = sb("negm4", [128, 512], BF16)
        cw = sb("cw", [128, 12, 4], F32)
        cb = sb("cb", [128, 12], F32)
        dtb_b = sb("dtb_b", [128, 16], F32)
        a_b = sb("a_b", [128, 16], F32)
        dsk_b = sb("dsk_b", [128, 16], F32)
        hal = sb("hal", [128, 12, 3], F32)
        rW = Res(); rC = Res()

        with contextlib.ExitStack() as stw:
            wsts = [sb(f"wstB{i}", [128, 8, 512], F32, stw) for i in range(3)]
            rwsts = [Res(), Res(), Res()]
            pieces = [(OFF_ZS, 0, 512, Wz), (OFF_ZS + 512, 512, 512, Wz),
                      (OFF_XBC, 0, 512, Wx), (OFF_XBC + 512, 512, 512, Wx), (OFF_XBC + 1024, 1024, 512, Wx),
                      (OFF_DT, 0, 16, Wdt)]
            for i, (off, dst0, n, Wt) in enumerate(pieces):
                wst, rwst = wsts[i % 3], rwsts[i % 3]
                T.dma("sync" if i % 2 == 0 else "gpsimd", wst[:, :, 0:n], w_in_v[:, :, off:off + n], f"ld_wB{i % 3}{i % 2}", writes=[rwst])
                T.op("vector", lambda e, o=Wt[:, 0:4, dst0:dst0 + n], i_=wst[:, 0:4, 0:n]: e.tensor_copy(out=o, in_=i_),
                     reads=[rwst], writes=[rW])
                T.op("scalar", lambda e, o=Wt[:, 4:8, dst0:dst0 + n], i_=wst[:, 4:8, 0:n]: e.activation(out=o, in_=i_, func=AF.Copy),
                     reads=[rwst], writes=[])
            T.dma("sync", tri[:], dr["tri"][:, :], "ld_c1", writes=[rC])
            T.dma("sync", identF[:], dr["ident"][:, :], "ld_c2", writes=[rC])
            T.dma("sync", negm4[:], dr["negm4"][:, :], "ld_c3", writes=[rC])
            T.dma("sync", cw[:], dr["conv_wT"][:, :, :], "ld_c4", writes=[rC])
            T.dma("sync", cb[:], dr["conv_b2"][:, :], "ld_c5", writes=[rC])
            T.dma("sync", dtb_b[:], dr["dt_bias"].partition_broadcast(128), "ld_c6", writes=[rC])
            T.dma("sync", a_b[:], dr["a_log"].partition_broadcast(128), "ld_c7", writes=[rC])
            T.dma("sync", dsk_b[:], dr["d_skip"].partition_broadcast(128), "ld_c8", writes=[rC])
            barrier(P)
            I_memset(P, "vector", onesF[:], 1.0)
            I_memset(P, "vector", hal[:], 0.0)
            I_copy(P, "vector", identB[:], identF[:])
            I_act(P, a_b[:], a_b[:], AF.Exp)
            barrier(P)
            I_ts(P, "vector", a_b[:], a_b[:], -1.0, None, ALU.mult)
            barrier(P)

        xb = [sb(f"xb{i}", [128, 515], F32) for i in range(2)]
        cvb = [sb(f"cvb{i}", [128, 512], F32) for i in range(2)]
        xtmp = [sb(f"xtmp{i}", [128, 512], F32) for i in range(2)]
        BT = sb("BT", [128, 2, 512], BF16)
        CT = sb("CT", [128, 2, 512], BF16)
        xs_tok = sb("xs_tok", [128, 4, 1024], F32)
        zs = sb("zs", [128, 4, 1024], F32)
        Btok = sb("Btok", [128, 4, 2, 128], BF16)
        Gm = sb("Gm", [128, 2, 4, 128], BF16)
        sm = {k: sb(k, [128, 64], F32) for k in ("dtx", "dt4", "adt4", "nacs4", "D04", "dte4", "ddt4", "cdb4")}
        xg = [sb(f"xg{i}", [128, 1024], BF16) for i in range(2)]
        xgd = [sb(f"xgd{i}", [128, 1024], BF16) for i in range(2)]
        Dm = [sb(f"Dm{i}", [128, 512], BF16) for i in range(2)]
        MT = [sb(f"MT{i}", [128, 512], BF16) for i in range(2)]
        H = sb("H", [128, 2, 512], F32)
        Hbf = sb("Hbf", [128, 2, 512], BF16)
        yo = [sb(f"yo{i}", [128, 512], F32) for i in range(2)]
        tsk = [sb(f"tsk{i}", [128, 512], F32) for i in range(4)]
        y = [sb(f"y{i}", [128, 1024], F32) for i in range(1)]
        ssq = [sb(f"ssq{i}", [128, 1], F32) for i in range(2)]
        rstd = [sb(f"rstd{i}", [128, 1], F32) for i in range(2)]
        mixn = [sb(f"mixn{i}", [128, 1024], BF16) for i in range(2)]
        mixT = [sb(f"mixT{i}", [128, 8, 512], BF16) for i in range(1)]

        R_ = lambda n: [Res() for _ in range(n)]
        rxb, rxbh, rcv, rxtmp = R_(2), R_(2), R_(2), R_(2)
        rhal = R_(12)
        rBT, rCT, rxs, rzs, rBtok, rGm = Res(), Res(), Res(), Res(), Res(), Res()
        rsm = {k: Res() for k in sm}
        rxg, rxgd, rDm, rMT, ryo, rtsk, ry, rssq, rrstd, rmixn = R_(2), R_(2), R_(2), R_(2), R_(2), R_(4), R_(1), R_(2), R_(2), R_(2)
        rmixT = [R_(2)]
        rH, rHbf = R_(2), R_(2)
        rbank = R_(8)
        pb0, pb1, smb, trb, Rb0, Rb1, Yb, osb = banks
        B_PB0, B_PB1, B_SM, B_TR, B_R0, B_R1, B_Y, B_OS = range(8)
        trb_b = trb[:].bitcast(BF16)

        I_memset(P, "vector", H[:], 0.0)
        I_memset(P, "vector", Hbf[:], 0.0)
        barrier(P)

        def v3(ap):
            return ap.rearrange("p (h e) -> p h e", e=64)

        pbi = 0
        for sc in range(n_sc):
            T0 = 512 * sc
            def stage_A(ft):
                nonlocal pbi
                b = pbi; pbi ^= 1
                pbk = banks[b]
                xbi = ft % 2
                T.mm_group([(pbk[:, :], Wx[:, kc, ft * 128:(ft + 1) * 128], XT[:, kc, T0:T0 + 512], kc == 0, kc == 7)
                            for kc in range(8)], reads=[rW, rXT], writes=[rbank[b]])
                T.op("scalar", lambda e, o=xb[xbi][:, 3:515], i_=pbk[:, :]: e.activation(out=o, in_=i_, func=AF.Copy),
                     reads=[rbank[b]], writes=[rxb[xbi]])
                cv = cvb[xbi]
                T.op("scalar", lambda e, o=cv[:, :], i_=pbk[:, :], s_=cw[:, ft, 3:4]: e.activation(out=o, in_=i_, func=AF.Copy, scale=s_),
                     reads=[rbank[b], rC], writes=[rcv[xbi]])
                T.op("gpsimd", lambda e, o=xb[xbi][:, 0:3], i_=hal[:, ft, :]: e.tensor_copy(out=o, in_=i_),
                     reads=[rhal[ft]], writes=[rxbh[xbi]])
                T.op("gpsimd", lambda e, o=hal[:, ft, :], i_=xb[xbi][:, 512:515]: e.tensor_copy(out=o, in_=i_),
                     reads=[rxb[xbi]], writes=[rhal[ft]])

            def stage_B(ft):
                xbi = ft % 2
                cv = cvb[xbi]
                for j in range(3):
                    T.op("vector", lambda e, o=cv[:, :], i_=xb[xbi][:, j:j + 512], s_=cw[:, ft, j:j + 1]:
                         e.scalar_tensor_tensor(out=o, in0=i_, scalar=s_, in1=o, op0=ALU.mult, op1=ALU.add),
                         reads=[rxb[xbi], rxbh[xbi]], writes=[rcv[xbi]])
                if ft < 8:
                    xi = ft % 2
                    T.op("scalar", lambda e, o=xtmp[xi][:, :], i_=cv[:, :], b_=cb[:, ft:ft + 1]:
                         e.activation(out=o, in_=i_, func=AF.Silu, bias=b_), reads=[rcv[xbi], rC], writes=[rxtmp[xi]])
                elif ft < 10:
                    T.op("scalar", lambda e, o=BT[:, ft - 8, :], i_=cv[:, :], b_=cb[:, ft:ft + 1]:
                         e.activation(out=o, in_=i_, func=AF.Silu, bias=b_), reads=[rcv[xbi], rC], writes=[rBT])
                else:
                    T.op("scalar", lambda e, o=CT[:, ft - 10, :], i_=cv[:, :], b_=cb[:, ft:ft + 1]:
                         e.activation(out=o, in_=i_, func=AF.Silu, bias=b_), reads=[rcv[xbi], rC], writes=[rCT])
            def stage_T(ft):
                if ft < 0 or ft >= 8:
                    return
                xi = ft % 2
                tbk = B_TR if ft % 2 == 0 else B_Y
                ev = None
                for ci in range(4):
                    ev = P.op("tensor", lambda e, o=banks[tbk][:, 128 * ci:128 * (ci + 1)], i_=xtmp[xi][:, 128 * ci:128 * (ci + 1)]:
                              e.transpose(o, i_, identF[:]), T.deps([rxtmp[xi], rC], [rbank[tbk]]) if ci == 0 else ())
                T.mark(ev, [rxtmp[xi], rC], [rbank[tbk]])

            def stage_C(ft):
                if ft < 0 or ft >= 8:
                    return
                tbk = B_TR if ft % 2 == 0 else B_Y
                T.op("scalar", lambda e, o=xs_tok[:, :, ft * 128:(ft + 1) * 128], i_=banks[tbk][:, :].rearrange("p (c f) -> p c f", f=128):
                     e.activation(out=o, in_=i_, func=AF.Copy), reads=[rbank[tbk]], writes=[rxs])

            def z_group(zi):
                nonlocal pbi
                ci, half = divmod(zi, 2)
                t0 = T0 + 128 * ci
                b = pbi; pbi ^= 1
                T.mm_group([(banks[b][:, :], XT[:, kc, t0:t0 + 128], Wz[:, kc, 512 * half:512 * (half + 1)], kc == 0, kc == 7)
                            for kc in range(8)], reads=[rW, rXT], writes=[rbank[b]])
                T.op("scalar", lambda e, o=zs[:, ci, 512 * half:512 * (half + 1)], i_=banks[b][:, :]:
                     e.activation(out=o, in_=i_, func=AF.Silu), reads=[rbank[b]], writes=[rzs])

            stage_A(0)
            for ft in range(13):
                if ft + 1 < 12:
                    stage_A(ft + 1)
                stage_T(ft - 1)
                if ft < 12:
                    stage_B(ft)
                stage_C(ft - 1)
                if 2 <= ft < 10:
                    z_group(ft - 2)
            ev = None
            for ci in range(4):
                for g in range(2):
                    j = 2 * ci + g
                    ev = P.op("tensor", lambda e, o=trb_b[:, 128 * j:128 * (j + 1)], i_=BT[:, g, 128 * ci:128 * (ci + 1)]:
                              e.transpose(o, i_, identB[:]), T.deps([rBT, rC], [rbank[B_TR]]) if j == 0 else ())
            T.mark(ev, [rBT, rC], [rbank[B_TR]])
            T.op("vector", lambda e: e.tensor_copy(out=Btok[:].rearrange("p c g n -> p (c g n)"), in_=trb_b[:, 0:1024]),
                 reads=[rbank[B_TR]], writes=[rBtok])
            for g in range(2):
                bk = B_R0 + g
                T.mm_group([(banks[bk][:, 128 * ci:128 * (ci + 1)], BT[:, g, 128 * ci:128 * (ci + 1)], CT[:, g, 128 * ci:128 * (ci + 1)],
                             True, True) for ci in range(4)], reads=[rBT, rCT], writes=[rbank[bk]])
                T.op("vector", lambda e, o=Gm[:, g, :, :].rearrange("p c l -> p (c l)"), i_=banks[bk][:, :]: e.tensor_copy(out=o, in_=i_),
                     reads=[rbank[bk]], writes=[rGm])
            mms = []
            for ci in range(4):
                t0 = T0 + 128 * ci
                for kc in range(8):
                    mms.append((smb[:, 16 * ci:16 * (ci + 1)], XT[:, kc, t0:t0 + 128], Wdt[:, kc, :], kc == 0 and ci == 0, kc == 7))
            T.mm_group(mms, reads=[rW, rXT], writes=[rbank[B_SM]])
            b16 = lambda ap: ap[:, :].unsqueeze(1).broadcast_to([128, 4, 16])
            c4 = lambda ap: ap[:, :].rearrange("p (c h) -> p c h", h=16)
            T.op("vector", lambda e: e.tensor_tensor(out=c4(sm["dtx"]), in0=c4(smb[:, 0:64]), in1=b16(dtb_b), op=ALU.add),
                 reads=[rbank[B_SM], rC], writes=[rsm["dtx"]])
            T.op("scalar", lambda e: e.activation(out=sm["dtx"][:], in_=sm["dtx"][:], func=AF.Exp), reads=[], writes=[rsm["dtx"]])
            T.op("scalar", lambda e: e.activation(out=sm["dt4"][:], in_=sm["dtx"][:], func=AF.Ln, bias=1.0),
                 reads=[rsm["dtx"]], writes=[rsm["dt4"]])
            T.op("vector", lambda e: e.tensor_tensor(out=c4(sm["adt4"]), in0=c4(sm["dt4"]), in1=b16(a_b), op=ALU.mult),
                 reads=[rsm["dt4"], rC], writes=[rsm["adt4"]])
            mms = []
            for ci in range(4):
                mms.append((smb[:, 64 + 16 * ci:64 + 16 * (ci + 1)], tri[:], sm["adt4"][:, 16 * ci:16 * (ci + 1)], False, True))
                mms.append((smb[:, 128 + 16 * ci:128 + 16 * (ci + 1)], onesF[:], sm["adt4"][:, 16 * ci:16 * (ci + 1)], False, True))
            T.mm_group(mms, reads=[rsm["adt4"], rC], writes=[rbank[B_SM]])
            T.op("vector", lambda e: e.tensor_scalar(out=sm["nacs4"][:], in0=smb[:, 64:128], scalar1=-1.0, scalar2=None, op0=ALU.mult),
                 reads=[rbank[B_SM]], writes=[rsm["nacs4"]])
            T.op("vector", lambda e: e.tensor_copy(out=sm["cdb4"][:], in_=smb[:, 128:192]), reads=[rbank[B_SM]], writes=[rsm["cdb4"]])
            T.op("vector", lambda e: e.tensor_tensor(out=sm["dte4"][:], in0=sm["cdb4"][:], in1=sm["nacs4"][:], op=ALU.add),
                 reads=[rsm["cdb4"], rsm["nacs4"]], writes=[rsm["dte4"]])
            T.op("scalar", lambda e: e.activation(out=sm["D04"][:], in_=sm["nacs4"][:], func=AF.Exp, scale=-1.0),
                 reads=[rsm["nacs4"]], writes=[rsm["D04"]])
            T.op("scalar", lambda e: e.activation(out=sm["dte4"][:], in_=sm["dte4"][:], func=AF.Exp), reads=[], writes=[rsm["dte4"]])
            T.op("scalar", lambda e: e.activation(out=sm["cdb4"][:], in_=sm["cdb4"][:], func=AF.Exp), reads=[rsm["dte4"]], writes=[rsm["cdb4"]])
            T.op("vector", lambda e: e.tensor_tensor(out=sm["ddt4"][:], in0=sm["dt4"][:], in1=sm["dte4"][:], op=ALU.mult),
                 reads=[rsm["dt4"], rsm["dte4"]], writes=[rsm["ddt4"]])

            def chunk_fns(ci):
                k = ci % 2
                cc = slice(128 * ci, 128 * ci + 128)
                hs = lambda ap, h0, n: ap[:, 16 * ci + h0:16 * ci + h0 + n]
                bch = lambda ap, h0, n: hs(ap, h0, n).unsqueeze(2).broadcast_to([128, n, 64])
                def prologue(cj):
                    kj = cj % 2
                    for g in range(2):
                        gs = slice(512 * g, 512 * (g + 1))
                        bcj = lambda ap, h0, n: ap[:, 16 * cj + h0:16 * cj + h0 + n].unsqueeze(2).broadcast_to([128, n, 64])
                        T.op("gpsimd", lambda e, o=v3(xg[kj][:, gs]), i_=v3(xs_tok[:, cj, gs]), b_=bcj(sm["dt4"], 8 * g, 8):
                             e.tensor_tensor(out=o, in0=i_, in1=b_, op=ALU.mult), reads=[rxs, rsm["dt4"]], writes=[rxg[kj]])
                        T.op("gpsimd", lambda e, o=v3(xgd[kj][:, gs]), i_=v3(xs_tok[:, cj, gs]), b_=bcj(sm["ddt4"], 8 * g, 8):
                             e.tensor_tensor(out=o, in0=i_, in1=b_, op=ALU.mult), reads=[rxs, rsm["ddt4"]], writes=[rxgd[kj]])
                        T.op("gpsimd", lambda e, o=v3(tsk[2 * kj + g][:, :]), i_=v3(xs_tok[:, cj, gs]), b_=bc64(dsk_b, 8 * g, 8):
                             e.tensor_tensor(out=o, in0=i_, in1=b_, op=ALU.mult), reads=[rxs, rC], writes=[rtsk[2 * kj + g]])

                def emit_R(q):
                    g, qd = divmod(q, 2)
                    h0 = 8 * g + 4 * qd
                    bk = B_R0 + (q % 2)
                    mms = [(banks[bk][:, :], identB[:], negm4[:], True, False)]
                    for j in range(4):
                        mms.append((banks[bk][:, 128 * j:128 * (j + 1)],
                                    hs(sm["adt4"], h0 + j, 1).broadcast_to([128, 128]), tri[:], False, j == 3))
                    T.mm_group(mms, reads=[rsm["adt4"], rC], writes=[rbank[bk]])

                def emit_exp(q):
                    g, qd = divmod(q, 2)
                    h0 = 8 * g + 4 * qd
                    bk = B_R0 + (q % 2)
                    for j in range(4):
                        T.op("scalar", lambda e, o=Dm[q % 2][:, 128 * j:128 * (j + 1)], i_=banks[bk][:, 128 * j:128 * (j + 1)],
                             b_=hs(sm["nacs4"], h0 + j, 1): e.activation(out=o, in_=i_, func=AF.Exp, bias=b_),
                             reads=[rbank[bk], rsm["nacs4"]], writes=[rDm[q % 2]])

                def emit_MT(q):
                    g, qd = divmod(q, 2)
                    T.op("vector", lambda e, o=MT[q % 2][:].rearrange("p (j l) -> p j l", l=128),
                         i_=Dm[q % 2][:].rearrange("p (j l) -> p j l", l=128),
                         b_=Gm[:, g, ci, :].unsqueeze(1).broadcast_to([128, 4, 128]):
                         e.tensor_tensor(out=o, in0=i_, in1=b_, op=ALU.mult), reads=[rDm[q % 2], rGm], writes=[rMT[q % 2]])

                def emit_ydiag(q):
                    g, qd = divmod(q, 2)
                    h0 = 8 * g + 4 * qd
                    ybk = B_Y if g == 0 else B_PB0
                    mms = []
                    for j in range(4):
                        hh = 4 * qd + j
                        mms.append((banks[ybk][:, 64 * hh:64 * (hh + 1)], MT[q % 2][:, 128 * j:128 * (j + 1)],
                                    xg[k][:, 64 * (h0 + j):64 * (h0 + j + 1)], (qd == 0 and j == 0), False))
                    T.mm_group(mms, reads=[rMT[q % 2], rxg[k]], writes=[rbank[ybk]])

                def emit_yoff(g):
                    obk = B_OS if g == 0 else B_PB1
                    T.mm_group([(banks[obk][:, :], CT[:, g, cc], Hbf[:, g, :], True, True)], reads=[rCT, rHbf[g]], writes=[rbank[obk]])

                def emit_comb(g):
                    gs = slice(512 * g, 512 * (g + 1))
                    obk = B_OS if g == 0 else B_PB1
                    ybk = B_Y if g == 0 else B_PB0
                    T.op("vector", lambda e, o=v3(yo[g][:, :]), i_=v3(banks[obk][:, :]), b_=bch(sm["D04"], 8 * g, 8):
                         e.tensor_tensor(out=o, in0=i_, in1=b_, op=ALU.mult), reads=[rbank[obk], rsm["D04"]], writes=[ryo[g]])
                    T.op("vector", lambda e, o=y[0][:, gs], i_=banks[ybk][:, :]: e.tensor_tensor(out=o, in0=i_, in1=yo[g][:, :], op=ALU.add),
                         reads=[rbank[ybk], ryo[g]], writes=[ry[0]])
                    T.op("gpsimd", lambda e, o=y[0][:, gs], t_=tsk[2 * k + g][:, :]: e.tensor_tensor(out=o, in0=o, in1=t_, op=ALU.add),
                         reads=[rtsk[2 * k + g]], writes=[ry[0]])

                def emit_state(g):
                    gs = slice(512 * g, 512 * (g + 1))
                    obk = B_OS if g == 0 else B_PB1
                    T.mm_group([(banks[obk][:, :], Btok[:, ci, g, :], xgd[k][:, gs], True, True)],
                               reads=[rBtok, rxgd[k]], writes=[rbank[obk]])
                    T.op("vector", lambda e, o=v3(H[:, g, :]), b_=bch(sm["cdb4"], 8 * g, 8):
                         e.tensor_tensor(out=o, in0=o, in1=b_, op=ALU.mult), reads=[rsm["cdb4"]], writes=[rH[g]])
                    T.op("vector", lambda e, o=H[:, g, :], i_=banks[obk][:, :]: e.tensor_tensor(out=o, in0=i_, in1=o, op=ALU.add),
                         reads=[rbank[obk]], writes=[rH[g]])
                    T.op("scalar", lambda e, o=Hbf[:, g, :], i_=H[:, g, :]: e.activation(out=o, in_=i_, func=AF.Copy),
                         reads=[rH[g]], writes=[rHbf[g]])

                def early():
                    emit_R(0); emit_R(1)
                    emit_yoff(0); emit_yoff(1)
                    emit_exp(0); emit_exp(1)
                    emit_MT(0); emit_MT(1)
                    emit_R(2); emit_R(3)
                    emit_ydiag(0); emit_ydiag(1)
                    emit_exp(2); emit_exp(3)
                    emit_MT(2); emit_MT(3)
                    emit_ydiag(2); emit_ydiag(3)

                def mid():
                    if ci + 1 < 4:
                        prologue(ci + 1)
                    emit_comb(0)
                    emit_state(0)
                    emit_comb(1)
                    emit_state(1)

                def mid_g(g):
                    emit_comb(g)
                    emit_state(g)

                def exps_g(g):
                    emit_R(2 * g); emit_R(2 * g + 1)
                    emit_yoff(g)
                    emit_exp(2 * g); emit_exp(2 * g + 1)

                def mt_g(g):
                    emit_MT(2 * g); emit_MT(2 * g + 1)
                    emit_ydiag(2 * g); emit_ydiag(2 * g + 1)

                def late():
                    yk, ssk, rsk, mxk = y[0], ssq[k], rstd[k], mixn[k]
                    T.op("vector", lambda e, yk=yk, z_=zs[:, ci, :]: e.tensor_tensor(out=yk[:], in0=yk[:], in1=z_, op=ALU.mult),
                         reads=[rzs], writes=[ry[0]])
                    T.op("vector", lambda e, ssk=ssk: e.memset(ssk[:], 0.0), reads=[], writes=[rssq[k]])
                    T.op("scalar", lambda e, yk=yk, ssk=ssk, mxk=mxk: e.activation(out=mxk[:], in_=yk[:], func=AF.Square, accum_out=ssk[:]),
                         reads=[ry[0]], writes=[rmixn[k], rssq[k]])
                    T.op("scalar", lambda e, ssk=ssk, rsk=rsk: e.activation(out=rsk[:], in_=ssk[:], func=AF.Ln, bias=EPS, scale=1.0 / 1024),
                         reads=[rssq[k]], writes=[rrstd[k]])
                    T.op("scalar", lambda e, rsk=rsk: e.activation(out=rsk[:], in_=rsk[:], func=AF.Exp, scale=-0.5),
                         reads=[], writes=[rrstd[k]])
                    T.op("vector", lambda e, yk=yk, rsk=rsk, mxk=mxk: e.tensor_scalar(out=mxk[:], in0=yk[:], scalar1=rsk[:, 0:1], scalar2=None, op0=ALU.mult),
                         reads=[ry[0], rrstd[k]], writes=[rmixn[k]])
                    mt = mixT[0]
                    for half in range(2):
                        ev = None
                        for j in range(4):
                            ft = 4 * half + j
                            ev = P.op("tensor", lambda e, o=trb_b[:, 128 * j:128 * (j + 1)], i_=mixn[k][:, 128 * ft:128 * (ft + 1)]:
                                      e.transpose(o, i_, identB[:]), T.deps([rmixn[k], rC], [rbank[B_TR]]) if j == 0 else ())
                        T.mark(ev, [rmixn[k], rC], [rbank[B_TR]])
                        src = trb_b[:, 0:512].rearrange("p (j t) -> p j t", t=128)
                        if half == 0:
                            T.op("vector", lambda e, o=mt[:, 0:4, cc], i_=src: e.tensor_copy(out=o, in_=i_),
                                 reads=[rbank[B_TR]], writes=[rmixT[0][0]])
                        else:
                            T.op("scalar", lambda e, o=mt[:, 4:8, cc], i_=src: e.activation(out=o, in_=i_, func=AF.Copy),
                                 reads=[rbank[B_TR]], writes=[rmixT[0][1]])
                return prologue, early, mid, late, mid_g, exps_g, mt_g

            fns = [chunk_fns(ci) for ci in range(4)]
            fns[0][0](0)
            fns[0][1]()
            for ci in range(4):
                if ci + 1 < 4:
                    nx = fns[ci + 1]
                    fns[ci][0](ci + 1)
                    fns[ci][4](0)
                    nx[5](0)
                    fns[ci][4](1)
                    nx[6](0)
                    nx[5](1)
                    fns[ci][3]()
                    nx[6](1)
                else:
                    fns[ci][2]()
                    fns[ci][3]()
            T.dma("gpsimd", U[1024:2048, T0:T0 + 512].rearrange("(f p) t -> p f t", p=128), mixT[0][:, :, :], "st_uB",
                  reads=rmixT[0])
        barrier(P)
```
